# Optimizing a Trainium2 kernel written in Bass

```python
import math
import jax
import jax.numpy as jnp
from jax import lax
import numpy as np

D_MODEL = 1024
BATCH = 2
SEQ = 8192
DEPTH = 2

D_FF = 4 * D_MODEL
NORM_EPS = 1e-6
N_EVEN = (DEPTH + 1) // 2
N_ODD = DEPTH // 2

GDN_HEADS = 8
GDN_HEAD_DIM = 64
GDN_WIDTH = GDN_HEADS * GDN_HEAD_DIM
GDN_CONV = 4
GDN_CHUNK = 64

SC_WIDTH = D_MODEL - GDN_WIDTH
SC_CONV = 3

EVEN_IN = 4 * GDN_WIDTH + 2 * GDN_HEADS + 3 * SC_WIDTH

NSA_HEADS = 16
NSA_HEAD_DIM = D_MODEL // NSA_HEADS
NSA_KV_GROUPS = 4
NSA_HPG = NSA_HEADS // NSA_KV_GROUPS
NSA_KV_WIDTH = NSA_KV_GROUPS * NSA_HEAD_DIM
CMP_BLOCK = 32
CMP_STRIDE = 16
CMP_HIDDEN = 256
SEL_BLOCK = 64
N_SELECT = 16
WINDOW = 512
Q_BLOCK = 128
ODD_IN = NSA_HEADS * NSA_HEAD_DIM + 6 * NSA_KV_WIDTH + 3 * NSA_HEADS

ROPE_THETA = 500000.0
ROPE_DIM = NSA_HEAD_DIM // 4

kernel_name = 'hybrid_gdn_shortconv_nsa_trunk'


def _split_points(sizes):
    return [int(v) for v in np.cumsum(sizes)[:-1]]


def rms_norm(x, w):
    xf = x.astype(jnp.float32)
    y = xf * lax.rsqrt(jnp.mean(xf * xf, axis=-1, keepdims=True) + NORM_EPS)
    return (y * w.astype(jnp.float32)).astype(x.dtype)


def l2_normalize(x):
    xf = x.astype(jnp.float32)
    return xf * lax.rsqrt(jnp.sum(xf * xf, axis=-1, keepdims=True) + NORM_EPS)


def causal_dwconv(x, w):
    k_width = w.shape[0]
    t_len = x.shape[1]
    xp = jnp.pad(x, ((0, 0), (k_width - 1, 0), (0, 0)))
    return sum(xp[:, j:j + t_len] * w[j] for j in range(k_width))


def masked_softmax(s, mask):
    s = jnp.where(mask, s, -jnp.inf)
    m = jnp.max(s, axis=-1, keepdims=True)
    m = jnp.where(jnp.isfinite(m), m, 0.0)
    e = jnp.exp(s - m)
    den = jnp.sum(e, axis=-1, keepdims=True)
    return e / jnp.where(den > 0.0, den, 1.0)


def rope_tables(t_len):
    inv_freq = ROPE_THETA ** (-jnp.arange(0, ROPE_DIM, 2, dtype=jnp.float32) / ROPE_DIM)
    ang = jnp.arange(t_len, dtype=jnp.float32)[:, None] * inv_freq[None, :]
    return jnp.cos(ang), jnp.sin(ang)


def apply_partial_rope(x, cos, sin):
    half = ROPE_DIM // 2
    xf = x.astype(jnp.float32)
    x1, x2, rest = xf[..., :half], xf[..., half:ROPE_DIM], xf[..., ROPE_DIM:]
    c = cos[None, :, None, :]
    s = sin[None, :, None, :]
    return jnp.concatenate([x1 * c - x2 * s, x2 * c + x1 * s, rest], axis=-1).astype(x.dtype)


def squared_relu_mlp(h, w_up, w_down):
    return jnp.square(jax.nn.relu(h @ w_up)) @ w_down


def gated_delta_rule_chunked(q, k, v, g, beta):
    f32 = jnp.float32
    b_sz, t_len, n_h, dk = q.shape
    dv = v.shape[-1]
    c = GDN_CHUNK
    n_chunks = t_len // c

    def chunks(a):
        a = a.astype(f32).reshape((b_sz, n_chunks, c, n_h) + a.shape[3:])
        return jnp.moveaxis(a, (1, 3), (0, 2))

    qc = chunks(q) * (dk ** -0.5)
    kc = chunks(k)
    vc = chunks(v)
    bc = chunks(beta)
    gc = jnp.cumsum(chunks(g), axis=-1)
    incl = jnp.tril(jnp.ones((c, c), bool))
    strict = jnp.tril(jnp.ones((c, c), bool), -1)
    decay = jnp.exp(jnp.where(incl, gc[..., :, None] - gc[..., None, :], -jnp.inf))
    kb = kc * bc[..., None]
    m_low = jnp.where(strict, jnp.einsum('nbhik,nbhjk->nbhij', kb, kc) * decay, 0.0)
    a_mat = m_low + jnp.eye(c, dtype=f32)
    u = lax.linalg.triangular_solve(a_mat, vc * bc[..., None], left_side=True, lower=True, unit_diagonal=True)
    w = lax.linalg.triangular_solve(a_mat, kb * jnp.exp(gc)[..., None], left_side=True, lower=True, unit_diagonal=True)
    attn = jnp.einsum('nbhik,nbhjk->nbhij', qc, kc) * decay

    def step(state, inp):
        q_c, k_c, u_c, w_c, attn_c, g_c = inp
        v_new = u_c - jnp.einsum('bhck,bhkv->bhcv', w_c, state)
        o_c = (jnp.einsum('bhck,bhkv->bhcv', q_c * jnp.exp(g_c)[..., None], state)
               + jnp.einsum('bhij,bhjv->bhiv', attn_c, v_new))
        g_last = g_c[..., -1:]
        state = (state * jnp.exp(g_last)[..., None]
                 + jnp.einsum('bhck,bhcv->bhkv', k_c * jnp.exp(g_last - g_c)[..., None], v_new))
        return state, o_c

    s0 = jnp.zeros((b_sz, n_h, dk, dv), f32)
    _, o = lax.scan(step, s0, (qc, kc, u, w, attn, gc))
    return jnp.moveaxis(o, (0, 2), (1, 3)).reshape(b_sz, t_len, n_h, dv)


def gdn_shortconv_mixer(h, w_in, qkv_conv, a_log, dt_bias, o_norm, sc_conv, w_out):
    f32 = jnp.float32
    b_sz, t_len, _ = h.shape
    sizes = (3 * GDN_WIDTH, GDN_WIDTH, GDN_HEADS, GDN_HEADS, SC_WIDTH, SC_WIDTH, SC_WIDTH)
    qkv, z, a, b, b_gate, c_gate, hs = jnp.split(h @ w_in, _split_points(sizes), axis=-1)
    qkv = jax.nn.silu(causal_dwconv(qkv, qkv_conv))
    q, k, v = [t.reshape(b_sz, t_len, GDN_HEADS, GDN_HEAD_DIM) for t in jnp.split(qkv, 3, axis=-1)]
    g = -jnp.exp(a_log.astype(f32)) * jax.nn.softplus(a.astype(f32) + dt_bias.astype(f32))
    beta = jax.nn.sigmoid(b.astype(f32))
    o = gated_delta_rule_chunked(l2_normalize(q), l2_normalize(k), v, g, beta)
    o = rms_norm(o, o_norm) * jax.nn.silu(z.reshape(b_sz, t_len, GDN_HEADS, GDN_HEAD_DIM).astype(f32))
    y_a = o.reshape(b_sz, t_len, GDN_WIDTH).astype(h.dtype)
    y_b = b_gate * causal_dwconv(c_gate * hs, sc_conv)
    return jnp.concatenate([y_a, y_b], axis=-1) @ w_out


def compress_blocks(x, pos, w1, w2):
    b_sz, t_len, g_sz, d = x.shape
    r = CMP_BLOCK // CMP_STRIDE
    n_cmp = t_len // CMP_STRIDE - r + 1
    ch = x.reshape(b_sz, t_len // CMP_STRIDE, CMP_STRIDE, g_sz, d)
    blk = jnp.concatenate([ch[:, i:i + n_cmp] for i in range(r)], axis=2)
    blk = blk + pos[None, None, :, None, :].astype(x.dtype)
    blk = jnp.moveaxis(blk, 3, 2).reshape(b_sz, n_cmp, g_sz, CMP_BLOCK * d)
    return jax.nn.silu(blk @ w1) @ w2


def gather_blocks(blocks, idx):
    return jax.vmap(jax.vmap(lambda bl, ix: bl[ix]))(blocks, idx)


def nsa_mixer(h, w_in, cmp_pos, cmp_k_w1, cmp_k_w2, cmp_v_w1, cmp_v_w2, w_out):
    f32 = jnp.float32
    b_sz, t_len, _ = h.shape
    n_g, hpg, dh = NSA_KV_GROUPS, NSA_HPG, NSA_HEAD_DIM
    sizes = (NSA_HEADS * dh,) + (NSA_KV_WIDTH,) * 6 + (3 * NSA_HEADS,)
    q, k_c, v_c, k_s, v_s, k_w, v_w, gl = jnp.split(h @ w_in, _split_points(sizes), axis=-1)
    q = q.reshape(b_sz, t_len, NSA_HEADS, dh)
    k_c, v_c, k_s, v_s, k_w, v_w = [a.reshape(b_sz, t_len, n_g, dh) for a in (k_c, v_c, k_s, v_s, k_w, v_w)]
    gates = jax.nn.sigmoid(gl.astype(f32)).reshape(b_sz, t_len, n_g, hpg, 3)
    cos, sin = rope_tables(t_len)
    q_r = apply_partial_rope(q, cos, sin)
    k_s = apply_partial_rope(k_s, cos, sin)
    k_w = apply_partial_rope(k_w, cos, sin)
    kc_blk = compress_blocks(k_c, cmp_pos, cmp_k_w1, cmp_k_w2).astype(f32)
    vc_blk = compress_blocks(v_c, cmp_pos, cmp_v_w1, cmp_v_w2).astype(f32)
    n_cmp = kc_blk.shape[1]
    cmp_end = jnp.arange(n_cmp) * CMP_STRIDE + CMP_BLOCK - 1
    n_sel_blocks = t_len // SEL_BLOCK
    n_sel = min(N_SELECT, n_sel_blocks)
    c_start = jnp.arange(n_cmp)[:, None] * CMP_STRIDE
    s_start = jnp.arange(n_sel_blocks)[None, :] * SEL_BLOCK
    overlap = ((c_start < s_start + SEL_BLOCK) & (c_start + CMP_BLOCK > s_start)).astype(f32)
    ks_blk = jnp.moveaxis(k_s.reshape(b_sz, n_sel_blocks, SEL_BLOCK, n_g, dh), 3, 1)
    vs_blk = jnp.moveaxis(v_s.reshape(b_sz, n_sel_blocks, SEL_BLOCK, n_g, dh), 3, 1)
    pad = ((0, 0), (WINDOW, 0), (0, 0), (0, 0))
    kw_pad = jnp.pad(k_w, pad)
    vw_pad = jnp.pad(v_w, pad)
    scale = dh ** -0.5
    blk_ids = jnp.arange(n_sel_blocks)
    in_blk = jnp.arange(SEL_BLOCK)
    win_off = jnp.arange(Q_BLOCK + WINDOW)

    def query_block(qb):
        q0 = qb * Q_BLOCK
        t = q0 + jnp.arange(Q_BLOCK)
        take = lambda a: lax.dynamic_slice_in_dim(a, q0, Q_BLOCK, axis=1)
        qn = take(q).astype(f32).reshape(b_sz, Q_BLOCK, n_g, hpg, dh) * scale
        qr = take(q_r).astype(f32).reshape(b_sz, Q_BLOCK, n_g, hpg, dh) * scale
        p_c = masked_softmax(jnp.einsum('bqghd,bcgd->bghqc', qn, kc_blk), cmp_end[None, :] <= t[:, None])
        o_c = jnp.einsum('bghqc,bcgd->bqghd', p_c, vc_blk)
        imp = jnp.einsum('bghqc,cs->bgqs', p_c, overlap)
        cur = (t // SEL_BLOCK)[:, None]
        js = blk_ids[None, :]
        forced = (js == 0) | (js == cur) | (js == cur - 1)
        imp = jnp.where(js > cur, -jnp.inf, jnp.where(forced, jnp.inf, imp))
        _, idx = lax.top_k(imp, n_sel)
        k_sel = gather_blocks(ks_blk, idx).astype(f32).reshape(b_sz, n_g, Q_BLOCK, n_sel * SEL_BLOCK, dh)
        v_sel = gather_blocks(vs_blk, idx).astype(f32).reshape(b_sz, n_g, Q_BLOCK, n_sel * SEL_BLOCK, dh)
        kpos = (idx[..., None] * SEL_BLOCK + in_blk).reshape(b_sz, n_g, Q_BLOCK, n_sel * SEL_BLOCK)
        p_s = masked_softmax(jnp.einsum('bqghd,bgqkd->bghqk', qr, k_sel), (kpos <= t[:, None])[:, :, None])
        o_s = jnp.einsum('bghqk,bgqkd->bqghd', p_s, v_sel)
        spos = q0 - WINDOW + win_off
        dist = t[:, None] - spos[None, :]
        m_w = (dist >= 0) & (dist < WINDOW) & (spos[None, :] >= 0)
        kwin = lax.dynamic_slice_in_dim(kw_pad, q0, Q_BLOCK + WINDOW, axis=1).astype(f32)
        vwin = lax.dynamic_slice_in_dim(vw_pad, q0, Q_BLOCK + WINDOW, axis=1).astype(f32)
        p_w = masked_softmax(jnp.einsum('bqghd,bkgd->bghqk', qr, kwin), m_w)
        o_w = jnp.einsum('bghqk,bkgd->bqghd', p_w, vwin)
        g = take(gates)
        o = g[..., 0:1] * o_c + g[..., 1:2] * o_s + g[..., 2:3] * o_w
        return o.reshape(b_sz, Q_BLOCK, NSA_HEADS * dh).astype(h.dtype)

    out = lax.map(query_block, jnp.arange(t_len // Q_BLOCK))
    out = jnp.moveaxis(out, 0, 1).reshape(b_sz, t_len, NSA_HEADS * dh)
    return out @ w_out


def setup_inputs(seed: int = 0) -> dict:
    key = jax.random.key(seed)
    ks = jax.random.split(key, 24)
    f32 = jnp.float32

    def nrm(k, shape, scale):
        return jax.random.normal(k, shape, f32) * scale

    def gain(k, shape):
        return 1.0 + 0.01 * jax.random.normal(k, shape, f32)

    dt = jnp.exp(jax.random.uniform(ks[9], (N_EVEN, GDN_HEADS), f32, math.log(1e-3), math.log(0.1)))
    return {
        'x': nrm(ks[0], (BATCH, SEQ, D_MODEL), 1.0),
        'mix_norm': gain(ks[1], (DEPTH, D_MODEL)),
        'mlp_norm': gain(ks[2], (DEPTH, D_MODEL)),
        'w_up': nrm(ks[3], (DEPTH, D_MODEL, D_FF), D_MODEL ** -0.5),
        'w_down': nrm(ks[4], (DEPTH, D_FF, D_MODEL), D_FF ** -0.5),
        'final_norm': gain(ks[5], (D_MODEL,)),
        'ev_w_in': nrm(ks[6], (N_EVEN, D_MODEL, EVEN_IN), D_MODEL ** -0.5),
        'ev_qkv_conv': nrm(ks[7], (N_EVEN, GDN_CONV, 3 * GDN_WIDTH), GDN_CONV ** -0.5),
        'ev_a_log': jnp.log(jax.random.uniform(ks[8], (N_EVEN, GDN_HEADS), f32, 1.0, 16.0)),
        'ev_dt_bias': dt + jnp.log(-jnp.expm1(-dt)),
        'ev_o_norm': gain(ks[10], (N_EVEN, GDN_HEAD_DIM)),
        'ev_sc_conv': nrm(ks[11], (N_EVEN, SC_CONV, SC_WIDTH), SC_CONV ** -0.5),
        'ev_w_out': nrm(ks[12], (N_EVEN, GDN_WIDTH + SC_WIDTH, D_MODEL), D_MODEL ** -0.5),
        'od_w_in': nrm(ks[13], (N_ODD, D_MODEL, ODD_IN), D_MODEL ** -0.5),
        'od_cmp_pos': nrm(ks[14], (N_ODD, CMP_BLOCK, NSA_HEAD_DIM), 0.1),
        'od_cmp_k_w1': nrm(ks[15], (N_ODD, CMP_BLOCK * NSA_HEAD_DIM, CMP_HIDDEN), (CMP_BLOCK * NSA_HEAD_DIM) ** -0.5),
        'od_cmp_k_w2': nrm(ks[16], (N_ODD, CMP_HIDDEN, NSA_HEAD_DIM), CMP_HIDDEN ** -0.5),
        'od_cmp_v_w1': nrm(ks[17], (N_ODD, CMP_BLOCK * NSA_HEAD_DIM, CMP_HIDDEN), (CMP_BLOCK * NSA_HEAD_DIM) ** -0.5),
        'od_cmp_v_w2': nrm(ks[18], (N_ODD, CMP_HIDDEN, NSA_HEAD_DIM), CMP_HIDDEN ** -0.5),
        'od_w_out': nrm(ks[19], (N_ODD, NSA_HEADS * NSA_HEAD_DIM, D_MODEL), D_MODEL ** -0.5),
    }


def reference(x, mix_norm, mlp_norm, w_up, w_down, final_norm,
              ev_w_in, ev_qkv_conv, ev_a_log, ev_dt_bias, ev_o_norm, ev_sc_conv, ev_w_out,
              od_w_in, od_cmp_pos, od_cmp_k_w1, od_cmp_k_w2, od_cmp_v_w1, od_cmp_v_w2, od_w_out):
    for layer in range(DEPTH):
        h = rms_norm(x, mix_norm[layer])
        i = layer // 2
        if layer % 2 == 0:
            x = x + gdn_shortconv_mixer(h, ev_w_in[i], ev_qkv_conv[i], ev_a_log[i], ev_dt_bias[i],
                                        ev_o_norm[i], ev_sc_conv[i], ev_w_out[i])
        else:
            x = x + nsa_mixer(h, od_w_in[i], od_cmp_pos[i], od_cmp_k_w1[i], od_cmp_k_w2[i],
                              od_cmp_v_w1[i], od_cmp_v_w2[i], od_w_out[i])
        x = x + squared_relu_mlp(rms_norm(x, mlp_norm[layer]), w_up[layer], w_down[layer])
    return rms_norm(x, final_norm)
```

```python
import numpy as np
import ml_dtypes
from contextlib import ExitStack
import concourse.bass as bass
import concourse.mybir as mybir
from concourse.bass_utils import run_bass_kernel_spmd

F32 = mybir.dt.float32
BF16 = mybir.dt.bfloat16
AF = mybir.ActivationFunctionType
ALU = mybir.AluOpType
AX = mybir.AxisListType

NCORES = 8
D_MODEL = 1024
D_FF = 4096
EPS = 1e-6
_DBG = {}


class _Ins:
    __slots__ = ("eng", "fn", "deps", "isdma", "need_inc", "tok", "q")

    def __init__(self, eng, fn, isdma):
        self.eng = eng
        self.fn = fn
        self.deps = []
        self.isdma = isdma
        self.need_inc = isdma
        self.tok = None


class Prog:
    ENGS = ("pe", "act", "dve", "pool", "sp")
    NDSEM = 12

    def __init__(self, nc, es):
        self.nc = nc
        self.es = es
        self.streams = {e: [] for e in self.ENGS}
        self.last_write = {}
        self.readers = {}
        self.all_dma = []

    def _add(self, eng, fn, reads, writes, isdma):
        ins = _Ins(eng, fn, isdma)
        excl = [k for k in reads if isinstance(k, tuple) and k[0] in ("bk", "ps")]
        if excl:
            writes = list(writes) + [k for k in excl if k not in writes]
        deps = {}
        for k in reads:
            w = self.last_write.get(k)
            if w is not None:
                deps[id(w)] = (w, "raw")
        for k in writes:
            w = self.last_write.get(k)
            if w is not None and id(w) not in deps:
                deps[id(w)] = (w, "waw")
            for r in self.readers.get(k, ()):
                if id(r) not in deps:
                    deps[id(r)] = (r, "war")
        for d, kind in deps.values():
            if d is ins:
                continue
            if d.eng == eng and not d.isdma and not isdma:
                if kind != "raw" or eng == "pe":
                    continue
            d.need_inc = True
            ins.deps.append(d)
        for k in reads:
            self.readers.setdefault(k, []).append(ins)
        for k in writes:
            self.last_write[k] = ins
            self.readers[k] = []
        self.streams[eng].append(ins)
        if isdma:
            self.all_dma.append(ins)
        return ins

    def op(self, eng, fn, reads=(), writes=()):
        return self._add(eng, fn, reads, writes, False)

    def dma(self, out, in_, reads=(), writes=(), q="sp"):
        return self._add(q, lambda e: e.dma_start(out=out, in_=in_), reads, writes, True)

    def _selfcheck(self, csem, dsem):
        val = {}
        pos = {e: 0 for e in self.ENGS}
        dl = {e: [i for i in self.streams[e] if i.isdma] for e in self.ENGS}
        total = sum(len(v) for v in self.streams.values())
        done = 0
        while done < total:
            progress = False
            for e in self.ENGS:
                while pos[e] < len(self.streams[e]):
                    ins = self.streams[e][pos[e]]
                    waits = [d.tok for d in ins.deps]
                    if ins.isdma and ins.q >= self.NDSEM:
                        waits.append(dl[e][ins.q - self.NDSEM].tok)
                    if any(w is None for w in waits):
                        raise RuntimeError("dep without token")
                    if all(val.get(id(sm), 0) >= v for sm, v in waits):
                        if ins.need_inc:
                            val[id(ins.tok[0])] = val.get(id(ins.tok[0]), 0) + (16 if ins.isdma else 1)
                            if val[id(ins.tok[0])] != ins.tok[1]:
                                raise RuntimeError("token mismatch %s %s" % (val[id(ins.tok[0])], ins.tok[1]))
                        pos[e] += 1
                        done += 1
                        progress = True
                    else:
                        break
            if not progress:
                raise RuntimeError("deadlock in program: %s" % {e: (pos[e], len(self.streams[e])) for e in self.ENGS})

    def emit(self):
        nc = self.nc
        es = self.es
        csem = {e: es.enter_context(nc.semaphore("cs_" + e)) for e in self.ENGS}
        dsem = {e: [es.enter_context(nc.semaphore("ds_%s%d" % (e, i))) for i in range(self.NDSEM)]
                for e in self.ENGS if any(i.isdma for i in self.streams[e])}
        for e in self.ENGS:
            cnt = 0
            dcnt = 0
            for ins in self.streams[e]:
                if ins.isdma:
                    ins.tok = (dsem[e][dcnt % self.NDSEM], 16 * (dcnt // self.NDSEM + 1))
                    ins.q = dcnt
                    dcnt += 1
                elif ins.need_inc:
                    cnt += 1
                    ins.tok = (csem[e], cnt)
        streams = self.streams
        NDSEM = self.NDSEM
        self._selfcheck(csem, dsem)

        def run(ename, eng):
            seen = {}
            dlist = [i for i in streams[ename] if i.isdma]
            for ins in streams[ename]:
                waits = [d.tok for d in ins.deps]
                if ins.isdma and ins.q >= NDSEM:
                    waits.append(dlist[ins.q - NDSEM].tok)
                for sem, val in waits:
                    if seen.get(id(sem), 0) >= val:
                        continue
                    seen[id(sem)] = val
                    eng.wait_ge(sem, val)
                r = ins.fn(eng)
                if ins.need_inc:
                    r.then_inc(ins.tok[0], 16 if ins.isdma else 1)
            for ins in dlist[-NDSEM:]:
                sem, val = ins.tok
                if seen.get(id(sem), 0) >= val:
                    continue
                seen[id(sem)] = val
                eng.wait_ge(sem, val)

        with nc.Block() as block:
            @block.tensor
            def _(e):
                run("pe", e)

            @block.scalar
            def _(e):
                run("act", e)

            @block.vector
            def _(e):
                run("dve", e)

            @block.gpsimd
            def _(e):
                run("pool", e)

            @block.sync
            def _(e):
                run("sp", e)


def _sb(nc, es, name, shape, dt):
    return es.enter_context(nc.sbuf_tensor(name, list(shape), dt))


def _ps(nc, es, name, shape, dt):
    return es.enter_context(nc.psum_tensor(name, list(shape), dt))


def build_dense(NT, has_mix, has_mlp, tail, n_in_cols=0):
    nc = bass.Bass("TRN2", target_bir_lowering=False)
    GT = 4
    NG = NT // (128 * GT)
    x = nc.dram_tensor("x", [NT, D_MODEL], F32, kind="ExternalInput").ap()
    ident_d = nc.dram_tensor("ident", [128, 128], F32, kind="ExternalInput").ap()
    if has_mix:
        yT = nc.dram_tensor("yT", [D_MODEL, NT], F32, kind="ExternalInput").ap()
        w_out = nc.dram_tensor("w_out", [D_MODEL, D_MODEL], F32, kind="ExternalInput").ap()
    if has_mlp:
        nw_mlp = nc.dram_tensor("nw_mlp", [128, D_MODEL], F32, kind="ExternalInput").ap()
        w_up = nc.dram_tensor("w_up", [D_MODEL, D_FF], F32, kind="ExternalInput").ap()
        w_down = nc.dram_tensor("w_down", [D_FF, D_MODEL], F32, kind="ExternalInput").ap()
    nw_tail = nc.dram_tensor("nw_tail", [128, D_MODEL], F32, kind="ExternalInput").ap()
    if tail == "inproj":
        w_in = nc.dram_tensor("w_in", [D_MODEL, n_in_cols], F32, kind="ExternalInput").ap()
        p_out = nc.dram_tensor("p_out", [NT, n_in_cols], F32, kind="ExternalOutput").ap()
        if has_mix or has_mlp:
            x_out = nc.dram_tensor("x_out", [NT, D_MODEL], F32, kind="ExternalOutput").ap()
    else:
        out = nc.dram_tensor("out", [NT, D_MODEL], F32, kind="ExternalOutput").ap()

    with ExitStack() as es:
        P = Prog(nc, es)
        ident = _sb(nc, es, "ident_sb", [128, 128], F32)
        xg = [_sb(nc, es, "xg%d" % i, [128, GT, D_MODEL], F32) for i in range(2)]
        hn = [_sb(nc, es, "hn%d" % i, [128, D_MODEL], F32) for i in range(2)]
        hT = _sb(nc, es, "hT", [128, 8, 128 * GT], BF16)
        wst = [_sb(nc, es, "wst%d" % i, [128, 4096], F32) for i in range(2)]
        wbf = [_sb(nc, es, "wbf%d" % i, [128, 4096], BF16) for i in range(2)]
        nwt = _sb(nc, es, "nwt", [128, D_MODEL], F32)
        ss = _sb(nc, es, "ss", [128, 8], F32)
        sq_scr = _sb(nc, es, "sq_scr", [128, D_MODEL], F32)
        ot = [_sb(nc, es, "ot%d" % i, [128, 512], F32) for i in range(2)]
        if has_mlp:
            nwm = _sb(nc, es, "nwm", [128, D_MODEL], F32)
            aT = _sb(nc, es, "aT", [128, 32, 128 * GT], BF16)
            rl = [_sb(nc, es, "rl%d" % i, [128, 512], F32) for i in range(2)]
        if has_mix:
            yst = _sb(nc, es, "yst", [128, 8, 128 * GT], F32)
        ps = [_ps(nc, es, "ps%d" % i, [128, 512], F32) for i in range(8)]

        P.dma(ident[:], ident_d, writes=["ident"])
        P.dma(nwt[:], nw_tail, writes=["nwt"])
        if has_mlp:
            P.dma(nwm[:], nw_mlp, writes=["nwm"])

        cnt = {"w": 0, "n": 0, "ps": 0, "ot": 0, "rl": 0}

        def load_w(src_ap, shape3):
            i = cnt["w"] % 2
            cnt["w"] += 1
            a, b = shape3
            stv = wst[i][:, 0:a * b].rearrange("p (a b) -> p a b", a=a)
            bfv = wbf[i][:, 0:a * b].rearrange("p (a b) -> p a b", a=a)
            P.dma(stv, src_ap, writes=[("wst", i)])
            ceng = "pool" if (cnt["w"] % 2 == 0) else "act"
            if ceng == "pool":
                P.op("pool", lambda e: e.tensor_copy(out=wbf[i][:, 0:a * b], in_=wst[i][:, 0:a * b]),
                     reads=[("wst", i)], writes=[("wbf", i)])
            else:
                P.op("act", lambda e: e.copy(out=wbf[i][:, 0:a * b], in_=wst[i][:, 0:a * b]),
                     reads=[("wst", i)], writes=[("wbf", i)])
            return bfv, ("wbf", i)

        def kmajor(w_ap, c0, ncols):
            return w_ap.rearrange("(kc p) c -> p kc c", p=128)[:, :, c0:c0 + ncols]

        def norm_to_hT(xb, gkey, nw_tile, nwkey):
            for t in range(GT):
                j = cnt["n"] % 2
                cnt["n"] += 1
                col = cnt["n"] % 8
                P.op("act", lambda e, t=t, col=col: e.activation(
                    out=sq_scr[:], in_=xg[xb][:, t, :], func=AF.Square, accum_out=ss[:, col:col + 1]),
                    reads=[gkey], writes=["sq_scr", ("ss", col)])
                P.op("act", lambda e, col=col: e.activation(
                    out=ss[:, col:col + 1], in_=ss[:, col:col + 1], func=AF.Sqrt,
                    scale=1.0 / D_MODEL, bias=EPS), reads=[("ss", col)], writes=[("ss", col)])
                P.op("dve", lambda e, col=col: e.reciprocal(out=ss[:, col:col + 1], in_=ss[:, col:col + 1]),
                     reads=[("ss", col)], writes=[("ss", col)])
                P.op("dve", lambda e, t=t, j=j, col=col: e.scalar_tensor_tensor(
                    out=hn[j][:], in0=xg[xb][:, t, :], scalar=ss[:, col:col + 1], in1=nw_tile[:],
                    op0=ALU.mult, op1=ALU.mult), reads=[gkey, ("ss", col), nwkey], writes=[("hn", j)])
                for half in range(2):
                    b = cnt["ps"] % 8
                    cnt["ps"] += 1
                    for q in range(4):
                        kc = half * 4 + q
                        P.op("pe", lambda e, b=b, q=q, kc=kc, j=j: e.transpose(
                            out=ps[b][:, q * 128:(q + 1) * 128], in_=hn[j][:, kc * 128:(kc + 1) * 128],
                            identity=ident[:]), reads=[("hn", j), "ident"], writes=[("ps", b)])
                    P.op("dve", lambda e, b=b, half=half, t=t: e.tensor_copy(
                        out=hT[:, half * 4:half * 4 + 4, t * 128:(t + 1) * 128],
                        in_=ps[b][:].rearrange("p (q c) -> p q c", q=4)),
                        reads=[("ps", b)], writes=["hT"])

        def do_group(g):
            xb = g % 2
            gkey = ("xg", xb)
            tok0 = g * GT * 128
            P.dma(xg[xb][:], x[tok0:tok0 + GT * 128, :].rearrange("(t p) d -> p t d", p=128), writes=[gkey])

            if has_mix:
                P.dma(yst[:], yT.rearrange("(kc p) n -> p kc n", p=128)[:, :, tok0:tok0 + GT * 128],
                      writes=["yst"])
                P.op("pool", lambda e: e.tensor_copy(out=hT[:], in_=yst[:]), reads=["yst"], writes=["hT"])
                for c in range(2):
                    wv, wk = load_w(kmajor(w_out, c * 512, 512), (8, 512))
                    for t in range(GT):
                        b = cnt["ps"] % 8
                        cnt["ps"] += 1
                        for kc in range(8):
                            P.op("pe", lambda e, b=b, kc=kc, t=t, wv=wv: e.matmul(
                                ps[b][:], lhsT=hT[:, kc, t * 128:(t + 1) * 128], rhs=wv[:, kc, :],
                                start=(kc == 0), stop=(kc == 7)), reads=["hT", wk], writes=[("ps", b)])
                        P.op("dve", lambda e, b=b, t=t, c=c: e.tensor_tensor(
                            out=xg[xb][:, t, c * 512:(c + 1) * 512], in0=xg[xb][:, t, c * 512:(c + 1) * 512],
                            in1=ps[b][:], op=ALU.add), reads=[("ps", b), gkey], writes=[gkey])

            if has_mlp:
                norm_to_hT(xb, gkey, nwm, "nwm")
                for uc in range(8):
                    wv, wk = load_w(kmajor(w_up, uc * 512, 512), (8, 512))
                    for fi in range(4):
                        fc = uc * 4 + fi
                        b = cnt["ps"] % 8
                        cnt["ps"] += 1
                        for kc in range(8):
                            P.op("pe", lambda e, b=b, kc=kc, fi=fi, wv=wv: e.matmul(
                                ps[b][:], lhsT=wv[:, kc, fi * 128:(fi + 1) * 128], rhs=hT[:, kc, :],
                                start=(kc == 0), stop=(kc == 7)), reads=["hT", wk], writes=[("ps", b)])
                        r = cnt["rl"] % 2
                        cnt["rl"] += 1
                        P.op("act", lambda e, b=b, r=r: e.activation(out=rl[r][:], in_=ps[b][:], func=AF.Relu),
                             reads=[("ps", b)], writes=[("rl", r)])
                        P.op("dve", lambda e, r=r, fc=fc: e.tensor_tensor(
                            out=aT[:, fc, :], in0=rl[r][:], in1=rl[r][:], op=ALU.mult),
                            reads=[("rl", r)], writes=["aT"])
                for dc in range(8):
                    wv, wk = load_w(w_down.rearrange("(fc p) d -> p fc d", p=128)[:, dc * 4:dc * 4 + 4, :],
                                    (4, 1024))
                    for j in range(4):
                        fc = dc * 4 + j
                        for t in range(GT):
                            for half in range(2):
                                b = t * 2 + half
                                P.op("pe", lambda e, b=b, fc=fc, t=t, j=j, half=half, wv=wv: e.matmul(
                                    ps[b][:], lhsT=aT[:, fc, t * 128:(t + 1) * 128],
                                    rhs=wv[:, j, half * 512:(half + 1) * 512],
                                    start=(fc == 0), stop=(fc == 31)), reads=["aT", wk], writes=[("ps", b)])
                for t in range(GT):
                    for half in range(2):
                        b = t * 2 + half
                        P.op("dve", lambda e, b=b, t=t, half=half: e.tensor_tensor(
                            out=xg[xb][:, t, half * 512:(half + 1) * 512],
                            in0=xg[xb][:, t, half * 512:(half + 1) * 512], in1=ps[b][:], op=ALU.add),
                            reads=[("ps", b), gkey], writes=[gkey])
                cnt["ps"] = 0

            if tail == "inproj":
                if has_mix or has_mlp:
                    P.dma(x_out[tok0:tok0 + GT * 128, :].rearrange("(t p) d -> p t d", p=128), xg[xb][:],
                          reads=[gkey])
                norm_to_hT(xb, gkey, nwt, "nwt")
                c0 = 0
                while c0 < n_in_cols:
                    ncol = min(512, n_in_cols - c0)
                    wv, wk = load_w(kmajor(w_in, c0, ncol), (8, ncol))
                    for t in range(GT):
                        b = cnt["ps"] % 8
                        cnt["ps"] += 1
                        for kc in range(8):
                            P.op("pe", lambda e, b=b, kc=kc, t=t, wv=wv, ncol=ncol: e.matmul(
                                ps[b][:, 0:ncol], lhsT=hT[:, kc, t * 128:(t + 1) * 128], rhs=wv[:, kc, :],
                                start=(kc == 0), stop=(kc == 7)), reads=["hT", wk], writes=[("ps", b)])
                        o = cnt["ot"] % 2
                        cnt["ot"] += 1
                        P.op("act", lambda e, b=b, o=o, ncol=ncol: e.copy(out=ot[o][:, 0:ncol], in_=ps[b][:, 0:ncol]),
                             reads=[("ps", b)], writes=[("ot", o)])
                        P.dma(p_out[tok0 + t * 128:tok0 + (t + 1) * 128, c0:c0 + ncol], ot[o][:, 0:ncol],
                              reads=[("ot", o)])
                    c0 += ncol
            else:
                for t in range(GT):
                    col = cnt["n"] % 8
                    cnt["n"] += 1
                    P.op("act", lambda e, t=t, col=col: e.activation(
                        out=sq_scr[:], in_=xg[xb][:, t, :], func=AF.Square, accum_out=ss[:, col:col + 1]),
                        reads=[gkey], writes=["sq_scr", ("ss", col)])
                    P.op("act", lambda e, col=col: e.activation(
                        out=ss[:, col:col + 1], in_=ss[:, col:col + 1], func=AF.Sqrt,
                        scale=1.0 / D_MODEL, bias=EPS), reads=[("ss", col)], writes=[("ss", col)])
                    P.op("dve", lambda e, col=col: e.reciprocal(out=ss[:, col:col + 1], in_=ss[:, col:col + 1]),
                         reads=[("ss", col)], writes=[("ss", col)])
                    j = cnt["n"] % 2
                    P.op("dve", lambda e, t=t, j=j, col=col: e.scalar_tensor_tensor(
                        out=hn[j][:], in0=xg[xb][:, t, :], scalar=ss[:, col:col + 1], in1=nwt[:],
                        op0=ALU.mult, op1=ALU.mult), reads=[gkey, ("ss", col), "nwt"], writes=[("hn", j)])
                    P.dma(out[tok0 + t * 128:tok0 + (t + 1) * 128, :], hn[j][:], reads=[("hn", j)])
        for g in range(NG):
            do_group(g)
        P.emit()
    return nc


def gdn_consts():
    p = np.arange(128)
    same = (p[:, None] // 64) == (p[None, :] // 64)
    c = np.zeros((8, 128, 128), np.float32)
    c[0] = np.eye(128)
    c[1] = same & (p[:, None] <= p[None, :])
    c[2] = same
    c[3] = (p[:, None] < 64) * np.ones((1, 128))
    c[4] = (p[:, None] >= 64) * np.ones((1, 128))
    c[5] = 0.125 * (same & (p[:, None] <= p[None, :]))
    c[6] = same & (p[None, :] < p[:, None])
    c[7] = 1.0
    return np.ascontiguousarray(c.transpose(1, 0, 2))


def build_gdn(T):
    nc = bass.Bass("TRN2", target_bir_lowering=False)
    NTL = T // 128
    NCH = T // 64
    SEG = min(T, 2048)
    NSEG = T // SEG
    qkvT = nc.dram_tensor("qkvT", [3, 128, T], F32, kind="ExternalInput").ap()
    cw_d = nc.dram_tensor("cw", [128, 12], F32, kind="ExternalInput").ap()
    z_d = nc.dram_tensor("z_l", [64, NCH, 128], F32, kind="ExternalInput").ap()
    ab_d = nc.dram_tensor("ab", [128, 4, NTL], F32, kind="ExternalInput").ap()
    hp_d = nc.dram_tensor("hp", [128, 4], F32, kind="ExternalInput").ap()
    onw_d = nc.dram_tensor("onw", [64, 64], F32, kind="ExternalInput").ap()
    scT = nc.dram_tensor("scT", [3, 128, T], F32, kind="ExternalInput").ap()
    scw_d = nc.dram_tensor("scw", [128, 3], F32, kind="ExternalInput").ap()
    cst_d = nc.dram_tensor("cst", [128, 8, 128], F32, kind="ExternalInput").ap()
    ya_d = nc.dram_tensor("ya_l", [64, NCH, 128], F32, kind="ExternalOutput").ap()
    yb_d = nc.dram_tensor("ybT", [128, T], F32, kind="ExternalOutput").ap()

    with ExitStack() as es:
        P = Prog(nc, es)
        S = lambda name, shape, dt=F32: _sb(nc, es, name, shape, dt)
        cst = S("cst_sb", [128, 8, 128])
        ident, LTm, BO, SEL0, SEL1, MUs, ML, ONES = [cst[:, i, :] for i in range(8)]
        ident_t = S("ident_t", [128, 128])
        ident = ident_t[:]
        cw = S("cw_sb", [128, 12])
        hp = S("hp_sb", [128, 4])
        onw = S("onw_sb", [64, 64])
        scw = S("scw_sb", [128, 3])
        ab = S("ab_sb", [128, 4, NTL])
        fT = [S("fT%d" % s, [128, T]) for s in range(3)]
        raw = S("raw", [128, SEG + 3])
        raw2 = S("raw2", [128, SEG + 3])
        raw3 = S("raw3", [128, SEG + 3])
        acc = S("acc", [128, SEG])
        sqs = S("sqs", [128, SEG])
        rn = [S("rn%d" % i, [128, 512]) for i in range(2)]
        ps = [_ps(nc, es, "ps%d" % i, [128, 512], F32) for i in range(8)]

        P.dma(cst[:], cst_d, writes=["cst"])
        P.dma(ident_t[:], cst_d[:, 0, :], writes=["cst"])
        P.dma(cw[:], cw_d, writes=["cw"])
        P.dma(hp[:], hp_d, writes=["hp"])
        P.dma(onw[:], onw_d, writes=["onw"])
        P.dma(scw[:], scw_d, writes=["scw"])
        P.dma(ab[:], ab_d, writes=["ab"])

        def do_sc(sg):
            t0 = sg * SEG
            lo = 0 if sg > 0 else 2
            for i, (buf, key) in enumerate(((raw, "raw"), (raw2, "raw2"), (raw3, "raw3"))):
                if sg == 0:
                    P.op("pool", lambda e, buf=buf: e.memset(buf[:, 0:2], 0.0), writes=[key])
                P.dma(buf[:, lo:SEG + 2], scT[i][:, t0 - 2 + lo:t0 + SEG], writes=[key])
            P.op("pool", lambda e: e.tensor_tensor(out=raw2[:, 0:SEG + 2], in0=raw2[:, 0:SEG + 2],
                                                    in1=raw3[:, 0:SEG + 2], op=ALU.mult),
                 reads=["raw2", "raw3"], writes=["raw2"])
            P.op("dve", lambda e: e.tensor_scalar(out=acc[:], in0=raw2[:, 0:SEG], scalar1=scw[:, 0:1], scalar2=None,
                                                  op0=ALU.mult), reads=["raw2", "scw"], writes=["acc"])
            for j in (1, 2):
                P.op("dve", lambda e, j=j: e.scalar_tensor_tensor(out=acc[:], in0=raw2[:, j:j + SEG],
                                                                   scalar=scw[:, j:j + 1], in1=acc[:],
                                                                   op0=ALU.mult, op1=ALU.add),
                     reads=["raw2", "scw", "acc"], writes=["acc"])
            P.op("pool", lambda e: e.tensor_tensor(out=acc[:], in0=acc[:], in1=raw[:, 2:SEG + 2], op=ALU.mult),
                 reads=["acc", "raw"], writes=["acc"])
            P.dma(yb_d[:, t0:t0 + SEG], acc[:], reads=["acc"])

        for sg in range(NSEG):
            do_sc(sg)

        def do_p1(sg, s):
            t0 = sg * SEG
            lo = 0 if sg > 0 else 3
            if sg == 0:
                P.op("pool", lambda e: e.memset(raw[:, 0:3], 0.0), writes=["raw"])
            P.dma(raw[:, lo:SEG + 3], qkvT[s][:, t0 - 3 + lo:t0 + SEG], writes=["raw"])
            P.op("dve", lambda e: e.tensor_scalar(out=acc[:], in0=raw[:, 0:SEG], scalar1=cw[:, s * 4:s * 4 + 1],
                                                  scalar2=None, op0=ALU.mult), reads=["raw", "cw"], writes=["acc"])
            for j in (1, 2, 3):
                P.op("dve", lambda e, j=j: e.scalar_tensor_tensor(out=acc[:], in0=raw[:, j:j + SEG],
                                                                   scalar=cw[:, s * 4 + j:s * 4 + j + 1], in1=acc[:],
                                                                   op0=ALU.mult, op1=ALU.add),
                     reads=["raw", "cw", "acc"], writes=["acc"])
            dst = fT[s][:, t0:t0 + SEG]
            fkey = ("fT", s)
            P.op("act", lambda e: e.activation(out=dst, in_=acc[:], func=AF.Silu), reads=["acc"], writes=[fkey])
            if s < 2:
                P.op("act", lambda e: e.activation(out=sqs[:], in_=dst, func=AF.Square), reads=[fkey], writes=["sqs"])
                for blk in range(SEG // 512):
                    r = blk % 2
                    sl = slice(blk * 512, (blk + 1) * 512)
                    P.op("pe", lambda e, sl=sl: e.matmul(ps[7][:], lhsT=BO, rhs=sqs[:, sl], start=True, stop=True),
                         reads=["sqs", "cst"], writes=[("ps", 7)])
                    P.op("act", lambda e, r=r: e.activation(out=rn[r][:], in_=ps[7][:], func=AF.Sqrt, bias=EPS),
                         reads=[("ps", 7)], writes=[("rn", r)])
                    P.op("dve", lambda e, r=r: e.reciprocal(out=rn[r][:], in_=rn[r][:]),
                         reads=[("rn", r)], writes=[("rn", r)])
                    P.op("dve", lambda e, r=r, sl=sl: e.tensor_tensor(
                        out=fT[s][:, t0 + sl.start:t0 + sl.stop], in0=fT[s][:, t0 + sl.start:t0 + sl.stop],
                        in1=rn[r][:], op=ALU.mult), reads=[("rn", r), fkey], writes=[fkey])

        for sg in range(NSEG):
            for s in range(3):
                if _DBG.get("stop", 9) >= 2:
                    do_p1(sg, s)

        sc = {}
        for nm in ("g", "beta", "gc", "ngc", "egc", "bg", "dk", "egl0", "egl1"):
            sc[nm] = [S("sc_%s%d" % (nm, h), [128, NTL]) for h in range(2)]
        nea = S("nea", [128, 2])
        P.op("act", lambda e: e.activation(out=nea[:], in_=hp[:, 0:2], func=AF.Exp), reads=["hp"], writes=["nea"])
        P.op("dve", lambda e: e.tensor_scalar(out=nea[:], in0=nea[:], scalar1=-1.0, scalar2=None, op0=ALU.mult),
             reads=["nea"], writes=["nea"])

        def do_scal(h):
            k = lambda nm: ("sc", nm, h)
            g, beta = sc["g"][h], sc["beta"][h]
            P.op("act", lambda e: e.activation(out=g[:], in_=ab[:, h, :], func=AF.Exp, bias=hp[:, 2 + h:3 + h]),
                 reads=["ab", "hp"], writes=[k("g")])
            P.op("act", lambda e: e.activation(out=g[:], in_=g[:], func=AF.Ln, bias=1.0),
                 reads=[k("g")], writes=[k("g")])
            P.op("dve", lambda e: e.tensor_scalar(out=g[:], in0=g[:], scalar1=nea[:, h:h + 1], scalar2=None,
                                                  op0=ALU.mult), reads=[k("g"), "nea"], writes=[k("g")])
            P.op("act", lambda e: e.activation(out=beta[:], in_=ab[:, 2 + h, :], func=AF.Sigmoid),
                 reads=["ab"], writes=[k("beta")])
            P.op("pe", lambda e: e.matmul(ps[7][:, 0:NTL], lhsT=LTm, rhs=g[:], start=True, stop=True),
                 reads=[k("g"), "cst"], writes=[("ps", 7)])
            P.op("dve", lambda e: e.tensor_copy(out=sc["gc"][h][:], in_=ps[7][:, 0:NTL]),
                 reads=[("ps", 7)], writes=[k("gc")])
            P.op("dve", lambda e: e.tensor_scalar(out=sc["ngc"][h][:], in0=ps[7][:, 0:NTL], scalar1=-1.0, scalar2=None,
                                                  op0=ALU.mult), reads=[("ps", 7)], writes=[k("ngc")])
            P.op("act", lambda e: e.activation(out=sc["egc"][h][:], in_=ps[7][:, 0:NTL], func=AF.Exp),
                 reads=[("ps", 7)], writes=[k("egc")])
            P.op("dve", lambda e: e.tensor_tensor(out=sc["bg"][h][:], in0=sc["egc"][h][:], in1=beta[:], op=ALU.mult),
                 reads=[k("egc"), k("beta")], writes=[k("bg")])
            P.op("pe", lambda e: e.matmul(ps[7][:, 0:NTL], lhsT=BO, rhs=g[:], start=True, stop=True),
                 reads=[k("g"), "cst"], writes=[("ps", 7)])
            P.op("dve", lambda e: e.tensor_tensor(out=sc["dk"][h][:], in0=ps[7][:, 0:NTL], in1=sc["gc"][h][:],
                                                  op=ALU.subtract), reads=[("ps", 7), k("gc")], writes=[k("dk")])
            P.op("act", lambda e: e.activation(out=sc["dk"][h][:], in_=sc["dk"][h][:], func=AF.Exp),
                 reads=[k("dk")], writes=[k("dk")])
            for c, SEL in ((0, SEL0), (1, SEL1)):
                P.op("pe", lambda e, SEL=SEL: e.matmul(ps[7][:, 0:NTL], lhsT=SEL, rhs=g[:], start=True, stop=True),
                     reads=[k("g"), "cst"], writes=[("ps", 7)])
                P.op("act", lambda e, c=c: e.activation(out=sc["egl%d" % c][h][:], in_=ps[7][:, 0:NTL], func=AF.Exp),
                     reads=[("ps", 7)], writes=[k("egl%d" % c)])

        for h in range(2):
            if _DBG.get("stop", 9) >= 3:
                do_scal(h)

        NQ = 24
        qcnt = [0]

        def nq():
            i = qcnt[0] % NQ
            qcnt[0] += 1
            bk, qt = i % 6, i // 6
            return ps[bk][:, qt * 128:(qt + 1) * 128], ("bk", bk)

        NB = 2
        tmp = {}

        def T_(nm, shape, n=NB * 2):
            tmp[nm] = [S("t_%s%d" % (nm, i), shape) for i in range(n)]

        T_("ktok", [128, 128], NB)
        T_("vtok", [128, 128], NB)
        T_("qtok", [128, 128], NB)
        for nm in ("dg", "dsym", "du", "dl", "attnT", "M", "MT", "RT", "Pa", "PaT", "Pb", "PbT"):
            T_(nm, [128, 128])
        T_("vb", [128, 64]); T_("kbg", [128, 64]); T_("kd0", [128, 64]); T_("kd1", [128, 64])
        T_("u", [128, 64]); T_("wT", [64, 128]); T_("qgT", [64, 128]); T_("qg", [128, 64])
        T_("tabs", [128, 128])
        vnew = [S("vnew%d" % h, [128, 64]) for h in range(2)]
        Sst = [S("Sst%d" % h, [64, 64]) for h in range(2)]
        for h in range(2):
            P.op("pool", lambda e, h=h: e.memset(vnew[h][:], 0.0), writes=[("vnew", h)])
            P.op("pool", lambda e, h=h: e.memset(Sst[h][:], 0.0), writes=[("S", h)])
        osb = [S("osb%d" % i, [64, 4, 64]) for i in range(2)]
        osq = [S("osq%d" % i, [64, 4, 64]) for i in range(2)]
        oss = [S("oss%d" % i, [64, 4]) for i in range(2)]
        zt = [S("zt%d" % i, [64, 2, 128]) for i in range(2)]
        yt = [S("yt%d" % i, [64, 2, 128]) for i in range(2)]

        def do_tile(tt):
            tb = tt % NB
            c0 = tt * 128
            toks = {}
            for s, nm in ((0, "qtok"), (1, "ktok"), (2, "vtok")):
                pq, kq = nq()
                P.op("pe", lambda e, pq=pq, s=s: e.matmul(pq, lhsT=fT[s][:, c0:c0 + 128], rhs=ident, start=True, stop=True),
                     reads=[("fT", s), "cst"], writes=[kq])
                dstt = tmp[nm][tb]
                if _DBG.get('nocopy'):
                    continue
                P.op("dve", lambda e, pq=pq, dstt=dstt: e.tensor_copy(out=dstt[:], in_=pq), reads=[kq], writes=[(nm, tb)])
                toks[nm] = dstt
            if _DBG.get('sub', 9) < 0.5:
                P.op("pool", lambda e: e.memset(acc[:, 0:4], 0.0), reads=[("qtok", tb), ("ktok", tb), ("vtok", tb)], writes=["acc"])
                P.dma(yb_d[:, 0:4], acc[:, 0:4], reads=["acc"])
                return
            P.dma(zt[tb][:], z_d[:, 2 * tt:2 * tt + 2, :], writes=[("zt", tb)])
            P.op("act", lambda e: e.activation(out=zt[tb][:], in_=zt[tb][:], func=AF.Silu),
                 reads=[("zt", tb)], writes=[("zt", tb)])

            if _DBG.get('sub', 9) < 1:
                return
            st = {}

            def prep_head(h):
                ix = tb * 2 + h
                hb = 64 * h
                kk = lambda nm, ix=ix: (nm, ix)
                g = lambda nm, ix=ix: tmp[nm][ix]
                col = lambda nm, h=h: sc[nm][h][:, tt:tt + 1]
                kTh = fT[1][hb:hb + 64, c0:c0 + 128]
                qTh = fT[0][hb:hb + 64, c0:c0 + 128]
                P.op("dve", lambda e, g=g, col=col: e.tensor_scalar(out=g("dg")[:], in0=ident, scalar1=col("gc"),
                                                                     scalar2=None, op0=ALU.mult),
                     reads=["cst", ("sc", "gc", h)], writes=[kk("dg")])
                pR, kR = nq()
                P.op("pe", lambda e, pR=pR, g=g: e.matmul(pR, lhsT=ONES, rhs=g("dg")[:], start=True, stop=True),
                     reads=[kk("dg"), "cst"], writes=[kR])
                P.op("act", lambda e, pR=pR, g=g, col=col: e.activation(
                    out=g("tabs")[:], in_=pR, func=AF.Abs, bias=col("ngc")),
                    reads=[kR, ("sc", "ngc", h)], writes=[kk("tabs")])
                P.op("act", lambda e, g=g: e.activation(out=g("dsym")[:], in_=g("tabs")[:], func=AF.Exp, scale=-1.0),
                     reads=[kk("tabs")], writes=[kk("dsym")])
                P.op("pool", lambda e, g=g: e.tensor_tensor(out=g("du")[:], in0=g("dsym")[:], in1=MUs, op=ALU.mult),
                     reads=[kk("dsym"), "cst"], writes=[kk("du")])
                P.op("pool", lambda e, g=g: e.tensor_tensor(out=g("dl")[:], in0=g("dsym")[:], in1=ML, op=ALU.mult),
                     reads=[kk("dsym"), "cst"], writes=[kk("dl")])
                pKK, kKK = nq()
                P.op("pe", lambda e, pKK=pKK, kTh=kTh: e.matmul(pKK, lhsT=kTh, rhs=kTh, start=True, stop=True),
                     reads=[("fT", 1)], writes=[kKK])
                pQK, kQK = nq()
                P.op("pe", lambda e, pQK=pQK, kTh=kTh, qTh=qTh: e.matmul(pQK, lhsT=kTh, rhs=qTh, start=True, stop=True),
                     reads=[("fT", 1), ("fT", 0)], writes=[kQK])
                P.op("dve", lambda e, pQK=pQK, g=g: e.tensor_tensor(out=g("attnT")[:], in0=pQK, in1=g("du")[:],
                                                                     op=ALU.mult),
                     reads=[kQK, kk("du")], writes=[kk("attnT")])
                P.op("dve", lambda e, pKK=pKK, g=g, col=col: e.scalar_tensor_tensor(
                    out=g("M")[:], in0=pKK, scalar=col("beta"), in1=g("dl")[:], op0=ALU.mult, op1=ALU.mult),
                    reads=[kKK, kk("dl"), ("sc", "beta", h)], writes=[kk("M")])
                ktok, vtok, qtok = toks["ktok"], toks["vtok"], toks["qtok"]
                P.op("pool", lambda e, g=g, col=col, vtok=vtok: e.tensor_scalar(
                    out=g("vb")[:], in0=vtok[:, hb:hb + 64], scalar1=col("beta"), scalar2=None, op0=ALU.mult),
                    reads=[("vtok", tb), ("sc", "beta", h)], writes=[kk("vb")])
                P.op("pool", lambda e, g=g, col=col, ktok=ktok: e.tensor_scalar(
                    out=g("kbg")[:], in0=ktok[:, hb:hb + 64], scalar1=col("bg"), scalar2=None, op0=ALU.mult),
                    reads=[("ktok", tb), ("sc", "bg", h)], writes=[kk("kbg")])
                P.op("pool", lambda e, g=g, col=col, ktok=ktok: e.tensor_scalar(
                    out=g("kd0")[:], in0=ktok[:, hb:hb + 64], scalar1=col("dk"), scalar2=SEL0[:, 0:1],
                    op0=ALU.mult, op1=ALU.mult), reads=[("ktok", tb), ("sc", "dk", h), "cst"], writes=[kk("kd0")])
                P.op("pool", lambda e, g=g, col=col, ktok=ktok: e.tensor_scalar(
                    out=g("kd1")[:], in0=ktok[:, hb:hb + 64], scalar1=col("dk"), scalar2=SEL1[:, 0:1],
                    op0=ALU.mult, op1=ALU.mult), reads=[("ktok", tb), ("sc", "dk", h), "cst"], writes=[kk("kd1")])
                P.op("dve", lambda e, g=g, col=col, qtok=qtok: e.tensor_scalar(
                    out=g("qg")[:], in0=qtok[:, hb:hb + 64], scalar1=col("egc"), scalar2=0.125,
                    op0=ALU.mult, op1=ALU.mult), reads=[("qtok", tb), ("sc", "egc", h)], writes=[kk("qg")])
                pqg, kqg = nq()
                P.op("pe", lambda e, pqg=pqg, g=g: e.matmul(pqg[0:64, :], lhsT=g("qg")[:], rhs=ident, start=True, stop=True),
                     reads=[kk("qg"), "cst"], writes=[kqg])
                P.op("act", lambda e, pqg=pqg, g=g: e.copy(out=g("qgT")[:], in_=pqg[0:64, :]),
                     reads=[kqg], writes=[kk("qgT")])
                pMT, kMT = nq()
                P.op("pe", lambda e, pMT=pMT, g=g: e.matmul(pMT, lhsT=g("M")[:], rhs=ident, start=True, stop=True),
                     reads=[kk("M"), "cst"], writes=[kMT])
                P.op("act", lambda e, pMT=pMT, g=g: e.copy(out=g("MT")[:], in_=pMT), reads=[kMT], writes=[kk("MT")])
                P.op("dve", lambda e, pMT=pMT, g=g: e.tensor_tensor(out=g("RT")[:], in0=ident, in1=pMT, op=ALU.subtract),
                     reads=[kMT, "cst"], writes=[kk("RT")])
                st[h] = dict(ix=ix, P=("M", "MT"))

            for h in range(2):
                prep_head(h)

            if _DBG.get('sub', 9) < 2:
                return
            for lvl in range(5):
                last = lvl == 4
                nxt = ("Pa", "PaT") if lvl % 2 == 0 else ("Pb", "PbT")
                pend = {}
                for h in range(2):
                    ix = st[h]["ix"]
                    Pn, PTn = st[h]["P"]
                    Pk, PTk = tmp[Pn][ix], tmp[PTn][ix]
                    p1, k1 = nq()
                    P.op("pe", lambda e, p1=p1, Pk=Pk, PTk=PTk: e.matmul(p1, lhsT=PTk[:], rhs=Pk[:], start=True, stop=True),
                         reads=[(Pn, ix), (PTn, ix)], writes=[k1])
                    p2 = k2 = None
                    if not last:
                        p2, k2 = nq()
                        P.op("pe", lambda e, p2=p2, Pk=Pk, PTk=PTk: e.matmul(p2, lhsT=Pk[:], rhs=PTk[:], start=True, stop=True),
                             reads=[(Pn, ix), (PTn, ix)], writes=[k2])
                    pend[h] = (p1, k1, p2, k2)
                for h in range(2):
                    ix = st[h]["ix"]
                    p1, k1, p2, k2 = pend[h]
                    Pnew, PTnew = tmp[nxt[0]][ix], tmp[nxt[1]][ix]
                    P.op("act", lambda e, p1=p1, Pnew=Pnew: e.copy(out=Pnew[:], in_=p1), reads=[k1], writes=[(nxt[0], ix)])
                    if not last:
                        P.op("dve", lambda e, p2=p2, PTnew=PTnew: e.tensor_copy(out=PTnew[:], in_=p2),
                             reads=[k2], writes=[(nxt[1], ix)])
                for h in range(2):
                    ix = st[h]["ix"]
                    Pnew = tmp[nxt[0]][ix]
                    RT = tmp["RT"][ix]
                    p3, k3 = nq()
                    P.op("pe", lambda e, p3=p3, Pnew=Pnew, RT=RT: e.matmul(p3, lhsT=Pnew[:], rhs=RT[:], start=True, stop=True),
                         reads=[(nxt[0], ix), ("RT", ix)], writes=[k3])
                    P.op("dve", lambda e, p3=p3, RT=RT: e.tensor_tensor(out=RT[:], in0=RT[:], in1=p3, op=ALU.add),
                         reads=[k3, ("RT", ix)], writes=[("RT", ix)])
                    st[h]["P"] = nxt

            if _DBG.get('sub', 9) < 3:
                return
            for h in range(2):
                ix = st[h]["ix"]
                RT = tmp["RT"][ix]
                pu, ku = nq()
                P.op("pe", lambda e, pu=pu, RT=RT, ix=ix: e.matmul(pu[:, 0:64], lhsT=RT[:], rhs=tmp["vb"][ix][:],
                                                                   start=True, stop=True),
                     reads=[("RT", ix), ("vb", ix)], writes=[ku])
                P.op("act", lambda e, pu=pu, ix=ix: e.copy(out=tmp["u"][ix][:], in_=pu[:, 0:64]),
                     reads=[ku], writes=[("u", ix)])
                pw, kw = nq()
                P.op("pe", lambda e, pw=pw, RT=RT, ix=ix: e.matmul(pw[0:64, :], lhsT=tmp["kbg"][ix][:], rhs=RT[:],
                                                                   start=True, stop=True),
                     reads=[("RT", ix), ("kbg", ix)], writes=[kw])
                P.op("dve", lambda e, pw=pw, ix=ix: e.tensor_copy(out=tmp["wT"][ix][:], in_=pw[0:64, :]),
                     reads=[kw], writes=[("wT", ix)])

            if _DBG.get('sub', 9) < 4:
                return
            ob = tt % 2
            for c in range(2):
                rs = slice(c * 64, (c + 1) * 64)
                for h in range(2):
                    ix = st[h]["ix"]
                    pws, kws = nq()
                    P.op("pe", lambda e, pws=pws, ix=ix, h=h: e.matmul(pws[:, 0:64], lhsT=tmp["wT"][ix][:], rhs=Sst[h][:],
                                                                       start=True, stop=True),
                         reads=[("wT", ix), ("S", h)], writes=[kws])
                    P.op("dve", lambda e, pws=pws, ix=ix, h=h, rs=rs: e.tensor_tensor(
                        out=vnew[h][rs, :], in0=tmp["u"][ix][rs, :], in1=pws[rs, 0:64], op=ALU.subtract),
                        reads=[kws, ("u", ix)], writes=[("vnew", h)])
                    po = ps[6 + h][0:64, c * 64:(c + 1) * 64]
                    ko = ("bk", 6 + h)
                    P.op("pe", lambda e, po=po, ix=ix, h=h, rs=rs: e.matmul(po, lhsT=tmp["qgT"][ix][:, rs], rhs=Sst[h][:],
                                                                            start=True, stop=False),
                         reads=[("qgT", ix), ("S", h)], writes=[ko])
                    P.op("pe", lambda e, po=po, ix=ix, h=h, rs=rs: e.matmul(po, lhsT=tmp["attnT"][ix][:, rs], rhs=vnew[h][:],
                                                                            start=False, stop=True),
                         reads=[("attnT", ix), ("vnew", h)], writes=[ko])
                    pS, kS = nq()
                    kdn = "kd%d" % c
                    P.op("pe", lambda e, pS=pS, ix=ix, h=h, kdn=kdn: e.matmul(pS[0:64, 0:64], lhsT=tmp[kdn][ix][:], rhs=vnew[h][:],
                                                                              start=True, stop=True),
                         reads=[(kdn, ix), ("vnew", h)], writes=[kS])
                    P.op("dve", lambda e, pS=pS, h=h, c=c: e.scalar_tensor_tensor(
                        out=Sst[h][:], in0=Sst[h][:], scalar=sc["egl%d" % c][h][0:64, tt:tt + 1], in1=pS[0:64, 0:64],
                        op0=ALU.mult, op1=ALU.add), reads=[kS, ("S", h), ("sc", "egl%d" % c, h)], writes=[("S", h)])
                    P.op("act", lambda e, po=po, c=c, h=h: e.copy(out=osb[ob][:, c * 2 + h, :], in_=po),
                         reads=[ko], writes=[("osb", ob)])
            if _DBG.get('sub', 9) < 5:
                return
            P.op("pool", lambda e: e.tensor_tensor(out=osq[ob][:], in0=osb[ob][:], in1=osb[ob][:], op=ALU.mult),
                 reads=[("osb", ob)], writes=[("osq", ob)])
            P.op("dve", lambda e: e.tensor_reduce(out=oss[ob][:], in_=osq[ob][:], axis=AX.X, op=ALU.add),
                 reads=[("osq", ob)], writes=[("oss", ob)])
            P.op("act", lambda e: e.activation(out=oss[ob][:], in_=oss[ob][:], func=AF.Sqrt, scale=1.0 / 64, bias=EPS),
                 reads=[("oss", ob)], writes=[("oss", ob)])
            P.op("dve", lambda e: e.reciprocal(out=oss[ob][:], in_=oss[ob][:]), reads=[("oss", ob)], writes=[("oss", ob)])
            for c in range(2):
                for h in range(2):
                    P.op("dve", lambda e, c=c, h=h: e.scalar_tensor_tensor(
                        out=yt[ob][:, c, h * 64:(h + 1) * 64], in0=osb[ob][:, c * 2 + h, :],
                        scalar=oss[ob][:, c * 2 + h:c * 2 + h + 1], in1=onw[:], op0=ALU.mult, op1=ALU.mult),
                        reads=[("osb", ob), ("oss", ob), "onw"], writes=[("yt", ob)])
            P.op("pool", lambda e: e.tensor_tensor(out=yt[ob][:], in0=yt[ob][:], in1=zt[tb][:], op=ALU.mult),
                 reads=[("yt", ob), ("zt", tb)], writes=[("yt", ob)])
            P.dma(ya_d[:, 2 * tt:2 * tt + 2, :], yt[ob][:], reads=[("yt", ob)])

        for tt in range(NTL):
            if _DBG.get("stop", 9) >= 4 and tt < _DBG.get("ntl", 999):
                do_tile(tt)
        P.emit()
    return nc


def gdn_in_maps(P0, T, qkv_conv, a_log, dt_bias, o_norm, sc_conv):
    P0 = P0.reshape(2, T, -1)
    cst = gdn_consts()
    maps = []
    for c in range(NCORES):
        b, hg = c // 4, c % 4
        o = 128 * hg
        pb = P0[b]
        qkvT = np.stack([pb[:, s * 512 + o:s * 512 + o + 128].T for s in range(3)])
        cw = np.concatenate([qkv_conv[:, s * 512 + o:s * 512 + o + 128].T for s in range(3)], axis=1)
        abc = np.concatenate([pb[:, 2048 + 2 * hg:2048 + 2 * hg + 2], pb[:, 2056 + 2 * hg:2056 + 2 * hg + 2]], axis=1)
        ab = abc.reshape(T // 128, 128, 4).transpose(1, 2, 0)
        hp = np.tile(np.concatenate([a_log[2 * hg:2 * hg + 2], dt_bias[2 * hg:2 * hg + 2]])[None], (128, 1))
        z_l = pb[:, 1536 + o:1536 + o + 128].reshape(T // 64, 64, 128).transpose(1, 0, 2)
        scT = np.stack([pb[:, 2064 + s * 512 + o:2064 + s * 512 + o + 128].T for s in range(3)])
        scw = sc_conv[:, o:o + 128].T
        maps.append(dict(qkvT=np.ascontiguousarray(qkvT, np.float32), cw=np.ascontiguousarray(cw, np.float32),
                         z_l=np.ascontiguousarray(z_l, np.float32), ab=np.ascontiguousarray(ab, np.float32),
                         hp=np.ascontiguousarray(hp, np.float32), onw=np.ascontiguousarray(np.tile(o_norm[None], (64, 1)), np.float32),
                         scT=np.ascontiguousarray(scT, np.float32), scw=np.ascontiguousarray(scw, np.float32), cst=cst))
    return maps


def gdn_gather(results, T):
    y = np.zeros((2, T, 1024), np.float32)
    for c in range(NCORES):
        b, hg = c // 4, c % 4
        o = 128 * hg
        y[b, :, o:o + 128] = results[c]["ya_l"].transpose(1, 0, 2).reshape(T, 128)
        y[b, :, 512 + o:512 + o + 128] = results[c]["ybT"].T
    return y.reshape(2 * T, 1024)


def nsa_consts(T):
    NTL = T // 128
    p = np.arange(128)
    half = 8
    inv_freq = (500000.0 ** (-(np.arange(0, 16, 2, dtype=np.float32)) / np.float32(16))).astype(np.float32)
    ang = np.arange(T, dtype=np.float32)[:, None] * inv_freq[None, :]
    cos, sin = np.cos(ang).astype(np.float32), np.sin(ang).astype(np.float32)
    C = np.ones((64, T), np.float32)
    C[0:8] = cos.T
    C[8:16] = cos.T
    Sg = np.zeros((16, T), np.float32)
    Sg[0:8] = -sin.T
    Sg[8:16] = sin.T
    C4 = np.ascontiguousarray(np.broadcast_to(C[:, None, :], (64, 4, T)))
    S4 = np.ascontiguousarray(np.broadcast_to(Sg[:, None, :], (16, 4, T)))
    E = np.zeros((128, NTL, 128), np.float32)
    for kt in range(NTL):
        E[2 * kt, kt, 0:64] = 1.0
        E[2 * kt + 1, kt, 64:128] = 1.0
    PM = np.zeros((128, 17, 128), np.float32)
    for r in range(17):
        PM[:, r, :] = (16 * p[:, None] + 31) <= (p[None, :] + 128 * r)
    CM = (p[:, None] <= p[None, :]).astype(np.float32)
    AM = (p[:, None] > p[None, :]).astype(np.float32)
    msk = np.ascontiguousarray(np.concatenate([PM, CM[:, None, :], AM[:, None, :], np.eye(128, dtype=np.float32)[:, None, :]], axis=1))
    c = np.arange(512)
    s = np.arange(128)
    ov = ((16 * c[:, None] < 64 * s[None, :] + 64) & (16 * c[:, None] + 32 > 64 * s[None, :])).astype(np.float32)
    ov[511] = 0.0
    ov = np.ascontiguousarray(ov.reshape(4, 128, 128).transpose(1, 0, 2))
    NQB = T // 128
    bq = np.zeros((NQB, 128, 128), np.float32)
    for qb in range(NQB):
        cur = 2 * qb + (p >= 64)
        js = s[None, :]
        forced = (js == 0) | (js == cur[:, None]) | (js == cur[:, None] - 1)
        bq[qb] = np.where(js > cur[:, None], -100.0, np.where(forced, 100.0, 0.0))
    return dict(C4=C4, S4=S4, E=E, msk=msk, ov=ov, bq=bq)


def build_nsa(T):
    nc = bass.Bass("TRN2", target_bir_lowering=False)
    NTL = T // 128
    NQB = NTL
    SEG = min(T, 2048)
    NSEG = T // SEG
    D = lambda name, shape: nc.dram_tensor(name, list(shape), F32, kind="ExternalInput").ap()
    qT_d = D("qT", [64, 4, T]); qsw_d = D("qsw", [16, 4, T])
    kT_d = D("kT", [2, 64, T]); ksw_d = D("ksw", [2, 16, T])
    v_d = D("vtok", [2, 128, NTL, 64])
    KP_d = D("KP", [2, 128, T // 2])
    posP_d = D("posP", [128, 16])
    w1_d = D("w1", [2, 128, 16, 256]); w2_d = D("w2", [2, 128, 2, 64])
    gat_d = D("gates", [128, NTL, 12])
    C4_d = D("C4", [64, 4, T]); S4_d = D("S4", [16, 4, T])
    E_d = D("E", [128, NTL, 128]); msk_d = D("msk", [128, 20, 128]); ov_d = D("ov", [128, 4, 128])
    bq_d = D("bq", [NQB, 128, 128])
    o_d = nc.dram_tensor("o_out", [T, 256], F32, kind="ExternalOutput").ap()

    with ExitStack() as es:
        P = Prog(nc, es)
        S = lambda name, shape, dt=F32: _sb(nc, es, name, shape, dt)
        msk = S("msk_sb", [128, 20, 128])
        CM, AM, ident = msk[:, 17, :], msk[:, 18, :], msk[:, 19, :]
        Eb = S("Eb", [128, NTL, 128], BF16)
        KT = [S("KTb%d" % i, [64, T], BF16) for i in range(2)]
        Vb = [S("Vb%d" % i, [128, NTL, 65], BF16) for i in range(2)]
        kcT = S("kcT", [64, 512], BF16)
        Vc = S("Vc", [128, 4, 193], BF16)
        gs = S("gs", [128, NTL, 12])
        posP = S("posP_sb", [128, 16])
        stg = [S("stg%d" % i, [128, 4096]) for i in range(2)]
        stg16 = [S("stg16_%d" % i, [16, SEG]) for i in range(3)]
        KPp = S("KPp", [128, 16, 512], BF16)
        w1b = S("w1b", [128, 16, 256], BF16)
        w2b = S("w2b", [128, 2, 64], BF16)
        HT = S("HT", [128, 2, 512], BF16)
        ps = [_ps(nc, es, "ps%d" % i, [128, 512], F32) for i in range(8)]
        bk = lambda i: ("bk", i)

        P.dma(msk[:], msk_d, writes=["msk"])
        P.dma(posP[:], posP_d, writes=["posP"])
        for ch in range((NTL + 31) // 32):
            n = min(32, NTL - ch * 32)
            sv = stg[ch % 2][:, 0:n * 128].rearrange("p (a b) -> p a b", a=n)
            P.dma(sv, E_d[:, ch * 32:ch * 32 + n, :], writes=[("stg", ch % 2)])
            P.op("pool", lambda e, sv=sv, ch=ch, n=n: e.tensor_copy(out=Eb[:, ch * 32:ch * 32 + n, :], in_=sv),
                 reads=[("stg", ch % 2)], writes=["Eb"])
        P.dma(gs[:], gat_d, writes=["gs"])
        P.op("act", lambda e: e.activation(out=gs[:], in_=gs[:], func=AF.Sigmoid), reads=["gs"], writes=["gs"])

        def do_k(i, sg):
            t0 = sg * SEG
            a, b = stg[0][0:64, 0:SEG], stg[1][0:64, 0:SEG]
            P.dma(a, kT_d[i][:, t0:t0 + SEG], writes=[("stg", 0)])
            P.dma(b, C4_d[:, 0, t0:t0 + SEG], writes=[("stg", 1)])
            P.dma(stg16[0][:], ksw_d[i][:, t0:t0 + SEG], writes=[("s16", 0)])
            P.dma(stg16[1][:], S4_d[:, 0, t0:t0 + SEG], writes=[("s16", 1)])
            P.op("dve", lambda e: e.tensor_tensor(out=KT[i][:, t0:t0 + SEG], in0=a, in1=b, op=ALU.mult),
                 reads=[("stg", 0), ("stg", 1)], writes=[("KT", i)])
            P.op("pool", lambda e: e.tensor_tensor(out=stg16[0][:], in0=stg16[0][:], in1=stg16[1][:], op=ALU.mult),
                 reads=[("s16", 0), ("s16", 1)], writes=[("s16", 0)])
            P.op("pool", lambda e: e.tensor_tensor(out=stg16[2][:], in0=a[0:16, :], in1=b[0:16, :], op=ALU.mult),
                 reads=[("stg", 0), ("stg", 1)], writes=[("s16", 2)])
            P.op("dve", lambda e: e.tensor_tensor(out=KT[i][0:16, t0:t0 + SEG], in0=stg16[0][:], in1=stg16[2][:], op=ALU.add),
                 reads=[("s16", 0), ("s16", 2), ("KT", i)], writes=[("KT", i)])

        for i in range(2):
            for sg in range(NSEG):
                do_k(i, sg)

        def do_v(i):
            sv = stg[i][:, 0:NTL * 64].rearrange("p (a b) -> p a b", a=NTL)
            P.dma(sv, v_d[i], writes=[("stg", i)])
            P.op("pool", lambda e: e.memset(Vb[i][:, :, 64:65], 1.0), writes=[("Vb", i)])
            P.op("dve", lambda e: e.tensor_copy(out=Vb[i][:, :, 0:64], in_=sv), reads=[("stg", i)], writes=[("Vb", i)])

        for i in range(2):
            do_v(i)

        P.op("pool", lambda e: e.memset(kcT[:], 0.0), writes=["kcT"])
        P.op("pool", lambda e: e.memset(Vc[:], 0.0), writes=["Vc"])

        def do_cmp(i):
            kp = stg[0][:, 0:T // 2]
            P.dma(kp, KP_d[i], writes=[("stg", 0)])
            kpv = kp.rearrange("p (c e) -> p c e", e=8)
            NCB = T // 16 - 1
            for lp in range(16):
                src = kpv[:, 0:NCB, lp] if lp < 8 else kpv[:, 1:NCB + 1, lp - 8]
                eng = "dve" if lp % 2 == 0 else "pool"
                P.op(eng, lambda e, src=src, lp=lp: e.tensor_scalar(out=KPp[:, lp, 0:NCB], in0=src, scalar1=posP[:, lp:lp + 1],
                                                                   scalar2=None, op0=ALU.add),
                     reads=[("stg", 0), "posP"], writes=["KPp"])
            w1v = stg[1][:, 0:4096].rearrange("p (a b) -> p a b", a=16)
            P.dma(w1v, w1_d[i], writes=[("stg", 1)])
            P.op("act", lambda e: e.copy(out=w1b[:], in_=w1v), reads=[("stg", 1)], writes=["w1b"])
            w2v = stg[0][:, 0:128].rearrange("p (a b) -> p a b", a=2)
            P.dma(w2v, w2_d[i], reads=["KPp"], writes=[("stg", 0)])
            P.op("act", lambda e: e.copy(out=w2b[:], in_=w2v), reads=[("stg", 0)], writes=["w2b"])
            for jc in range(2):
                for lp in range(16):
                    P.op("pe", lambda e, jc=jc, lp=lp: e.matmul(ps[jc][:, 0:NCB], lhsT=w1b[:, lp, jc * 128:(jc + 1) * 128],
                                                                rhs=KPp[:, lp, 0:NCB], start=(lp == 0), stop=(lp == 15)),
                         reads=["w1b", "KPp"], writes=[bk(jc)])
                P.op("act", lambda e, jc=jc: e.activation(out=HT[:, jc, 0:NCB], in_=ps[jc][:, 0:NCB], func=AF.Silu),
                     reads=[bk(jc)], writes=["HT"])
            if i == 0:
                for jc in range(2):
                    P.op("pe", lambda e, jc=jc: e.matmul(ps[2][0:64, 0:NCB], lhsT=w2b[:, jc, :], rhs=HT[:, jc, 0:NCB],
                                                         start=(jc == 0), stop=(jc == 1)), reads=["w2b", "HT"], writes=[bk(2)])
                P.op("act", lambda e: e.copy(out=kcT[:, 0:NCB], in_=ps[2][0:64, 0:NCB]), reads=[bk(2)], writes=["kcT"])
            else:
                for ct in range((NCB + 127) // 128):
                    n = min(128, NCB - ct * 128)
                    for jc in range(2):
                        P.op("pe", lambda e, jc=jc, ct=ct, n=n: e.matmul(ps[3][0:n, 0:64], lhsT=HT[:, jc, ct * 128:ct * 128 + n],
                                                                         rhs=w2b[:, jc, :], start=(jc == 0), stop=(jc == 1)),
                             reads=["w2b", "HT"], writes=[bk(3)])
                    P.op("act", lambda e, ct=ct, n=n: e.copy(out=Vc[0:n, ct, 0:64], in_=ps[3][0:n, 0:64]),
                         reads=[bk(3)], writes=["Vc"])
                    P.op("pool", lambda e, ct=ct, n=n: e.memset(Vc[0:n, ct, 64:65], 1.0), reads=["Vc"], writes=["Vc"])

        do_cmp(0)
        do_cmp(1)
        ovs = stg[1][:, 0:512].rearrange("p (a b) -> p a b", a=4)
        P.dma(ovs, ov_d, reads=["w1b"], writes=[("stg", 1)])
        P.op("dve", lambda e: e.tensor_copy(out=Vc[:, :, 65:193], in_=ovs), reads=[("stg", 1), "Vc"], writes=["Vc"])

        qf = [S("qf%d" % i, [64, 4, 128]) for i in range(2)]
        c4 = [S("c4_%d" % i, [64, 4, 128]) for i in range(2)]
        qs = [S("qs%d" % i, [16, 4, 128]) for i in range(2)]
        s4 = [S("s4_%d" % i, [16, 4, 128]) for i in range(2)]
        v16 = [S("v16_%d" % i, [16, 4, 128]) for i in range(2)]
        Qn = [S("Qn%d" % i, [64, 512], BF16) for i in range(2)]
        Qr = [S("Qr%d" % i, [64, 512], BF16) for i in range(2)]
        bqs = [S("bqs%d" % i, [128, 128]) for i in range(2)]
        Pc = S("Pc", [128, 4, 512], BF16)
        NPB = 4
        Pb = [S("Pb%d" % i, [128, 512], BF16) for i in range(NPB)]
        ocmp = S("ocmp", [128, 4, 193])
        ow = S("ow", [128, 4, 65])
        osl = S("osl", [128, 4, 65])
        rd = S("rd", [128, 12])
        impm = S("impm", [128, 128])
        imp2 = S("imp2", [128, 128])
        m8 = S("m8", [128, 16])
        nsel = S("nsel", [128, 128])
        nsT = S("nsT", [128, 512], BF16)
        osum = [S("osum%d" % i, [128, 4, 64]) for i in range(2)]
        pcnt = [0]
        scnt = [0]

        def do_qb(qb):
            j = qb % 2
            q0 = qb * 128
            P.dma(qf[j][:], qT_d[:, :, q0:q0 + 128], writes=[("qf", j)])
            P.dma(c4[j][:], C4_d[:, :, q0:q0 + 128], writes=[("c4", j)])
            P.dma(qs[j][:], qsw_d[:, :, q0:q0 + 128], writes=[("qs", j)])
            P.dma(s4[j][:], S4_d[:, :, q0:q0 + 128], writes=[("s4", j)])
            P.dma(bqs[j][:], bq_d[qb], writes=[("bqs", j)])
            fl = lambda t: t[:].rearrange("p a b -> p (a b)")
            P.op("pool", lambda e: e.tensor_copy(out=Qn[j][:], in_=fl(qf[j])), reads=[("qf", j)], writes=[("Qn", j)])
            P.op("pool", lambda e: e.tensor_tensor(out=Qr[j][:], in0=fl(qf[j]), in1=fl(c4[j]), op=ALU.mult),
                 reads=[("qf", j), ("c4", j)], writes=[("Qr", j)])
            P.op("pool", lambda e: e.tensor_tensor(out=fl(qs[j]), in0=fl(qs[j]), in1=fl(s4[j]), op=ALU.mult),
                 reads=[("qs", j), ("s4", j)], writes=[("qs", j)])
            P.op("pool", lambda e: e.tensor_tensor(out=fl(v16[j]), in0=qf[j][0:16, :, :].rearrange("p a b -> p (a b)"),
                                                    in1=c4[j][0:16, :, :].rearrange("p a b -> p (a b)"), op=ALU.mult),
                 reads=[("qf", j), ("c4", j)], writes=[("v16", j)])
            P.op("pool", lambda e: e.tensor_tensor(out=Qr[j][0:16, :], in0=fl(qs[j]), in1=fl(v16[j]), op=ALU.add),
                 reads=[("qs", j), ("v16", j), ("Qr", j)], writes=[("Qr", j)])

            cts = [ct for ct in range(4) if qb - 16 * ct >= 0 and ct * 128 < T // 16 - 1]
            for ct in cts:
                r = qb - 16 * ct
                sb_ = scnt[0] % 2
                scnt[0] += 1
                P.op("pe", lambda e, ct=ct, sb_=sb_: e.matmul(ps[sb_][:], lhsT=kcT[:, ct * 128:(ct + 1) * 128], rhs=Qn[j][:],
                                                              start=True, stop=True), reads=["kcT", ("Qn", j)], writes=[bk(sb_)])
                P.op("act", lambda e, ct=ct, sb_=sb_: e.activation(out=Pc[:, ct, :], in_=ps[sb_][:], func=AF.Exp, scale=0.125),
                     reads=[bk(sb_)], writes=[("Pc", ct)])
                if r <= 16:
                    for hh in range(4):
                        P.op("dve", lambda e, ct=ct, hh=hh, r=r: e.tensor_tensor(
                            out=Pc[:, ct, hh * 128:(hh + 1) * 128], in0=Pc[:, ct, hh * 128:(hh + 1) * 128],
                            in1=msk[:, r, :], op=ALU.mult), reads=[("Pc", ct), "msk"], writes=[("Pc", ct)])
            for hh in range(4):
                bank = 2 + hh // 2
                off = (hh % 2) * 193
                for n_, ct in enumerate(cts):
                    P.op("pe", lambda e, hh=hh, ct=ct, bank=bank, off=off, n_=n_: e.matmul(
                        ps[bank][:, off:off + 193], lhsT=Pc[:, ct, hh * 128:(hh + 1) * 128], rhs=Vc[:, ct, :],
                        start=(n_ == 0), stop=(n_ == len(cts) - 1)), reads=[("Pc", ct), "Vc"], writes=[bk(bank)])
            for half in range(2):
                P.op("act", lambda e, half=half: e.copy(
                    out=ocmp[:, 2 * half:2 * half + 2, :].rearrange("p a b -> p (a b)"), in_=ps[2 + half][:, 0:386]),
                    reads=[bk(2 + half)], writes=["ocmp"])
            P.op("dve", lambda e: e.tensor_scalar(out=rd[:, 0:4], in0=ocmp[:, :, 64], scalar1=1e-30, scalar2=None, op0=ALU.add),
                 reads=["ocmp"], writes=["rd"])
            P.op("dve", lambda e: e.reciprocal(out=rd[:, 0:4], in_=rd[:, 0:4]), reads=["rd"], writes=["rd"])
            P.op("dve", lambda e: e.scalar_tensor_tensor(out=impm[:], in0=ocmp[:, 0, 65:193], scalar=rd[:, 0:1], in1=bqs[j][:],
                                                         op0=ALU.mult, op1=ALU.add), reads=["ocmp", "rd", ("bqs", j)], writes=["impm"])
            for hh in range(1, 4):
                P.op("dve", lambda e, hh=hh: e.scalar_tensor_tensor(out=impm[:], in0=ocmp[:, hh, 65:193], scalar=rd[:, hh:hh + 1],
                                                                    in1=impm[:], op0=ALU.mult, op1=ALU.add),
                     reads=["ocmp", "rd", "impm"], writes=["impm"])
            P.op("dve", lambda e: e.max(out=m8[:, 0:8], in_=impm[:]), reads=["impm"], writes=["m8"])
            P.op("dve", lambda e: e.match_replace(out=imp2[:], in_to_replace=m8[:, 0:8], in_values=impm[:], imm_value=-1e9),
                 reads=["impm", "m8"], writes=["imp2"])
            P.op("dve", lambda e: e.max(out=m8[:, 8:16], in_=imp2[:]), reads=["imp2"], writes=["m8"])
            P.op("dve", lambda e: e.tensor_scalar(out=nsel[:], in0=impm[:], scalar1=m8[:, 15:16], scalar2=1.0,
                                                  op0=ALU.is_ge, op1=ALU.subtract), reads=["impm", "m8"], writes=["nsel"])
            P.op("pe", lambda e: e.transpose(out=ps[4][:, 0:128], in_=nsel[:], identity=ident), reads=["nsel", "msk"], writes=[bk(4)])
            for hh in range(4):
                P.op("act", lambda e, hh=hh: e.activation(out=nsT[:, hh * 128:(hh + 1) * 128], in_=ps[4][:, 0:128],
                                                          func=AF.Copy, scale=30000.0), reads=[bk(4)], writes=["nsT"])

            def attn(kts, i, with_sel, odst, okey):
                for n_, kt in enumerate(kts):
                    sb_ = scnt[0] % 2
                    scnt[0] += 1
                    P.op("pe", lambda e, kt=kt, sb_=sb_: e.matmul(ps[sb_][:], lhsT=KT[i][:, kt * 128:(kt + 1) * 128], rhs=Qr[j][:],
                                                                  start=True, stop=not with_sel),
                         reads=[("KT", i), ("Qr", j)], writes=[bk(sb_)])
                    if with_sel:
                        P.op("pe", lambda e, kt=kt, sb_=sb_: e.matmul(ps[sb_][:], lhsT=Eb[:, kt, :], rhs=nsT[:], start=False, stop=True),
                             reads=["Eb", "nsT"], writes=[bk(sb_)])
                    pb = pcnt[0] % NPB
                    pcnt[0] += 1
                    P.op("act", lambda e, sb_=sb_, pb=pb: e.activation(out=Pb[pb][:], in_=ps[sb_][:], func=AF.Exp, scale=0.125),
                         reads=[bk(sb_)], writes=[("Pb", pb)])
                    mk = None
                    if kt == qb:
                        mk = CM
                    elif (not with_sel) and kt == qb - 4:
                        mk = AM
                    if mk is not None:
                        for hh in range(4):
                            P.op("dve", lambda e, hh=hh, pb=pb, mk=mk: e.tensor_tensor(
                                out=Pb[pb][:, hh * 128:(hh + 1) * 128], in0=Pb[pb][:, hh * 128:(hh + 1) * 128], in1=mk, op=ALU.mult),
                                reads=[("Pb", pb), "msk"], writes=[("Pb", pb)])
                    for hh in range(4):
                        P.op("pe", lambda e, hh=hh, pb=pb, kt=kt, n_=n_: e.matmul(
                            ps[4 + hh][:, 0:65], lhsT=Pb[pb][:, hh * 128:(hh + 1) * 128], rhs=Vb[i][:, kt, :],
                            start=(n_ == 0), stop=(n_ == len(kts) - 1)), reads=[("Pb", pb), ("Vb", i)], writes=[bk(4 + hh)])
                for hh in range(4):
                    P.op("act", lambda e, hh=hh: e.copy(out=odst[:, hh, :], in_=ps[4 + hh][:, 0:65]), reads=[bk(4 + hh)], writes=[okey])

            attn(list(range(max(0, qb - 4), qb + 1)), 1, False, ow, "ow")
            attn(list(range(0, qb + 1)), 0, True, osl, "osl")

            ob = osum[j]
            for x, (src, key) in enumerate(((ocmp, "ocmp"), (osl, "osl"), (ow, "ow"))):
                if x > 0:
                    P.op("dve", lambda e, src=src, x=x: e.reciprocal(out=rd[:, 4 * x:4 * x + 4], in_=src[:, :, 64]),
                         reads=[key], writes=["rd"])
                P.op("dve", lambda e, x=x: e.tensor_tensor(
                    out=rd[:, 4 * x:4 * x + 4], in0=rd[:, 4 * x:4 * x + 4],
                    in1=gs[:, qb, :].rearrange("p (h x) -> p h x", x=3)[:, :, x], op=ALU.mult), reads=["rd", "gs"], writes=["rd"])
                for hh in range(4):
                    if x == 0:
                        P.op("dve", lambda e, hh=hh, src=src, x=x: e.tensor_scalar(
                            out=ob[:, hh, :], in0=src[:, hh, 0:64], scalar1=rd[:, 4 * x + hh:4 * x + hh + 1], scalar2=None,
                            op0=ALU.mult), reads=[key, "rd"], writes=[("osum", j)])
                    else:
                        P.op("dve", lambda e, hh=hh, src=src, x=x: e.scalar_tensor_tensor(
                            out=ob[:, hh, :], in0=src[:, hh, 0:64], scalar=rd[:, 4 * x + hh:4 * x + hh + 1], in1=ob[:, hh, :],
                            op0=ALU.mult, op1=ALU.add), reads=[key, "rd", ("osum", j)], writes=[("osum", j)])
            P.dma(o_d[q0:q0 + 128, :], ob[:].rearrange("p a b -> p (a b)"), reads=[("osum", j)])

        for qb in range(NQB):
            do_qb(qb)
        P.emit()
    return nc


def nsa_in_maps(P1, T, cmp_pos, k_w1, k_w2, v_w1, v_w2):
    P1 = P1.reshape(2, T, -1)
    cst = nsa_consts(T)
    NTL = T // 128
    maps = []
    posP = np.zeros((128, 16), np.float32)
    for lp in range(16):
        posP[0:64, lp] = cmp_pos[2 * lp]
        posP[64:128, lp] = cmp_pos[2 * lp + 1]
    w1 = np.stack([w.reshape(16, 128, 256).transpose(1, 0, 2) for w in (k_w1, v_w1)])
    w2 = np.stack([w.reshape(2, 128, 64).transpose(1, 0, 2) for w in (k_w2, v_w2)])
    swap = np.concatenate([np.arange(8, 16), np.arange(0, 8)])
    for c in range(NCORES):
        b, g = c // 4, c % 4
        pb = P1[b]
        q = pb[:, 256 * g:256 * g + 256].reshape(T, 4, 64)
        qT = q.transpose(2, 1, 0)
        col = lambda i: pb[:, 1024 + 256 * i + 64 * g:1024 + 256 * i + 64 * g + 64]
        k_c, v_c, k_s, v_s, k_w, v_w = [col(i) for i in range(6)]
        kT = np.stack([k_s.T, k_w.T])
        vt = np.stack([v.reshape(NTL, 128, 64).transpose(1, 0, 2) for v in (v_s, v_w)])
        KP = np.stack([np.concatenate([a[0::2].T, a[1::2].T], axis=0) for a in (k_c, v_c)])
        gates = pb[:, 2560 + 12 * g:2560 + 12 * g + 12].reshape(NTL, 128, 12).transpose(1, 0, 2)
        f = lambda a: np.ascontiguousarray(a, np.float32)
        maps.append(dict(qT=f(qT), qsw=f(qT[swap]), kT=f(kT), ksw=f(kT[:, swap]), vtok=f(vt), KP=f(KP), posP=posP,
                         w1=f(w1), w2=f(w2), gates=f(gates), C4=cst["C4"], S4=cst["S4"], E=cst["E"], msk=cst["msk"],
                         ov=cst["ov"], bq=cst["bq"]))
    return maps


def nsa_gather(results, T):
    o = np.zeros((2, T, 1024), np.float32)
    for c in range(NCORES):
        b, g = c // 4, c % 4
        o[b, :, 256 * g:256 * g + 256] = results[c]["o_out"]
    return o.reshape(2 * T, 1024)


_T = 8192
_NT = 2048


def _run(nc, maps):
    res = run_bass_kernel_spmd(nc, maps, core_ids=list(range(NCORES)))
    return res.results


def kernel(x, mix_norm, mlp_norm, w_up, w_down, final_norm,
           ev_w_in, ev_qkv_conv, ev_a_log, ev_dt_bias, ev_o_norm, ev_sc_conv, ev_w_out,
           od_w_in, od_cmp_pos, od_cmp_k_w1, od_cmp_k_w2, od_cmp_v_w1, od_cmp_v_w2, od_w_out):
    f = lambda a: np.ascontiguousarray(np.asarray(a), np.float32)
    x = f(x).reshape(2 * _T, D_MODEL)
    ident = np.eye(128, dtype=np.float32)
    rep = lambda v: np.ascontiguousarray(np.tile(f(v)[None, :], (128, 1)))
    sh = lambda a, c: np.ascontiguousarray(a[c * _NT:(c + 1) * _NT])
    shT = lambda a, c: np.ascontiguousarray(a[c * _NT:(c + 1) * _NT].T)
    cat = lambda rs, k: np.concatenate([r[k] for r in rs], axis=0)

    nc = build_dense(_NT, False, False, "inproj", 3600)
    rs = _run(nc, [dict(x=sh(x, c), ident=ident, nw_tail=rep(mix_norm[0]), w_in=f(ev_w_in[0])) for c in range(NCORES)])
    P0 = cat(rs, "p_out")
    nc = build_gdn(_T)
    rs = _run(nc, gdn_in_maps(P0, _T, f(ev_qkv_conv[0]), f(ev_a_log[0]), f(ev_dt_bias[0]), f(ev_o_norm[0]), f(ev_sc_conv[0])))
    y0 = gdn_gather(rs, _T)
    nc = build_dense(_NT, True, True, "inproj", 2608)
    rs = _run(nc, [dict(x=sh(x, c), ident=ident, yT=shT(y0, c), w_out=f(ev_w_out[0]), nw_mlp=rep(mlp_norm[0]),
                        w_up=f(w_up[0]), w_down=f(w_down[0]), nw_tail=rep(mix_norm[1]), w_in=f(od_w_in[0]))
                   for c in range(NCORES)])
    x2 = cat(rs, "x_out")
    P1 = cat(rs, "p_out")
    nc = build_nsa(_T)
    rs = _run(nc, nsa_in_maps(P1, _T, f(od_cmp_pos[0]), f(od_cmp_k_w1[0]), f(od_cmp_k_w2[0]), f(od_cmp_v_w1[0]), f(od_cmp_v_w2[0])))
    o1 = nsa_gather(rs, _T)
    nc = build_dense(_NT, True, True, "final")
    rs = _run(nc, [dict(x=sh(x2, c), ident=ident, yT=shT(o1, c), w_out=f(od_w_out[0]), nw_mlp=rep(mlp_norm[1]),
                        w_up=f(w_up[1]), w_down=f(w_down[1]), nw_tail=rep(final_norm)) for c in range(NCORES)])
    out = cat(rs, "out")
    return out.reshape(2, _T, D_MODEL).astype(np.float32)
```

```python
import numpy as np
import ml_dtypes
from contextlib import ExitStack
import concourse.bass as bass
import concourse.mybir as mybir
from concourse.bass_utils import run_bass_kernel_spmd

F32 = mybir.dt.float32
BF16 = mybir.dt.bfloat16
AF = mybir.ActivationFunctionType
ALU = mybir.AluOpType
AX = mybir.AxisListType

NCORES = 8
D_MODEL = 1024
D_FF = 4096
EPS = 1e-6
_DBG = {}


class _Ins:
    __slots__ = ("eng", "fn", "deps", "isdma", "need_inc", "tok", "q")

    def __init__(self, eng, fn, isdma):
        self.eng = eng
        self.fn = fn
        self.deps = []
        self.isdma = isdma
        self.need_inc = isdma
        self.tok = None


class Prog:
    ENGS = ("pe", "act", "dve", "pool", "sp")
    NDSEM = 12

    def __init__(self, nc, es):
        self.nc = nc
        self.es = es
        self.streams = {e: [] for e in self.ENGS}
        self.last_write = {}
        self.readers = {}
        self.all_dma = []

    def _add(self, eng, fn, reads, writes, isdma):
        ins = _Ins(eng, fn, isdma)
        excl = [k for k in reads if isinstance(k, tuple) and k[0] in ("bk", "ps")]
        if excl:
            writes = list(writes) + [k for k in excl if k not in writes]
        deps = {}
        for k in reads:
            w = self.last_write.get(k)
            if w is not None:
                deps[id(w)] = (w, "raw")
        for k in writes:
            w = self.last_write.get(k)
            if w is not None and id(w) not in deps:
                deps[id(w)] = (w, "waw")
            for r in self.readers.get(k, ()):
                if id(r) not in deps:
                    deps[id(r)] = (r, "war")
        for d, kind in deps.values():
            if d is ins:
                continue
            if d.eng == eng and not d.isdma and not isdma:
                if kind != "raw" or eng == "pe":
                    continue
            d.need_inc = True
            ins.deps.append(d)
        for k in reads:
            self.readers.setdefault(k, []).append(ins)
        for k in writes:
            self.last_write[k] = ins
            self.readers[k] = []
        self.streams[eng].append(ins)
        if isdma:
            self.all_dma.append(ins)
        return ins

    def op(self, eng, fn, reads=(), writes=()):
        return self._add(eng, fn, reads, writes, False)

    def dma(self, out, in_, reads=(), writes=(), q="sp"):
        return self._add(q, lambda e: e.dma_start(out=out, in_=in_), reads, writes, True)

    def _selfcheck(self, csem, dsem):
        val = {}
        pos = {e: 0 for e in self.ENGS}
        dl = {e: [i for i in self.streams[e] if i.isdma] for e in self.ENGS}
        total = sum(len(v) for v in self.streams.values())
        done = 0
        while done < total:
            progress = False
            for e in self.ENGS:
                while pos[e] < len(self.streams[e]):
                    ins = self.streams[e][pos[e]]
                    waits = [d.tok for d in ins.deps]
                    if ins.isdma and ins.q >= self.NDSEM:
                        waits.append(dl[e][ins.q - self.NDSEM].tok)
                    if any(w is None for w in waits):
                        raise RuntimeError("dep without token")
                    if all(val.get(id(sm), 0) >= v for sm, v in waits):
                        if ins.need_inc:
                            val[id(ins.tok[0])] = val.get(id(ins.tok[0]), 0) + (16 if ins.isdma else 1)
                            if val[id(ins.tok[0])] != ins.tok[1]:
                                raise RuntimeError("token mismatch %s %s" % (val[id(ins.tok[0])], ins.tok[1]))
                        pos[e] += 1
                        done += 1
                        progress = True
                    else:
                        break
            if not progress:
                raise RuntimeError("deadlock in program: %s" % {e: (pos[e], len(self.streams[e])) for e in self.ENGS})

    def emit(self):
        nc = self.nc
        es = self.es
        csem = {e: es.enter_context(nc.semaphore("cs_" + e)) for e in self.ENGS}
        dsem = {e: [es.enter_context(nc.semaphore("ds_%s%d" % (e, i))) for i in range(self.NDSEM)]
                for e in self.ENGS if any(i.isdma for i in self.streams[e])}
        for e in self.ENGS:
            cnt = 0
            dcnt = 0
            for ins in self.streams[e]:
                if ins.isdma:
                    ins.tok = (dsem[e][dcnt % self.NDSEM], 16 * (dcnt // self.NDSEM + 1))
                    ins.q = dcnt
                    dcnt += 1
                elif ins.need_inc:
                    cnt += 1
                    ins.tok = (csem[e], cnt)
        streams = self.streams
        NDSEM = self.NDSEM
        self._selfcheck(csem, dsem)

        def run(ename, eng):
            seen = {}
            dlist = [i for i in streams[ename] if i.isdma]
            for ins in streams[ename]:
                waits = [d.tok for d in ins.deps]
                if ins.isdma and ins.q >= NDSEM:
                    waits.append(dlist[ins.q - NDSEM].tok)
                for sem, val in waits:
                    if seen.get(id(sem), 0) >= val:
                        continue
                    seen[id(sem)] = val
                    eng.wait_ge(sem, val)
                r = ins.fn(eng)
                if ins.need_inc:
                    r.then_inc(ins.tok[0], 16 if ins.isdma else 1)
            for ins in dlist[-NDSEM:]:
                sem, val = ins.tok
                if seen.get(id(sem), 0) >= val:
                    continue
                seen[id(sem)] = val
                eng.wait_ge(sem, val)

        with nc.Block() as block:
            @block.tensor
            def _(e):
                run("pe", e)

            @block.scalar
            def _(e):
                run("act", e)

            @block.vector
            def _(e):
                run("dve", e)

            @block.gpsimd
            def _(e):
                run("pool", e)

            @block.sync
            def _(e):
                run("sp", e)


def _sb(nc, es, name, shape, dt):
    return es.enter_context(nc.sbuf_tensor(name, list(shape), dt))


def _ps(nc, es, name, shape, dt):
    return es.enter_context(nc.psum_tensor(name, list(shape), dt))


def build_dense(NT, has_mix, has_mlp, tail, n_in_cols=0):
    nc = bass.Bass("TRN2", target_bir_lowering=False)
    GT = 4
    NG = NT // (128 * GT)
    x = nc.dram_tensor("x", [NT, D_MODEL], F32, kind="ExternalInput").ap()
    ident_d = nc.dram_tensor("ident", [128, 128], F32, kind="ExternalInput").ap()
    if has_mix:
        yT = nc.dram_tensor("yT", [D_MODEL, NT], F32, kind="ExternalInput").ap()
        w_out = nc.dram_tensor("w_out", [D_MODEL, D_MODEL], F32, kind="ExternalInput").ap()
    if has_mlp:
        nw_mlp = nc.dram_tensor("nw_mlp", [128, D_MODEL], F32, kind="ExternalInput").ap()
        w_up = nc.dram_tensor("w_up", [D_MODEL, D_FF], F32, kind="ExternalInput").ap()
        w_down = nc.dram_tensor("w_down", [D_FF, D_MODEL], F32, kind="ExternalInput").ap()
    nw_tail = nc.dram_tensor("nw_tail", [128, D_MODEL], F32, kind="ExternalInput").ap()
    if tail == "inproj":
        w_in = nc.dram_tensor("w_in", [D_MODEL, n_in_cols], F32, kind="ExternalInput").ap()
        p_out = nc.dram_tensor("p_out", [NT, n_in_cols], F32, kind="ExternalOutput").ap()
        if has_mix or has_mlp:
            x_out = nc.dram_tensor("x_out", [NT, D_MODEL], F32, kind="ExternalOutput").ap()
    else:
        out = nc.dram_tensor("out", [NT, D_MODEL], F32, kind="ExternalOutput").ap()

    with ExitStack() as es:
        P = Prog(nc, es)
        ident = _sb(nc, es, "ident_sb", [128, 128], F32)
        xg = [_sb(nc, es, "xg%d" % i, [128, GT, D_MODEL], F32) for i in range(2)]
        hn = [_sb(nc, es, "hn%d" % i, [128, D_MODEL], F32) for i in range(2)]
        hT = _sb(nc, es, "hT", [128, 8, 128 * GT], BF16)
        wst = [_sb(nc, es, "wst%d" % i, [128, 4096], F32) for i in range(2)]
        wbf = [_sb(nc, es, "wbf%d" % i, [128, 4096], BF16) for i in range(2)]
        nwt = _sb(nc, es, "nwt", [128, D_MODEL], F32)
        ss = _sb(nc, es, "ss", [128, 8], F32)
        sq_scr = _sb(nc, es, "sq_scr", [128, D_MODEL], F32)
        ot = [_sb(nc, es, "ot%d" % i, [128, 512], F32) for i in range(2)]
        if has_mlp:
            nwm = _sb(nc, es, "nwm", [128, D_MODEL], F32)
            aT = _sb(nc, es, "aT", [128, 32, 128 * GT], BF16)
            rl = [_sb(nc, es, "rl%d" % i, [128, 512], F32) for i in range(2)]
        if has_mix:
            yst = _sb(nc, es, "yst", [128, 8, 128 * GT], F32)
        ps = [_ps(nc, es, "ps%d" % i, [128, 512], F32) for i in range(8)]

        P.dma(ident[:], ident_d, writes=["ident"])
        P.dma(nwt[:], nw_tail, writes=["nwt"])
        if has_mlp:
            P.dma(nwm[:], nw_mlp, writes=["nwm"])

        cnt = {"w": 0, "n": 0, "ps": 0, "ot": 0, "rl": 0}

        def load_w(src_ap, shape3):
            i = cnt["w"] % 2
            cnt["w"] += 1
            a, b = shape3
            stv = wst[i][:, 0:a * b].rearrange("p (a b) -> p a b", a=a)
            bfv = wbf[i][:, 0:a * b].rearrange("p (a b) -> p a b", a=a)
            P.dma(stv, src_ap, writes=[("wst", i)])
            ceng = "pool" if (cnt["w"] % 2 == 0) else "act"
            if ceng == "pool":
                P.op("pool", lambda e: e.tensor_copy(out=wbf[i][:, 0:a * b], in_=wst[i][:, 0:a * b]),
                     reads=[("wst", i)], writes=[("wbf", i)])
            else:
                P.op("act", lambda e: e.copy(out=wbf[i][:, 0:a * b], in_=wst[i][:, 0:a * b]),
                     reads=[("wst", i)], writes=[("wbf", i)])
            return bfv, ("wbf", i)

        def kmajor(w_ap, c0, ncols):
            return w_ap.rearrange("(kc p) c -> p kc c", p=128)[:, :, c0:c0 + ncols]

        def norm_to_hT(xb, gkey, nw_tile, nwkey):
            for t in range(GT):
                j = cnt["n"] % 2
                cnt["n"] += 1
                col = cnt["n"] % 8
                P.op("act", lambda e, t=t, col=col: e.activation(
                    out=sq_scr[:], in_=xg[xb][:, t, :], func=AF.Square, accum_out=ss[:, col:col + 1]),
                    reads=[gkey], writes=["sq_scr", ("ss", col)])
                P.op("act", lambda e, col=col: e.activation(
                    out=ss[:, col:col + 1], in_=ss[:, col:col + 1], func=AF.Sqrt,
                    scale=1.0 / D_MODEL, bias=EPS), reads=[("ss", col)], writes=[("ss", col)])
                P.op("dve", lambda e, col=col: e.reciprocal(out=ss[:, col:col + 1], in_=ss[:, col:col + 1]),
                     reads=[("ss", col)], writes=[("ss", col)])
                P.op("dve", lambda e, t=t, j=j, col=col: e.scalar_tensor_tensor(
                    out=hn[j][:], in0=xg[xb][:, t, :], scalar=ss[:, col:col + 1], in1=nw_tile[:],
                    op0=ALU.mult, op1=ALU.mult), reads=[gkey, ("ss", col), nwkey], writes=[("hn", j)])
                for half in range(2):
                    b = cnt["ps"] % 8
                    cnt["ps"] += 1
                    for q in range(4):
                        kc = half * 4 + q
                        P.op("pe", lambda e, b=b, q=q, kc=kc, j=j: e.transpose(
                            out=ps[b][:, q * 128:(q + 1) * 128], in_=hn[j][:, kc * 128:(kc + 1) * 128],
                            identity=ident[:]), reads=[("hn", j), "ident"], writes=[("ps", b)])
                    P.op("dve", lambda e, b=b, half=half, t=t: e.tensor_copy(
                        out=hT[:, half * 4:half * 4 + 4, t * 128:(t + 1) * 128],
                        in_=ps[b][:].rearrange("p (q c) -> p q c", q=4)),
                        reads=[("ps", b)], writes=["hT"])

        def do_group(g):
            xb = g % 2
            gkey = ("xg", xb)
            tok0 = g * GT * 128
            P.dma(xg[xb][:], x[tok0:tok0 + GT * 128, :].rearrange("(t p) d -> p t d", p=128), writes=[gkey])

            if has_mix:
                P.dma(yst[:], yT.rearrange("(kc p) n -> p kc n", p=128)[:, :, tok0:tok0 + GT * 128],
                      writes=["yst"])
                P.op("pool", lambda e: e.tensor_copy(out=hT[:], in_=yst[:]), reads=["yst"], writes=["hT"])
                for c in range(2):
                    wv, wk = load_w(kmajor(w_out, c * 512, 512), (8, 512))
                    for t in range(GT):
                        b = cnt["ps"] % 8
                        cnt["ps"] += 1
                        for kc in range(8):
                            P.op("pe", lambda e, b=b, kc=kc, t=t, wv=wv: e.matmul(
                                ps[b][:], lhsT=hT[:, kc, t * 128:(t + 1) * 128], rhs=wv[:, kc, :],
                                start=(kc == 0), stop=(kc == 7)), reads=["hT", wk], writes=[("ps", b)])
                        P.op("dve", lambda e, b=b, t=t, c=c: e.tensor_tensor(
                            out=xg[xb][:, t, c * 512:(c + 1) * 512], in0=xg[xb][:, t, c * 512:(c + 1) * 512],
                            in1=ps[b][:], op=ALU.add), reads=[("ps", b), gkey], writes=[gkey])

            if has_mlp:
                norm_to_hT(xb, gkey, nwm, "nwm")
                for uc in range(8):
                    wv, wk = load_w(kmajor(w_up, uc * 512, 512), (8, 512))
                    for fi in range(4):
                        fc = uc * 4 + fi
                        b = cnt["ps"] % 8
                        cnt["ps"] += 1
                        for kc in range(8):
                            P.op("pe", lambda e, b=b, kc=kc, fi=fi, wv=wv: e.matmul(
                                ps[b][:], lhsT=wv[:, kc, fi * 128:(fi + 1) * 128], rhs=hT[:, kc, :],
                                start=(kc == 0), stop=(kc == 7)), reads=["hT", wk], writes=[("ps", b)])
                        r = cnt["rl"] % 2
                        cnt["rl"] += 1
                        P.op("act", lambda e, b=b, r=r: e.activation(out=rl[r][:], in_=ps[b][:], func=AF.Relu),
                             reads=[("ps", b)], writes=[("rl", r)])
                        P.op("dve", lambda e, r=r, fc=fc: e.tensor_tensor(
                            out=aT[:, fc, :], in0=rl[r][:], in1=rl[r][:], op=ALU.mult),
                            reads=[("rl", r)], writes=["aT"])
                for dc in range(8):
                    wv, wk = load_w(w_down.rearrange("(fc p) d -> p fc d", p=128)[:, dc * 4:dc * 4 + 4, :],
                                    (4, 1024))
                    for j in range(4):
                        fc = dc * 4 + j
                        for t in range(GT):
                            for half in range(2):
                                b = t * 2 + half
                                P.op("pe", lambda e, b=b, fc=fc, t=t, j=j, half=half, wv=wv: e.matmul(
                                    ps[b][:], lhsT=aT[:, fc, t * 128:(t + 1) * 128],
                                    rhs=wv[:, j, half * 512:(half + 1) * 512],
                                    start=(fc == 0), stop=(fc == 31)), reads=["aT", wk], writes=[("ps", b)])
                for t in range(GT):
                    for half in range(2):
                        b = t * 2 + half
                        P.op("dve", lambda e, b=b, t=t, half=half: e.tensor_tensor(
                            out=xg[xb][:, t, half * 512:(half + 1) * 512],
                            in0=xg[xb][:, t, half * 512:(half + 1) * 512], in1=ps[b][:], op=ALU.add),
                            reads=[("ps", b), gkey], writes=[gkey])
                cnt["ps"] = 0

            if tail == "inproj":
                if has_mix or has_mlp:
                    P.dma(x_out[tok0:tok0 + GT * 128, :].rearrange("(t p) d -> p t d", p=128), xg[xb][:],
                          reads=[gkey])
                norm_to_hT(xb, gkey, nwt, "nwt")
                c0 = 0
                while c0 < n_in_cols:
                    ncol = min(512, n_in_cols - c0)
                    wv, wk = load_w(kmajor(w_in, c0, ncol), (8, ncol))
                    for t in range(GT):
                        b = cnt["ps"] % 8
                        cnt["ps"] += 1
                        for kc in range(8):
                            P.op("pe", lambda e, b=b, kc=kc, t=t, wv=wv, ncol=ncol: e.matmul(
                                ps[b][:, 0:ncol], lhsT=hT[:, kc, t * 128:(t + 1) * 128], rhs=wv[:, kc, :],
                                start=(kc == 0), stop=(kc == 7)), reads=["hT", wk], writes=[("ps", b)])
                        o = cnt["ot"] % 2
                        cnt["ot"] += 1
                        P.op("act", lambda e, b=b, o=o, ncol=ncol: e.copy(out=ot[o][:, 0:ncol], in_=ps[b][:, 0:ncol]),
                             reads=[("ps", b)], writes=[("ot", o)])
                        P.dma(p_out[tok0 + t * 128:tok0 + (t + 1) * 128, c0:c0 + ncol], ot[o][:, 0:ncol],
                              reads=[("ot", o)])
                    c0 += ncol
            else:
                for t in range(GT):
                    col = cnt["n"] % 8
                    cnt["n"] += 1
                    P.op("act", lambda e, t=t, col=col: e.activation(
                        out=sq_scr[:], in_=xg[xb][:, t, :], func=AF.Square, accum_out=ss[:, col:col + 1]),
                        reads=[gkey], writes=["sq_scr", ("ss", col)])
                    P.op("act", lambda e, col=col: e.activation(
                        out=ss[:, col:col + 1], in_=ss[:, col:col + 1], func=AF.Sqrt,
                        scale=1.0 / D_MODEL, bias=EPS), reads=[("ss", col)], writes=[("ss", col)])
                    P.op("dve", lambda e, col=col: e.reciprocal(out=ss[:, col:col + 1], in_=ss[:, col:col + 1]),
                         reads=[("ss", col)], writes=[("ss", col)])
                    j = cnt["n"] % 2
                    P.op("dve", lambda e, t=t, j=j, col=col: e.scalar_tensor_tensor(
                        out=hn[j][:], in0=xg[xb][:, t, :], scalar=ss[:, col:col + 1], in1=nwt[:],
                        op0=ALU.mult, op1=ALU.mult), reads=[gkey, ("ss", col), "nwt"], writes=[("hn", j)])
                    P.dma(out[tok0 + t * 128:tok0 + (t + 1) * 128, :], hn[j][:], reads=[("hn", j)])
        for g in range(NG):
            do_group(g)
        P.emit()
    return nc


def gdn_consts():
    p = np.arange(128)
    same = (p[:, None] // 64) == (p[None, :] // 64)
    c = np.zeros((8, 128, 128), np.float32)
    c[0] = np.eye(128)
    c[1] = same & (p[:, None] <= p[None, :])
    c[2] = same
    c[3] = (p[:, None] < 64) * np.ones((1, 128))
    c[4] = (p[:, None] >= 64) * np.ones((1, 128))
    c[5] = 0.125 * (same & (p[:, None] <= p[None, :]))
    c[6] = same & (p[None, :] < p[:, None])
    c[7] = 1.0
    return np.ascontiguousarray(c.transpose(1, 0, 2))


def build_gdn(T):
    nc = bass.Bass("TRN2", target_bir_lowering=False)
    NTL = T // 128
    NCH = T // 64
    SEG = min(T, 2048)
    NSEG = T // SEG
    qkvT = nc.dram_tensor("qkvT", [3, 128, T], F32, kind="ExternalInput").ap()
    cw_d = nc.dram_tensor("cw", [128, 12], F32, kind="ExternalInput").ap()
    z_d = nc.dram_tensor("z_l", [64, NCH, 128], F32, kind="ExternalInput").ap()
    ab_d = nc.dram_tensor("ab", [128, 4, NTL], F32, kind="ExternalInput").ap()
    hp_d = nc.dram_tensor("hp", [128, 4], F32, kind="ExternalInput").ap()
    onw_d = nc.dram_tensor("onw", [64, 64], F32, kind="ExternalInput").ap()
    scT = nc.dram_tensor("scT", [3, 128, T], F32, kind="ExternalInput").ap()
    scw_d = nc.dram_tensor("scw", [128, 3], F32, kind="ExternalInput").ap()
    cst_d = nc.dram_tensor("cst", [128, 8, 128], F32, kind="ExternalInput").ap()
    ya_d = nc.dram_tensor("ya_l", [64, NCH, 128], F32, kind="ExternalOutput").ap()
    yb_d = nc.dram_tensor("ybT", [128, T], F32, kind="ExternalOutput").ap()

    with ExitStack() as es:
        P = Prog(nc, es)
        S = lambda name, shape, dt=F32: _sb(nc, es, name, shape, dt)
        cst = S("cst_sb", [128, 8, 128])
        ident, LTm, BO, SEL0, SEL1, MUs, ML, ONES = [cst[:, i, :] for i in range(8)]
        ident_t = S("ident_t", [128, 128])
        ident = ident_t[:]
        cw = S("cw_sb", [128, 12])
        hp = S("hp_sb", [128, 4])
        onw = S("onw_sb", [64, 64])
        scw = S("scw_sb", [128, 3])
        ab = S("ab_sb", [128, 4, NTL])
        fT = [S("fT%d" % s, [128, T]) for s in range(3)]
        raw = S("raw", [128, SEG + 3])
        raw2 = S("raw2", [128, SEG + 3])
        raw3 = S("raw3", [128, SEG + 3])
        acc = S("acc", [128, SEG])
        sqs = S("sqs", [128, SEG])
        rn = [S("rn%d" % i, [128, 512]) for i in range(2)]
        ps = [_ps(nc, es, "ps%d" % i, [128, 512], F32) for i in range(8)]

        P.dma(cst[:], cst_d, writes=["cst"])
        P.dma(ident_t[:], cst_d[:, 0, :], writes=["cst"])
        P.dma(cw[:], cw_d, writes=["cw"])
        P.dma(hp[:], hp_d, writes=["hp"])
        P.dma(onw[:], onw_d, writes=["onw"])
        P.dma(scw[:], scw_d, writes=["scw"])
        P.dma(ab[:], ab_d, writes=["ab"])

        def do_sc(sg):
            t0 = sg * SEG
            lo = 0 if sg > 0 else 2
            for i, (buf, key) in enumerate(((raw, "raw"), (raw2, "raw2"), (raw3, "raw3"))):
                if sg == 0:
                    P.op("pool", lambda e, buf=buf: e.memset(buf[:, 0:2], 0.0), writes=[key])
                P.dma(buf[:, lo:SEG + 2], scT[i][:, t0 - 2 + lo:t0 + SEG], writes=[key])
            P.op("pool", lambda e: e.tensor_tensor(out=raw2[:, 0:SEG + 2], in0=raw2[:, 0:SEG + 2],
                                                    in1=raw3[:, 0:SEG + 2], op=ALU.mult),
                 reads=["raw2", "raw3"], writes=["raw2"])
            P.op("dve", lambda e: e.tensor_scalar(out=acc[:], in0=raw2[:, 0:SEG], scalar1=scw[:, 0:1], scalar2=None,
                                                  op0=ALU.mult), reads=["raw2", "scw"], writes=["acc"])
            for j in (1, 2):
                P.op("dve", lambda e, j=j: e.scalar_tensor_tensor(out=acc[:], in0=raw2[:, j:j + SEG],
                                                                   scalar=scw[:, j:j + 1], in1=acc[:],
                                                                   op0=ALU.mult, op1=ALU.add),
                     reads=["raw2", "scw", "acc"], writes=["acc"])
            P.op("pool", lambda e: e.tensor_tensor(out=acc[:], in0=acc[:], in1=raw[:, 2:SEG + 2], op=ALU.mult),
                 reads=["acc", "raw"], writes=["acc"])
            P.dma(yb_d[:, t0:t0 + SEG], acc[:], reads=["acc"])

        for sg in range(NSEG):
            do_sc(sg)

        def do_p1(sg, s):
            t0 = sg * SEG
            lo = 0 if sg > 0 else 3
            if sg == 0:
                P.op("pool", lambda e: e.memset(raw[:, 0:3], 0.0), writes=["raw"])
            P.dma(raw[:, lo:SEG + 3], qkvT[s][:, t0 - 3 + lo:t0 + SEG], writes=["raw"])
            P.op("dve", lambda e: e.tensor_scalar(out=acc[:], in0=raw[:, 0:SEG], scalar1=cw[:, s * 4:s * 4 + 1],
                                                  scalar2=None, op0=ALU.mult), reads=["raw", "cw"], writes=["acc"])
            for j in (1, 2, 3):
                P.op("dve", lambda e, j=j: e.scalar_tensor_tensor(out=acc[:], in0=raw[:, j:j + SEG],
                                                                   scalar=cw[:, s * 4 + j:s * 4 + j + 1], in1=acc[:],
                                                                   op0=ALU.mult, op1=ALU.add),
                     reads=["raw", "cw", "acc"], writes=["acc"])
            dst = fT[s][:, t0:t0 + SEG]
            fkey = ("fT", s)
            P.op("act", lambda e: e.activation(out=dst, in_=acc[:], func=AF.Silu), reads=["acc"], writes=[fkey])
            if s < 2:
                P.op("act", lambda e: e.activation(out=sqs[:], in_=dst, func=AF.Square), reads=[fkey], writes=["sqs"])
                for blk in range(SEG // 512):
                    r = blk % 2
                    sl = slice(blk * 512, (blk + 1) * 512)
                    P.op("pe", lambda e, sl=sl: e.matmul(ps[7][:], lhsT=BO, rhs=sqs[:, sl], start=True, stop=True),
                         reads=["sqs", "cst"], writes=[("ps", 7)])
                    P.op("act", lambda e, r=r: e.activation(out=rn[r][:], in_=ps[7][:], func=AF.Sqrt, bias=EPS),
                         reads=[("ps", 7)], writes=[("rn", r)])
                    P.op("dve", lambda e, r=r: e.reciprocal(out=rn[r][:], in_=rn[r][:]),
                         reads=[("rn", r)], writes=[("rn", r)])
                    P.op("dve", lambda e, r=r, sl=sl: e.tensor_tensor(
                        out=fT[s][:, t0 + sl.start:t0 + sl.stop], in0=fT[s][:, t0 + sl.start:t0 + sl.stop],
                        in1=rn[r][:], op=ALU.mult), reads=[("rn", r), fkey], writes=[fkey])

        for sg in range(NSEG):
            for s in range(3):
                if _DBG.get("stop", 9) >= 2:
                    do_p1(sg, s)

        sc = {}
        for nm in ("g", "beta", "gc", "ngc", "egc", "bg", "dk", "egl0", "egl1"):
            sc[nm] = [S("sc_%s%d" % (nm, h), [128, NTL]) for h in range(2)]
        nea = S("nea", [128, 2])
        P.op("act", lambda e: e.activation(out=nea[:], in_=hp[:, 0:2], func=AF.Exp), reads=["hp"], writes=["nea"])
        P.op("dve", lambda e: e.tensor_scalar(out=nea[:], in0=nea[:], scalar1=-1.0, scalar2=None, op0=ALU.mult),
             reads=["nea"], writes=["nea"])

        def do_scal(h):
            k = lambda nm: ("sc", nm, h)
            g, beta = sc["g"][h], sc["beta"][h]
            P.op("act", lambda e: e.activation(out=g[:], in_=ab[:, h, :], func=AF.Exp, bias=hp[:, 2 + h:3 + h]),
                 reads=["ab", "hp"], writes=[k("g")])
            P.op("act", lambda e: e.activation(out=g[:], in_=g[:], func=AF.Ln, bias=1.0),
                 reads=[k("g")], writes=[k("g")])
            P.op("dve", lambda e: e.tensor_scalar(out=g[:], in0=g[:], scalar1=nea[:, h:h + 1], scalar2=None,
                                                  op0=ALU.mult), reads=[k("g"), "nea"], writes=[k("g")])
            P.op("act", lambda e: e.activation(out=beta[:], in_=ab[:, 2 + h, :], func=AF.Sigmoid),
                 reads=["ab"], writes=[k("beta")])
            P.op("pe", lambda e: e.matmul(ps[7][:, 0:NTL], lhsT=LTm, rhs=g[:], start=True, stop=True),
                 reads=[k("g"), "cst"], writes=[("ps", 7)])
            P.op("dve", lambda e: e.tensor_copy(out=sc["gc"][h][:], in_=ps[7][:, 0:NTL]),
                 reads=[("ps", 7)], writes=[k("gc")])
            P.op("dve", lambda e: e.tensor_scalar(out=sc["ngc"][h][:], in0=ps[7][:, 0:NTL], scalar1=-1.0, scalar2=None,
                                                  op0=ALU.mult), reads=[("ps", 7)], writes=[k("ngc")])
            P.op("act", lambda e: e.activation(out=sc["egc"][h][:], in_=ps[7][:, 0:NTL], func=AF.Exp),
                 reads=[("ps", 7)], writes=[k("egc")])
            P.op("dve", lambda e: e.tensor_tensor(out=sc["bg"][h][:], in0=sc["egc"][h][:], in1=beta[:], op=ALU.mult),
                 reads=[k("egc"), k("beta")], writes=[k("bg")])
            P.op("pe", lambda e: e.matmul(ps[7][:, 0:NTL], lhsT=BO, rhs=g[:], start=True, stop=True),
                 reads=[k("g"), "cst"], writes=[("ps", 7)])
            P.op("dve", lambda e: e.tensor_tensor(out=sc["dk"][h][:], in0=ps[7][:, 0:NTL], in1=sc["gc"][h][:],
                                                  op=ALU.subtract), reads=[("ps", 7), k("gc")], writes=[k("dk")])
            P.op("act", lambda e: e.activation(out=sc["dk"][h][:], in_=sc["dk"][h][:], func=AF.Exp),
                 reads=[k("dk")], writes=[k("dk")])
            for c, SEL in ((0, SEL0), (1, SEL1)):
                P.op("pe", lambda e, SEL=SEL: e.matmul(ps[7][:, 0:NTL], lhsT=SEL, rhs=g[:], start=True, stop=True),
                     reads=[k("g"), "cst"], writes=[("ps", 7)])
                P.op("act", lambda e, c=c: e.activation(out=sc["egl%d" % c][h][:], in_=ps[7][:, 0:NTL], func=AF.Exp),
                     reads=[("ps", 7)], writes=[k("egl%d" % c)])

        for h in range(2):
            if _DBG.get("stop", 9) >= 3:
                do_scal(h)

        NQ = 24
        qcnt = [0]

        def nq():
            i = qcnt[0] % NQ
            qcnt[0] += 1
            bk, qt = i % 6, i // 6
            return ps[bk][:, qt * 128:(qt + 1) * 128], ("bk", bk)

        NB = 2
        tmp = {}

        def T_(nm, shape, n=NB * 2):
            tmp[nm] = [S("t_%s%d" % (nm, i), shape) for i in range(n)]

        T_("ktok", [128, 128], NB)
        T_("vtok", [128, 128], NB)
        T_("qtok", [128, 128], NB)
        for nm in ("dg", "dsym", "du", "dl", "attnT", "M", "MT", "RT", "Pa", "PaT", "Pb", "PbT"):
            T_(nm, [128, 128])
        T_("vb", [128, 64]); T_("kbg", [128, 64]); T_("kd0", [128, 64]); T_("kd1", [128, 64])
        T_("u", [128, 64]); T_("wT", [64, 128]); T_("qgT", [64, 128]); T_("qg", [128, 64])
        T_("tabs", [128, 128])
        vnew = [S("vnew%d" % h, [128, 64]) for h in range(2)]
        Sst = [S("Sst%d" % h, [64, 64]) for h in range(2)]
        for h in range(2):
            P.op("pool", lambda e, h=h: e.memset(vnew[h][:], 0.0), writes=[("vnew", h)])
            P.op("pool", lambda e, h=h: e.memset(Sst[h][:], 0.0), writes=[("S", h)])
        osb = [S("osb%d" % i, [64, 4, 64]) for i in range(2)]
        osq = [S("osq%d" % i, [64, 4, 64]) for i in range(2)]
        oss = [S("oss%d" % i, [64, 4]) for i in range(2)]
        zt = [S("zt%d" % i, [64, 2, 128]) for i in range(2)]
        yt = [S("yt%d" % i, [64, 2, 128]) for i in range(2)]

        stt = {}

        def front(tt):
            tb = tt % NB
            c0 = tt * 128
            toks = {}
            for s, nm in ((0, "qtok"), (1, "ktok"), (2, "vtok")):
                pq, kq = nq()
                P.op("pe", lambda e, pq=pq, s=s: e.matmul(pq, lhsT=fT[s][:, c0:c0 + 128], rhs=ident, start=True, stop=True),
                     reads=[("fT", s), "cst"], writes=[kq])
                dstt = tmp[nm][tb]
                if _DBG.get('nocopy'):
                    continue
                P.op("dve", lambda e, pq=pq, dstt=dstt: e.tensor_copy(out=dstt[:], in_=pq), reads=[kq], writes=[(nm, tb)])
                toks[nm] = dstt
            P.dma(zt[tb][:], z_d[:, 2 * tt:2 * tt + 2, :], writes=[("zt", tb)])
            P.op("act", lambda e: e.activation(out=zt[tb][:], in_=zt[tb][:], func=AF.Silu),
                 reads=[("zt", tb)], writes=[("zt", tb)])

            st = {}
            stt[tt] = st

            def prep_head(h):
                ix = tb * 2 + h
                hb = 64 * h
                kk = lambda nm, ix=ix: (nm, ix)
                g = lambda nm, ix=ix: tmp[nm][ix]
                col = lambda nm, h=h: sc[nm][h][:, tt:tt + 1]
                kTh = fT[1][hb:hb + 64, c0:c0 + 128]
                qTh = fT[0][hb:hb + 64, c0:c0 + 128]
                P.op("dve", lambda e, g=g, col=col: e.tensor_scalar(out=g("dg")[:], in0=ident, scalar1=col("gc"),
                                                                     scalar2=None, op0=ALU.mult),
                     reads=["cst", ("sc", "gc", h)], writes=[kk("dg")])
                pR, kR = nq()
                P.op("pe", lambda e, pR=pR, g=g: e.matmul(pR, lhsT=ONES, rhs=g("dg")[:], start=True, stop=True),
                     reads=[kk("dg"), "cst"], writes=[kR])
                P.op("act", lambda e, pR=pR, g=g, col=col: e.activation(
                    out=g("tabs")[:], in_=pR, func=AF.Abs, bias=col("ngc")),
                    reads=[kR, ("sc", "ngc", h)], writes=[kk("tabs")])
                P.op("act", lambda e, g=g: e.activation(out=g("dsym")[:], in_=g("tabs")[:], func=AF.Exp, scale=-1.0),
                     reads=[kk("tabs")], writes=[kk("dsym")])
                P.op("pool", lambda e, g=g: e.tensor_tensor(out=g("du")[:], in0=g("dsym")[:], in1=MUs, op=ALU.mult),
                     reads=[kk("dsym"), "cst"], writes=[kk("du")])
                P.op("pool", lambda e, g=g: e.tensor_tensor(out=g("dl")[:], in0=g("dsym")[:], in1=ML, op=ALU.mult),
                     reads=[kk("dsym"), "cst"], writes=[kk("dl")])
                pKK, kKK = nq()
                P.op("pe", lambda e, pKK=pKK, kTh=kTh: e.matmul(pKK, lhsT=kTh, rhs=kTh, start=True, stop=True),
                     reads=[("fT", 1)], writes=[kKK])
                pQK, kQK = nq()
                P.op("pe", lambda e, pQK=pQK, kTh=kTh, qTh=qTh: e.matmul(pQK, lhsT=kTh, rhs=qTh, start=True, stop=True),
                     reads=[("fT", 1), ("fT", 0)], writes=[kQK])
                P.op("dve", lambda e, pQK=pQK, g=g: e.tensor_tensor(out=g("attnT")[:], in0=pQK, in1=g("du")[:],
                                                                     op=ALU.mult),
                     reads=[kQK, kk("du")], writes=[kk("attnT")])
                P.op("dve", lambda e, pKK=pKK, g=g, col=col: e.scalar_tensor_tensor(
                    out=g("M")[:], in0=pKK, scalar=col("beta"), in1=g("dl")[:], op0=ALU.mult, op1=ALU.mult),
                    reads=[kKK, kk("dl"), ("sc", "beta", h)], writes=[kk("M")])
                ktok, vtok, qtok = toks["ktok"], toks["vtok"], toks["qtok"]
                P.op("pool", lambda e, g=g, col=col, vtok=vtok: e.tensor_scalar(
                    out=g("vb")[:], in0=vtok[:, hb:hb + 64], scalar1=col("beta"), scalar2=None, op0=ALU.mult),
                    reads=[("vtok", tb), ("sc", "beta", h)], writes=[kk("vb")])
                P.op("pool", lambda e, g=g, col=col, ktok=ktok: e.tensor_scalar(
                    out=g("kbg")[:], in0=ktok[:, hb:hb + 64], scalar1=col("bg"), scalar2=None, op0=ALU.mult),
                    reads=[("ktok", tb), ("sc", "bg", h)], writes=[kk("kbg")])
                P.op("pool", lambda e, g=g, col=col, ktok=ktok: e.tensor_scalar(
                    out=g("kd0")[:], in0=ktok[:, hb:hb + 64], scalar1=col("dk"), scalar2=SEL0[:, 0:1],
                    op0=ALU.mult, op1=ALU.mult), reads=[("ktok", tb), ("sc", "dk", h), "cst"], writes=[kk("kd0")])
                P.op("pool", lambda e, g=g, col=col, ktok=ktok: e.tensor_scalar(
                    out=g("kd1")[:], in0=ktok[:, hb:hb + 64], scalar1=col("dk"), scalar2=SEL1[:, 0:1],
                    op0=ALU.mult, op1=ALU.mult), reads=[("ktok", tb), ("sc", "dk", h), "cst"], writes=[kk("kd1")])
                P.op("dve", lambda e, g=g, col=col, qtok=qtok: e.tensor_scalar(
                    out=g("qg")[:], in0=qtok[:, hb:hb + 64], scalar1=col("egc"), scalar2=0.125,
                    op0=ALU.mult, op1=ALU.mult), reads=[("qtok", tb), ("sc", "egc", h)], writes=[kk("qg")])
                pqg, kqg = nq()
                P.op("pe", lambda e, pqg=pqg, g=g: e.matmul(pqg[0:64, :], lhsT=g("qg")[:], rhs=ident, start=True, stop=True),
                     reads=[kk("qg"), "cst"], writes=[kqg])
                P.op("act", lambda e, pqg=pqg, g=g: e.copy(out=g("qgT")[:], in_=pqg[0:64, :]),
                     reads=[kqg], writes=[kk("qgT")])
                pMT, kMT = nq()
                P.op("pe", lambda e, pMT=pMT, g=g: e.matmul(pMT, lhsT=g("M")[:], rhs=ident, start=True, stop=True),
                     reads=[kk("M"), "cst"], writes=[kMT])
                P.op("act", lambda e, pMT=pMT, g=g: e.copy(out=g("MT")[:], in_=pMT), reads=[kMT], writes=[kk("MT")])
                P.op("dve", lambda e, pMT=pMT, g=g: e.tensor_tensor(out=g("RT")[:], in0=ident, in1=pMT, op=ALU.subtract),
                     reads=[kMT, "cst"], writes=[kk("RT")])
                st[h] = dict(ix=ix, P=("M", "MT"))

            for h in range(2):
                prep_head(h)

            yield
            for lvl in range(5):
                if lvl > 0:
                    yield
                last = lvl == 4
                nxt = ("Pa", "PaT") if lvl % 2 == 0 else ("Pb", "PbT")
                pend = {}
                for h in range(2):
                    ix = st[h]["ix"]
                    Pn, PTn = st[h]["P"]
                    Pk, PTk = tmp[Pn][ix], tmp[PTn][ix]
                    p1, k1 = nq()
                    P.op("pe", lambda e, p1=p1, Pk=Pk, PTk=PTk: e.matmul(p1, lhsT=PTk[:], rhs=Pk[:], start=True, stop=True),
                         reads=[(Pn, ix), (PTn, ix)], writes=[k1])
                    p2 = k2 = None
                    if not last:
                        p2, k2 = nq()
                        P.op("pe", lambda e, p2=p2, Pk=Pk, PTk=PTk: e.matmul(p2, lhsT=Pk[:], rhs=PTk[:], start=True, stop=True),
                             reads=[(Pn, ix), (PTn, ix)], writes=[k2])
                    pend[h] = (p1, k1, p2, k2)
                for h in range(2):
                    ix = st[h]["ix"]
                    p1, k1, p2, k2 = pend[h]
                    Pnew, PTnew = tmp[nxt[0]][ix], tmp[nxt[1]][ix]
                    P.op("act", lambda e, p1=p1, Pnew=Pnew: e.copy(out=Pnew[:], in_=p1), reads=[k1], writes=[(nxt[0], ix)])
                    if not last:
                        P.op("dve", lambda e, p2=p2, PTnew=PTnew: e.tensor_copy(out=PTnew[:], in_=p2),
                             reads=[k2], writes=[(nxt[1], ix)])
                for h in range(2):
                    ix = st[h]["ix"]
                    Pnew = tmp[nxt[0]][ix]
                    RT = tmp["RT"][ix]
                    p3, k3 = nq()
                    P.op("pe", lambda e, p3=p3, Pnew=Pnew, RT=RT: e.matmul(p3, lhsT=Pnew[:], rhs=RT[:], start=True, stop=True),
                         reads=[(nxt[0], ix), ("RT", ix)], writes=[k3])
                    P.op("dve", lambda e, p3=p3, RT=RT: e.tensor_tensor(out=RT[:], in0=RT[:], in1=p3, op=ALU.add),
                         reads=[k3, ("RT", ix)], writes=[("RT", ix)])
                    st[h]["P"] = nxt

            yield
            for h in range(2):
                ix = st[h]["ix"]
                RT = tmp["RT"][ix]
                pu, ku = nq()
                P.op("pe", lambda e, pu=pu, RT=RT, ix=ix: e.matmul(pu[:, 0:64], lhsT=RT[:], rhs=tmp["vb"][ix][:],
                                                                   start=True, stop=True),
                     reads=[("RT", ix), ("vb", ix)], writes=[ku])
                P.op("act", lambda e, pu=pu, ix=ix: e.copy(out=tmp["u"][ix][:], in_=pu[:, 0:64]),
                     reads=[ku], writes=[("u", ix)])
                pw, kw = nq()
                P.op("pe", lambda e, pw=pw, RT=RT, ix=ix: e.matmul(pw[0:64, :], lhsT=tmp["kbg"][ix][:], rhs=RT[:],
                                                                   start=True, stop=True),
                     reads=[("RT", ix), ("kbg", ix)], writes=[kw])
                P.op("dve", lambda e, pw=pw, ix=ix: e.tensor_copy(out=tmp["wT"][ix][:], in_=pw[0:64, :]),
                     reads=[kw], writes=[("wT", ix)])

            yield

        def back(tt):
            tb = tt % NB
            st = stt.pop(tt)
            ob = tt % 2
            for c in range(2):
                yield
                rs = slice(c * 64, (c + 1) * 64)
                for h in range(2):
                    ix = st[h]["ix"]
                    pws, kws = nq()
                    P.op("pe", lambda e, pws=pws, ix=ix, h=h: e.matmul(pws[:, 0:64], lhsT=tmp["wT"][ix][:], rhs=Sst[h][:],
                                                                       start=True, stop=True),
                         reads=[("wT", ix), ("S", h)], writes=[kws])
                    P.op("dve", lambda e, pws=pws, ix=ix, h=h, rs=rs: e.tensor_tensor(
                        out=vnew[h][rs, :], in0=tmp["u"][ix][rs, :], in1=pws[rs, 0:64], op=ALU.subtract),
                        reads=[kws, ("u", ix)], writes=[("vnew", h)])
                    po = ps[6 + h][0:64, c * 64:(c + 1) * 64]
                    ko = ("bk", 6 + h)
                    P.op("pe", lambda e, po=po, ix=ix, h=h, rs=rs: e.matmul(po, lhsT=tmp["qgT"][ix][:, rs], rhs=Sst[h][:],
                                                                            start=True, stop=False),
                         reads=[("qgT", ix), ("S", h)], writes=[ko])
                    P.op("pe", lambda e, po=po, ix=ix, h=h, rs=rs: e.matmul(po, lhsT=tmp["attnT"][ix][:, rs], rhs=vnew[h][:],
                                                                            start=False, stop=True),
                         reads=[("attnT", ix), ("vnew", h)], writes=[ko])
                    pS, kS = nq()
                    kdn = "kd%d" % c
                    P.op("pe", lambda e, pS=pS, ix=ix, h=h, kdn=kdn: e.matmul(pS[0:64, 0:64], lhsT=tmp[kdn][ix][:], rhs=vnew[h][:],
                                                                              start=True, stop=True),
                         reads=[(kdn, ix), ("vnew", h)], writes=[kS])
                    P.op("dve", lambda e, pS=pS, h=h, c=c: e.scalar_tensor_tensor(
                        out=Sst[h][:], in0=Sst[h][:], scalar=sc["egl%d" % c][h][0:64, tt:tt + 1], in1=pS[0:64, 0:64],
                        op0=ALU.mult, op1=ALU.add), reads=[kS, ("S", h), ("sc", "egl%d" % c, h)], writes=[("S", h)])
                    P.op("act", lambda e, po=po, c=c, h=h: e.copy(out=osb[ob][:, c * 2 + h, :], in_=po),
                         reads=[ko], writes=[("osb", ob)])
            yield
            P.op("pool", lambda e: e.tensor_tensor(out=osq[ob][:], in0=osb[ob][:], in1=osb[ob][:], op=ALU.mult),
                 reads=[("osb", ob)], writes=[("osq", ob)])
            P.op("dve", lambda e: e.tensor_reduce(out=oss[ob][:], in_=osq[ob][:], axis=AX.X, op=ALU.add),
                 reads=[("osq", ob)], writes=[("oss", ob)])
            P.op("act", lambda e: e.activation(out=oss[ob][:], in_=oss[ob][:], func=AF.Sqrt, scale=1.0 / 64, bias=EPS),
                 reads=[("oss", ob)], writes=[("oss", ob)])
            P.op("dve", lambda e: e.reciprocal(out=oss[ob][:], in_=oss[ob][:]), reads=[("oss", ob)], writes=[("oss", ob)])
            for c in range(2):
                for h in range(2):
                    P.op("dve", lambda e, c=c, h=h: e.scalar_tensor_tensor(
                        out=yt[ob][:, c, h * 64:(h + 1) * 64], in0=osb[ob][:, c * 2 + h, :],
                        scalar=oss[ob][:, c * 2 + h:c * 2 + h + 1], in1=onw[:], op0=ALU.mult, op1=ALU.mult),
                        reads=[("osb", ob), ("oss", ob), "onw"], writes=[("yt", ob)])
            P.op("pool", lambda e: e.tensor_tensor(out=yt[ob][:], in0=yt[ob][:], in1=zt[tb][:], op=ALU.mult),
                 reads=[("yt", ob), ("zt", tb)], writes=[("yt", ob)])
            P.dma(ya_d[:, 2 * tt:2 * tt + 2, :], yt[ob][:], reads=[("yt", ob)])

        def drive(gens):
            gens = list(gens)
            while gens:
                for g_ in list(gens):
                    try:
                        next(g_)
                    except StopIteration:
                        gens.remove(g_)

        drive([front(0)])
        for tt in range(NTL):
            drive([back(tt)] + ([front(tt + 1)] if tt + 1 < NTL else []))
        P.emit()
    return nc


def gdn_in_maps(P0, T, qkv_conv, a_log, dt_bias, o_norm, sc_conv):
    P0 = P0.reshape(2, T, -1)
    cst = gdn_consts()
    maps = []
    for c in range(NCORES):
        b, hg = c // 4, c % 4
        o = 128 * hg
        pb = P0[b]
        qkvT = np.stack([pb[:, s * 512 + o:s * 512 + o + 128].T for s in range(3)])
        cw = np.concatenate([qkv_conv[:, s * 512 + o:s * 512 + o + 128].T for s in range(3)], axis=1)
        abc = np.concatenate([pb[:, 2048 + 2 * hg:2048 + 2 * hg + 2], pb[:, 2056 + 2 * hg:2056 + 2 * hg + 2]], axis=1)
        ab = abc.reshape(T // 128, 128, 4).transpose(1, 2, 0)
        hp = np.tile(np.concatenate([a_log[2 * hg:2 * hg + 2], dt_bias[2 * hg:2 * hg + 2]])[None], (128, 1))
        z_l = pb[:, 1536 + o:1536 + o + 128].reshape(T // 64, 64, 128).transpose(1, 0, 2)
        scT = np.stack([pb[:, 2064 + s * 512 + o:2064 + s * 512 + o + 128].T for s in range(3)])
        scw = sc_conv[:, o:o + 128].T
        maps.append(dict(qkvT=np.ascontiguousarray(qkvT, np.float32), cw=np.ascontiguousarray(cw, np.float32),
                         z_l=np.ascontiguousarray(z_l, np.float32), ab=np.ascontiguousarray(ab, np.float32),
                         hp=np.ascontiguousarray(hp, np.float32), onw=np.ascontiguousarray(np.tile(o_norm[None], (64, 1)), np.float32),
                         scT=np.ascontiguousarray(scT, np.float32), scw=np.ascontiguousarray(scw, np.float32), cst=cst))
    return maps


def gdn_gather(results, T):
    y = np.zeros((2, T, 1024), np.float32)
    for c in range(NCORES):
        b, hg = c // 4, c % 4
        o = 128 * hg
        y[b, :, o:o + 128] = results[c]["ya_l"].transpose(1, 0, 2).reshape(T, 128)
        y[b, :, 512 + o:512 + o + 128] = results[c]["ybT"].T
    return y.reshape(2 * T, 1024)


def nsa_consts(T):
    NTL = T // 128
    p = np.arange(128)
    half = 8
    inv_freq = (500000.0 ** (-(np.arange(0, 16, 2, dtype=np.float32)) / np.float32(16))).astype(np.float32)
    ang = np.arange(T, dtype=np.float32)[:, None] * inv_freq[None, :]
    cos, sin = np.cos(ang).astype(np.float32), np.sin(ang).astype(np.float32)
    C = np.ones((64, T), np.float32)
    C[0:8] = cos.T
    C[8:16] = cos.T
    Sg = np.zeros((16, T), np.float32)
    Sg[0:8] = -sin.T
    Sg[8:16] = sin.T
    C4 = np.ascontiguousarray(np.broadcast_to(C[:, None, :], (64, 4, T)))
    S4 = np.ascontiguousarray(np.broadcast_to(Sg[:, None, :], (16, 4, T)))
    E = np.zeros((128, NTL, 128), np.float32)
    for kt in range(NTL):
        E[2 * kt, kt, 0:64] = 1.0
        E[2 * kt + 1, kt, 64:128] = 1.0
    PM = np.zeros((128, 17, 128), np.float32)
    for r in range(17):
        PM[:, r, :] = (16 * p[:, None] + 31) <= (p[None, :] + 128 * r)
    CM = (p[:, None] <= p[None, :]).astype(np.float32)
    AM = (p[:, None] > p[None, :]).astype(np.float32)
    msk = np.ascontiguousarray(np.concatenate([PM, CM[:, None, :], AM[:, None, :], np.eye(128, dtype=np.float32)[:, None, :]], axis=1))
    c = np.arange(512)
    s = np.arange(128)
    ov = ((16 * c[:, None] < 64 * s[None, :] + 64) & (16 * c[:, None] + 32 > 64 * s[None, :])).astype(np.float32)
    ov[511] = 0.0
    ov = np.ascontiguousarray(ov.reshape(4, 128, 128).transpose(1, 0, 2))
    NQB = T // 128
    bq = np.zeros((NQB, 128, 128), np.float32)
    for qb in range(NQB):
        cur = 2 * qb + (p >= 64)
        js = s[None, :]
        forced = (js == 0) | (js == cur[:, None]) | (js == cur[:, None] - 1)
        bq[qb] = np.where(js > cur[:, None], -100.0, np.where(forced, 100.0, 0.0))
    return dict(C4=C4, S4=S4, E=E, msk=msk, ov=ov, bq=bq)


def build_nsa(T):
    nc = bass.Bass("TRN2", target_bir_lowering=False)
    NTL = T // 128
    NQB = NTL
    SEG = min(T, 2048)
    NSEG = T // SEG
    D = lambda name, shape: nc.dram_tensor(name, list(shape), F32, kind="ExternalInput").ap()
    qT_d = D("qT", [64, 4, T]); qsw_d = D("qsw", [16, 4, T])
    kT_d = D("kT", [2, 64, T]); ksw_d = D("ksw", [2, 16, T])
    v_d = D("vtok", [2, 128, NTL, 64])
    KP_d = D("KP", [2, 128, T // 2])
    posP_d = D("posP", [128, 16])
    w1_d = D("w1", [2, 128, 16, 256]); w2_d = D("w2", [2, 128, 2, 64])
    gat_d = D("gates", [128, NTL, 12])
    C4_d = D("C4", [64, 4, T]); S4_d = D("S4", [16, 4, T])
    E_d = D("E", [128, NTL, 128]); msk_d = D("msk", [128, 20, 128]); ov_d = D("ov", [128, 4, 128])
    bq_d = D("bq", [NQB, 128, 128])
    o_d = nc.dram_tensor("o_out", [T, 256], F32, kind="ExternalOutput").ap()

    with ExitStack() as es:
        P = Prog(nc, es)
        S = lambda name, shape, dt=F32: _sb(nc, es, name, shape, dt)
        msk = S("msk_sb", [128, 20, 128])
        CM, AM, ident = msk[:, 17, :], msk[:, 18, :], msk[:, 19, :]
        Eb = S("Eb", [128, NTL, 128], BF16)
        KT = [S("KTb%d" % i, [64, T], BF16) for i in range(2)]
        Vb = [S("Vb%d" % i, [128, NTL, 65], BF16) for i in range(2)]
        kcT = S("kcT", [64, 512], BF16)
        Vc = S("Vc", [128, 4, 193], BF16)
        gs = S("gs", [128, NTL, 12])
        posP = S("posP_sb", [128, 16])
        stg = [S("stg%d" % i, [128, 4096]) for i in range(2)]
        stg16 = [S("stg16_%d" % i, [16, SEG]) for i in range(3)]
        KPp = S("KPp", [128, 16, 512], BF16)
        w1b = S("w1b", [128, 16, 256], BF16)
        w2b = S("w2b", [128, 2, 64], BF16)
        HT = S("HT", [128, 2, 512], BF16)
        ps = [_ps(nc, es, "ps%d" % i, [128, 512], F32) for i in range(8)]
        bk = lambda i: ("bk", i)

        P.dma(msk[:], msk_d, writes=["msk"])
        P.dma(posP[:], posP_d, writes=["posP"])
        for ch in range((NTL + 31) // 32):
            n = min(32, NTL - ch * 32)
            sv = stg[ch % 2][:, 0:n * 128].rearrange("p (a b) -> p a b", a=n)
            P.dma(sv, E_d[:, ch * 32:ch * 32 + n, :], writes=[("stg", ch % 2)])
            P.op("pool", lambda e, sv=sv, ch=ch, n=n: e.tensor_copy(out=Eb[:, ch * 32:ch * 32 + n, :], in_=sv),
                 reads=[("stg", ch % 2)], writes=["Eb"])
        P.dma(gs[:], gat_d, writes=["gs"])
        P.op("act", lambda e: e.activation(out=gs[:], in_=gs[:], func=AF.Sigmoid), reads=["gs"], writes=["gs"])

        def do_k(i, sg):
            t0 = sg * SEG
            a, b = stg[0][0:64, 0:SEG], stg[1][0:64, 0:SEG]
            P.dma(a, kT_d[i][:, t0:t0 + SEG], writes=[("stg", 0)])
            P.dma(b, C4_d[:, 0, t0:t0 + SEG], writes=[("stg", 1)])
            P.dma(stg16[0][:], ksw_d[i][:, t0:t0 + SEG], writes=[("s16", 0)])
            P.dma(stg16[1][:], S4_d[:, 0, t0:t0 + SEG], writes=[("s16", 1)])
            P.op("dve", lambda e: e.tensor_tensor(out=KT[i][:, t0:t0 + SEG], in0=a, in1=b, op=ALU.mult),
                 reads=[("stg", 0), ("stg", 1)], writes=[("KT", i)])
            P.op("pool", lambda e: e.tensor_tensor(out=stg16[0][:], in0=stg16[0][:], in1=stg16[1][:], op=ALU.mult),
                 reads=[("s16", 0), ("s16", 1)], writes=[("s16", 0)])
            P.op("pool", lambda e: e.tensor_tensor(out=stg16[2][:], in0=a[0:16, :], in1=b[0:16, :], op=ALU.mult),
                 reads=[("stg", 0), ("stg", 1)], writes=[("s16", 2)])
            P.op("dve", lambda e: e.tensor_tensor(out=KT[i][0:16, t0:t0 + SEG], in0=stg16[0][:], in1=stg16[2][:], op=ALU.add),
                 reads=[("s16", 0), ("s16", 2), ("KT", i)], writes=[("KT", i)])

        for i in range(2):
            for sg in range(NSEG):
                do_k(i, sg)

        def do_v(i):
            sv = stg[i][:, 0:NTL * 64].rearrange("p (a b) -> p a b", a=NTL)
            P.dma(sv, v_d[i], writes=[("stg", i)])
            P.op("pool", lambda e: e.memset(Vb[i][:, :, 64:65], 1.0), writes=[("Vb", i)])
            P.op("dve", lambda e: e.tensor_copy(out=Vb[i][:, :, 0:64], in_=sv), reads=[("stg", i)], writes=[("Vb", i)])

        for i in range(2):
            do_v(i)

        P.op("pool", lambda e: e.memset(kcT[:], 0.0), writes=["kcT"])
        P.op("pool", lambda e: e.memset(Vc[:], 0.0), writes=["Vc"])

        def do_cmp(i):
            kp = stg[0][:, 0:T // 2]
            P.dma(kp, KP_d[i], writes=[("stg", 0)])
            kpv = kp.rearrange("p (c e) -> p c e", e=8)
            NCB = T // 16 - 1
            for lp in range(16):
                src = kpv[:, 0:NCB, lp] if lp < 8 else kpv[:, 1:NCB + 1, lp - 8]
                eng = "dve" if lp % 2 == 0 else "pool"
                P.op(eng, lambda e, src=src, lp=lp: e.tensor_scalar(out=KPp[:, lp, 0:NCB], in0=src, scalar1=posP[:, lp:lp + 1],
                                                                   scalar2=None, op0=ALU.add),
                     reads=[("stg", 0), "posP"], writes=["KPp"])
            w1v = stg[1][:, 0:4096].rearrange("p (a b) -> p a b", a=16)
            P.dma(w1v, w1_d[i], writes=[("stg", 1)])
            P.op("act", lambda e: e.copy(out=w1b[:], in_=w1v), reads=[("stg", 1)], writes=["w1b"])
            w2v = stg[0][:, 0:128].rearrange("p (a b) -> p a b", a=2)
            P.dma(w2v, w2_d[i], reads=["KPp"], writes=[("stg", 0)])
            P.op("act", lambda e: e.copy(out=w2b[:], in_=w2v), reads=[("stg", 0)], writes=["w2b"])
            for jc in range(2):
                for lp in range(16):
                    P.op("pe", lambda e, jc=jc, lp=lp: e.matmul(ps[jc][:, 0:NCB], lhsT=w1b[:, lp, jc * 128:(jc + 1) * 128],
                                                                rhs=KPp[:, lp, 0:NCB], start=(lp == 0), stop=(lp == 15)),
                         reads=["w1b", "KPp"], writes=[bk(jc)])
                P.op("act", lambda e, jc=jc: e.activation(out=HT[:, jc, 0:NCB], in_=ps[jc][:, 0:NCB], func=AF.Silu),
                     reads=[bk(jc)], writes=["HT"])
            if i == 0:
                for jc in range(2):
                    P.op("pe", lambda e, jc=jc: e.matmul(ps[2][0:64, 0:NCB], lhsT=w2b[:, jc, :], rhs=HT[:, jc, 0:NCB],
                                                         start=(jc == 0), stop=(jc == 1)), reads=["w2b", "HT"], writes=[bk(2)])
                P.op("act", lambda e: e.copy(out=kcT[:, 0:NCB], in_=ps[2][0:64, 0:NCB]), reads=[bk(2)], writes=["kcT"])
            else:
                for ct in range((NCB + 127) // 128):
                    n = min(128, NCB - ct * 128)
                    for jc in range(2):
                        P.op("pe", lambda e, jc=jc, ct=ct, n=n: e.matmul(ps[3][0:n, 0:64], lhsT=HT[:, jc, ct * 128:ct * 128 + n],
                                                                         rhs=w2b[:, jc, :], start=(jc == 0), stop=(jc == 1)),
                             reads=["w2b", "HT"], writes=[bk(3)])
                    P.op("act", lambda e, ct=ct, n=n: e.copy(out=Vc[0:n, ct, 0:64], in_=ps[3][0:n, 0:64]),
                         reads=[bk(3)], writes=["Vc"])
                    P.op("pool", lambda e, ct=ct, n=n: e.memset(Vc[0:n, ct, 64:65], 1.0), reads=["Vc"], writes=["Vc"])

        do_cmp(0)
        do_cmp(1)
        ovs = stg[1][:, 0:512].rearrange("p (a b) -> p a b", a=4)
        P.dma(ovs, ov_d, reads=["w1b"], writes=[("stg", 1)])
        P.op("dve", lambda e: e.tensor_copy(out=Vc[:, :, 65:193], in_=ovs), reads=[("stg", 1), "Vc"], writes=["Vc"])

        qf = [S("qf%d" % i, [64, 4, 128]) for i in range(2)]
        c4 = [S("c4_%d" % i, [64, 4, 128]) for i in range(2)]
        qs = [S("qs%d" % i, [16, 4, 128]) for i in range(2)]
        s4 = [S("s4_%d" % i, [16, 4, 128]) for i in range(2)]
        v16 = [S("v16_%d" % i, [16, 4, 128]) for i in range(2)]
        Qn = [S("Qn%d" % i, [64, 512], BF16) for i in range(2)]
        Qr = [S("Qr%d" % i, [64, 512], BF16) for i in range(2)]
        bqs = [S("bqs%d" % i, [128, 128]) for i in range(2)]
        Pc = S("Pc", [128, 4, 512], BF16)
        NPB = 4
        Pb = [S("Pb%d" % i, [128, 512], BF16) for i in range(NPB)]
        ocmp = S("ocmp", [128, 4, 193])
        ow = S("ow", [128, 4, 65])
        osl = S("osl", [128, 4, 65])
        rd = S("rd", [128, 12])
        impm = S("impm", [128, 128])
        imp2 = S("imp2", [128, 128])
        m8 = S("m8", [128, 16])
        nsel = S("nsel", [128, 128])
        nsT = S("nsT", [128, 512], BF16)
        osum = [S("osum%d" % i, [128, 4, 64]) for i in range(2)]
        pcnt = [0]
        scnt = [0]

        def do_qb(qb):
            j = qb % 2
            q0 = qb * 128
            P.dma(qf[j][:], qT_d[:, :, q0:q0 + 128], writes=[("qf", j)])
            P.dma(c4[j][:], C4_d[:, :, q0:q0 + 128], writes=[("c4", j)])
            P.dma(qs[j][:], qsw_d[:, :, q0:q0 + 128], writes=[("qs", j)])
            P.dma(s4[j][:], S4_d[:, :, q0:q0 + 128], writes=[("s4", j)])
            P.dma(bqs[j][:], bq_d[qb], writes=[("bqs", j)])
            fl = lambda t: t[:].rearrange("p a b -> p (a b)")
            P.op("pool", lambda e: e.tensor_copy(out=Qn[j][:], in_=fl(qf[j])), reads=[("qf", j)], writes=[("Qn", j)])
            P.op("pool", lambda e: e.tensor_tensor(out=Qr[j][:], in0=fl(qf[j]), in1=fl(c4[j]), op=ALU.mult),
                 reads=[("qf", j), ("c4", j)], writes=[("Qr", j)])
            P.op("pool", lambda e: e.tensor_tensor(out=fl(qs[j]), in0=fl(qs[j]), in1=fl(s4[j]), op=ALU.mult),
                 reads=[("qs", j), ("s4", j)], writes=[("qs", j)])
            P.op("pool", lambda e: e.tensor_tensor(out=fl(v16[j]), in0=qf[j][0:16, :, :].rearrange("p a b -> p (a b)"),
                                                    in1=c4[j][0:16, :, :].rearrange("p a b -> p (a b)"), op=ALU.mult),
                 reads=[("qf", j), ("c4", j)], writes=[("v16", j)])
            P.op("pool", lambda e: e.tensor_tensor(out=Qr[j][0:16, :], in0=fl(qs[j]), in1=fl(v16[j]), op=ALU.add),
                 reads=[("qs", j), ("v16", j), ("Qr", j)], writes=[("Qr", j)])

            cts = [ct for ct in range(4) if qb - 16 * ct >= 0 and ct * 128 < T // 16 - 1]
            for ct in cts:
                r = qb - 16 * ct
                sb_ = scnt[0] % 2
                scnt[0] += 1
                P.op("pe", lambda e, ct=ct, sb_=sb_: e.matmul(ps[sb_][:], lhsT=kcT[:, ct * 128:(ct + 1) * 128], rhs=Qn[j][:],
                                                              start=True, stop=True), reads=["kcT", ("Qn", j)], writes=[bk(sb_)])
                P.op("act", lambda e, ct=ct, sb_=sb_: e.activation(out=Pc[:, ct, :], in_=ps[sb_][:], func=AF.Exp, scale=0.125),
                     reads=[bk(sb_)], writes=[("Pc", ct)])
                if r <= 16:
                    for hh in range(4):
                        P.op("dve", lambda e, ct=ct, hh=hh, r=r: e.tensor_tensor(
                            out=Pc[:, ct, hh * 128:(hh + 1) * 128], in0=Pc[:, ct, hh * 128:(hh + 1) * 128],
                            in1=msk[:, r, :], op=ALU.mult), reads=[("Pc", ct), "msk"], writes=[("Pc", ct)])
            for hh in range(4):
                bank = 2 + hh // 2
                off = (hh % 2) * 193
                for n_, ct in enumerate(cts):
                    P.op("pe", lambda e, hh=hh, ct=ct, bank=bank, off=off, n_=n_: e.matmul(
                        ps[bank][:, off:off + 193], lhsT=Pc[:, ct, hh * 128:(hh + 1) * 128], rhs=Vc[:, ct, :],
                        start=(n_ == 0), stop=(n_ == len(cts) - 1)), reads=[("Pc", ct), "Vc"], writes=[bk(bank)])
            for half in range(2):
                P.op("act", lambda e, half=half: e.copy(
                    out=ocmp[:, 2 * half:2 * half + 2, :].rearrange("p a b -> p (a b)"), in_=ps[2 + half][:, 0:386]),
                    reads=[bk(2 + half)], writes=["ocmp"])
            P.op("dve", lambda e: e.tensor_scalar(out=rd[:, 0:4], in0=ocmp[:, :, 64], scalar1=1e-30, scalar2=None, op0=ALU.add),
                 reads=["ocmp"], writes=["rd"])
            P.op("dve", lambda e: e.reciprocal(out=rd[:, 0:4], in_=rd[:, 0:4]), reads=["rd"], writes=["rd"])
            P.op("dve", lambda e: e.scalar_tensor_tensor(out=impm[:], in0=ocmp[:, 0, 65:193], scalar=rd[:, 0:1], in1=bqs[j][:],
                                                         op0=ALU.mult, op1=ALU.add), reads=["ocmp", "rd", ("bqs", j)], writes=["impm"])
            for hh in range(1, 4):
                P.op("dve", lambda e, hh=hh: e.scalar_tensor_tensor(out=impm[:], in0=ocmp[:, hh, 65:193], scalar=rd[:, hh:hh + 1],
                                                                    in1=impm[:], op0=ALU.mult, op1=ALU.add),
                     reads=["ocmp", "rd", "impm"], writes=["impm"])
            P.op("dve", lambda e: e.max(out=m8[:, 0:8], in_=impm[:]), reads=["impm"], writes=["m8"])
            P.op("dve", lambda e: e.match_replace(out=imp2[:], in_to_replace=m8[:, 0:8], in_values=impm[:], imm_value=-1e9),
                 reads=["impm", "m8"], writes=["imp2"])
            P.op("dve", lambda e: e.max(out=m8[:, 8:16], in_=imp2[:]), reads=["imp2"], writes=["m8"])
            P.op("dve", lambda e: e.tensor_scalar(out=nsel[:], in0=impm[:], scalar1=m8[:, 15:16], scalar2=1.0,
                                                  op0=ALU.is_ge, op1=ALU.subtract), reads=["impm", "m8"], writes=["nsel"])
            P.op("pe", lambda e: e.transpose(out=ps[4][:, 0:128], in_=nsel[:], identity=ident), reads=["nsel", "msk"], writes=[bk(4)])
            for hh in range(4):
                P.op("act", lambda e, hh=hh: e.activation(out=nsT[:, hh * 128:(hh + 1) * 128], in_=ps[4][:, 0:128],
                                                          func=AF.Copy, scale=30000.0), reads=[bk(4)], writes=["nsT"])

            def attn(kts, i, with_sel, odst, okey):
                def score(kt):
                    sb_ = scnt[0] % 2
                    scnt[0] += 1
                    P.op("pe", lambda e, kt=kt, sb_=sb_: e.matmul(ps[sb_][:], lhsT=KT[i][:, kt * 128:(kt + 1) * 128], rhs=Qr[j][:],
                                                                  start=True, stop=not with_sel),
                         reads=[("KT", i), ("Qr", j)], writes=[bk(sb_)])
                    if with_sel:
                        P.op("pe", lambda e, kt=kt, sb_=sb_: e.matmul(ps[sb_][:], lhsT=Eb[:, kt, :], rhs=nsT[:], start=False, stop=True),
                             reads=["Eb", "nsT"], writes=[bk(sb_)])
                    pb = pcnt[0] % NPB
                    pcnt[0] += 1
                    P.op("act", lambda e, sb_=sb_, pb=pb: e.activation(out=Pb[pb][:], in_=ps[sb_][:], func=AF.Exp, scale=0.125),
                         reads=[bk(sb_)], writes=[("Pb", pb)])
                    mk = None
                    if kt == qb:
                        mk = CM
                    elif (not with_sel) and kt == qb - 4:
                        mk = AM
                    if mk is not None:
                        for hh in range(4):
                            P.op("dve", lambda e, hh=hh, pb=pb, mk=mk: e.tensor_tensor(
                                out=Pb[pb][:, hh * 128:(hh + 1) * 128], in0=Pb[pb][:, hh * 128:(hh + 1) * 128], in1=mk, op=ALU.mult),
                                reads=[("Pb", pb), "msk"], writes=[("Pb", pb)])
                    return pb

                def pv(n_, kt, pb):
                    for hh in range(4):
                        P.op("pe", lambda e, hh=hh, pb=pb, kt=kt, n_=n_: e.matmul(
                            ps[4 + hh][:, 0:65], lhsT=Pb[pb][:, hh * 128:(hh + 1) * 128], rhs=Vb[i][:, kt, :],
                            start=(n_ == 0), stop=(n_ == len(kts) - 1)), reads=[("Pb", pb), ("Vb", i)], writes=[bk(4 + hh)])

                pbs = {0: score(kts[0])}
                for n_, kt in enumerate(kts):
                    if n_ + 1 < len(kts):
                        pbs[n_ + 1] = score(kts[n_ + 1])
                    pv(n_, kt, pbs.pop(n_))
                for hh in range(4):
                    P.op("act", lambda e, hh=hh: e.copy(out=odst[:, hh, :], in_=ps[4 + hh][:, 0:65]), reads=[bk(4 + hh)], writes=[okey])

            attn(list(range(max(0, qb - 4), qb + 1)), 1, False, ow, "ow")
            attn(list(range(0, qb + 1)), 0, True, osl, "osl")

            ob = osum[j]
            for x, (src, key) in enumerate(((ocmp, "ocmp"), (osl, "osl"), (ow, "ow"))):
                if x > 0:
                    P.op("dve", lambda e, src=src, x=x: e.reciprocal(out=rd[:, 4 * x:4 * x + 4], in_=src[:, :, 64]),
                         reads=[key], writes=["rd"])
                P.op("dve", lambda e, x=x: e.tensor_tensor(
                    out=rd[:, 4 * x:4 * x + 4], in0=rd[:, 4 * x:4 * x + 4],
                    in1=gs[:, qb, :].rearrange("p (h x) -> p h x", x=3)[:, :, x], op=ALU.mult), reads=["rd", "gs"], writes=["rd"])
                for hh in range(4):
                    if x == 0:
                        P.op("dve", lambda e, hh=hh, src=src, x=x: e.tensor_scalar(
                            out=ob[:, hh, :], in0=src[:, hh, 0:64], scalar1=rd[:, 4 * x + hh:4 * x + hh + 1], scalar2=None,
                            op0=ALU.mult), reads=[key, "rd"], writes=[("osum", j)])
                    else:
                        P.op("dve", lambda e, hh=hh, src=src, x=x: e.scalar_tensor_tensor(
                            out=ob[:, hh, :], in0=src[:, hh, 0:64], scalar=rd[:, 4 * x + hh:4 * x + hh + 1], in1=ob[:, hh, :],
                            op0=ALU.mult, op1=ALU.add), reads=[key, "rd", ("osum", j)], writes=[("osum", j)])
            P.dma(o_d[q0:q0 + 128, :], ob[:].rearrange("p a b -> p (a b)"), reads=[("osum", j)])

        for qb in range(NQB):
            do_qb(qb)
        P.emit()
    return nc


def nsa_in_maps(P1, T, cmp_pos, k_w1, k_w2, v_w1, v_w2):
    P1 = P1.reshape(2, T, -1)
    cst = nsa_consts(T)
    NTL = T // 128
    maps = []
    posP = np.zeros((128, 16), np.float32)
    for lp in range(16):
        posP[0:64, lp] = cmp_pos[2 * lp]
        posP[64:128, lp] = cmp_pos[2 * lp + 1]
    w1 = np.stack([w.reshape(16, 128, 256).transpose(1, 0, 2) for w in (k_w1, v_w1)])
    w2 = np.stack([w.reshape(2, 128, 64).transpose(1, 0, 2) for w in (k_w2, v_w2)])
    swap = np.concatenate([np.arange(8, 16), np.arange(0, 8)])
    for c in range(NCORES):
        b, g = c // 4, c % 4
        pb = P1[b]
        q = pb[:, 256 * g:256 * g + 256].reshape(T, 4, 64)
        qT = q.transpose(2, 1, 0)
        col = lambda i: pb[:, 1024 + 256 * i + 64 * g:1024 + 256 * i + 64 * g + 64]
        k_c, v_c, k_s, v_s, k_w, v_w = [col(i) for i in range(6)]
        kT = np.stack([k_s.T, k_w.T])
        vt = np.stack([v.reshape(NTL, 128, 64).transpose(1, 0, 2) for v in (v_s, v_w)])
        KP = np.stack([np.concatenate([a[0::2].T, a[1::2].T], axis=0) for a in (k_c, v_c)])
        gates = pb[:, 2560 + 12 * g:2560 + 12 * g + 12].reshape(NTL, 128, 12).transpose(1, 0, 2)
        f = lambda a: np.ascontiguousarray(a, np.float32)
        maps.append(dict(qT=f(qT), qsw=f(qT[swap]), kT=f(kT), ksw=f(kT[:, swap]), vtok=f(vt), KP=f(KP), posP=posP,
                         w1=f(w1), w2=f(w2), gates=f(gates), C4=cst["C4"], S4=cst["S4"], E=cst["E"], msk=cst["msk"],
                         ov=cst["ov"], bq=cst["bq"]))
    return maps


def nsa_gather(results, T):
    o = np.zeros((2, T, 1024), np.float32)
    for c in range(NCORES):
        b, g = c // 4, c % 4
        o[b, :, 256 * g:256 * g + 256] = results[c]["o_out"]
    return o.reshape(2 * T, 1024)


_T = 8192
_NT = 2048


def _run(nc, maps):
    res = run_bass_kernel_spmd(nc, maps, core_ids=list(range(NCORES)))
    return res.results


def kernel(x, mix_norm, mlp_norm, w_up, w_down, final_norm,
           ev_w_in, ev_qkv_conv, ev_a_log, ev_dt_bias, ev_o_norm, ev_sc_conv, ev_w_out,
           od_w_in, od_cmp_pos, od_cmp_k_w1, od_cmp_k_w2, od_cmp_v_w1, od_cmp_v_w2, od_w_out):
    f = lambda a: np.ascontiguousarray(np.asarray(a), np.float32)
    x = f(x).reshape(2 * _T, D_MODEL)
    ident = np.eye(128, dtype=np.float32)
    rep = lambda v: np.ascontiguousarray(np.tile(f(v)[None, :], (128, 1)))
    sh = lambda a, c: np.ascontiguousarray(a[c * _NT:(c + 1) * _NT])
    shT = lambda a, c: np.ascontiguousarray(a[c * _NT:(c + 1) * _NT].T)
    cat = lambda rs, k: np.concatenate([r[k] for r in rs], axis=0)

    nc = build_dense(_NT, False, False, "inproj", 3600)
    rs = _run(nc, [dict(x=sh(x, c), ident=ident, nw_tail=rep(mix_norm[0]), w_in=f(ev_w_in[0])) for c in range(NCORES)])
    P0 = cat(rs, "p_out")
    nc = build_gdn(_T)
    rs = _run(nc, gdn_in_maps(P0, _T, f(ev_qkv_conv[0]), f(ev_a_log[0]), f(ev_dt_bias[0]), f(ev_o_norm[0]), f(ev_sc_conv[0])))
    y0 = gdn_gather(rs, _T)
    nc = build_dense(_NT, True, True, "inproj", 2608)
    rs = _run(nc, [dict(x=sh(x, c), ident=ident, yT=shT(y0, c), w_out=f(ev_w_out[0]), nw_mlp=rep(mlp_norm[0]),
                        w_up=f(w_up[0]), w_down=f(w_down[0]), nw_tail=rep(mix_norm[1]), w_in=f(od_w_in[0]))
                   for c in range(NCORES)])
    x2 = cat(rs, "x_out")
    P1 = cat(rs, "p_out")
    nc = build_nsa(_T)
    rs = _run(nc, nsa_in_maps(P1, _T, f(od_cmp_pos[0]), f(od_cmp_k_w1[0]), f(od_cmp_k_w2[0]), f(od_cmp_v_w1[0]), f(od_cmp_v_w2[0])))
    o1 = nsa_gather(rs, _T)
    nc = build_dense(_NT, True, True, "final")
    rs = _run(nc, [dict(x=sh(x2, c), ident=ident, yT=shT(o1, c), w_out=f(od_w_out[0]), nw_mlp=rep(mlp_norm[1]),
                        w_up=f(w_up[1]), w_down=f(w_down[1]), nw_tail=rep(final_norm)) for c in range(NCORES)])
    out = cat(rs, "out")
    return out.reshape(2, _T, D_MODEL).astype(np.float32)
```

```python
import numpy as np
import ml_dtypes
from contextlib import ExitStack
import concourse.bass as bass
import concourse.mybir as mybir
from concourse.bass_utils import run_bass_kernel_spmd

F32 = mybir.dt.float32
BF16 = mybir.dt.bfloat16
AF = mybir.ActivationFunctionType
ALU = mybir.AluOpType
AX = mybir.AxisListType

NCORES = 8
D_MODEL = 1024
D_FF = 4096
EPS = 1e-6
_DBG = {}


class _Ins:
    __slots__ = ("eng", "fn", "deps", "isdma", "need_inc", "tok", "q")

    def __init__(self, eng, fn, isdma):
        self.eng = eng
        self.fn = fn
        self.deps = []
        self.isdma = isdma
        self.need_inc = isdma
        self.tok = None


class Prog:
    ENGS = ("pe", "act", "dve", "pool", "sp")
    NDSEM = 12

    def __init__(self, nc, es):
        self.nc = nc
        self.es = es
        self.streams = {e: [] for e in self.ENGS}
        self.last_write = {}
        self.readers = {}
        self.all_dma = []

    def _add(self, eng, fn, reads, writes, isdma):
        ins = _Ins(eng, fn, isdma)
        excl = [k for k in reads if isinstance(k, tuple) and k[0] in ("bk", "ps")]
        if excl:
            writes = list(writes) + [k for k in excl if k not in writes]
        deps = {}
        for k in reads:
            w = self.last_write.get(k)
            if w is not None:
                deps[id(w)] = (w, "raw")
        for k in writes:
            w = self.last_write.get(k)
            if w is not None and id(w) not in deps:
                deps[id(w)] = (w, "waw")
            for r in self.readers.get(k, ()):
                if id(r) not in deps:
                    deps[id(r)] = (r, "war")
        for d, kind in deps.values():
            if d is ins:
                continue
            if d.eng == eng and not d.isdma and not isdma:
                if kind != "raw" or eng == "pe":
                    continue
            d.need_inc = True
            ins.deps.append(d)
        for k in reads:
            self.readers.setdefault(k, []).append(ins)
        for k in writes:
            self.last_write[k] = ins
            self.readers[k] = []
        self.streams[eng].append(ins)
        if isdma:
            self.all_dma.append(ins)
        return ins

    def op(self, eng, fn, reads=(), writes=()):
        return self._add(eng, fn, reads, writes, False)

    def dma(self, out, in_, reads=(), writes=(), q="sp"):
        return self._add(q, lambda e: e.dma_start(out=out, in_=in_), reads, writes, True)

    def _selfcheck(self, csem, dsem):
        val = {}
        pos = {e: 0 for e in self.ENGS}
        dl = {e: [i for i in self.streams[e] if i.isdma] for e in self.ENGS}
        total = sum(len(v) for v in self.streams.values())
        done = 0
        while done < total:
            progress = False
            for e in self.ENGS:
                while pos[e] < len(self.streams[e]):
                    ins = self.streams[e][pos[e]]
                    waits = [d.tok for d in ins.deps]
                    if ins.isdma and ins.q >= self.NDSEM:
                        waits.append(dl[e][ins.q - self.NDSEM].tok)
                    if any(w is None for w in waits):
                        raise RuntimeError("dep without token")
                    if all(val.get(id(sm), 0) >= v for sm, v in waits):
                        if ins.need_inc:
                            val[id(ins.tok[0])] = val.get(id(ins.tok[0]), 0) + (16 if ins.isdma else 1)
                            if val[id(ins.tok[0])] != ins.tok[1]:
                                raise RuntimeError("token mismatch %s %s" % (val[id(ins.tok[0])], ins.tok[1]))
                        pos[e] += 1
                        done += 1
                        progress = True
                    else:
                        break
            if not progress:
                raise RuntimeError("deadlock in program: %s" % {e: (pos[e], len(self.streams[e])) for e in self.ENGS})

    def emit(self):
        nc = self.nc
        es = self.es
        csem = {e: es.enter_context(nc.semaphore("cs_" + e)) for e in self.ENGS}
        dsem = {e: [es.enter_context(nc.semaphore("ds_%s%d" % (e, i))) for i in range(self.NDSEM)]
                for e in self.ENGS if any(i.isdma for i in self.streams[e])}
        for e in self.ENGS:
            cnt = 0
            dcnt = 0
            for ins in self.streams[e]:
                if ins.isdma:
                    ins.tok = (dsem[e][dcnt % self.NDSEM], 16 * (dcnt // self.NDSEM + 1))
                    ins.q = dcnt
                    dcnt += 1
                elif ins.need_inc:
                    cnt += 1
                    ins.tok = (csem[e], cnt)
        streams = self.streams
        NDSEM = self.NDSEM
        self._selfcheck(csem, dsem)

        def run(ename, eng):
            seen = {}
            dlist = [i for i in streams[ename] if i.isdma]
            for ins in streams[ename]:
                waits = [d.tok for d in ins.deps]
                if ins.isdma and ins.q >= NDSEM:
                    waits.append(dlist[ins.q - NDSEM].tok)
                for sem, val in waits:
                    if seen.get(id(sem), 0) >= val:
                        continue
                    seen[id(sem)] = val
                    eng.wait_ge(sem, val)
                r = ins.fn(eng)
                if ins.need_inc:
                    r.then_inc(ins.tok[0], 16 if ins.isdma else 1)
            for ins in dlist[-NDSEM:]:
                sem, val = ins.tok
                if seen.get(id(sem), 0) >= val:
                    continue
                seen[id(sem)] = val
                eng.wait_ge(sem, val)

        with nc.Block() as block:
            @block.tensor
            def _(e):
                run("pe", e)

            @block.scalar
            def _(e):
                run("act", e)

            @block.vector
            def _(e):
                run("dve", e)

            @block.gpsimd
            def _(e):
                run("pool", e)

            @block.sync
            def _(e):
                run("sp", e)


def _sb(nc, es, name, shape, dt):
    return es.enter_context(nc.sbuf_tensor(name, list(shape), dt))


def _ps(nc, es, name, shape, dt):
    return es.enter_context(nc.psum_tensor(name, list(shape), dt))


def build_dense(NT, has_mix, has_mlp, tail, n_in_cols=0):
    nc = bass.Bass("TRN2", target_bir_lowering=False)
    GT = 4
    NG = NT // (128 * GT)
    x = nc.dram_tensor("x", [NT, D_MODEL], F32, kind="ExternalInput").ap()
    ident_d = nc.dram_tensor("ident", [128, 128], F32, kind="ExternalInput").ap()
    if has_mix:
        yT = nc.dram_tensor("yT", [D_MODEL, NT], F32, kind="ExternalInput").ap()
        w_out = nc.dram_tensor("w_out", [D_MODEL, D_MODEL], F32, kind="ExternalInput").ap()
    if has_mlp:
        nw_mlp = nc.dram_tensor("nw_mlp", [128, D_MODEL], F32, kind="ExternalInput").ap()
        w_up = nc.dram_tensor("w_up", [D_MODEL, D_FF], F32, kind="ExternalInput").ap()
        w_down = nc.dram_tensor("w_down", [D_FF, D_MODEL], F32, kind="ExternalInput").ap()
    nw_tail = nc.dram_tensor("nw_tail", [128, D_MODEL], F32, kind="ExternalInput").ap()
    if tail == "inproj":
        w_in = nc.dram_tensor("w_in", [D_MODEL, n_in_cols], F32, kind="ExternalInput").ap()
        p_out = nc.dram_tensor("p_out", [NT, n_in_cols], F32, kind="ExternalOutput").ap()
        if has_mix or has_mlp:
            x_out = nc.dram_tensor("x_out", [NT, D_MODEL], F32, kind="ExternalOutput").ap()
    else:
        out = nc.dram_tensor("out", [NT, D_MODEL], F32, kind="ExternalOutput").ap()

    with ExitStack() as es:
        P = Prog(nc, es)
        ident = _sb(nc, es, "ident_sb", [128, 128], F32)
        xg = [_sb(nc, es, "xg%d" % i, [128, GT, D_MODEL], F32) for i in range(2)]
        hn = [_sb(nc, es, "hn%d" % i, [128, D_MODEL], F32) for i in range(2)]
        hT = _sb(nc, es, "hT", [128, 8, 128 * GT], BF16)
        wst = [_sb(nc, es, "wst%d" % i, [128, 4096], F32) for i in range(2)]
        wbf = [_sb(nc, es, "wbf%d" % i, [128, 4096], BF16) for i in range(2)]
        nwt = _sb(nc, es, "nwt", [128, D_MODEL], F32)
        ss = _sb(nc, es, "ss", [128, 8], F32)
        sq_scr = _sb(nc, es, "sq_scr", [128, D_MODEL], F32)
        ot = [_sb(nc, es, "ot%d" % i, [128, 512], F32) for i in range(2)]
        if has_mlp:
            nwm = _sb(nc, es, "nwm", [128, D_MODEL], F32)
            aT = _sb(nc, es, "aT", [128, 32, 128 * GT], BF16)
            rl = [_sb(nc, es, "rl%d" % i, [128, 512], F32) for i in range(2)]
        if has_mix:
            yst = _sb(nc, es, "yst", [128, 8, 128 * GT], F32)
        ps = [_ps(nc, es, "ps%d" % i, [128, 512], F32) for i in range(8)]

        P.dma(ident[:], ident_d, writes=["ident"])
        P.dma(nwt[:], nw_tail, writes=["nwt"])
        if has_mlp:
            P.dma(nwm[:], nw_mlp, writes=["nwm"])

        cnt = {"w": 0, "n": 0, "ps": 0, "ot": 0, "rl": 0}

        def load_w(src_ap, shape3):
            i = cnt["w"] % 2
            cnt["w"] += 1
            a, b = shape3
            stv = wst[i][:, 0:a * b].rearrange("p (a b) -> p a b", a=a)
            bfv = wbf[i][:, 0:a * b].rearrange("p (a b) -> p a b", a=a)
            P.dma(stv, src_ap, writes=[("wst", i)])
            ceng = "dve" if (cnt["w"] % 2 == 0) else "act"
            if ceng == "dve":
                P.op("dve", lambda e: e.tensor_copy(out=wbf[i][:, 0:a * b], in_=wst[i][:, 0:a * b]),
                     reads=[("wst", i)], writes=[("wbf", i)])
            else:
                P.op("act", lambda e: e.copy(out=wbf[i][:, 0:a * b], in_=wst[i][:, 0:a * b]),
                     reads=[("wst", i)], writes=[("wbf", i)])
            return bfv, ("wbf", i)

        def kmajor(w_ap, c0, ncols):
            return w_ap.rearrange("(kc p) c -> p kc c", p=128)[:, :, c0:c0 + ncols]

        def norm_to_hT(xb, gkey, nw_tile, nwkey):
            for t in range(GT):
                j = cnt["n"] % 2
                cnt["n"] += 1
                col = cnt["n"] % 8
                P.op("act", lambda e, t=t, col=col: e.activation(
                    out=sq_scr[:], in_=xg[xb][:, t, :], func=AF.Square, accum_out=ss[:, col:col + 1]),
                    reads=[gkey], writes=["sq_scr", ("ss", col)])
                P.op("act", lambda e, col=col: e.activation(
                    out=ss[:, col:col + 1], in_=ss[:, col:col + 1], func=AF.Sqrt,
                    scale=1.0 / D_MODEL, bias=EPS), reads=[("ss", col)], writes=[("ss", col)])
                P.op("dve", lambda e, col=col: e.reciprocal(out=ss[:, col:col + 1], in_=ss[:, col:col + 1]),
                     reads=[("ss", col)], writes=[("ss", col)])
                P.op("dve", lambda e, t=t, j=j, col=col: e.scalar_tensor_tensor(
                    out=hn[j][:], in0=xg[xb][:, t, :], scalar=ss[:, col:col + 1], in1=nw_tile[:],
                    op0=ALU.mult, op1=ALU.mult), reads=[gkey, ("ss", col), nwkey], writes=[("hn", j)])
                for half in range(2):
                    b = cnt["ps"] % 8
                    cnt["ps"] += 1
                    for q in range(4):
                        kc = half * 4 + q
                        P.op("pe", lambda e, b=b, q=q, kc=kc, j=j: e.transpose(
                            out=ps[b][:, q * 128:(q + 1) * 128], in_=hn[j][:, kc * 128:(kc + 1) * 128],
                            identity=ident[:]), reads=[("hn", j), "ident"], writes=[("ps", b)])
                    P.op("dve", lambda e, b=b, half=half, t=t: e.tensor_copy(
                        out=hT[:, half * 4:half * 4 + 4, t * 128:(t + 1) * 128],
                        in_=ps[b][:].rearrange("p (q c) -> p q c", q=4)),
                        reads=[("ps", b)], writes=["hT"])

        def do_group(g):
            xb = g % 2
            gkey = ("xg", xb)
            tok0 = g * GT * 128
            P.dma(xg[xb][:], x[tok0:tok0 + GT * 128, :].rearrange("(t p) d -> p t d", p=128), writes=[gkey])

            if has_mix:
                P.dma(yst[:], yT.rearrange("(kc p) n -> p kc n", p=128)[:, :, tok0:tok0 + GT * 128],
                      writes=["yst"])
                P.op("pool", lambda e: e.tensor_copy(out=hT[:], in_=yst[:]), reads=["yst"], writes=["hT"])
                for c in range(2):
                    wv, wk = load_w(kmajor(w_out, c * 512, 512), (8, 512))
                    for t in range(GT):
                        b = cnt["ps"] % 8
                        cnt["ps"] += 1
                        for kc in range(8):
                            P.op("pe", lambda e, b=b, kc=kc, t=t, wv=wv: e.matmul(
                                ps[b][:], lhsT=hT[:, kc, t * 128:(t + 1) * 128], rhs=wv[:, kc, :],
                                start=(kc == 0), stop=(kc == 7)), reads=["hT", wk], writes=[("ps", b)])
                        P.op("dve", lambda e, b=b, t=t, c=c: e.tensor_tensor(
                            out=xg[xb][:, t, c * 512:(c + 1) * 512], in0=xg[xb][:, t, c * 512:(c + 1) * 512],
                            in1=ps[b][:], op=ALU.add), reads=[("ps", b), gkey], writes=[gkey])

            if has_mlp:
                norm_to_hT(xb, gkey, nwm, "nwm")
                for uc in range(8):
                    wv, wk = load_w(kmajor(w_up, uc * 512, 512), (8, 512))
                    for fi in range(4):
                        fc = uc * 4 + fi
                        b = cnt["ps"] % 8
                        cnt["ps"] += 1
                        for kc in range(8):
                            P.op("pe", lambda e, b=b, kc=kc, fi=fi, wv=wv: e.matmul(
                                ps[b][:], lhsT=wv[:, kc, fi * 128:(fi + 1) * 128], rhs=hT[:, kc, :],
                                start=(kc == 0), stop=(kc == 7)), reads=["hT", wk], writes=[("ps", b)])
                        r = cnt["rl"] % 2
                        cnt["rl"] += 1
                        P.op("act", lambda e, b=b, r=r: e.activation(out=rl[r][:], in_=ps[b][:], func=AF.Relu),
                             reads=[("ps", b)], writes=[("rl", r)])
                        P.op("dve", lambda e, r=r, fc=fc: e.tensor_tensor(
                            out=aT[:, fc, :], in0=rl[r][:], in1=rl[r][:], op=ALU.mult),
                            reads=[("rl", r)], writes=["aT"])
                for dc in range(8):
                    wv, wk = load_w(w_down.rearrange("(fc p) d -> p fc d", p=128)[:, dc * 4:dc * 4 + 4, :],
                                    (4, 1024))
                    for j in range(4):
                        fc = dc * 4 + j
                        for t in range(GT):
                            for half in range(2):
                                b = t * 2 + half
                                P.op("pe", lambda e, b=b, fc=fc, t=t, j=j, half=half, wv=wv: e.matmul(
                                    ps[b][:], lhsT=aT[:, fc, t * 128:(t + 1) * 128],
                                    rhs=wv[:, j, half * 512:(half + 1) * 512],
                                    start=(fc == 0), stop=(fc == 31)), reads=["aT", wk], writes=[("ps", b)])
                for t in range(GT):
                    for half in range(2):
                        b = t * 2 + half
                        P.op("dve", lambda e, b=b, t=t, half=half: e.tensor_tensor(
                            out=xg[xb][:, t, half * 512:(half + 1) * 512],
                            in0=xg[xb][:, t, half * 512:(half + 1) * 512], in1=ps[b][:], op=ALU.add),
                            reads=[("ps", b), gkey], writes=[gkey])
                cnt["ps"] = 0

            if tail == "inproj":
                if has_mix or has_mlp:
                    P.dma(x_out[tok0:tok0 + GT * 128, :].rearrange("(t p) d -> p t d", p=128), xg[xb][:],
                          reads=[gkey], q="act")
                norm_to_hT(xb, gkey, nwt, "nwt")
                c0 = 0
                while c0 < n_in_cols:
                    ncol = min(512, n_in_cols - c0)
                    wv, wk = load_w(kmajor(w_in, c0, ncol), (8, ncol))
                    for t in range(GT):
                        b = cnt["ps"] % 8
                        cnt["ps"] += 1
                        for kc in range(8):
                            P.op("pe", lambda e, b=b, kc=kc, t=t, wv=wv, ncol=ncol: e.matmul(
                                ps[b][:, 0:ncol], lhsT=hT[:, kc, t * 128:(t + 1) * 128], rhs=wv[:, kc, :],
                                start=(kc == 0), stop=(kc == 7)), reads=["hT", wk], writes=[("ps", b)])
                        o = cnt["ot"] % 2
                        cnt["ot"] += 1
                        P.op("act", lambda e, b=b, o=o, ncol=ncol: e.copy(out=ot[o][:, 0:ncol], in_=ps[b][:, 0:ncol]),
                             reads=[("ps", b)], writes=[("ot", o)])
                        P.dma(p_out[tok0 + t * 128:tok0 + (t + 1) * 128, c0:c0 + ncol], ot[o][:, 0:ncol],
                              reads=[("ot", o)], q="act")
                    c0 += ncol
            else:
                for t in range(GT):
                    col = cnt["n"] % 8
                    cnt["n"] += 1
                    P.op("act", lambda e, t=t, col=col: e.activation(
                        out=sq_scr[:], in_=xg[xb][:, t, :], func=AF.Square, accum_out=ss[:, col:col + 1]),
                        reads=[gkey], writes=["sq_scr", ("ss", col)])
                    P.op("act", lambda e, col=col: e.activation(
                        out=ss[:, col:col + 1], in_=ss[:, col:col + 1], func=AF.Sqrt,
                        scale=1.0 / D_MODEL, bias=EPS), reads=[("ss", col)], writes=[("ss", col)])
                    P.op("dve", lambda e, col=col: e.reciprocal(out=ss[:, col:col + 1], in_=ss[:, col:col + 1]),
                         reads=[("ss", col)], writes=[("ss", col)])
                    j = cnt["n"] % 2
                    P.op("dve", lambda e, t=t, j=j, col=col: e.scalar_tensor_tensor(
                        out=hn[j][:], in0=xg[xb][:, t, :], scalar=ss[:, col:col + 1], in1=nwt[:],
                        op0=ALU.mult, op1=ALU.mult), reads=[gkey, ("ss", col), "nwt"], writes=[("hn", j)])
                    P.dma(out[tok0 + t * 128:tok0 + (t + 1) * 128, :], hn[j][:], reads=[("hn", j)], q="act")
        for g in range(NG):
            do_group(g)
        P.emit()
    return nc


def gdn_consts():
    p = np.arange(128)
    same = (p[:, None] // 64) == (p[None, :] // 64)
    c = np.zeros((8, 128, 128), np.float32)
    c[0] = np.eye(128)
    c[1] = same & (p[:, None] <= p[None, :])
    c[2] = same
    c[3] = (p[:, None] < 64) * np.ones((1, 128))
    c[4] = (p[:, None] >= 64) * np.ones((1, 128))
    c[5] = 0.125 * (same & (p[:, None] <= p[None, :]))
    c[6] = same & (p[None, :] < p[:, None])
    c[7] = 1.0
    return np.ascontiguousarray(c.transpose(1, 0, 2))


def build_gdn(T):
    nc = bass.Bass("TRN2", target_bir_lowering=False)
    NTL = T // 128
    NCH = T // 64
    SEG = min(T, 2048)
    NSEG = T // SEG
    qkvT = nc.dram_tensor("qkvT", [3, 128, T], F32, kind="ExternalInput").ap()
    cw_d = nc.dram_tensor("cw", [128, 12], F32, kind="ExternalInput").ap()
    z_d = nc.dram_tensor("z_l", [64, NCH, 128], F32, kind="ExternalInput").ap()
    ab_d = nc.dram_tensor("ab", [128, 4, NTL], F32, kind="ExternalInput").ap()
    hp_d = nc.dram_tensor("hp", [128, 4], F32, kind="ExternalInput").ap()
    onw_d = nc.dram_tensor("onw", [64, 64], F32, kind="ExternalInput").ap()
    scT = nc.dram_tensor("scT", [3, 128, T], F32, kind="ExternalInput").ap()
    scw_d = nc.dram_tensor("scw", [128, 3], F32, kind="ExternalInput").ap()
    cst_d = nc.dram_tensor("cst", [128, 8, 128], F32, kind="ExternalInput").ap()
    ya_d = nc.dram_tensor("ya_l", [64, NCH, 128], F32, kind="ExternalOutput").ap()
    yb_d = nc.dram_tensor("ybT", [128, T], F32, kind="ExternalOutput").ap()

    with ExitStack() as es:
        P = Prog(nc, es)
        S = lambda name, shape, dt=F32: _sb(nc, es, name, shape, dt)
        cst = S("cst_sb", [128, 8, 128])
        ident, LTm, BO, SEL0, SEL1, MUs, ML, ONES = [cst[:, i, :] for i in range(8)]
        ident_t = S("ident_t", [128, 128])
        ident = ident_t[:]
        cw = S("cw_sb", [128, 12])
        hp = S("hp_sb", [128, 4])
        onw = S("onw_sb", [64, 64])
        scw = S("scw_sb", [128, 3])
        ab = S("ab_sb", [128, 4, NTL])
        fT = [S("fT%d" % s, [128, T]) for s in range(3)]
        raw = S("raw", [128, SEG + 3])
        raw2 = S("raw2", [128, SEG + 3])
        raw3 = S("raw3", [128, SEG + 3])
        acc = S("acc", [128, SEG])
        sqs = S("sqs", [128, SEG])
        rn = [S("rn%d" % i, [128, 512]) for i in range(2)]
        ps = [_ps(nc, es, "ps%d" % i, [128, 512], F32) for i in range(8)]

        P.dma(cst[:], cst_d, writes=["cst"])
        P.dma(ident_t[:], cst_d[:, 0, :], writes=["cst"])
        P.dma(cw[:], cw_d, writes=["cw"])
        P.dma(hp[:], hp_d, writes=["hp"])
        P.dma(onw[:], onw_d, writes=["onw"])
        P.dma(scw[:], scw_d, writes=["scw"])
        P.dma(ab[:], ab_d, writes=["ab"])

        def do_sc(sg):
            t0 = sg * SEG
            lo = 0 if sg > 0 else 2
            for i, (buf, key) in enumerate(((raw, "raw"), (raw2, "raw2"), (raw3, "raw3"))):
                if sg == 0:
                    P.op("pool", lambda e, buf=buf: e.memset(buf[:, 0:2], 0.0), writes=[key])
                P.dma(buf[:, lo:SEG + 2], scT[i][:, t0 - 2 + lo:t0 + SEG], writes=[key])
            P.op("pool", lambda e: e.tensor_tensor(out=raw2[:, 0:SEG + 2], in0=raw2[:, 0:SEG + 2],
                                                    in1=raw3[:, 0:SEG + 2], op=ALU.mult),
                 reads=["raw2", "raw3"], writes=["raw2"])
            P.op("dve", lambda e: e.tensor_scalar(out=acc[:], in0=raw2[:, 0:SEG], scalar1=scw[:, 0:1], scalar2=None,
                                                  op0=ALU.mult), reads=["raw2", "scw"], writes=["acc"])
            for j in (1, 2):
                P.op("dve", lambda e, j=j: e.scalar_tensor_tensor(out=acc[:], in0=raw2[:, j:j + SEG],
                                                                   scalar=scw[:, j:j + 1], in1=acc[:],
                                                                   op0=ALU.mult, op1=ALU.add),
                     reads=["raw2", "scw", "acc"], writes=["acc"])
            P.op("pool", lambda e: e.tensor_tensor(out=acc[:], in0=acc[:], in1=raw[:, 2:SEG + 2], op=ALU.mult),
                 reads=["acc", "raw"], writes=["acc"])
            P.dma(yb_d[:, t0:t0 + SEG], acc[:], reads=["acc"])

        for sg in range(NSEG):
            do_sc(sg)

        def do_p1(sg, s):
            t0 = sg * SEG
            lo = 0 if sg > 0 else 3
            if sg == 0:
                P.op("pool", lambda e: e.memset(raw[:, 0:3], 0.0), writes=["raw"])
            P.dma(raw[:, lo:SEG + 3], qkvT[s][:, t0 - 3 + lo:t0 + SEG], writes=["raw"])
            P.op("dve", lambda e: e.tensor_scalar(out=acc[:], in0=raw[:, 0:SEG], scalar1=cw[:, s * 4:s * 4 + 1],
                                                  scalar2=None, op0=ALU.mult), reads=["raw", "cw"], writes=["acc"])
            for j in (1, 2, 3):
                P.op("dve", lambda e, j=j: e.scalar_tensor_tensor(out=acc[:], in0=raw[:, j:j + SEG],
                                                                   scalar=cw[:, s * 4 + j:s * 4 + j + 1], in1=acc[:],
                                                                   op0=ALU.mult, op1=ALU.add),
                     reads=["raw", "cw", "acc"], writes=["acc"])
            dst = fT[s][:, t0:t0 + SEG]
            fkey = ("fT", s)
            P.op("act", lambda e: e.activation(out=dst, in_=acc[:], func=AF.Silu), reads=["acc"], writes=[fkey])
            if s < 2:
                P.op("act", lambda e: e.activation(out=sqs[:], in_=dst, func=AF.Square), reads=[fkey], writes=["sqs"])
                for blk in range(SEG // 512):
                    r = blk % 2
                    sl = slice(blk * 512, (blk + 1) * 512)
                    P.op("pe", lambda e, sl=sl: e.matmul(ps[7][:], lhsT=BO, rhs=sqs[:, sl], start=True, stop=True),
                         reads=["sqs", "cst"], writes=[("ps", 7)])
                    P.op("act", lambda e, r=r: e.activation(out=rn[r][:], in_=ps[7][:], func=AF.Sqrt, bias=EPS),
                         reads=[("ps", 7)], writes=[("rn", r)])
                    P.op("dve", lambda e, r=r: e.reciprocal(out=rn[r][:], in_=rn[r][:]),
                         reads=[("rn", r)], writes=[("rn", r)])
                    P.op("dve", lambda e, r=r, sl=sl: e.tensor_tensor(
                        out=fT[s][:, t0 + sl.start:t0 + sl.stop], in0=fT[s][:, t0 + sl.start:t0 + sl.stop],
                        in1=rn[r][:], op=ALU.mult), reads=[("rn", r), fkey], writes=[fkey])

        for sg in range(NSEG):
            for s in range(3):
                if _DBG.get("stop", 9) >= 2:
                    do_p1(sg, s)

        sc = {}
        for nm in ("g", "beta", "gc", "ngc", "egc", "bg", "dk", "egl0", "egl1"):
            sc[nm] = [S("sc_%s%d" % (nm, h), [128, NTL]) for h in range(2)]
        nea = S("nea", [128, 2])
        P.op("act", lambda e: e.activation(out=nea[:], in_=hp[:, 0:2], func=AF.Exp), reads=["hp"], writes=["nea"])
        P.op("dve", lambda e: e.tensor_scalar(out=nea[:], in0=nea[:], scalar1=-1.0, scalar2=None, op0=ALU.mult),
             reads=["nea"], writes=["nea"])

        def do_scal(h):
            k = lambda nm: ("sc", nm, h)
            g, beta = sc["g"][h], sc["beta"][h]
            P.op("act", lambda e: e.activation(out=g[:], in_=ab[:, h, :], func=AF.Exp, bias=hp[:, 2 + h:3 + h]),
                 reads=["ab", "hp"], writes=[k("g")])
            P.op("act", lambda e: e.activation(out=g[:], in_=g[:], func=AF.Ln, bias=1.0),
                 reads=[k("g")], writes=[k("g")])
            P.op("dve", lambda e: e.tensor_scalar(out=g[:], in0=g[:], scalar1=nea[:, h:h + 1], scalar2=None,
                                                  op0=ALU.mult), reads=[k("g"), "nea"], writes=[k("g")])
            P.op("act", lambda e: e.activation(out=beta[:], in_=ab[:, 2 + h, :], func=AF.Sigmoid),
                 reads=["ab"], writes=[k("beta")])
            P.op("pe", lambda e: e.matmul(ps[7][:, 0:NTL], lhsT=LTm, rhs=g[:], start=True, stop=True),
                 reads=[k("g"), "cst"], writes=[("ps", 7)])
            P.op("dve", lambda e: e.tensor_copy(out=sc["gc"][h][:], in_=ps[7][:, 0:NTL]),
                 reads=[("ps", 7)], writes=[k("gc")])
            P.op("dve", lambda e: e.tensor_scalar(out=sc["ngc"][h][:], in0=ps[7][:, 0:NTL], scalar1=-1.0, scalar2=None,
                                                  op0=ALU.mult), reads=[("ps", 7)], writes=[k("ngc")])
            P.op("act", lambda e: e.activation(out=sc["egc"][h][:], in_=ps[7][:, 0:NTL], func=AF.Exp),
                 reads=[("ps", 7)], writes=[k("egc")])
            P.op("dve", lambda e: e.tensor_tensor(out=sc["bg"][h][:], in0=sc["egc"][h][:], in1=beta[:], op=ALU.mult),
                 reads=[k("egc"), k("beta")], writes=[k("bg")])
            P.op("pe", lambda e: e.matmul(ps[7][:, 0:NTL], lhsT=BO, rhs=g[:], start=True, stop=True),
                 reads=[k("g"), "cst"], writes=[("ps", 7)])
            P.op("dve", lambda e: e.tensor_tensor(out=sc["dk"][h][:], in0=ps[7][:, 0:NTL], in1=sc["gc"][h][:],
                                                  op=ALU.subtract), reads=[("ps", 7), k("gc")], writes=[k("dk")])
            P.op("act", lambda e: e.activation(out=sc["dk"][h][:], in_=sc["dk"][h][:], func=AF.Exp),
                 reads=[k("dk")], writes=[k("dk")])
            for c, SEL in ((0, SEL0), (1, SEL1)):
                P.op("pe", lambda e, SEL=SEL: e.matmul(ps[7][:, 0:NTL], lhsT=SEL, rhs=g[:], start=True, stop=True),
                     reads=[k("g"), "cst"], writes=[("ps", 7)])
                P.op("act", lambda e, c=c: e.activation(out=sc["egl%d" % c][h][:], in_=ps[7][:, 0:NTL], func=AF.Exp),
                     reads=[("ps", 7)], writes=[k("egl%d" % c)])

        for h in range(2):
            if _DBG.get("stop", 9) >= 3:
                do_scal(h)

        NQ = 24
        qcnt = [0]

        def nq():
            i = qcnt[0] % NQ
            qcnt[0] += 1
            bk, qt = i % 6, i // 6
            return ps[bk][:, qt * 128:(qt + 1) * 128], ("bk", bk)

        NB = 2
        tmp = {}

        def T_(nm, shape, n=NB * 2):
            tmp[nm] = [S("t_%s%d" % (nm, i), shape) for i in range(n)]

        T_("ktok", [128, 128], NB)
        T_("vtok", [128, 128], NB)
        T_("qtok", [128, 128], NB)
        for nm in ("dg", "dsym", "du", "dl", "attnT", "M", "MT", "RT", "Pa", "PaT", "Pb", "PbT"):
            T_(nm, [128, 128])
        T_("vb", [128, 64]); T_("kbg", [128, 64]); T_("kd0", [128, 64]); T_("kd1", [128, 64])
        T_("u", [128, 64]); T_("wT", [64, 128]); T_("qgT", [64, 128]); T_("qg", [128, 64])
        T_("tabs", [128, 128])
        vnew = [S("vnew%d" % h, [128, 64]) for h in range(2)]
        Sst = [S("Sst%d" % h, [64, 64]) for h in range(2)]
        for h in range(2):
            P.op("pool", lambda e, h=h: e.memset(vnew[h][:], 0.0), writes=[("vnew", h)])
            P.op("pool", lambda e, h=h: e.memset(Sst[h][:], 0.0), writes=[("S", h)])
        osb = [S("osb%d" % i, [64, 4, 64]) for i in range(2)]
        osq = [S("osq%d" % i, [64, 4, 64]) for i in range(2)]
        oss = [S("oss%d" % i, [64, 4]) for i in range(2)]
        zt = [S("zt%d" % i, [64, 2, 128]) for i in range(2)]
        yt = [S("yt%d" % i, [64, 2, 128]) for i in range(2)]

        stt = {}

        def front(tt):
            tb = tt % NB
            c0 = tt * 128
            toks = {}
            for s, nm in ((0, "qtok"), (1, "ktok"), (2, "vtok")):
                pq, kq = nq()
                P.op("pe", lambda e, pq=pq, s=s: e.matmul(pq, lhsT=fT[s][:, c0:c0 + 128], rhs=ident, start=True, stop=True),
                     reads=[("fT", s), "cst"], writes=[kq])
                dstt = tmp[nm][tb]
                if _DBG.get('nocopy'):
                    continue
                P.op("dve", lambda e, pq=pq, dstt=dstt: e.tensor_copy(out=dstt[:], in_=pq), reads=[kq], writes=[(nm, tb)])
                toks[nm] = dstt
            P.dma(zt[tb][:], z_d[:, 2 * tt:2 * tt + 2, :], writes=[("zt", tb)])
            P.op("act", lambda e: e.activation(out=zt[tb][:], in_=zt[tb][:], func=AF.Silu),
                 reads=[("zt", tb)], writes=[("zt", tb)])

            st = {}
            stt[tt] = st

            def prep_head(h):
                ix = tb * 2 + h
                hb = 64 * h
                kk = lambda nm, ix=ix: (nm, ix)
                g = lambda nm, ix=ix: tmp[nm][ix]
                col = lambda nm, h=h: sc[nm][h][:, tt:tt + 1]
                kTh = fT[1][hb:hb + 64, c0:c0 + 128]
                qTh = fT[0][hb:hb + 64, c0:c0 + 128]
                P.op("dve", lambda e, g=g, col=col: e.tensor_scalar(out=g("dg")[:], in0=ident, scalar1=col("gc"),
                                                                     scalar2=None, op0=ALU.mult),
                     reads=["cst", ("sc", "gc", h)], writes=[kk("dg")])
                pR, kR = nq()
                P.op("pe", lambda e, pR=pR, g=g: e.matmul(pR, lhsT=ONES, rhs=g("dg")[:], start=True, stop=True),
                     reads=[kk("dg"), "cst"], writes=[kR])
                P.op("act", lambda e, pR=pR, g=g, col=col: e.activation(
                    out=g("tabs")[:], in_=pR, func=AF.Abs, bias=col("ngc")),
                    reads=[kR, ("sc", "ngc", h)], writes=[kk("tabs")])
                P.op("act", lambda e, g=g: e.activation(out=g("dsym")[:], in_=g("tabs")[:], func=AF.Exp, scale=-1.0),
                     reads=[kk("tabs")], writes=[kk("dsym")])
                P.op("pool", lambda e, g=g: e.tensor_tensor(out=g("du")[:], in0=g("dsym")[:], in1=MUs, op=ALU.mult),
                     reads=[kk("dsym"), "cst"], writes=[kk("du")])
                P.op("pool", lambda e, g=g: e.tensor_tensor(out=g("dl")[:], in0=g("dsym")[:], in1=ML, op=ALU.mult),
                     reads=[kk("dsym"), "cst"], writes=[kk("dl")])
                pKK, kKK = nq()
                P.op("pe", lambda e, pKK=pKK, kTh=kTh: e.matmul(pKK, lhsT=kTh, rhs=kTh, start=True, stop=True),
                     reads=[("fT", 1)], writes=[kKK])
                pQK, kQK = nq()
                P.op("pe", lambda e, pQK=pQK, kTh=kTh, qTh=qTh: e.matmul(pQK, lhsT=kTh, rhs=qTh, start=True, stop=True),
                     reads=[("fT", 1), ("fT", 0)], writes=[kQK])
                P.op("dve", lambda e, pQK=pQK, g=g: e.tensor_tensor(out=g("attnT")[:], in0=pQK, in1=g("du")[:],
                                                                     op=ALU.mult),
                     reads=[kQK, kk("du")], writes=[kk("attnT")])
                P.op("dve", lambda e, pKK=pKK, g=g, col=col: e.scalar_tensor_tensor(
                    out=g("M")[:], in0=pKK, scalar=col("beta"), in1=g("dl")[:], op0=ALU.mult, op1=ALU.mult),
                    reads=[kKK, kk("dl"), ("sc", "beta", h)], writes=[kk("M")])
                ktok, vtok, qtok = toks["ktok"], toks["vtok"], toks["qtok"]
                P.op("pool", lambda e, g=g, col=col, vtok=vtok: e.tensor_scalar(
                    out=g("vb")[:], in0=vtok[:, hb:hb + 64], scalar1=col("beta"), scalar2=None, op0=ALU.mult),
                    reads=[("vtok", tb), ("sc", "beta", h)], writes=[kk("vb")])
                P.op("pool", lambda e, g=g, col=col, ktok=ktok: e.tensor_scalar(
                    out=g("kbg")[:], in0=ktok[:, hb:hb + 64], scalar1=col("bg"), scalar2=None, op0=ALU.mult),
                    reads=[("ktok", tb), ("sc", "bg", h)], writes=[kk("kbg")])
                P.op("pool", lambda e, g=g, col=col, ktok=ktok: e.tensor_scalar(
                    out=g("kd0")[:], in0=ktok[:, hb:hb + 64], scalar1=col("dk"), scalar2=SEL0[:, 0:1],
                    op0=ALU.mult, op1=ALU.mult), reads=[("ktok", tb), ("sc", "dk", h), "cst"], writes=[kk("kd0")])
                P.op("pool", lambda e, g=g, col=col, ktok=ktok: e.tensor_scalar(
                    out=g("kd1")[:], in0=ktok[:, hb:hb + 64], scalar1=col("dk"), scalar2=SEL1[:, 0:1],
                    op0=ALU.mult, op1=ALU.mult), reads=[("ktok", tb), ("sc", "dk", h), "cst"], writes=[kk("kd1")])
                P.op("dve", lambda e, g=g, col=col, qtok=qtok: e.tensor_scalar(
                    out=g("qg")[:], in0=qtok[:, hb:hb + 64], scalar1=col("egc"), scalar2=0.125,
                    op0=ALU.mult, op1=ALU.mult), reads=[("qtok", tb), ("sc", "egc", h)], writes=[kk("qg")])
                pqg, kqg = nq()
                P.op("pe", lambda e, pqg=pqg, g=g: e.matmul(pqg[0:64, :], lhsT=g("qg")[:], rhs=ident, start=True, stop=True),
                     reads=[kk("qg"), "cst"], writes=[kqg])
                P.op("act", lambda e, pqg=pqg, g=g: e.copy(out=g("qgT")[:], in_=pqg[0:64, :]),
                     reads=[kqg], writes=[kk("qgT")])
                pMT, kMT = nq()
                P.op("pe", lambda e, pMT=pMT, g=g: e.matmul(pMT, lhsT=g("M")[:], rhs=ident, start=True, stop=True),
                     reads=[kk("M"), "cst"], writes=[kMT])
                P.op("act", lambda e, pMT=pMT, g=g: e.copy(out=g("MT")[:], in_=pMT), reads=[kMT], writes=[kk("MT")])
                P.op("dve", lambda e, pMT=pMT, g=g: e.tensor_tensor(out=g("RT")[:], in0=ident, in1=pMT, op=ALU.subtract),
                     reads=[kMT, "cst"], writes=[kk("RT")])
                st[h] = dict(ix=ix, P=("M", "MT"))

            for h in range(2):
                prep_head(h)

            yield
            for lvl in range(5):
                if lvl > 0:
                    yield
                last = lvl == 4
                nxt = ("Pa", "PaT") if lvl % 2 == 0 else ("Pb", "PbT")
                pend = {}
                for h in range(2):
                    ix = st[h]["ix"]
                    Pn, PTn = st[h]["P"]
                    Pk, PTk = tmp[Pn][ix], tmp[PTn][ix]
                    p1, k1 = nq()
                    P.op("pe", lambda e, p1=p1, Pk=Pk, PTk=PTk: e.matmul(p1, lhsT=PTk[:], rhs=Pk[:], start=True, stop=True),
                         reads=[(Pn, ix), (PTn, ix)], writes=[k1])
                    p2 = k2 = None
                    if not last:
                        p2, k2 = nq()
                        P.op("pe", lambda e, p2=p2, Pk=Pk, PTk=PTk: e.matmul(p2, lhsT=Pk[:], rhs=PTk[:], start=True, stop=True),
                             reads=[(Pn, ix), (PTn, ix)], writes=[k2])
                    pend[h] = (p1, k1, p2, k2)
                for h in range(2):
                    ix = st[h]["ix"]
                    p1, k1, p2, k2 = pend[h]
                    Pnew, PTnew = tmp[nxt[0]][ix], tmp[nxt[1]][ix]
                    P.op("act", lambda e, p1=p1, Pnew=Pnew: e.copy(out=Pnew[:], in_=p1), reads=[k1], writes=[(nxt[0], ix)])
                    if not last:
                        P.op("dve", lambda e, p2=p2, PTnew=PTnew: e.tensor_copy(out=PTnew[:], in_=p2),
                             reads=[k2], writes=[(nxt[1], ix)])
                for h in range(2):
                    ix = st[h]["ix"]
                    Pnew = tmp[nxt[0]][ix]
                    RT = tmp["RT"][ix]
                    p3, k3 = nq()
                    P.op("pe", lambda e, p3=p3, Pnew=Pnew, RT=RT: e.matmul(p3, lhsT=Pnew[:], rhs=RT[:], start=True, stop=True),
                         reads=[(nxt[0], ix), ("RT", ix)], writes=[k3])
                    P.op("dve", lambda e, p3=p3, RT=RT: e.tensor_tensor(out=RT[:], in0=RT[:], in1=p3, op=ALU.add),
                         reads=[k3, ("RT", ix)], writes=[("RT", ix)])
                    st[h]["P"] = nxt

            yield
            for h in range(2):
                ix = st[h]["ix"]
                RT = tmp["RT"][ix]
                pu, ku = nq()
                P.op("pe", lambda e, pu=pu, RT=RT, ix=ix: e.matmul(pu[:, 0:64], lhsT=RT[:], rhs=tmp["vb"][ix][:],
                                                                   start=True, stop=True),
                     reads=[("RT", ix), ("vb", ix)], writes=[ku])
                P.op("act", lambda e, pu=pu, ix=ix: e.copy(out=tmp["u"][ix][:], in_=pu[:, 0:64]),
                     reads=[ku], writes=[("u", ix)])
                pw, kw = nq()
                P.op("pe", lambda e, pw=pw, RT=RT, ix=ix: e.matmul(pw[0:64, :], lhsT=tmp["kbg"][ix][:], rhs=RT[:],
                                                                   start=True, stop=True),
                     reads=[("RT", ix), ("kbg", ix)], writes=[kw])
                P.op("dve", lambda e, pw=pw, ix=ix: e.tensor_copy(out=tmp["wT"][ix][:], in_=pw[0:64, :]),
                     reads=[kw], writes=[("wT", ix)])

            yield

        def back(tt):
            tb = tt % NB
            st = stt.pop(tt)
            ob = tt % 2
            for c in range(2):
                yield
                rs = slice(c * 64, (c + 1) * 64)
                for h in range(2):
                    ix = st[h]["ix"]
                    pws, kws = nq()
                    P.op("pe", lambda e, pws=pws, ix=ix, h=h: e.matmul(pws[:, 0:64], lhsT=tmp["wT"][ix][:], rhs=Sst[h][:],
                                                                       start=True, stop=True),
                         reads=[("wT", ix), ("S", h)], writes=[kws])
                    P.op("dve", lambda e, pws=pws, ix=ix, h=h, rs=rs: e.tensor_tensor(
                        out=vnew[h][rs, :], in0=tmp["u"][ix][rs, :], in1=pws[rs, 0:64], op=ALU.subtract),
                        reads=[kws, ("u", ix)], writes=[("vnew", h)])
                    po = ps[6 + h][0:64, c * 64:(c + 1) * 64]
                    ko = ("bk", 6 + h)
                    P.op("pe", lambda e, po=po, ix=ix, h=h, rs=rs: e.matmul(po, lhsT=tmp["qgT"][ix][:, rs], rhs=Sst[h][:],
                                                                            start=True, stop=False),
                         reads=[("qgT", ix), ("S", h)], writes=[ko])
                    P.op("pe", lambda e, po=po, ix=ix, h=h, rs=rs: e.matmul(po, lhsT=tmp["attnT"][ix][:, rs], rhs=vnew[h][:],
                                                                            start=False, stop=True),
                         reads=[("attnT", ix), ("vnew", h)], writes=[ko])
                    pS, kS = nq()
                    kdn = "kd%d" % c
                    P.op("pe", lambda e, pS=pS, ix=ix, h=h, kdn=kdn: e.matmul(pS[0:64, 0:64], lhsT=tmp[kdn][ix][:], rhs=vnew[h][:],
                                                                              start=True, stop=True),
                         reads=[(kdn, ix), ("vnew", h)], writes=[kS])
                    P.op("dve", lambda e, pS=pS, h=h, c=c: e.scalar_tensor_tensor(
                        out=Sst[h][:], in0=Sst[h][:], scalar=sc["egl%d" % c][h][0:64, tt:tt + 1], in1=pS[0:64, 0:64],
                        op0=ALU.mult, op1=ALU.add), reads=[kS, ("S", h), ("sc", "egl%d" % c, h)], writes=[("S", h)])
                    P.op("act", lambda e, po=po, c=c, h=h: e.copy(out=osb[ob][:, c * 2 + h, :], in_=po),
                         reads=[ko], writes=[("osb", ob)])
            yield
            P.op("pool", lambda e: e.tensor_tensor(out=osq[ob][:], in0=osb[ob][:], in1=osb[ob][:], op=ALU.mult),
                 reads=[("osb", ob)], writes=[("osq", ob)])
            P.op("dve", lambda e: e.tensor_reduce(out=oss[ob][:], in_=osq[ob][:], axis=AX.X, op=ALU.add),
                 reads=[("osq", ob)], writes=[("oss", ob)])
            P.op("act", lambda e: e.activation(out=oss[ob][:], in_=oss[ob][:], func=AF.Sqrt, scale=1.0 / 64, bias=EPS),
                 reads=[("oss", ob)], writes=[("oss", ob)])
            P.op("dve", lambda e: e.reciprocal(out=oss[ob][:], in_=oss[ob][:]), reads=[("oss", ob)], writes=[("oss", ob)])
            for c in range(2):
                for h in range(2):
                    P.op("dve", lambda e, c=c, h=h: e.scalar_tensor_tensor(
                        out=yt[ob][:, c, h * 64:(h + 1) * 64], in0=osb[ob][:, c * 2 + h, :],
                        scalar=oss[ob][:, c * 2 + h:c * 2 + h + 1], in1=onw[:], op0=ALU.mult, op1=ALU.mult),
                        reads=[("osb", ob), ("oss", ob), "onw"], writes=[("yt", ob)])
            P.op("pool", lambda e: e.tensor_tensor(out=yt[ob][:], in0=yt[ob][:], in1=zt[tb][:], op=ALU.mult),
                 reads=[("yt", ob), ("zt", tb)], writes=[("yt", ob)])
            P.dma(ya_d[:, 2 * tt:2 * tt + 2, :], yt[ob][:], reads=[("yt", ob)])

        def drive(gens):
            gens = list(gens)
            while gens:
                for g_ in list(gens):
                    try:
                        next(g_)
                    except StopIteration:
                        gens.remove(g_)

        drive([front(0)])
        for tt in range(NTL):
            drive([back(tt)] + ([front(tt + 1)] if tt + 1 < NTL else []))
        P.emit()
    return nc


def gdn_in_maps(P0, T, qkv_conv, a_log, dt_bias, o_norm, sc_conv):
    P0 = P0.reshape(2, T, -1)
    cst = gdn_consts()
    maps = []
    for c in range(NCORES):
        b, hg = c // 4, c % 4
        o = 128 * hg
        pb = P0[b]
        qkvT = np.stack([pb[:, s * 512 + o:s * 512 + o + 128].T for s in range(3)])
        cw = np.concatenate([qkv_conv[:, s * 512 + o:s * 512 + o + 128].T for s in range(3)], axis=1)
        abc = np.concatenate([pb[:, 2048 + 2 * hg:2048 + 2 * hg + 2], pb[:, 2056 + 2 * hg:2056 + 2 * hg + 2]], axis=1)
        ab = abc.reshape(T // 128, 128, 4).transpose(1, 2, 0)
        hp = np.tile(np.concatenate([a_log[2 * hg:2 * hg + 2], dt_bias[2 * hg:2 * hg + 2]])[None], (128, 1))
        z_l = pb[:, 1536 + o:1536 + o + 128].reshape(T // 64, 64, 128).transpose(1, 0, 2)
        scT = np.stack([pb[:, 2064 + s * 512 + o:2064 + s * 512 + o + 128].T for s in range(3)])
        scw = sc_conv[:, o:o + 128].T
        maps.append(dict(qkvT=np.ascontiguousarray(qkvT, np.float32), cw=np.ascontiguousarray(cw, np.float32),
                         z_l=np.ascontiguousarray(z_l, np.float32), ab=np.ascontiguousarray(ab, np.float32),
                         hp=np.ascontiguousarray(hp, np.float32), onw=np.ascontiguousarray(np.tile(o_norm[None], (64, 1)), np.float32),
                         scT=np.ascontiguousarray(scT, np.float32), scw=np.ascontiguousarray(scw, np.float32), cst=cst))
    return maps


def gdn_gather(results, T):
    y = np.zeros((2, T, 1024), np.float32)
    for c in range(NCORES):
        b, hg = c // 4, c % 4
        o = 128 * hg
        y[b, :, o:o + 128] = results[c]["ya_l"].transpose(1, 0, 2).reshape(T, 128)
        y[b, :, 512 + o:512 + o + 128] = results[c]["ybT"].T
    return y.reshape(2 * T, 1024)


def nsa_consts(T):
    NTL = T // 128
    p = np.arange(128)
    half = 8
    inv_freq = (500000.0 ** (-(np.arange(0, 16, 2, dtype=np.float32)) / np.float32(16))).astype(np.float32)
    ang = np.arange(T, dtype=np.float32)[:, None] * inv_freq[None, :]
    cos, sin = np.cos(ang).astype(np.float32), np.sin(ang).astype(np.float32)
    C = np.ones((64, T), np.float32)
    C[0:8] = cos.T
    C[8:16] = cos.T
    Sg = np.zeros((16, T), np.float32)
    Sg[0:8] = -sin.T
    Sg[8:16] = sin.T
    C4 = np.ascontiguousarray(np.broadcast_to(C[:, None, :], (64, 4, T)))
    S4 = np.ascontiguousarray(np.broadcast_to(Sg[:, None, :], (16, 4, T)))
    key = np.arange(T)
    Eg = np.zeros((128, T), np.float32)
    Eg[64:128] = (np.arange(64)[:, None] == ((key[None, :] // 64) % 64))
    PM = np.zeros((128, 17, 128), np.float32)
    for r in range(17):
        PM[:, r, :] = (16 * p[:, None] + 31) <= (p[None, :] + 128 * r)
    CM = (p[:, None] <= p[None, :]).astype(np.float32)
    AM = (p[:, None] > p[None, :]).astype(np.float32)
    msk = np.ascontiguousarray(np.concatenate([PM, CM[:, None, :], AM[:, None, :], np.eye(128, dtype=np.float32)[:, None, :]], axis=1))
    c = np.arange(512)
    s = np.arange(128)
    ov = ((16 * c[:, None] < 64 * s[None, :] + 64) & (16 * c[:, None] + 32 > 64 * s[None, :])).astype(np.float32)
    ov[511] = 0.0
    ov = np.ascontiguousarray(ov.reshape(4, 128, 128).transpose(1, 0, 2))
    NQB = T // 128
    bq = np.zeros((NQB, 128, 128), np.float32)
    for qb in range(NQB):
        cur = 2 * qb + (p >= 64)
        js = s[None, :]
        forced = (js == 0) | (js == cur[:, None]) | (js == cur[:, None] - 1)
        bq[qb] = np.where(js > cur[:, None], -100.0, np.where(forced, 100.0, 0.0))
    return dict(C4=C4, S4=S4, Eg=Eg, msk=msk, ov=ov, bq=bq)


def build_nsa(T):
    nc = bass.Bass("TRN2", target_bir_lowering=False)
    NTL = T // 128
    NQB = NTL
    SEG = min(T, 2048)
    NSEG = T // SEG
    D = lambda name, shape: nc.dram_tensor(name, list(shape), F32, kind="ExternalInput").ap()
    qT_d = D("qT", [64, 4, T]); qsw_d = D("qsw", [16, 4, T])
    kT_d = D("kT", [2, 64, T]); ksw_d = D("ksw", [2, 16, T])
    v_d = D("vtok", [2, 128, NTL, 64])
    KP_d = D("KP", [2, 128, T // 2])
    posP_d = D("posP", [128, 16])
    w1_d = D("w1", [2, 128, 16, 256]); w2_d = D("w2", [2, 128, 2, 64])
    gat_d = D("gates", [128, NTL, 12])
    C4_d = D("C4", [64, 4, T]); S4_d = D("S4", [16, 4, T])
    Eg_d = D("Eg", [128, T]); msk_d = D("msk", [128, 20, 128]); ov_d = D("ov", [128, 4, 128])
    bq_d = D("bq", [NQB, 128, 128])
    o_d = nc.dram_tensor("o_out", [T, 256], F32, kind="ExternalOutput").ap()

    with ExitStack() as es:
        P = Prog(nc, es)
        S = lambda name, shape, dt=F32: _sb(nc, es, name, shape, dt)
        msk = S("msk_sb", [128, 20, 128])
        CM, AM, ident = msk[:, 17, :], msk[:, 18, :], msk[:, 19, :]
        KT = [S("KTb%d" % i, [128 if i == 0 else 64, T], BF16) for i in range(2)]
        NGRP = (2 * NTL + 63) // 64
        Vb = [S("Vb%d" % i, [128, NTL, 65], BF16) for i in range(2)]
        kcT = S("kcT", [64, 512], BF16)
        Vc = S("Vc", [128, 4, 193], BF16)
        gs = S("gs", [128, NTL, 12])
        posP = S("posP_sb", [128, 16])
        stg = [S("stg%d" % i, [128, 4096]) for i in range(2)]
        stg16 = [S("stg16_%d" % i, [16, SEG]) for i in range(3)]
        KPp = S("KPp", [128, 16, 512], BF16)
        w1b = S("w1b", [128, 16, 256], BF16)
        w2b = S("w2b", [128, 2, 64], BF16)
        HT = S("HT", [128, 2, 512], BF16)
        ps = [_ps(nc, es, "ps%d" % i, [128, 512], F32) for i in range(8)]
        bk = lambda i: ("bk", i)

        P.dma(msk[:], msk_d, writes=["msk"])
        P.dma(posP[:], posP_d, writes=["posP"])
        for sg in range(NSEG):
            sv = stg[sg % 2][64:128, 0:SEG]
            P.dma(sv, Eg_d[64:128, sg * SEG:(sg + 1) * SEG], writes=[("stg", sg % 2)])
            P.op("pool", lambda e, sv=sv, sg=sg: e.tensor_copy(out=KT[0][64:128, sg * SEG:(sg + 1) * SEG], in_=sv),
                 reads=[("stg", sg % 2)], writes=[("KTe", 0)])
        P.dma(gs[:], gat_d, writes=["gs"])
        P.op("act", lambda e: e.activation(out=gs[:], in_=gs[:], func=AF.Sigmoid), reads=["gs"], writes=["gs"])

        def do_k(i, sg):
            t0 = sg * SEG
            a, b = stg[0][0:64, 0:SEG], stg[1][0:64, 0:SEG]
            P.dma(a, kT_d[i][:, t0:t0 + SEG], writes=[("stg", 0)])
            P.dma(b, C4_d[:, 0, t0:t0 + SEG], writes=[("stg", 1)])
            P.dma(stg16[0][:], ksw_d[i][:, t0:t0 + SEG], writes=[("s16", 0)])
            P.dma(stg16[1][:], S4_d[:, 0, t0:t0 + SEG], writes=[("s16", 1)])
            P.op("dve", lambda e: e.tensor_tensor(out=KT[i][0:64, t0:t0 + SEG], in0=a, in1=b, op=ALU.mult),
                 reads=[("stg", 0), ("stg", 1)], writes=[("KT", i)])
            P.op("pool", lambda e: e.tensor_tensor(out=stg16[0][:], in0=stg16[0][:], in1=stg16[1][:], op=ALU.mult),
                 reads=[("s16", 0), ("s16", 1)], writes=[("s16", 0)])
            P.op("pool", lambda e: e.tensor_tensor(out=stg16[2][:], in0=a[0:16, :], in1=b[0:16, :], op=ALU.mult),
                 reads=[("stg", 0), ("stg", 1)], writes=[("s16", 2)])
            P.op("dve", lambda e: e.tensor_tensor(out=KT[i][0:16, t0:t0 + SEG], in0=stg16[0][:], in1=stg16[2][:], op=ALU.add),
                 reads=[("s16", 0), ("s16", 2), ("KT", i)], writes=[("KT", i)])

        for i in range(2):
            for sg in range(NSEG):
                do_k(i, sg)

        def do_v(i):
            sv = stg[i][:, 0:NTL * 64].rearrange("p (a b) -> p a b", a=NTL)
            P.dma(sv, v_d[i], writes=[("stg", i)])
            P.op("pool", lambda e: e.memset(Vb[i][:, :, 64:65], 1.0), writes=[("Vb", i)])
            P.op("dve", lambda e: e.tensor_copy(out=Vb[i][:, :, 0:64], in_=sv), reads=[("stg", i)], writes=[("Vb", i)])

        for i in range(2):
            do_v(i)

        P.op("pool", lambda e: e.memset(kcT[:], 0.0), writes=["kcT"])
        P.op("pool", lambda e: e.memset(Vc[:], 0.0), writes=["Vc"])

        def do_cmp(i):
            kp = stg[0][:, 0:T // 2]
            P.dma(kp, KP_d[i], writes=[("stg", 0)])
            kpv = kp.rearrange("p (c e) -> p c e", e=8)
            NCB = T // 16 - 1
            for lp in range(16):
                src = kpv[:, 0:NCB, lp] if lp < 8 else kpv[:, 1:NCB + 1, lp - 8]
                eng = "dve" if lp % 2 == 0 else "pool"
                P.op(eng, lambda e, src=src, lp=lp: e.tensor_scalar(out=KPp[:, lp, 0:NCB], in0=src, scalar1=posP[:, lp:lp + 1],
                                                                   scalar2=None, op0=ALU.add),
                     reads=[("stg", 0), "posP"], writes=["KPp"])
            w1v = stg[1][:, 0:4096].rearrange("p (a b) -> p a b", a=16)
            P.dma(w1v, w1_d[i], writes=[("stg", 1)])
            P.op("act", lambda e: e.copy(out=w1b[:], in_=w1v), reads=[("stg", 1)], writes=["w1b"])
            w2v = stg[0][:, 0:128].rearrange("p (a b) -> p a b", a=2)
            P.dma(w2v, w2_d[i], reads=["KPp"], writes=[("stg", 0)])
            P.op("act", lambda e: e.copy(out=w2b[:], in_=w2v), reads=[("stg", 0)], writes=["w2b"])
            for jc in range(2):
                for lp in range(16):
                    P.op("pe", lambda e, jc=jc, lp=lp: e.matmul(ps[jc][:, 0:NCB], lhsT=w1b[:, lp, jc * 128:(jc + 1) * 128],
                                                                rhs=KPp[:, lp, 0:NCB], start=(lp == 0), stop=(lp == 15)),
                         reads=["w1b", "KPp"], writes=[bk(jc)])
                P.op("act", lambda e, jc=jc: e.activation(out=HT[:, jc, 0:NCB], in_=ps[jc][:, 0:NCB], func=AF.Silu),
                     reads=[bk(jc)], writes=["HT"])
            if i == 0:
                for jc in range(2):
                    P.op("pe", lambda e, jc=jc: e.matmul(ps[2][0:64, 0:NCB], lhsT=w2b[:, jc, :], rhs=HT[:, jc, 0:NCB],
                                                         start=(jc == 0), stop=(jc == 1)), reads=["w2b", "HT"], writes=[bk(2)])
                P.op("act", lambda e: e.copy(out=kcT[:, 0:NCB], in_=ps[2][0:64, 0:NCB]), reads=[bk(2)], writes=["kcT"])
            else:
                for ct in range((NCB + 127) // 128):
                    n = min(128, NCB - ct * 128)
                    for jc in range(2):
                        P.op("pe", lambda e, jc=jc, ct=ct, n=n: e.matmul(ps[3][0:n, 0:64], lhsT=HT[:, jc, ct * 128:ct * 128 + n],
                                                                         rhs=w2b[:, jc, :], start=(jc == 0), stop=(jc == 1)),
                             reads=["w2b", "HT"], writes=[bk(3)])
                    P.op("act", lambda e, ct=ct, n=n: e.copy(out=Vc[0:n, ct, 0:64], in_=ps[3][0:n, 0:64]),
                         reads=[bk(3)], writes=["Vc"])
                    P.op("pool", lambda e, ct=ct, n=n: e.memset(Vc[0:n, ct, 64:65], 1.0), reads=["Vc"], writes=["Vc"])

        do_cmp(0)
        do_cmp(1)
        ovs = stg[1][:, 0:512].rearrange("p (a b) -> p a b", a=4)
        P.dma(ovs, ov_d, reads=["w1b"], writes=[("stg", 1)])
        P.op("dve", lambda e: e.tensor_copy(out=Vc[:, :, 65:193], in_=ovs), reads=[("stg", 1), "Vc"], writes=["Vc"])

        qf = [S("qf%d" % i, [64, 4, 128]) for i in range(2)]
        c4 = [S("c4_%d" % i, [64, 4, 128]) for i in range(2)]
        qs = [S("qs%d" % i, [16, 4, 128]) for i in range(2)]
        s4 = [S("s4_%d" % i, [16, 4, 128]) for i in range(2)]
        v16 = [S("v16_%d" % i, [16, 4, 128]) for i in range(2)]
        Qn = [S("Qn%d" % i, [64, 512], BF16) for i in range(2)]
        QA = [[S("QA%d_%d" % (i, g_), [128, 512], BF16) for g_ in range(NGRP)] for i in range(2)]
        Qr = [QA[i][0][0:64, :] for i in range(2)]
        nselsw = S("nselsw", [128, 128])
        oT = [S("oT%d" % i, [65, 512]) for i in range(2)]
        bqs = [S("bqs%d" % i, [128, 128]) for i in range(2)]
        Pc = S("Pc", [128, 4, 512], BF16)
        NPB = 4
        Pb = [S("Pb%d" % i, [128, 512], BF16) for i in range(NPB)]
        ocmp = S("ocmp", [128, 4, 193])
        ow = S("ow", [128, 4, 65])
        osl = S("osl", [128, 4, 65])
        rd = S("rd", [128, 12])
        impm = S("impm", [128, 128])
        imp2 = S("imp2", [128, 128])
        m8 = S("m8", [128, 16])
        nsel = S("nsel", [128, 128])
        osum = [S("osum%d" % i, [128, 4, 64]) for i in range(2)]
        pcnt = [0]
        scnt = [0]

        def do_qb(qb):
            j = qb % 2
            q0 = qb * 128
            P.dma(qf[j][:], qT_d[:, :, q0:q0 + 128], writes=[("qf", j)])
            P.dma(c4[j][:], C4_d[:, :, q0:q0 + 128], writes=[("c4", j)])
            P.dma(qs[j][:], qsw_d[:, :, q0:q0 + 128], writes=[("qs", j)])
            P.dma(s4[j][:], S4_d[:, :, q0:q0 + 128], writes=[("s4", j)])
            P.dma(bqs[j][:], bq_d[qb], writes=[("bqs", j)])
            fl = lambda t: t[:].rearrange("p a b -> p (a b)")
            P.op("pool", lambda e: e.tensor_copy(out=Qn[j][:], in_=fl(qf[j])), reads=[("qf", j)], writes=[("Qn", j)])
            P.op("pool", lambda e: e.tensor_tensor(out=Qr[j], in0=fl(qf[j]), in1=fl(c4[j]), op=ALU.mult),
                 reads=[("qf", j), ("c4", j)], writes=[("Qr", j)])
            P.op("pool", lambda e: e.tensor_tensor(out=fl(qs[j]), in0=fl(qs[j]), in1=fl(s4[j]), op=ALU.mult),
                 reads=[("qs", j), ("s4", j)], writes=[("qs", j)])
            P.op("pool", lambda e: e.tensor_tensor(out=fl(v16[j]), in0=qf[j][0:16, :, :].rearrange("p a b -> p (a b)"),
                                                    in1=c4[j][0:16, :, :].rearrange("p a b -> p (a b)"), op=ALU.mult),
                 reads=[("qf", j), ("c4", j)], writes=[("v16", j)])
            P.op("pool", lambda e: e.tensor_tensor(out=QA[j][0][0:16, :], in0=fl(qs[j]), in1=fl(v16[j]), op=ALU.add),
                 reads=[("qs", j), ("v16", j), ("Qr", j)], writes=[("Qr", j)])
            ngrp = min(NGRP, (2 * qb + 1) // 64 + 1)
            for g_ in range(1, ngrp):
                P.op("pool", lambda e, g_=g_: e.tensor_copy(out=QA[j][g_][0:64, :], in_=QA[j][0][0:64, :]),
                     reads=[("Qr", j)], writes=[("QAq", j, g_)])

            cts = [ct for ct in range(4) if qb - 16 * ct >= 0 and ct * 128 < T // 16 - 1]
            for ct in cts:
                r = qb - 16 * ct
                sb_ = (0, 1, 6, 7)[scnt[0] % 4]
                scnt[0] += 1
                P.op("pe", lambda e, ct=ct, sb_=sb_: e.matmul(ps[sb_][:], lhsT=kcT[:, ct * 128:(ct + 1) * 128], rhs=Qn[j][:],
                                                              start=True, stop=True), reads=["kcT", ("Qn", j)], writes=[bk(sb_)])
                P.op("act", lambda e, ct=ct, sb_=sb_: e.activation(out=Pc[:, ct, :], in_=ps[sb_][:], func=AF.Exp, scale=0.125),
                     reads=[bk(sb_)], writes=[("Pc", ct)])
                if r <= 16:
                    for hh in range(4):
                        P.op("dve", lambda e, ct=ct, hh=hh, r=r: e.tensor_tensor(
                            out=Pc[:, ct, hh * 128:(hh + 1) * 128], in0=Pc[:, ct, hh * 128:(hh + 1) * 128],
                            in1=msk[:, r, :], op=ALU.mult), reads=[("Pc", ct), "msk"], writes=[("Pc", ct)])
            for hh in range(4):
                bank = 2 + hh // 2
                off = (hh % 2) * 193
                for n_, ct in enumerate(cts):
                    P.op("pe", lambda e, hh=hh, ct=ct, bank=bank, off=off, n_=n_: e.matmul(
                        ps[bank][:, off:off + 193], lhsT=Pc[:, ct, hh * 128:(hh + 1) * 128], rhs=Vc[:, ct, :],
                        start=(n_ == 0), stop=(n_ == len(cts) - 1)), reads=[("Pc", ct), "Vc"], writes=[bk(bank)])
            for half in range(2):
                P.op("act", lambda e, half=half: e.copy(
                    out=ocmp[:, 2 * half:2 * half + 2, :].rearrange("p a b -> p (a b)"), in_=ps[2 + half][:, 0:386]),
                    reads=[bk(2 + half)], writes=["ocmp"])
            P.op("dve", lambda e: e.tensor_scalar(out=rd[:, 0:4], in0=ocmp[:, :, 64], scalar1=1e-30, scalar2=None, op0=ALU.add),
                 reads=["ocmp"], writes=["rd"])
            P.op("dve", lambda e: e.reciprocal(out=rd[:, 0:4], in_=rd[:, 0:4]), reads=["rd"], writes=["rd"])
            P.op("dve", lambda e: e.scalar_tensor_tensor(out=impm[:], in0=ocmp[:, 0, 65:193], scalar=rd[:, 0:1], in1=bqs[j][:],
                                                         op0=ALU.mult, op1=ALU.add), reads=["ocmp", "rd", ("bqs", j)], writes=["impm"])
            for hh in range(1, 4):
                P.op("dve", lambda e, hh=hh: e.scalar_tensor_tensor(out=impm[:], in0=ocmp[:, hh, 65:193], scalar=rd[:, hh:hh + 1],
                                                                    in1=impm[:], op0=ALU.mult, op1=ALU.add),
                     reads=["ocmp", "rd", "impm"], writes=["impm"])
            P.op("dve", lambda e: e.max(out=m8[:, 0:8], in_=impm[:]), reads=["impm"], writes=["m8"])
            P.op("dve", lambda e: e.match_replace(out=imp2[:], in_to_replace=m8[:, 0:8], in_values=impm[:], imm_value=-1e9),
                 reads=["impm", "m8"], writes=["imp2"])
            P.op("dve", lambda e: e.max(out=m8[:, 8:16], in_=imp2[:]), reads=["imp2"], writes=["m8"])
            P.op("dve", lambda e: e.tensor_scalar(out=nsel[:], in0=impm[:], scalar1=m8[:, 15:16], scalar2=1.0,
                                                  op0=ALU.is_ge, op1=ALU.subtract), reads=["impm", "m8"], writes=["nsel"])
            for g_ in range(ngrp):
                if g_ == 0:
                    P.op("pool", lambda e: e.tensor_copy(out=nselsw[:, 64:128], in_=nsel[:, 0:64]), reads=["nsel"], writes=["nselsw"])
                    src, skey = nselsw, "nselsw"
                else:
                    src, skey = nsel, "nsel"
                P.op("pe", lambda e, src=src: e.transpose(out=ps[4][:, 0:128], in_=src[:], identity=ident),
                     reads=[skey, "msk"], writes=[bk(4)])
                for hh in range(4):
                    P.op("act", lambda e, hh=hh, g_=g_: e.activation(out=QA[j][g_][64:128, hh * 128:(hh + 1) * 128],
                                                                    in_=ps[4][64:128, 0:128], func=AF.Copy, scale=30000.0),
                         reads=[bk(4)], writes=[("QAs", j, g_)])

            SB = (0, 1, 6, 7)

            def attn(kts, i, with_sel, odst, okey):
                obank = 5 if with_sel else 4
                ot_ = oT[1 if with_sel else 0]
                otk = ("oT", 1 if with_sel else 0)

                def score(kt):
                    sb_ = SB[scnt[0] % 4]
                    scnt[0] += 1
                    if with_sel:
                        g_ = kt // 32
                        P.op("pe", lambda e, kt=kt, sb_=sb_, g_=g_: e.matmul(ps[sb_][:], lhsT=KT[0][:, kt * 128:(kt + 1) * 128],
                                                                            rhs=QA[j][g_][:], start=True, stop=True),
                             reads=[("KT", 0), ("KTe", 0), ("Qr", j), ("QAq", j, g_), ("QAs", j, g_)], writes=[bk(sb_)])
                    else:
                        P.op("pe", lambda e, kt=kt, sb_=sb_: e.matmul(ps[sb_][:], lhsT=KT[1][:, kt * 128:(kt + 1) * 128], rhs=Qr[j],
                                                                      start=True, stop=True),
                             reads=[("KT", 1), ("Qr", j)], writes=[bk(sb_)])
                    pb = pcnt[0] % NPB
                    pcnt[0] += 1
                    P.op("act", lambda e, sb_=sb_, pb=pb: e.activation(out=Pb[pb][:], in_=ps[sb_][:], func=AF.Exp, scale=0.125),
                         reads=[bk(sb_)], writes=[("Pb", pb)])
                    mk = None
                    if kt == qb:
                        mk = CM
                    elif (not with_sel) and kt == qb - 4:
                        mk = AM
                    if mk is not None:
                        for hh in range(4):
                            P.op("dve", lambda e, hh=hh, pb=pb, mk=mk: e.tensor_tensor(
                                out=Pb[pb][:, hh * 128:(hh + 1) * 128], in0=Pb[pb][:, hh * 128:(hh + 1) * 128], in1=mk, op=ALU.mult),
                                reads=[("Pb", pb), "msk"], writes=[("Pb", pb)])
                    return pb

                def pv(n_, kt, pb):
                    P.op("pe", lambda e, pb=pb, kt=kt, n_=n_: e.matmul(
                        ps[obank][0:65, :], lhsT=Vb[i][:, kt, :], rhs=Pb[pb][:],
                        start=(n_ == 0), stop=(n_ == len(kts) - 1)), reads=[("Pb", pb), ("Vb", i)], writes=[bk(obank)])

                DEPTH = 2
                pbs = {}
                for n_ in range(min(DEPTH, len(kts))):
                    pbs[n_] = score(kts[n_])
                for n_, kt in enumerate(kts):
                    if n_ + DEPTH < len(kts):
                        pbs[n_ + DEPTH] = score(kts[n_ + DEPTH])
                    pv(n_, kt, pbs.pop(n_))
                P.op("act", lambda e: e.copy(out=ot_[:], in_=ps[obank][0:65, :]), reads=[bk(obank)], writes=[otk])
                for hh in range(4):
                    P.op("pe", lambda e, hh=hh: e.transpose(out=ps[obank][:, hh * 65:(hh + 1) * 65], in_=ot_[:, hh * 128:(hh + 1) * 128],
                                                            identity=ident[0:65, 0:65]), reads=[otk, "msk"], writes=[bk(obank)])
                P.op("act", lambda e: e.copy(out=odst[:].rearrange("p a b -> p (a b)"), in_=ps[obank][:, 0:260]),
                     reads=[bk(obank)], writes=[okey])

            attn(list(range(max(0, qb - 4), qb + 1)), 1, False, ow, "ow")
            attn(list(range(0, qb + 1)), 0, True, osl, "osl")

            ob = osum[j]
            for x, (src, key) in enumerate(((ocmp, "ocmp"), (osl, "osl"), (ow, "ow"))):
                if x > 0:
                    P.op("dve", lambda e, src=src, x=x: e.reciprocal(out=rd[:, 4 * x:4 * x + 4], in_=src[:, :, 64]),
                         reads=[key], writes=["rd"])
                P.op("dve", lambda e, x=x: e.tensor_tensor(
                    out=rd[:, 4 * x:4 * x + 4], in0=rd[:, 4 * x:4 * x + 4],
                    in1=gs[:, qb, :].rearrange("p (h x) -> p h x", x=3)[:, :, x], op=ALU.mult), reads=["rd", "gs"], writes=["rd"])
                for hh in range(4):
                    if x == 0:
                        P.op("dve", lambda e, hh=hh, src=src, x=x: e.tensor_scalar(
                            out=ob[:, hh, :], in0=src[:, hh, 0:64], scalar1=rd[:, 4 * x + hh:4 * x + hh + 1], scalar2=None,
                            op0=ALU.mult), reads=[key, "rd"], writes=[("osum", j)])
                    else:
                        P.op("dve", lambda e, hh=hh, src=src, x=x: e.scalar_tensor_tensor(
                            out=ob[:, hh, :], in0=src[:, hh, 0:64], scalar=rd[:, 4 * x + hh:4 * x + hh + 1], in1=ob[:, hh, :],
                            op0=ALU.mult, op1=ALU.add), reads=[key, "rd", ("osum", j)], writes=[("osum", j)])
            P.dma(o_d[q0:q0 + 128, :], ob[:].rearrange("p a b -> p (a b)"), reads=[("osum", j)])

        for qb in range(NQB):
            do_qb(qb)
        P.emit()
    return nc


def nsa_in_maps(P1, T, cmp_pos, k_w1, k_w2, v_w1, v_w2):
    P1 = P1.reshape(2, T, -1)
    cst = nsa_consts(T)
    NTL = T // 128
    maps = []
    posP = np.zeros((128, 16), np.float32)
    for lp in range(16):
        posP[0:64, lp] = cmp_pos[2 * lp]
        posP[64:128, lp] = cmp_pos[2 * lp + 1]
    w1 = np.stack([w.reshape(16, 128, 256).transpose(1, 0, 2) for w in (k_w1, v_w1)])
    w2 = np.stack([w.reshape(2, 128, 64).transpose(1, 0, 2) for w in (k_w2, v_w2)])
    swap = np.concatenate([np.arange(8, 16), np.arange(0, 8)])
    for c in range(NCORES):
        b, g = c // 4, c % 4
        pb = P1[b]
        q = pb[:, 256 * g:256 * g + 256].reshape(T, 4, 64)
        qT = q.transpose(2, 1, 0)
        col = lambda i: pb[:, 1024 + 256 * i + 64 * g:1024 + 256 * i + 64 * g + 64]
        k_c, v_c, k_s, v_s, k_w, v_w = [col(i) for i in range(6)]
        kT = np.stack([k_s.T, k_w.T])
        vt = np.stack([v.reshape(NTL, 128, 64).transpose(1, 0, 2) for v in (v_s, v_w)])
        KP = np.stack([np.concatenate([a[0::2].T, a[1::2].T], axis=0) for a in (k_c, v_c)])
        gates = pb[:, 2560 + 12 * g:2560 + 12 * g + 12].reshape(NTL, 128, 12).transpose(1, 0, 2)
        f = lambda a: np.ascontiguousarray(a, np.float32)
        maps.append(dict(qT=f(qT), qsw=f(qT[swap]), kT=f(kT), ksw=f(kT[:, swap]), vtok=f(vt), KP=f(KP), posP=posP,
                         w1=f(w1), w2=f(w2), gates=f(gates), C4=cst["C4"], S4=cst["S4"], Eg=cst["Eg"], msk=cst["msk"],
                         ov=cst["ov"], bq=cst["bq"]))
    return maps


def nsa_gather(results, T):
    o = np.zeros((2, T, 1024), np.float32)
    for c in range(NCORES):
        b, g = c // 4, c % 4
        o[b, :, 256 * g:256 * g + 256] = results[c]["o_out"]
    return o.reshape(2 * T, 1024)


_T = 8192
_NT = 2048


def _run(nc, maps):
    res = run_bass_kernel_spmd(nc, maps, core_ids=list(range(NCORES)))
    return res.results


def kernel(x, mix_norm, mlp_norm, w_up, w_down, final_norm,
           ev_w_in, ev_qkv_conv, ev_a_log, ev_dt_bias, ev_o_norm, ev_sc_conv, ev_w_out,
           od_w_in, od_cmp_pos, od_cmp_k_w1, od_cmp_k_w2, od_cmp_v_w1, od_cmp_v_w2, od_w_out):
    f = lambda a: np.ascontiguousarray(np.asarray(a), np.float32)
    x = f(x).reshape(2 * _T, D_MODEL)
    ident = np.eye(128, dtype=np.float32)
    rep = lambda v: np.ascontiguousarray(np.tile(f(v)[None, :], (128, 1)))
    sh = lambda a, c: np.ascontiguousarray(a[c * _NT:(c + 1) * _NT])
    shT = lambda a, c: np.ascontiguousarray(a[c * _NT:(c + 1) * _NT].T)
    cat = lambda rs, k: np.concatenate([r[k] for r in rs], axis=0)

    nc = build_dense(_NT, False, False, "inproj", 3600)
    rs = _run(nc, [dict(x=sh(x, c), ident=ident, nw_tail=rep(mix_norm[0]), w_in=f(ev_w_in[0])) for c in range(NCORES)])
    P0 = cat(rs, "p_out")
    nc = build_gdn(_T)
    rs = _run(nc, gdn_in_maps(P0, _T, f(ev_qkv_conv[0]), f(ev_a_log[0]), f(ev_dt_bias[0]), f(ev_o_norm[0]), f(ev_sc_conv[0])))
    y0 = gdn_gather(rs, _T)
    nc = build_dense(_NT, True, True, "inproj", 2608)
    rs = _run(nc, [dict(x=sh(x, c), ident=ident, yT=shT(y0, c), w_out=f(ev_w_out[0]), nw_mlp=rep(mlp_norm[0]),
                        w_up=f(w_up[0]), w_down=f(w_down[0]), nw_tail=rep(mix_norm[1]), w_in=f(od_w_in[0]))
                   for c in range(NCORES)])
    x2 = cat(rs, "x_out")
    P1 = cat(rs, "p_out")
    nc = build_nsa(_T)
    rs = _run(nc, nsa_in_maps(P1, _T, f(od_cmp_pos[0]), f(od_cmp_k_w1[0]), f(od_cmp_k_w2[0]), f(od_cmp_v_w1[0]), f(od_cmp_v_w2[0])))
    o1 = nsa_gather(rs, _T)
    nc = build_dense(_NT, True, True, "final")
    rs = _run(nc, [dict(x=sh(x2, c), ident=ident, yT=shT(o1, c), w_out=f(od_w_out[0]), nw_mlp=rep(mlp_norm[1]),
                        w_up=f(w_up[1]), w_down=f(w_down[1]), nw_tail=rep(final_norm)) for c in range(NCORES)])
    out = cat(rs, "out")
    return out.reshape(2, _T, D_MODEL).astype(np.float32)
```

```python
import numpy as np
import ml_dtypes
from contextlib import ExitStack
import concourse.bass as bass
import concourse.mybir as mybir
from concourse.bass_utils import run_bass_kernel_spmd

F32 = mybir.dt.float32
BF16 = mybir.dt.bfloat16
AF = mybir.ActivationFunctionType
ALU = mybir.AluOpType
AX = mybir.AxisListType

NCORES = 8
D_MODEL = 1024
D_FF = 4096
EPS = 1e-6
_DBG = {}


class _Ins:
    __slots__ = ("eng", "fn", "deps", "isdma", "need_inc", "tok", "q")

    def __init__(self, eng, fn, isdma):
        self.eng = eng
        self.fn = fn
        self.deps = []
        self.isdma = isdma
        self.need_inc = isdma
        self.tok = None


class Prog:
    ENGS = ("pe", "act", "dve", "pool", "sp")
    NDSEM = 12

    def __init__(self, nc, es):
        self.nc = nc
        self.es = es
        self.streams = {e: [] for e in self.ENGS}
        self.last_write = {}
        self.readers = {}
        self.all_dma = []

    def _add(self, eng, fn, reads, writes, isdma):
        ins = _Ins(eng, fn, isdma)
        excl = [k for k in reads if isinstance(k, tuple) and k[0] in ("bk", "ps")]
        if excl:
            writes = list(writes) + [k for k in excl if k not in writes]
        deps = {}
        for k in reads:
            w = self.last_write.get(k)
            if w is not None:
                deps[id(w)] = (w, "raw")
        for k in writes:
            w = self.last_write.get(k)
            if w is not None and id(w) not in deps:
                deps[id(w)] = (w, "waw")
            for r in self.readers.get(k, ()):
                if id(r) not in deps:
                    deps[id(r)] = (r, "war")
        for d, kind in deps.values():
            if d is ins:
                continue
            if d.eng == eng and not d.isdma and not isdma:
                if kind != "raw" or eng == "pe":
                    continue
            d.need_inc = True
            ins.deps.append(d)
        for k in reads:
            self.readers.setdefault(k, []).append(ins)
        for k in writes:
            self.last_write[k] = ins
            self.readers[k] = []
        self.streams[eng].append(ins)
        if isdma:
            self.all_dma.append(ins)
        return ins

    def op(self, eng, fn, reads=(), writes=()):
        return self._add(eng, fn, reads, writes, False)

    def dma(self, out, in_, reads=(), writes=(), q="sp"):
        return self._add(q, lambda e: e.dma_start(out=out, in_=in_), reads, writes, True)

    def _selfcheck(self, csem, dsem):
        val = {}
        pos = {e: 0 for e in self.ENGS}
        dl = {e: [i for i in self.streams[e] if i.isdma] for e in self.ENGS}
        total = sum(len(v) for v in self.streams.values())
        done = 0
        while done < total:
            progress = False
            for e in self.ENGS:
                while pos[e] < len(self.streams[e]):
                    ins = self.streams[e][pos[e]]
                    waits = [d.tok for d in ins.deps]
                    if ins.isdma and ins.q >= self.NDSEM:
                        waits.append(dl[e][ins.q - self.NDSEM].tok)
                    if any(w is None for w in waits):
                        raise RuntimeError("dep without token")
                    if all(val.get(id(sm), 0) >= v for sm, v in waits):
                        if ins.need_inc:
                            val[id(ins.tok[0])] = val.get(id(ins.tok[0]), 0) + (16 if ins.isdma else 1)
                            if val[id(ins.tok[0])] != ins.tok[1]:
                                raise RuntimeError("token mismatch %s %s" % (val[id(ins.tok[0])], ins.tok[1]))
                        pos[e] += 1
                        done += 1
                        progress = True
                    else:
                        break
            if not progress:
                raise RuntimeError("deadlock in program: %s" % {e: (pos[e], len(self.streams[e])) for e in self.ENGS})

    def emit(self):
        nc = self.nc
        es = self.es
        csem = {e: es.enter_context(nc.semaphore("cs_" + e)) for e in self.ENGS}
        dsem = {e: [es.enter_context(nc.semaphore("ds_%s%d" % (e, i))) for i in range(self.NDSEM)]
                for e in self.ENGS if any(i.isdma for i in self.streams[e])}
        for e in self.ENGS:
            cnt = 0
            dcnt = 0
            for ins in self.streams[e]:
                if ins.isdma:
                    ins.tok = (dsem[e][dcnt % self.NDSEM], 16 * (dcnt // self.NDSEM + 1))
                    ins.q = dcnt
                    dcnt += 1
                elif ins.need_inc:
                    cnt += 1
                    ins.tok = (csem[e], cnt)
        streams = self.streams
        NDSEM = self.NDSEM
        self._selfcheck(csem, dsem)

        def run(ename, eng):
            seen = {}
            dlist = [i for i in streams[ename] if i.isdma]
            for ins in streams[ename]:
                waits = [d.tok for d in ins.deps]
                if ins.isdma and ins.q >= NDSEM:
                    waits.append(dlist[ins.q - NDSEM].tok)
                for sem, val in waits:
                    if seen.get(id(sem), 0) >= val:
                        continue
                    seen[id(sem)] = val
                    eng.wait_ge(sem, val)
                r = ins.fn(eng)
                if ins.need_inc:
                    r.then_inc(ins.tok[0], 16 if ins.isdma else 1)
            for ins in dlist[-NDSEM:]:
                sem, val = ins.tok
                if seen.get(id(sem), 0) >= val:
                    continue
                seen[id(sem)] = val
                eng.wait_ge(sem, val)

        with nc.Block() as block:
            @block.tensor
            def _(e):
                run("pe", e)

            @block.scalar
            def _(e):
                run("act", e)

            @block.vector
            def _(e):
                run("dve", e)

            @block.gpsimd
            def _(e):
                run("pool", e)

            @block.sync
            def _(e):
                run("sp", e)


def _sb(nc, es, name, shape, dt):
    return es.enter_context(nc.sbuf_tensor(name, list(shape), dt))


def _ps(nc, es, name, shape, dt):
    return es.enter_context(nc.psum_tensor(name, list(shape), dt))


def build_dense(NT, has_mix, has_mlp, tail, n_in_cols=0):
    nc = bass.Bass("TRN2", target_bir_lowering=False)
    GT = 4
    NG = NT // (128 * GT)
    x = nc.dram_tensor("x", [NT, D_MODEL], F32, kind="ExternalInput").ap()
    ident_d = nc.dram_tensor("ident", [128, 128], F32, kind="ExternalInput").ap()
    if has_mix:
        yT = nc.dram_tensor("yT", [D_MODEL, NT], F32, kind="ExternalInput").ap()
        w_out = nc.dram_tensor("w_out", [D_MODEL, D_MODEL], F32, kind="ExternalInput").ap()
    if has_mlp:
        nw_mlp = nc.dram_tensor("nw_mlp", [128, D_MODEL], F32, kind="ExternalInput").ap()
        w_up = nc.dram_tensor("w_up", [D_MODEL, D_FF], F32, kind="ExternalInput").ap()
        w_down = nc.dram_tensor("w_down", [D_FF, D_MODEL], F32, kind="ExternalInput").ap()
    nw_tail = nc.dram_tensor("nw_tail", [128, D_MODEL], F32, kind="ExternalInput").ap()
    if tail == "inproj":
        w_in = nc.dram_tensor("w_in", [D_MODEL, n_in_cols], F32, kind="ExternalInput").ap()
        p_out = nc.dram_tensor("p_out", [NT, n_in_cols], F32, kind="ExternalOutput").ap()
        if has_mix or has_mlp:
            x_out = nc.dram_tensor("x_out", [NT, D_MODEL], F32, kind="ExternalOutput").ap()
    else:
        out = nc.dram_tensor("out", [NT, D_MODEL], F32, kind="ExternalOutput").ap()

    with ExitStack() as es:
        P = Prog(nc, es)
        ident = _sb(nc, es, "ident_sb", [128, 128], F32)
        xg = [_sb(nc, es, "xg%d" % i, [128, GT, D_MODEL], F32) for i in range(2)]
        hn = [_sb(nc, es, "hn%d" % i, [128, D_MODEL], F32) for i in range(2)]
        hT = _sb(nc, es, "hT", [128, 8, 128 * GT], BF16)
        wst = [_sb(nc, es, "wst%d" % i, [128, 4096], F32) for i in range(2)]
        wbf = [_sb(nc, es, "wbf%d" % i, [128, 4096], BF16) for i in range(2)]
        nwt = _sb(nc, es, "nwt", [128, D_MODEL], F32)
        ss = _sb(nc, es, "ss", [128, 8], F32)
        sq_scr = _sb(nc, es, "sq_scr", [128, D_MODEL], F32)
        ot = [_sb(nc, es, "ot%d" % i, [128, 512], F32) for i in range(2)]
        if has_mlp:
            nwm = _sb(nc, es, "nwm", [128, D_MODEL], F32)
            aT = _sb(nc, es, "aT", [128, 32, 128 * GT], BF16)
            rl = [_sb(nc, es, "rl%d" % i, [128, 512], F32) for i in range(2)]
        if has_mix:
            yst = _sb(nc, es, "yst", [128, 8, 128 * GT], F32)
        ps = [_ps(nc, es, "ps%d" % i, [128, 512], F32) for i in range(8)]

        P.dma(ident[:], ident_d, writes=["ident"])
        P.dma(nwt[:], nw_tail, writes=["nwt"])
        if has_mlp:
            P.dma(nwm[:], nw_mlp, writes=["nwm"])

        cnt = {"w": 0, "n": 0, "ps": 0, "ot": 0, "rl": 0}

        def load_w(src_ap, shape3):
            i = cnt["w"] % 2
            cnt["w"] += 1
            a, b = shape3
            stv = wst[i][:, 0:a * b].rearrange("p (a b) -> p a b", a=a)
            bfv = wbf[i][:, 0:a * b].rearrange("p (a b) -> p a b", a=a)
            P.dma(stv, src_ap, writes=[("wst", i)])
            ceng = "dve" if (cnt["w"] % 2 == 0) else "act"
            if ceng == "dve":
                P.op("dve", lambda e: e.tensor_copy(out=wbf[i][:, 0:a * b], in_=wst[i][:, 0:a * b]),
                     reads=[("wst", i)], writes=[("wbf", i)])
            else:
                P.op("act", lambda e: e.copy(out=wbf[i][:, 0:a * b], in_=wst[i][:, 0:a * b]),
                     reads=[("wst", i)], writes=[("wbf", i)])
            return bfv, ("wbf", i)

        def kmajor(w_ap, c0, ncols):
            return w_ap.rearrange("(kc p) c -> p kc c", p=128)[:, :, c0:c0 + ncols]

        def norm_to_hT(xb, gkey, nw_tile, nwkey):
            for t in range(GT):
                j = cnt["n"] % 2
                cnt["n"] += 1
                col = cnt["n"] % 8
                P.op("act", lambda e, t=t, col=col: e.activation(
                    out=sq_scr[:], in_=xg[xb][:, t, :], func=AF.Square, accum_out=ss[:, col:col + 1]),
                    reads=[gkey], writes=["sq_scr", ("ss", col)])
                P.op("act", lambda e, col=col: e.activation(
                    out=ss[:, col:col + 1], in_=ss[:, col:col + 1], func=AF.Sqrt,
                    scale=1.0 / D_MODEL, bias=EPS), reads=[("ss", col)], writes=[("ss", col)])
                P.op("dve", lambda e, col=col: e.reciprocal(out=ss[:, col:col + 1], in_=ss[:, col:col + 1]),
                     reads=[("ss", col)], writes=[("ss", col)])
                P.op("dve", lambda e, t=t, j=j, col=col: e.scalar_tensor_tensor(
                    out=hn[j][:], in0=xg[xb][:, t, :], scalar=ss[:, col:col + 1], in1=nw_tile[:],
                    op0=ALU.mult, op1=ALU.mult), reads=[gkey, ("ss", col), nwkey], writes=[("hn", j)])
                for half in range(2):
                    b = cnt["ps"] % 8
                    cnt["ps"] += 1
                    for q in range(4):
                        kc = half * 4 + q
                        P.op("pe", lambda e, b=b, q=q, kc=kc, j=j: e.transpose(
                            out=ps[b][:, q * 128:(q + 1) * 128], in_=hn[j][:, kc * 128:(kc + 1) * 128],
                            identity=ident[:]), reads=[("hn", j), "ident"], writes=[("ps", b)])
                    P.op("dve", lambda e, b=b, half=half, t=t: e.tensor_copy(
                        out=hT[:, half * 4:half * 4 + 4, t * 128:(t + 1) * 128],
                        in_=ps[b][:].rearrange("p (q c) -> p q c", q=4)),
                        reads=[("ps", b)], writes=["hT"])

        def do_group(g):
            xb = g % 2
            gkey = ("xg", xb)
            tok0 = g * GT * 128
            P.dma(xg[xb][:], x[tok0:tok0 + GT * 128, :].rearrange("(t p) d -> p t d", p=128), writes=[gkey])

            if has_mix:
                P.dma(yst[:], yT.rearrange("(kc p) n -> p kc n", p=128)[:, :, tok0:tok0 + GT * 128],
                      writes=["yst"])
                P.op("pool", lambda e: e.tensor_copy(out=hT[:], in_=yst[:]), reads=["yst"], writes=["hT"])
                for c in range(2):
                    wv, wk = load_w(kmajor(w_out, c * 512, 512), (8, 512))
                    for t in range(GT):
                        b = cnt["ps"] % 8
                        cnt["ps"] += 1
                        for kc in range(8):
                            P.op("pe", lambda e, b=b, kc=kc, t=t, wv=wv: e.matmul(
                                ps[b][:], lhsT=hT[:, kc, t * 128:(t + 1) * 128], rhs=wv[:, kc, :],
                                start=(kc == 0), stop=(kc == 7)), reads=["hT", wk], writes=[("ps", b)])
                        P.op("dve", lambda e, b=b, t=t, c=c: e.tensor_tensor(
                            out=xg[xb][:, t, c * 512:(c + 1) * 512], in0=xg[xb][:, t, c * 512:(c + 1) * 512],
                            in1=ps[b][:], op=ALU.add), reads=[("ps", b), gkey], writes=[gkey])

            if has_mlp:
                norm_to_hT(xb, gkey, nwm, "nwm")
                for uc in range(8):
                    wv, wk = load_w(kmajor(w_up, uc * 512, 512), (8, 512))
                    for fi in range(4):
                        fc = uc * 4 + fi
                        b = cnt["ps"] % 8
                        cnt["ps"] += 1
                        for kc in range(8):
                            P.op("pe", lambda e, b=b, kc=kc, fi=fi, wv=wv: e.matmul(
                                ps[b][:], lhsT=wv[:, kc, fi * 128:(fi + 1) * 128], rhs=hT[:, kc, :],
                                start=(kc == 0), stop=(kc == 7)), reads=["hT", wk], writes=[("ps", b)])
                        r = cnt["rl"] % 2
                        cnt["rl"] += 1
                        P.op("act", lambda e, b=b, r=r: e.activation(out=rl[r][:], in_=ps[b][:], func=AF.Relu),
                             reads=[("ps", b)], writes=[("rl", r)])
                        P.op("dve", lambda e, r=r, fc=fc: e.tensor_tensor(
                            out=aT[:, fc, :], in0=rl[r][:], in1=rl[r][:], op=ALU.mult),
                            reads=[("rl", r)], writes=["aT"])
                for dc in range(8):
                    wv, wk = load_w(w_down.rearrange("(fc p) d -> p fc d", p=128)[:, dc * 4:dc * 4 + 4, :],
                                    (4, 1024))
                    for j in range(4):
                        fc = dc * 4 + j
                        for t in range(GT):
                            for half in range(2):
                                b = t * 2 + half
                                P.op("pe", lambda e, b=b, fc=fc, t=t, j=j, half=half, wv=wv: e.matmul(
                                    ps[b][:], lhsT=aT[:, fc, t * 128:(t + 1) * 128],
                                    rhs=wv[:, j, half * 512:(half + 1) * 512],
                                    start=(fc == 0), stop=(fc == 31)), reads=["aT", wk], writes=[("ps", b)])
                for t in range(GT):
                    for half in range(2):
                        b = t * 2 + half
                        P.op("dve", lambda e, b=b, t=t, half=half: e.tensor_tensor(
                            out=xg[xb][:, t, half * 512:(half + 1) * 512],
                            in0=xg[xb][:, t, half * 512:(half + 1) * 512], in1=ps[b][:], op=ALU.add),
                            reads=[("ps", b), gkey], writes=[gkey])
                cnt["ps"] = 0

            if tail == "inproj":
                if has_mix or has_mlp:
                    P.dma(x_out[tok0:tok0 + GT * 128, :].rearrange("(t p) d -> p t d", p=128), xg[xb][:],
                          reads=[gkey], q="act")
                norm_to_hT(xb, gkey, nwt, "nwt")
                c0 = 0
                while c0 < n_in_cols:
                    ncol = min(512, n_in_cols - c0)
                    wv, wk = load_w(kmajor(w_in, c0, ncol), (8, ncol))
                    for t in range(GT):
                        b = cnt["ps"] % 8
                        cnt["ps"] += 1
                        for kc in range(8):
                            P.op("pe", lambda e, b=b, kc=kc, t=t, wv=wv, ncol=ncol: e.matmul(
                                ps[b][:, 0:ncol], lhsT=hT[:, kc, t * 128:(t + 1) * 128], rhs=wv[:, kc, :],
                                start=(kc == 0), stop=(kc == 7)), reads=["hT", wk], writes=[("ps", b)])
                        o = cnt["ot"] % 2
                        cnt["ot"] += 1
                        P.op("act", lambda e, b=b, o=o, ncol=ncol: e.copy(out=ot[o][:, 0:ncol], in_=ps[b][:, 0:ncol]),
                             reads=[("ps", b)], writes=[("ot", o)])
                        P.dma(p_out[tok0 + t * 128:tok0 + (t + 1) * 128, c0:c0 + ncol], ot[o][:, 0:ncol],
                              reads=[("ot", o)], q="act")
                    c0 += ncol
            else:
                for t in range(GT):
                    col = cnt["n"] % 8
                    cnt["n"] += 1
                    P.op("act", lambda e, t=t, col=col: e.activation(
                        out=sq_scr[:], in_=xg[xb][:, t, :], func=AF.Square, accum_out=ss[:, col:col + 1]),
                        reads=[gkey], writes=["sq_scr", ("ss", col)])
                    P.op("act", lambda e, col=col: e.activation(
                        out=ss[:, col:col + 1], in_=ss[:, col:col + 1], func=AF.Sqrt,
                        scale=1.0 / D_MODEL, bias=EPS), reads=[("ss", col)], writes=[("ss", col)])
                    P.op("dve", lambda e, col=col: e.reciprocal(out=ss[:, col:col + 1], in_=ss[:, col:col + 1]),
                         reads=[("ss", col)], writes=[("ss", col)])
                    j = cnt["n"] % 2
                    P.op("dve", lambda e, t=t, j=j, col=col: e.scalar_tensor_tensor(
                        out=hn[j][:], in0=xg[xb][:, t, :], scalar=ss[:, col:col + 1], in1=nwt[:],
                        op0=ALU.mult, op1=ALU.mult), reads=[gkey, ("ss", col), "nwt"], writes=[("hn", j)])
                    P.dma(out[tok0 + t * 128:tok0 + (t + 1) * 128, :], hn[j][:], reads=[("hn", j)], q="act")
        for g in range(NG):
            do_group(g)
        P.emit()
    return nc


def gdn_consts():
    p = np.arange(128)
    same = (p[:, None] // 64) == (p[None, :] // 64)
    c = np.zeros((8, 128, 128), np.float32)
    c[0] = np.eye(128)
    c[1] = same & (p[:, None] <= p[None, :])
    c[2] = same
    c[3] = (p[:, None] < 64) * np.ones((1, 128))
    c[4] = (p[:, None] >= 64) * np.ones((1, 128))
    c[5] = 0.125 * (same & (p[:, None] <= p[None, :]))
    c[6] = same & (p[None, :] < p[:, None])
    c[7] = 1.0
    return np.ascontiguousarray(c.transpose(1, 0, 2))


def build_gdn(T):
    nc = bass.Bass("TRN2", target_bir_lowering=False)
    NTL = T // 128
    NCH = T // 64
    SEG = min(T, 2048)
    NSEG = T // SEG
    qkvT = nc.dram_tensor("qkvT", [3, 128, T], F32, kind="ExternalInput").ap()
    cw_d = nc.dram_tensor("cw", [128, 12], F32, kind="ExternalInput").ap()
    z_d = nc.dram_tensor("z_l", [64, NCH, 128], F32, kind="ExternalInput").ap()
    ab_d = nc.dram_tensor("ab", [128, 4, NTL], F32, kind="ExternalInput").ap()
    hp_d = nc.dram_tensor("hp", [128, 4], F32, kind="ExternalInput").ap()
    onw_d = nc.dram_tensor("onw", [64, 64], F32, kind="ExternalInput").ap()
    scT = nc.dram_tensor("scT", [3, 128, T], F32, kind="ExternalInput").ap()
    scw_d = nc.dram_tensor("scw", [128, 3], F32, kind="ExternalInput").ap()
    cst_d = nc.dram_tensor("cst", [128, 8, 128], F32, kind="ExternalInput").ap()
    ya_d = nc.dram_tensor("ya_l", [64, NCH, 128], F32, kind="ExternalOutput").ap()
    yb_d = nc.dram_tensor("ybT", [128, T], F32, kind="ExternalOutput").ap()

    with ExitStack() as es:
        P = Prog(nc, es)
        S = lambda name, shape, dt=F32: _sb(nc, es, name, shape, dt)
        cst = S("cst_sb", [128, 8, 128])
        ident, LTm, BO, SEL0, SEL1, MUs, ML, ONES = [cst[:, i, :] for i in range(8)]
        ident_t = S("ident_t", [128, 128])
        ident = ident_t[:]
        cw = S("cw_sb", [128, 12])
        hp = S("hp_sb", [128, 4])
        onw = S("onw_sb", [64, 64])
        scw = S("scw_sb", [128, 3])
        ab = S("ab_sb", [128, 4, NTL])
        fT = [S("fT%d" % s, [128, T]) for s in range(3)]
        raw = S("raw", [128, SEG + 3])
        raw2 = S("raw2", [128, SEG + 3])
        raw3 = S("raw3", [128, SEG + 3])
        acc = S("acc", [128, SEG])
        sqs = S("sqs", [128, SEG])
        rn = [S("rn%d" % i, [128, 512]) for i in range(2)]
        ps = [_ps(nc, es, "ps%d" % i, [128, 512], F32) for i in range(8)]

        P.dma(cst[:], cst_d, writes=["cst"])
        P.dma(ident_t[:], cst_d[:, 0, :], writes=["cst"])
        P.dma(cw[:], cw_d, writes=["cw"])
        P.dma(hp[:], hp_d, writes=["hp"])
        P.dma(onw[:], onw_d, writes=["onw"])
        P.dma(scw[:], scw_d, writes=["scw"])
        P.dma(ab[:], ab_d, writes=["ab"])

        def do_sc(sg):
            t0 = sg * SEG
            lo = 0 if sg > 0 else 2
            for i, (buf, key) in enumerate(((raw, "raw"), (raw2, "raw2"), (raw3, "raw3"))):
                if sg == 0:
                    P.op("pool", lambda e, buf=buf: e.memset(buf[:, 0:2], 0.0), writes=[key])
                P.dma(buf[:, lo:SEG + 2], scT[i][:, t0 - 2 + lo:t0 + SEG], writes=[key])
            P.op("pool", lambda e: e.tensor_tensor(out=raw2[:, 0:SEG + 2], in0=raw2[:, 0:SEG + 2],
                                                    in1=raw3[:, 0:SEG + 2], op=ALU.mult),
                 reads=["raw2", "raw3"], writes=["raw2"])
            P.op("dve", lambda e: e.tensor_scalar(out=acc[:], in0=raw2[:, 0:SEG], scalar1=scw[:, 0:1], scalar2=None,
                                                  op0=ALU.mult), reads=["raw2", "scw"], writes=["acc"])
            for j in (1, 2):
                P.op("dve", lambda e, j=j: e.scalar_tensor_tensor(out=acc[:], in0=raw2[:, j:j + SEG],
                                                                   scalar=scw[:, j:j + 1], in1=acc[:],
                                                                   op0=ALU.mult, op1=ALU.add),
                     reads=["raw2", "scw", "acc"], writes=["acc"])
            P.op("pool", lambda e: e.tensor_tensor(out=acc[:], in0=acc[:], in1=raw[:, 2:SEG + 2], op=ALU.mult),
                 reads=["acc", "raw"], writes=["acc"])
            P.dma(yb_d[:, t0:t0 + SEG], acc[:], reads=["acc"])

        for sg in range(NSEG):
            do_sc(sg)

        def do_p1(sg, s):
            t0 = sg * SEG
            lo = 0 if sg > 0 else 3
            if sg == 0:
                P.op("pool", lambda e: e.memset(raw[:, 0:3], 0.0), writes=["raw"])
            P.dma(raw[:, lo:SEG + 3], qkvT[s][:, t0 - 3 + lo:t0 + SEG], writes=["raw"])
            P.op("dve", lambda e: e.tensor_scalar(out=acc[:], in0=raw[:, 0:SEG], scalar1=cw[:, s * 4:s * 4 + 1],
                                                  scalar2=None, op0=ALU.mult), reads=["raw", "cw"], writes=["acc"])
            for j in (1, 2, 3):
                P.op("dve", lambda e, j=j: e.scalar_tensor_tensor(out=acc[:], in0=raw[:, j:j + SEG],
                                                                   scalar=cw[:, s * 4 + j:s * 4 + j + 1], in1=acc[:],
                                                                   op0=ALU.mult, op1=ALU.add),
                     reads=["raw", "cw", "acc"], writes=["acc"])
            dst = fT[s][:, t0:t0 + SEG]
            fkey = ("fT", s)
            P.op("act", lambda e: e.activation(out=dst, in_=acc[:], func=AF.Silu), reads=["acc"], writes=[fkey])
            if s < 2:
                P.op("act", lambda e: e.activation(out=sqs[:], in_=dst, func=AF.Square), reads=[fkey], writes=["sqs"])
                for blk in range(SEG // 512):
                    r = blk % 2
                    sl = slice(blk * 512, (blk + 1) * 512)
                    P.op("pe", lambda e, sl=sl: e.matmul(ps[7][:], lhsT=BO, rhs=sqs[:, sl], start=True, stop=True),
                         reads=["sqs", "cst"], writes=[("ps", 7)])
                    P.op("act", lambda e, r=r: e.activation(out=rn[r][:], in_=ps[7][:], func=AF.Sqrt, bias=EPS),
                         reads=[("ps", 7)], writes=[("rn", r)])
                    P.op("dve", lambda e, r=r: e.reciprocal(out=rn[r][:], in_=rn[r][:]),
                         reads=[("rn", r)], writes=[("rn", r)])
                    P.op("dve", lambda e, r=r, sl=sl: e.tensor_tensor(
                        out=fT[s][:, t0 + sl.start:t0 + sl.stop], in0=fT[s][:, t0 + sl.start:t0 + sl.stop],
                        in1=rn[r][:], op=ALU.mult), reads=[("rn", r), fkey], writes=[fkey])

        for sg in range(NSEG):
            for s in range(3):
                if _DBG.get("stop", 9) >= 2:
                    do_p1(sg, s)

        sc = {}
        for nm in ("g", "beta", "gc", "ngc", "egc", "bg", "dk", "egl0", "egl1"):
            sc[nm] = [S("sc_%s%d" % (nm, h), [128, NTL]) for h in range(2)]
        nea = S("nea", [128, 2])
        P.op("act", lambda e: e.activation(out=nea[:], in_=hp[:, 0:2], func=AF.Exp), reads=["hp"], writes=["nea"])
        P.op("dve", lambda e: e.tensor_scalar(out=nea[:], in0=nea[:], scalar1=-1.0, scalar2=None, op0=ALU.mult),
             reads=["nea"], writes=["nea"])

        def do_scal(h):
            k = lambda nm: ("sc", nm, h)
            g, beta = sc["g"][h], sc["beta"][h]
            P.op("act", lambda e: e.activation(out=g[:], in_=ab[:, h, :], func=AF.Exp, bias=hp[:, 2 + h:3 + h]),
                 reads=["ab", "hp"], writes=[k("g")])
            P.op("act", lambda e: e.activation(out=g[:], in_=g[:], func=AF.Ln, bias=1.0),
                 reads=[k("g")], writes=[k("g")])
            P.op("dve", lambda e: e.tensor_scalar(out=g[:], in0=g[:], scalar1=nea[:, h:h + 1], scalar2=None,
                                                  op0=ALU.mult), reads=[k("g"), "nea"], writes=[k("g")])
            P.op("act", lambda e: e.activation(out=beta[:], in_=ab[:, 2 + h, :], func=AF.Sigmoid),
                 reads=["ab"], writes=[k("beta")])
            P.op("pe", lambda e: e.matmul(ps[7][:, 0:NTL], lhsT=LTm, rhs=g[:], start=True, stop=True),
                 reads=[k("g"), "cst"], writes=[("ps", 7)])
            P.op("dve", lambda e: e.tensor_copy(out=sc["gc"][h][:], in_=ps[7][:, 0:NTL]),
                 reads=[("ps", 7)], writes=[k("gc")])
            P.op("dve", lambda e: e.tensor_scalar(out=sc["ngc"][h][:], in0=ps[7][:, 0:NTL], scalar1=-1.0, scalar2=None,
                                                  op0=ALU.mult), reads=[("ps", 7)], writes=[k("ngc")])
            P.op("act", lambda e: e.activation(out=sc["egc"][h][:], in_=ps[7][:, 0:NTL], func=AF.Exp),
                 reads=[("ps", 7)], writes=[k("egc")])
            P.op("dve", lambda e: e.tensor_tensor(out=sc["bg"][h][:], in0=sc["egc"][h][:], in1=beta[:], op=ALU.mult),
                 reads=[k("egc"), k("beta")], writes=[k("bg")])
            P.op("pe", lambda e: e.matmul(ps[7][:, 0:NTL], lhsT=BO, rhs=g[:], start=True, stop=True),
                 reads=[k("g"), "cst"], writes=[("ps", 7)])
            P.op("dve", lambda e: e.tensor_tensor(out=sc["dk"][h][:], in0=ps[7][:, 0:NTL], in1=sc["gc"][h][:],
                                                  op=ALU.subtract), reads=[("ps", 7), k("gc")], writes=[k("dk")])
            P.op("act", lambda e: e.activation(out=sc["dk"][h][:], in_=sc["dk"][h][:], func=AF.Exp),
                 reads=[k("dk")], writes=[k("dk")])
            for c, SEL in ((0, SEL0), (1, SEL1)):
                P.op("pe", lambda e, SEL=SEL: e.matmul(ps[7][:, 0:NTL], lhsT=SEL, rhs=g[:], start=True, stop=True),
                     reads=[k("g"), "cst"], writes=[("ps", 7)])
                P.op("act", lambda e, c=c: e.activation(out=sc["egl%d" % c][h][:], in_=ps[7][:, 0:NTL], func=AF.Exp),
                     reads=[("ps", 7)], writes=[k("egl%d" % c)])

        for h in range(2):
            if _DBG.get("stop", 9) >= 3:
                do_scal(h)

        NQ = 24
        qcnt = [0]

        def nq():
            i = qcnt[0] % NQ
            qcnt[0] += 1
            bk, qt = i % 6, i // 6
            return ps[bk][:, qt * 128:(qt + 1) * 128], ("bk", bk)

        NB = 2
        tmp = {}

        def T_(nm, shape, n=NB * 2):
            tmp[nm] = [S("t_%s%d" % (nm, i), shape) for i in range(n)]

        T_("ktok", [128, 128], NB)
        T_("vtok", [128, 128], NB)
        T_("qtok", [128, 128], NB)
        for nm in ("dg", "dsym", "du", "dl", "attnT", "M", "MT", "RT", "Pa", "PaT", "Pb", "PbT"):
            T_(nm, [128, 128])
        T_("vb", [128, 64]); T_("kbg", [128, 64]); T_("kd0", [128, 64]); T_("kd1", [128, 64])
        T_("u", [128, 64]); T_("wT", [64, 128]); T_("qgT", [64, 128]); T_("qg", [128, 64])
        T_("tabs", [128, 128])
        vnew = [S("vnew%d" % h, [128, 64]) for h in range(2)]
        Sst = [S("Sst%d" % h, [64, 64]) for h in range(2)]
        for h in range(2):
            P.op("pool", lambda e, h=h: e.memset(vnew[h][:], 0.0), writes=[("vnew", h)])
            P.op("pool", lambda e, h=h: e.memset(Sst[h][:], 0.0), writes=[("S", h)])
        osb = [S("osb%d" % i, [64, 4, 64]) for i in range(2)]
        osq = [S("osq%d" % i, [64, 4, 64]) for i in range(2)]
        oss = [S("oss%d" % i, [64, 4]) for i in range(2)]
        zt = [S("zt%d" % i, [64, 2, 128]) for i in range(2)]
        yt = [S("yt%d" % i, [64, 2, 128]) for i in range(2)]

        stt = {}

        def front(tt):
            tb = tt % NB
            c0 = tt * 128
            toks = {}
            for s, nm in ((0, "qtok"), (1, "ktok"), (2, "vtok")):
                pq, kq = nq()
                P.op("pe", lambda e, pq=pq, s=s: e.matmul(pq, lhsT=fT[s][:, c0:c0 + 128], rhs=ident, start=True, stop=True),
                     reads=[("fT", s), "cst"], writes=[kq])
                dstt = tmp[nm][tb]
                if _DBG.get('nocopy'):
                    continue
                P.op("dve", lambda e, pq=pq, dstt=dstt: e.tensor_copy(out=dstt[:], in_=pq), reads=[kq], writes=[(nm, tb)])
                toks[nm] = dstt
            P.dma(zt[tb][:], z_d[:, 2 * tt:2 * tt + 2, :], writes=[("zt", tb)])
            P.op("act", lambda e: e.activation(out=zt[tb][:], in_=zt[tb][:], func=AF.Silu),
                 reads=[("zt", tb)], writes=[("zt", tb)])

            st = {}
            stt[tt] = st

            def prep_head(h):
                ix = tb * 2 + h
                hb = 64 * h
                kk = lambda nm, ix=ix: (nm, ix)
                g = lambda nm, ix=ix: tmp[nm][ix]
                col = lambda nm, h=h: sc[nm][h][:, tt:tt + 1]
                kTh = fT[1][hb:hb + 64, c0:c0 + 128]
                qTh = fT[0][hb:hb + 64, c0:c0 + 128]
                P.op("dve", lambda e, g=g, col=col: e.tensor_scalar(out=g("dg")[:], in0=ident, scalar1=col("gc"),
                                                                     scalar2=None, op0=ALU.mult),
                     reads=["cst", ("sc", "gc", h)], writes=[kk("dg")])
                pR, kR = nq()
                P.op("pe", lambda e, pR=pR, g=g: e.matmul(pR, lhsT=ONES, rhs=g("dg")[:], start=True, stop=True),
                     reads=[kk("dg"), "cst"], writes=[kR])
                P.op("act", lambda e, pR=pR, g=g, col=col: e.activation(
                    out=g("tabs")[:], in_=pR, func=AF.Abs, bias=col("ngc")),
                    reads=[kR, ("sc", "ngc", h)], writes=[kk("tabs")])
                P.op("act", lambda e, g=g: e.activation(out=g("dsym")[:], in_=g("tabs")[:], func=AF.Exp, scale=-1.0),
                     reads=[kk("tabs")], writes=[kk("dsym")])
                P.op("pool", lambda e, g=g: e.tensor_tensor(out=g("du")[:], in0=g("dsym")[:], in1=MUs, op=ALU.mult),
                     reads=[kk("dsym"), "cst"], writes=[kk("du")])
                P.op("pool", lambda e, g=g: e.tensor_tensor(out=g("dl")[:], in0=g("dsym")[:], in1=ML, op=ALU.mult),
                     reads=[kk("dsym"), "cst"], writes=[kk("dl")])
                pKK, kKK = nq()
                P.op("pe", lambda e, pKK=pKK, kTh=kTh: e.matmul(pKK, lhsT=kTh, rhs=kTh, start=True, stop=True),
                     reads=[("fT", 1)], writes=[kKK])
                pQK, kQK = nq()
                P.op("pe", lambda e, pQK=pQK, kTh=kTh, qTh=qTh: e.matmul(pQK, lhsT=kTh, rhs=qTh, start=True, stop=True),
                     reads=[("fT", 1), ("fT", 0)], writes=[kQK])
                P.op("dve", lambda e, pQK=pQK, g=g: e.tensor_tensor(out=g("attnT")[:], in0=pQK, in1=g("du")[:],
                                                                     op=ALU.mult),
                     reads=[kQK, kk("du")], writes=[kk("attnT")])
                P.op("dve", lambda e, pKK=pKK, g=g, col=col: e.scalar_tensor_tensor(
                    out=g("M")[:], in0=pKK, scalar=col("beta"), in1=g("dl")[:], op0=ALU.mult, op1=ALU.mult),
                    reads=[kKK, kk("dl"), ("sc", "beta", h)], writes=[kk("M")])
                ktok, vtok, qtok = toks["ktok"], toks["vtok"], toks["qtok"]
                P.op("pool", lambda e, g=g, col=col, vtok=vtok: e.tensor_scalar(
                    out=g("vb")[:], in0=vtok[:, hb:hb + 64], scalar1=col("beta"), scalar2=None, op0=ALU.mult),
                    reads=[("vtok", tb), ("sc", "beta", h)], writes=[kk("vb")])
                P.op("pool", lambda e, g=g, col=col, ktok=ktok: e.tensor_scalar(
                    out=g("kbg")[:], in0=ktok[:, hb:hb + 64], scalar1=col("bg"), scalar2=None, op0=ALU.mult),
                    reads=[("ktok", tb), ("sc", "bg", h)], writes=[kk("kbg")])
                P.op("pool", lambda e, g=g, col=col, ktok=ktok: e.tensor_scalar(
                    out=g("kd0")[:], in0=ktok[:, hb:hb + 64], scalar1=col("dk"), scalar2=SEL0[:, 0:1],
                    op0=ALU.mult, op1=ALU.mult), reads=[("ktok", tb), ("sc", "dk", h), "cst"], writes=[kk("kd0")])
                P.op("pool", lambda e, g=g, col=col, ktok=ktok: e.tensor_scalar(
                    out=g("kd1")[:], in0=ktok[:, hb:hb + 64], scalar1=col("dk"), scalar2=SEL1[:, 0:1],
                    op0=ALU.mult, op1=ALU.mult), reads=[("ktok", tb), ("sc", "dk", h), "cst"], writes=[kk("kd1")])
                P.op("dve", lambda e, g=g, col=col, qtok=qtok: e.tensor_scalar(
                    out=g("qg")[:], in0=qtok[:, hb:hb + 64], scalar1=col("egc"), scalar2=0.125,
                    op0=ALU.mult, op1=ALU.mult), reads=[("qtok", tb), ("sc", "egc", h)], writes=[kk("qg")])
                pqg, kqg = nq()
                P.op("pe", lambda e, pqg=pqg, g=g: e.matmul(pqg[0:64, :], lhsT=g("qg")[:], rhs=ident, start=True, stop=True),
                     reads=[kk("qg"), "cst"], writes=[kqg])
                P.op("act", lambda e, pqg=pqg, g=g: e.copy(out=g("qgT")[:], in_=pqg[0:64, :]),
                     reads=[kqg], writes=[kk("qgT")])
                pMT, kMT = nq()
                P.op("pe", lambda e, pMT=pMT, g=g: e.matmul(pMT, lhsT=g("M")[:], rhs=ident, start=True, stop=True),
                     reads=[kk("M"), "cst"], writes=[kMT])
                P.op("act", lambda e, pMT=pMT, g=g: e.copy(out=g("MT")[:], in_=pMT), reads=[kMT], writes=[kk("MT")])
                P.op("dve", lambda e, pMT=pMT, g=g: e.tensor_tensor(out=g("RT")[:], in0=ident, in1=pMT, op=ALU.subtract),
                     reads=[kMT, "cst"], writes=[kk("RT")])
                st[h] = dict(ix=ix, P=("M", "MT"))

            for h in range(2):
                prep_head(h)

            yield
            for lvl in range(5):
                if lvl > 0:
                    yield
                last = lvl == 4
                nxt = ("Pa", "PaT") if lvl % 2 == 0 else ("Pb", "PbT")
                pend = {}
                for h in range(2):
                    ix = st[h]["ix"]
                    Pn, PTn = st[h]["P"]
                    Pk, PTk = tmp[Pn][ix], tmp[PTn][ix]
                    p1, k1 = nq()
                    P.op("pe", lambda e, p1=p1, Pk=Pk, PTk=PTk: e.matmul(p1, lhsT=PTk[:], rhs=Pk[:], start=True, stop=True),
                         reads=[(Pn, ix), (PTn, ix)], writes=[k1])
                    p2 = k2 = None
                    if not last:
                        p2, k2 = nq()
                        P.op("pe", lambda e, p2=p2, Pk=Pk, PTk=PTk: e.matmul(p2, lhsT=Pk[:], rhs=PTk[:], start=True, stop=True),
                             reads=[(Pn, ix), (PTn, ix)], writes=[k2])
                    pend[h] = (p1, k1, p2, k2)
                for h in range(2):
                    ix = st[h]["ix"]
                    p1, k1, p2, k2 = pend[h]
                    Pnew, PTnew = tmp[nxt[0]][ix], tmp[nxt[1]][ix]
                    P.op("act", lambda e, p1=p1, Pnew=Pnew: e.copy(out=Pnew[:], in_=p1), reads=[k1], writes=[(nxt[0], ix)])
                    if not last:
                        P.op("dve", lambda e, p2=p2, PTnew=PTnew: e.tensor_copy(out=PTnew[:], in_=p2),
                             reads=[k2], writes=[(nxt[1], ix)])
                for h in range(2):
                    ix = st[h]["ix"]
                    Pnew = tmp[nxt[0]][ix]
                    RT = tmp["RT"][ix]
                    p3, k3 = nq()
                    P.op("pe", lambda e, p3=p3, Pnew=Pnew, RT=RT: e.matmul(p3, lhsT=Pnew[:], rhs=RT[:], start=True, stop=True),
                         reads=[(nxt[0], ix), ("RT", ix)], writes=[k3])
                    P.op("dve", lambda e, p3=p3, RT=RT: e.tensor_tensor(out=RT[:], in0=RT[:], in1=p3, op=ALU.add),
                         reads=[k3, ("RT", ix)], writes=[("RT", ix)])
                    st[h]["P"] = nxt

            yield
            for h in range(2):
                ix = st[h]["ix"]
                RT = tmp["RT"][ix]
                pu, ku = nq()
                P.op("pe", lambda e, pu=pu, RT=RT, ix=ix: e.matmul(pu[:, 0:64], lhsT=RT[:], rhs=tmp["vb"][ix][:],
                                                                   start=True, stop=True),
                     reads=[("RT", ix), ("vb", ix)], writes=[ku])
                P.op("act", lambda e, pu=pu, ix=ix: e.copy(out=tmp["u"][ix][:], in_=pu[:, 0:64]),
                     reads=[ku], writes=[("u", ix)])
                pw, kw = nq()
                P.op("pe", lambda e, pw=pw, RT=RT, ix=ix: e.matmul(pw[0:64, :], lhsT=tmp["kbg"][ix][:], rhs=RT[:],
                                                                   start=True, stop=True),
                     reads=[("RT", ix), ("kbg", ix)], writes=[kw])
                P.op("dve", lambda e, pw=pw, ix=ix: e.tensor_copy(out=tmp["wT"][ix][:], in_=pw[0:64, :]),
                     reads=[kw], writes=[("wT", ix)])

            yield

        def back(tt):
            tb = tt % NB
            st = stt.pop(tt)
            ob = tt % 2
            for c in range(2):
                yield
                rs = slice(c * 64, (c + 1) * 64)
                for h in range(2):
                    ix = st[h]["ix"]
                    pws, kws = nq()
                    P.op("pe", lambda e, pws=pws, ix=ix, h=h: e.matmul(pws[:, 0:64], lhsT=tmp["wT"][ix][:], rhs=Sst[h][:],
                                                                       start=True, stop=True),
                         reads=[("wT", ix), ("S", h)], writes=[kws])
                    P.op("dve", lambda e, pws=pws, ix=ix, h=h, rs=rs: e.tensor_tensor(
                        out=vnew[h][rs, :], in0=tmp["u"][ix][rs, :], in1=pws[rs, 0:64], op=ALU.subtract),
                        reads=[kws, ("u", ix)], writes=[("vnew", h)])
                    po = ps[6 + h][0:64, c * 64:(c + 1) * 64]
                    ko = ("bk", 6 + h)
                    P.op("pe", lambda e, po=po, ix=ix, h=h, rs=rs: e.matmul(po, lhsT=tmp["qgT"][ix][:, rs], rhs=Sst[h][:],
                                                                            start=True, stop=False),
                         reads=[("qgT", ix), ("S", h)], writes=[ko])
                    P.op("pe", lambda e, po=po, ix=ix, h=h, rs=rs: e.matmul(po, lhsT=tmp["attnT"][ix][:, rs], rhs=vnew[h][:],
                                                                            start=False, stop=True),
                         reads=[("attnT", ix), ("vnew", h)], writes=[ko])
                    pS, kS = nq()
                    kdn = "kd%d" % c
                    P.op("pe", lambda e, pS=pS, ix=ix, h=h, kdn=kdn: e.matmul(pS[0:64, 0:64], lhsT=tmp[kdn][ix][:], rhs=vnew[h][:],
                                                                              start=True, stop=True),
                         reads=[(kdn, ix), ("vnew", h)], writes=[kS])
                    P.op("dve", lambda e, pS=pS, h=h, c=c: e.scalar_tensor_tensor(
                        out=Sst[h][:], in0=Sst[h][:], scalar=sc["egl%d" % c][h][0:64, tt:tt + 1], in1=pS[0:64, 0:64],
                        op0=ALU.mult, op1=ALU.add), reads=[kS, ("S", h), ("sc", "egl%d" % c, h)], writes=[("S", h)])
                    P.op("act", lambda e, po=po, c=c, h=h: e.copy(out=osb[ob][:, c * 2 + h, :], in_=po),
                         reads=[ko], writes=[("osb", ob)])
            yield
            P.op("pool", lambda e: e.tensor_tensor(out=osq[ob][:], in0=osb[ob][:], in1=osb[ob][:], op=ALU.mult),
                 reads=[("osb", ob)], writes=[("osq", ob)])
            P.op("dve", lambda e: e.tensor_reduce(out=oss[ob][:], in_=osq[ob][:], axis=AX.X, op=ALU.add),
                 reads=[("osq", ob)], writes=[("oss", ob)])
            P.op("act", lambda e: e.activation(out=oss[ob][:], in_=oss[ob][:], func=AF.Sqrt, scale=1.0 / 64, bias=EPS),
                 reads=[("oss", ob)], writes=[("oss", ob)])
            P.op("dve", lambda e: e.reciprocal(out=oss[ob][:], in_=oss[ob][:]), reads=[("oss", ob)], writes=[("oss", ob)])
            for c in range(2):
                for h in range(2):
                    P.op("dve", lambda e, c=c, h=h: e.scalar_tensor_tensor(
                        out=yt[ob][:, c, h * 64:(h + 1) * 64], in0=osb[ob][:, c * 2 + h, :],
                        scalar=oss[ob][:, c * 2 + h:c * 2 + h + 1], in1=onw[:], op0=ALU.mult, op1=ALU.mult),
                        reads=[("osb", ob), ("oss", ob), "onw"], writes=[("yt", ob)])
            P.op("pool", lambda e: e.tensor_tensor(out=yt[ob][:], in0=yt[ob][:], in1=zt[tb][:], op=ALU.mult),
                 reads=[("yt", ob), ("zt", tb)], writes=[("yt", ob)])
            P.dma(ya_d[:, 2 * tt:2 * tt + 2, :], yt[ob][:], reads=[("yt", ob)])

        def drive(gens):
            gens = list(gens)
            while gens:
                for g_ in list(gens):
                    try:
                        next(g_)
                    except StopIteration:
                        gens.remove(g_)

        drive([front(0)])
        for tt in range(NTL):
            drive([back(tt)] + ([front(tt + 1)] if tt + 1 < NTL else []))
        P.emit()
    return nc


def gdn_in_maps(P0, T, qkv_conv, a_log, dt_bias, o_norm, sc_conv):
    P0 = P0.reshape(2, T, -1)
    cst = gdn_consts()
    maps = []
    for c in range(NCORES):
        b, hg = c // 4, c % 4
        o = 128 * hg
        pb = P0[b]
        qkvT = np.stack([pb[:, s * 512 + o:s * 512 + o + 128].T for s in range(3)])
        cw = np.concatenate([qkv_conv[:, s * 512 + o:s * 512 + o + 128].T for s in range(3)], axis=1)
        abc = np.concatenate([pb[:, 2048 + 2 * hg:2048 + 2 * hg + 2], pb[:, 2056 + 2 * hg:2056 + 2 * hg + 2]], axis=1)
        ab = abc.reshape(T // 128, 128, 4).transpose(1, 2, 0)
        hp = np.tile(np.concatenate([a_log[2 * hg:2 * hg + 2], dt_bias[2 * hg:2 * hg + 2]])[None], (128, 1))
        z_l = pb[:, 1536 + o:1536 + o + 128].reshape(T // 64, 64, 128).transpose(1, 0, 2)
        scT = np.stack([pb[:, 2064 + s * 512 + o:2064 + s * 512 + o + 128].T for s in range(3)])
        scw = sc_conv[:, o:o + 128].T
        maps.append(dict(qkvT=np.ascontiguousarray(qkvT, np.float32), cw=np.ascontiguousarray(cw, np.float32),
                         z_l=np.ascontiguousarray(z_l, np.float32), ab=np.ascontiguousarray(ab, np.float32),
                         hp=np.ascontiguousarray(hp, np.float32), onw=np.ascontiguousarray(np.tile(o_norm[None], (64, 1)), np.float32),
                         scT=np.ascontiguousarray(scT, np.float32), scw=np.ascontiguousarray(scw, np.float32), cst=cst))
    return maps


def gdn_gather(results, T):
    y = np.zeros((2, T, 1024), np.float32)
    for c in range(NCORES):
        b, hg = c // 4, c % 4
        o = 128 * hg
        y[b, :, o:o + 128] = results[c]["ya_l"].transpose(1, 0, 2).reshape(T, 128)
        y[b, :, 512 + o:512 + o + 128] = results[c]["ybT"].T
    return y.reshape(2 * T, 1024)


def nsa_consts(T):
    NTL = T // 128
    p = np.arange(128)
    half = 8
    inv_freq = (500000.0 ** (-(np.arange(0, 16, 2, dtype=np.float32)) / np.float32(16))).astype(np.float32)
    ang = np.arange(T, dtype=np.float32)[:, None] * inv_freq[None, :]
    cos, sin = np.cos(ang).astype(np.float32), np.sin(ang).astype(np.float32)
    C = np.ones((64, T), np.float32)
    C[0:8] = cos.T
    C[8:16] = cos.T
    Sg = np.zeros((16, T), np.float32)
    Sg[0:8] = -sin.T
    Sg[8:16] = sin.T
    C4 = np.ascontiguousarray(np.broadcast_to(C[:, None, :], (64, 4, T)))
    S4 = np.ascontiguousarray(np.broadcast_to(Sg[:, None, :], (16, 4, T)))
    key = np.arange(T)
    Eg = np.zeros((128, T), np.float32)
    Eg[64:128] = (np.arange(64)[:, None] == ((key[None, :] // 64) % 64))
    PM = np.zeros((128, 17, 128), np.float32)
    for r in range(17):
        PM[:, r, :] = (16 * p[:, None] + 31) <= (p[None, :] + 128 * r)
    CM = (p[:, None] <= p[None, :]).astype(np.float32)
    AM = (p[:, None] > p[None, :]).astype(np.float32)
    msk = np.ascontiguousarray(np.concatenate([PM, CM[:, None, :], AM[:, None, :], np.eye(128, dtype=np.float32)[:, None, :]], axis=1))
    c = np.arange(512)
    s = np.arange(128)
    ov = ((16 * c[:, None] < 64 * s[None, :] + 64) & (16 * c[:, None] + 32 > 64 * s[None, :])).astype(np.float32)
    ov[511] = 0.0
    ov = np.ascontiguousarray(ov.reshape(4, 128, 128).transpose(1, 0, 2))
    NQB = T // 128
    bq = np.zeros((NQB, 128, 128), np.float32)
    for qb in range(NQB):
        cur = 2 * qb + (p >= 64)
        js = s[None, :]
        forced = (js == 0) | (js == cur[:, None]) | (js == cur[:, None] - 1)
        bq[qb] = np.where(js > cur[:, None], -100.0, np.where(forced, 100.0, 0.0))
    return dict(C4=C4, S4=S4, Eg=Eg, msk=msk, ov=ov, bq=bq)


def build_nsa(T):
    nc = bass.Bass("TRN2", target_bir_lowering=False)
    NTL = T // 128
    NQB = NTL
    SEG = min(T, 2048)
    NSEG = T // SEG
    D = lambda name, shape: nc.dram_tensor(name, list(shape), F32, kind="ExternalInput").ap()
    qT_d = D("qT", [64, 4, T]); qsw_d = D("qsw", [16, 4, T])
    kT_d = D("kT", [2, 64, T]); ksw_d = D("ksw", [2, 16, T])
    v_d = D("vtok", [2, 128, NTL, 64])
    KP_d = D("KP", [2, 128, T // 2])
    posP_d = D("posP", [128, 16])
    w1_d = D("w1", [2, 128, 16, 256]); w2_d = D("w2", [2, 128, 2, 64])
    gat_d = D("gates", [128, NTL, 12])
    C4_d = D("C4", [64, 4, T]); S4_d = D("S4", [16, 4, T])
    Eg_d = D("Eg", [128, T]); msk_d = D("msk", [128, 20, 128]); ov_d = D("ov", [128, 4, 128])
    bq_d = D("bq", [NQB, 128, 128])
    o_d = nc.dram_tensor("o_out", [T, 256], F32, kind="ExternalOutput").ap()

    with ExitStack() as es:
        P = Prog(nc, es)
        S = lambda name, shape, dt=F32: _sb(nc, es, name, shape, dt)
        msk = S("msk_sb", [128, 20, 128])
        CM, AM, ident = msk[:, 17, :], msk[:, 18, :], msk[:, 19, :]
        KT = [S("KTb%d" % i, [128 if i == 0 else 64, T], BF16) for i in range(2)]
        NGRP = (2 * NTL + 63) // 64
        Vb = [S("Vb%d" % i, [128, NTL, 65], BF16) for i in range(2)]
        kcT = S("kcT", [64, 512], BF16)
        Vc = S("Vc", [128, 4, 193], BF16)
        gs = S("gs", [128, NTL, 12])
        posP = S("posP_sb", [128, 16])
        stg = [S("stg%d" % i, [128, 4096]) for i in range(2)]
        stg16 = [S("stg16_%d" % i, [16, SEG]) for i in range(3)]
        KPp = S("KPp", [128, 16, 512], BF16)
        w1b = S("w1b", [128, 16, 256], BF16)
        w2b = S("w2b", [128, 2, 64], BF16)
        HT = S("HT", [128, 2, 512], BF16)
        ps = [_ps(nc, es, "ps%d" % i, [128, 512], F32) for i in range(8)]
        bk = lambda i: ("bk", i)

        P.dma(msk[:], msk_d, writes=["msk"])
        P.dma(posP[:], posP_d, writes=["posP"])
        for sg in range(NSEG):
            sv = stg[sg % 2][64:128, 0:SEG]
            P.dma(sv, Eg_d[64:128, sg * SEG:(sg + 1) * SEG], writes=[("stg", sg % 2)])
            P.op("pool", lambda e, sv=sv, sg=sg: e.tensor_copy(out=KT[0][64:128, sg * SEG:(sg + 1) * SEG], in_=sv),
                 reads=[("stg", sg % 2)], writes=[("KTe", 0)])
        P.dma(gs[:], gat_d, writes=["gs"])
        P.op("act", lambda e: e.activation(out=gs[:], in_=gs[:], func=AF.Sigmoid), reads=["gs"], writes=["gs"])

        def do_k(i, sg):
            t0 = sg * SEG
            a, b = stg[0][0:64, 0:SEG], stg[1][0:64, 0:SEG]
            P.dma(a, kT_d[i][:, t0:t0 + SEG], writes=[("stg", 0)])
            P.dma(b, C4_d[:, 0, t0:t0 + SEG], writes=[("stg", 1)])
            P.dma(stg16[0][:], ksw_d[i][:, t0:t0 + SEG], writes=[("s16", 0)])
            P.dma(stg16[1][:], S4_d[:, 0, t0:t0 + SEG], writes=[("s16", 1)])
            P.op("dve", lambda e: e.tensor_tensor(out=KT[i][0:64, t0:t0 + SEG], in0=a, in1=b, op=ALU.mult),
                 reads=[("stg", 0), ("stg", 1)], writes=[("KT", i)])
            P.op("pool", lambda e: e.tensor_tensor(out=stg16[0][:], in0=stg16[0][:], in1=stg16[1][:], op=ALU.mult),
                 reads=[("s16", 0), ("s16", 1)], writes=[("s16", 0)])
            P.op("pool", lambda e: e.tensor_tensor(out=stg16[2][:], in0=a[0:16, :], in1=b[0:16, :], op=ALU.mult),
                 reads=[("stg", 0), ("stg", 1)], writes=[("s16", 2)])
            P.op("dve", lambda e: e.tensor_tensor(out=KT[i][0:16, t0:t0 + SEG], in0=stg16[0][:], in1=stg16[2][:], op=ALU.add),
                 reads=[("s16", 0), ("s16", 2), ("KT", i)], writes=[("KT", i)])

        for i in range(2):
            for sg in range(NSEG):
                do_k(i, sg)

        def do_v(i):
            sv = stg[i][:, 0:NTL * 64].rearrange("p (a b) -> p a b", a=NTL)
            P.dma(sv, v_d[i], writes=[("stg", i)])
            P.op("pool", lambda e: e.memset(Vb[i][:, :, 64:65], 1.0), writes=[("Vb", i)])
            P.op("dve", lambda e: e.tensor_copy(out=Vb[i][:, :, 0:64], in_=sv), reads=[("stg", i)], writes=[("Vb", i)])

        for i in range(2):
            do_v(i)

        P.op("pool", lambda e: e.memset(kcT[:], 0.0), writes=["kcT"])
        P.op("pool", lambda e: e.memset(Vc[:], 0.0), writes=["Vc"])

        def do_cmp(i):
            kp = stg[0][:, 0:T // 2]
            P.dma(kp, KP_d[i], writes=[("stg", 0)])
            kpv = kp.rearrange("p (c e) -> p c e", e=8)
            NCB = T // 16 - 1
            for lp in range(16):
                src = kpv[:, 0:NCB, lp] if lp < 8 else kpv[:, 1:NCB + 1, lp - 8]
                eng = "dve" if lp % 2 == 0 else "pool"
                P.op(eng, lambda e, src=src, lp=lp: e.tensor_scalar(out=KPp[:, lp, 0:NCB], in0=src, scalar1=posP[:, lp:lp + 1],
                                                                   scalar2=None, op0=ALU.add),
                     reads=[("stg", 0), "posP"], writes=["KPp"])
            w1v = stg[1][:, 0:4096].rearrange("p (a b) -> p a b", a=16)
            P.dma(w1v, w1_d[i], writes=[("stg", 1)])
            P.op("act", lambda e: e.copy(out=w1b[:], in_=w1v), reads=[("stg", 1)], writes=["w1b"])
            w2v = stg[0][:, 0:128].rearrange("p (a b) -> p a b", a=2)
            P.dma(w2v, w2_d[i], reads=["KPp"], writes=[("stg", 0)])
            P.op("act", lambda e: e.copy(out=w2b[:], in_=w2v), reads=[("stg", 0)], writes=["w2b"])
            for jc in range(2):
                for lp in range(16):
                    P.op("pe", lambda e, jc=jc, lp=lp: e.matmul(ps[jc][:, 0:NCB], lhsT=w1b[:, lp, jc * 128:(jc + 1) * 128],
                                                                rhs=KPp[:, lp, 0:NCB], start=(lp == 0), stop=(lp == 15)),
                         reads=["w1b", "KPp"], writes=[bk(jc)])
                P.op("act", lambda e, jc=jc: e.activation(out=HT[:, jc, 0:NCB], in_=ps[jc][:, 0:NCB], func=AF.Silu),
                     reads=[bk(jc)], writes=["HT"])
            if i == 0:
                for jc in range(2):
                    P.op("pe", lambda e, jc=jc: e.matmul(ps[2][0:64, 0:NCB], lhsT=w2b[:, jc, :], rhs=HT[:, jc, 0:NCB],
                                                         start=(jc == 0), stop=(jc == 1)), reads=["w2b", "HT"], writes=[bk(2)])
                P.op("act", lambda e: e.copy(out=kcT[:, 0:NCB], in_=ps[2][0:64, 0:NCB]), reads=[bk(2)], writes=["kcT"])
            else:
                for ct in range((NCB + 127) // 128):
                    n = min(128, NCB - ct * 128)
                    for jc in range(2):
                        P.op("pe", lambda e, jc=jc, ct=ct, n=n: e.matmul(ps[3][0:n, 0:64], lhsT=HT[:, jc, ct * 128:ct * 128 + n],
                                                                         rhs=w2b[:, jc, :], start=(jc == 0), stop=(jc == 1)),
                             reads=["w2b", "HT"], writes=[bk(3)])
                    P.op("act", lambda e, ct=ct, n=n: e.copy(out=Vc[0:n, ct, 0:64], in_=ps[3][0:n, 0:64]),
                         reads=[bk(3)], writes=["Vc"])
                    P.op("pool", lambda e, ct=ct, n=n: e.memset(Vc[0:n, ct, 64:65], 1.0), reads=["Vc"], writes=["Vc"])

        do_cmp(0)
        do_cmp(1)
        ovs = stg[1][:, 0:512].rearrange("p (a b) -> p a b", a=4)
        P.dma(ovs, ov_d, reads=["w1b"], writes=[("stg", 1)])
        P.op("dve", lambda e: e.tensor_copy(out=Vc[:, :, 65:193], in_=ovs), reads=[("stg", 1), "Vc"], writes=["Vc"])

        qf = [S("qf%d" % i, [64, 4, 128]) for i in range(2)]
        c4 = [S("c4_%d" % i, [64, 4, 128]) for i in range(2)]
        qs = [S("qs%d" % i, [16, 4, 128]) for i in range(2)]
        s4 = [S("s4_%d" % i, [16, 4, 128]) for i in range(2)]
        v16 = [S("v16_%d" % i, [16, 4, 128]) for i in range(2)]
        Qn = [S("Qn%d" % i, [64, 512], BF16) for i in range(2)]
        QA = [[S("QA%d_%d" % (i, g_), [128, 512], BF16) for g_ in range(NGRP)] for i in range(2)]
        Qr = [QA[i][0][0:64, :] for i in range(2)]
        nselsw = S("nselsw", [128, 128])
        oT = [S("oT%d" % i, [65, 512]) for i in range(2)]
        bqs = [S("bqs%d" % i, [128, 128]) for i in range(2)]
        Pc = S("Pc", [128, 4, 512], BF16)
        NPB = 4
        Pb = [S("Pb%d" % i, [128, 512], BF16) for i in range(NPB)]
        ocmp = S("ocmp", [128, 4, 193])
        ow = S("ow", [128, 4, 65])
        osl = S("osl", [128, 4, 65])
        rd = S("rd", [128, 12])
        impm = S("impm", [128, 128])
        imp2 = S("imp2", [128, 128])
        m8 = S("m8", [128, 16])
        nsel = S("nsel", [128, 128])
        osum = [S("osum%d" % i, [128, 4, 64]) for i in range(2)]
        pcnt = [0]
        scnt = [0]

        def prep_qb(qb):
            j = qb % 2
            q0 = qb * 128
            P.dma(qf[j][:], qT_d[:, :, q0:q0 + 128], writes=[("qf", j)])
            P.dma(c4[j][:], C4_d[:, :, q0:q0 + 128], writes=[("c4", j)])
            P.dma(qs[j][:], qsw_d[:, :, q0:q0 + 128], writes=[("qs", j)])
            P.dma(s4[j][:], S4_d[:, :, q0:q0 + 128], writes=[("s4", j)])
            P.dma(bqs[j][:], bq_d[qb], writes=[("bqs", j)])
            fl = lambda t: t[:].rearrange("p a b -> p (a b)")
            P.op("pool", lambda e: e.tensor_copy(out=Qn[j][:], in_=fl(qf[j])), reads=[("qf", j)], writes=[("Qn", j)])
            P.op("pool", lambda e: e.tensor_tensor(out=Qr[j], in0=fl(qf[j]), in1=fl(c4[j]), op=ALU.mult),
                 reads=[("qf", j), ("c4", j)], writes=[("Qr", j)])
            P.op("pool", lambda e: e.tensor_tensor(out=fl(qs[j]), in0=fl(qs[j]), in1=fl(s4[j]), op=ALU.mult),
                 reads=[("qs", j), ("s4", j)], writes=[("qs", j)])
            P.op("pool", lambda e: e.tensor_tensor(out=fl(v16[j]), in0=qf[j][0:16, :, :].rearrange("p a b -> p (a b)"),
                                                    in1=c4[j][0:16, :, :].rearrange("p a b -> p (a b)"), op=ALU.mult),
                 reads=[("qf", j), ("c4", j)], writes=[("v16", j)])
            P.op("pool", lambda e: e.tensor_tensor(out=QA[j][0][0:16, :], in0=fl(qs[j]), in1=fl(v16[j]), op=ALU.add),
                 reads=[("qs", j), ("v16", j), ("Qr", j)], writes=[("Qr", j)])
            ngrp = min(NGRP, (2 * qb + 1) // 64 + 1)
            for g_ in range(1, ngrp):
                P.op("pool", lambda e, g_=g_: e.tensor_copy(out=QA[j][g_][0:64, :], in_=QA[j][0][0:64, :]),
                     reads=[("Qr", j)], writes=[("QAq", j, g_)])

        def do_qb(qb):
            j = qb % 2
            q0 = qb * 128
            ngrp = min(NGRP, (2 * qb + 1) // 64 + 1)
            cts = [ct for ct in range(4) if qb - 16 * ct >= 0 and ct * 128 < T // 16 - 1]
            for ct in cts:
                r = qb - 16 * ct
                sb_ = (0, 1, 6, 7)[scnt[0] % 4]
                scnt[0] += 1
                P.op("pe", lambda e, ct=ct, sb_=sb_: e.matmul(ps[sb_][:], lhsT=kcT[:, ct * 128:(ct + 1) * 128], rhs=Qn[j][:],
                                                              start=True, stop=True), reads=["kcT", ("Qn", j)], writes=[bk(sb_)])
                P.op("act", lambda e, ct=ct, sb_=sb_: e.activation(out=Pc[:, ct, :], in_=ps[sb_][:], func=AF.Exp, scale=0.125),
                     reads=[bk(sb_)], writes=[("Pc", ct)])
                if r <= 16:
                    for hh in range(4):
                        P.op("dve", lambda e, ct=ct, hh=hh, r=r: e.tensor_tensor(
                            out=Pc[:, ct, hh * 128:(hh + 1) * 128], in0=Pc[:, ct, hh * 128:(hh + 1) * 128],
                            in1=msk[:, r, :], op=ALU.mult), reads=[("Pc", ct), "msk"], writes=[("Pc", ct)])
            for hh in range(4):
                bank = 2 + hh // 2
                off = (hh % 2) * 193
                for n_, ct in enumerate(cts):
                    P.op("pe", lambda e, hh=hh, ct=ct, bank=bank, off=off, n_=n_: e.matmul(
                        ps[bank][:, off:off + 193], lhsT=Pc[:, ct, hh * 128:(hh + 1) * 128], rhs=Vc[:, ct, :],
                        start=(n_ == 0), stop=(n_ == len(cts) - 1)), reads=[("Pc", ct), "Vc"], writes=[bk(bank)])
            for half in range(2):
                P.op("act", lambda e, half=half: e.copy(
                    out=ocmp[:, 2 * half:2 * half + 2, :].rearrange("p a b -> p (a b)"), in_=ps[2 + half][:, 0:386]),
                    reads=[bk(2 + half)], writes=["ocmp"])
            P.op("dve", lambda e: e.tensor_scalar(out=rd[:, 0:4], in0=ocmp[:, :, 64], scalar1=1e-30, scalar2=None, op0=ALU.add),
                 reads=["ocmp"], writes=["rd"])
            P.op("dve", lambda e: e.reciprocal(out=rd[:, 0:4], in_=rd[:, 0:4]), reads=["rd"], writes=["rd"])
            P.op("dve", lambda e: e.scalar_tensor_tensor(out=impm[:], in0=ocmp[:, 0, 65:193], scalar=rd[:, 0:1], in1=bqs[j][:],
                                                         op0=ALU.mult, op1=ALU.add), reads=["ocmp", "rd", ("bqs", j)], writes=["impm"])
            for hh in range(1, 4):
                P.op("dve", lambda e, hh=hh: e.scalar_tensor_tensor(out=impm[:], in0=ocmp[:, hh, 65:193], scalar=rd[:, hh:hh + 1],
                                                                    in1=impm[:], op0=ALU.mult, op1=ALU.add),
                     reads=["ocmp", "rd", "impm"], writes=["impm"])
            P.op("dve", lambda e: e.max(out=m8[:, 0:8], in_=impm[:]), reads=["impm"], writes=["m8"])
            P.op("dve", lambda e: e.match_replace(out=imp2[:], in_to_replace=m8[:, 0:8], in_values=impm[:], imm_value=-1e9),
                 reads=["impm", "m8"], writes=["imp2"])
            P.op("dve", lambda e: e.max(out=m8[:, 8:16], in_=imp2[:]), reads=["imp2"], writes=["m8"])
            P.op("dve", lambda e: e.tensor_scalar(out=nsel[:], in0=impm[:], scalar1=m8[:, 15:16], scalar2=1.0,
                                                  op0=ALU.is_ge, op1=ALU.subtract), reads=["impm", "m8"], writes=["nsel"])
            SB = (0, 1, 6, 7)

            def attn(kts, i, with_sel, odst, okey):
                obank = 5 if with_sel else 4
                ot_ = oT[1 if with_sel else 0]
                otk = ("oT", 1 if with_sel else 0)

                def score(kt):
                    sb_ = SB[scnt[0] % 4]
                    scnt[0] += 1
                    if with_sel:
                        g_ = kt // 32
                        P.op("pe", lambda e, kt=kt, sb_=sb_, g_=g_: e.matmul(ps[sb_][:], lhsT=KT[0][:, kt * 128:(kt + 1) * 128],
                                                                            rhs=QA[j][g_][:], start=True, stop=True),
                             reads=[("KT", 0), ("KTe", 0), ("Qr", j), ("QAq", j, g_), ("QAs", j, g_)], writes=[bk(sb_)])
                    else:
                        P.op("pe", lambda e, kt=kt, sb_=sb_: e.matmul(ps[sb_][:], lhsT=KT[1][:, kt * 128:(kt + 1) * 128], rhs=Qr[j],
                                                                      start=True, stop=True),
                             reads=[("KT", 1), ("Qr", j)], writes=[bk(sb_)])
                    pb = pcnt[0] % NPB
                    pcnt[0] += 1
                    P.op("act", lambda e, sb_=sb_, pb=pb: e.activation(out=Pb[pb][:], in_=ps[sb_][:], func=AF.Exp, scale=0.125),
                         reads=[bk(sb_)], writes=[("Pb", pb)])
                    mk = None
                    if kt == qb:
                        mk = CM
                    elif (not with_sel) and kt == qb - 4:
                        mk = AM
                    if mk is not None:
                        for hh in range(4):
                            P.op("dve", lambda e, hh=hh, pb=pb, mk=mk: e.tensor_tensor(
                                out=Pb[pb][:, hh * 128:(hh + 1) * 128], in0=Pb[pb][:, hh * 128:(hh + 1) * 128], in1=mk, op=ALU.mult),
                                reads=[("Pb", pb), "msk"], writes=[("Pb", pb)])
                    return pb

                def pv(n_, kt, pb):
                    P.op("pe", lambda e, pb=pb, kt=kt, n_=n_: e.matmul(
                        ps[obank][0:65, :], lhsT=Vb[i][:, kt, :], rhs=Pb[pb][:],
                        start=(n_ == 0), stop=(n_ == len(kts) - 1)), reads=[("Pb", pb), ("Vb", i)], writes=[bk(obank)])

                DEPTH = 2
                pbs = {}
                for n_ in range(min(DEPTH, len(kts))):
                    pbs[n_] = score(kts[n_])
                for n_, kt in enumerate(kts):
                    if n_ + DEPTH < len(kts):
                        pbs[n_ + DEPTH] = score(kts[n_ + DEPTH])
                    pv(n_, kt, pbs.pop(n_))
                P.op("act", lambda e: e.copy(out=ot_[:], in_=ps[obank][0:65, :]), reads=[bk(obank)], writes=[otk])
                for hh in range(4):
                    P.op("pe", lambda e, hh=hh: e.transpose(out=ps[obank][:, hh * 65:(hh + 1) * 65], in_=ot_[:, hh * 128:(hh + 1) * 128],
                                                            identity=ident[0:65, 0:65]), reads=[otk, "msk"], writes=[bk(obank)])
                P.op("act", lambda e: e.copy(out=odst[:].rearrange("p a b -> p (a b)"), in_=ps[obank][:, 0:260]),
                     reads=[bk(obank)], writes=[okey])

            attn(list(range(max(0, qb - 4), qb + 1)), 1, False, ow, "ow")
            for g_ in range(ngrp):
                if g_ == 0:
                    P.op("pool", lambda e: e.tensor_copy(out=nselsw[:, 64:128], in_=nsel[:, 0:64]), reads=["nsel"], writes=["nselsw"])
                    src, skey = nselsw, "nselsw"
                else:
                    src, skey = nsel, "nsel"
                P.op("pe", lambda e, src=src: e.transpose(out=ps[4][:, 0:128], in_=src[:], identity=ident),
                     reads=[skey, "msk"], writes=[bk(4)])
                for hh in range(4):
                    P.op("act", lambda e, hh=hh, g_=g_: e.activation(out=QA[j][g_][64:128, hh * 128:(hh + 1) * 128],
                                                                    in_=ps[4][64:128, 0:128], func=AF.Copy, scale=30000.0),
                         reads=[bk(4)], writes=[("QAs", j, g_)])

            attn(list(range(0, qb + 1)), 0, True, osl, "osl")

            ob = osum[j]
            for x, (src, key) in enumerate(((ocmp, "ocmp"), (osl, "osl"), (ow, "ow"))):
                if x > 0:
                    P.op("dve", lambda e, src=src, x=x: e.reciprocal(out=rd[:, 4 * x:4 * x + 4], in_=src[:, :, 64]),
                         reads=[key], writes=["rd"])
                P.op("dve", lambda e, x=x: e.tensor_tensor(
                    out=rd[:, 4 * x:4 * x + 4], in0=rd[:, 4 * x:4 * x + 4],
                    in1=gs[:, qb, :].rearrange("p (h x) -> p h x", x=3)[:, :, x], op=ALU.mult), reads=["rd", "gs"], writes=["rd"])
                for hh in range(4):
                    if x == 0:
                        P.op("dve", lambda e, hh=hh, src=src, x=x: e.tensor_scalar(
                            out=ob[:, hh, :], in0=src[:, hh, 0:64], scalar1=rd[:, 4 * x + hh:4 * x + hh + 1], scalar2=None,
                            op0=ALU.mult), reads=[key, "rd"], writes=[("osum", j)])
                    else:
                        P.op("dve", lambda e, hh=hh, src=src, x=x: e.scalar_tensor_tensor(
                            out=ob[:, hh, :], in0=src[:, hh, 0:64], scalar=rd[:, 4 * x + hh:4 * x + hh + 1], in1=ob[:, hh, :],
                            op0=ALU.mult, op1=ALU.add), reads=[key, "rd", ("osum", j)], writes=[("osum", j)])
            P.dma(o_d[q0:q0 + 128, :], ob[:].rearrange("p a b -> p (a b)"), reads=[("osum", j)])

        prep_qb(0)
        for qb in range(NQB):
            if qb + 1 < NQB:
                prep_qb(qb + 1)
            do_qb(qb)
        P.emit()
    return nc


def nsa_in_maps(P1, T, cmp_pos, k_w1, k_w2, v_w1, v_w2):
    P1 = P1.reshape(2, T, -1)
    cst = nsa_consts(T)
    NTL = T // 128
    maps = []
    posP = np.zeros((128, 16), np.float32)
    for lp in range(16):
        posP[0:64, lp] = cmp_pos[2 * lp]
        posP[64:128, lp] = cmp_pos[2 * lp + 1]
    w1 = np.stack([w.reshape(16, 128, 256).transpose(1, 0, 2) for w in (k_w1, v_w1)])
    w2 = np.stack([w.reshape(2, 128, 64).transpose(1, 0, 2) for w in (k_w2, v_w2)])
    swap = np.concatenate([np.arange(8, 16), np.arange(0, 8)])
    for c in range(NCORES):
        b, g = c // 4, c % 4
        pb = P1[b]
        q = pb[:, 256 * g:256 * g + 256].reshape(T, 4, 64)
        qT = q.transpose(2, 1, 0)
        col = lambda i: pb[:, 1024 + 256 * i + 64 * g:1024 + 256 * i + 64 * g + 64]
        k_c, v_c, k_s, v_s, k_w, v_w = [col(i) for i in range(6)]
        kT = np.stack([k_s.T, k_w.T])
        vt = np.stack([v.reshape(NTL, 128, 64).transpose(1, 0, 2) for v in (v_s, v_w)])
        KP = np.stack([np.concatenate([a[0::2].T, a[1::2].T], axis=0) for a in (k_c, v_c)])
        gates = pb[:, 2560 + 12 * g:2560 + 12 * g + 12].reshape(NTL, 128, 12).transpose(1, 0, 2)
        f = lambda a: np.ascontiguousarray(a, np.float32)
        maps.append(dict(qT=f(qT), qsw=f(qT[swap]), kT=f(kT), ksw=f(kT[:, swap]), vtok=f(vt), KP=f(KP), posP=posP,
                         w1=f(w1), w2=f(w2), gates=f(gates), C4=cst["C4"], S4=cst["S4"], Eg=cst["Eg"], msk=cst["msk"],
                         ov=cst["ov"], bq=cst["bq"]))
    return maps


def nsa_gather(results, T):
    o = np.zeros((2, T, 1024), np.float32)
    for c in range(NCORES):
        b, g = c // 4, c % 4
        o[b, :, 256 * g:256 * g + 256] = results[c]["o_out"]
    return o.reshape(2 * T, 1024)


_T = 8192
_NT = 2048


def _run(nc, maps):
    res = run_bass_kernel_spmd(nc, maps, core_ids=list(range(NCORES)))
    return res.results


def kernel(x, mix_norm, mlp_norm, w_up, w_down, final_norm,
           ev_w_in, ev_qkv_conv, ev_a_log, ev_dt_bias, ev_o_norm, ev_sc_conv, ev_w_out,
           od_w_in, od_cmp_pos, od_cmp_k_w1, od_cmp_k_w2, od_cmp_v_w1, od_cmp_v_w2, od_w_out):
    f = lambda a: np.ascontiguousarray(np.asarray(a), np.float32)
    x = f(x).reshape(2 * _T, D_MODEL)
    ident = np.eye(128, dtype=np.float32)
    rep = lambda v: np.ascontiguousarray(np.tile(f(v)[None, :], (128, 1)))
    sh = lambda a, c: np.ascontiguousarray(a[c * _NT:(c + 1) * _NT])
    shT = lambda a, c: np.ascontiguousarray(a[c * _NT:(c + 1) * _NT].T)
    cat = lambda rs, k: np.concatenate([r[k] for r in rs], axis=0)

    nc = build_dense(_NT, False, False, "inproj", 3600)
    rs = _run(nc, [dict(x=sh(x, c), ident=ident, nw_tail=rep(mix_norm[0]), w_in=f(ev_w_in[0])) for c in range(NCORES)])
    P0 = cat(rs, "p_out")
    nc = build_gdn(_T)
    rs = _run(nc, gdn_in_maps(P0, _T, f(ev_qkv_conv[0]), f(ev_a_log[0]), f(ev_dt_bias[0]), f(ev_o_norm[0]), f(ev_sc_conv[0])))
    y0 = gdn_gather(rs, _T)
    nc = build_dense(_NT, True, True, "inproj", 2608)
    rs = _run(nc, [dict(x=sh(x, c), ident=ident, yT=shT(y0, c), w_out=f(ev_w_out[0]), nw_mlp=rep(mlp_norm[0]),
                        w_up=f(w_up[0]), w_down=f(w_down[0]), nw_tail=rep(mix_norm[1]), w_in=f(od_w_in[0]))
                   for c in range(NCORES)])
    x2 = cat(rs, "x_out")
    P1 = cat(rs, "p_out")
    nc = build_nsa(_T)
    rs = _run(nc, nsa_in_maps(P1, _T, f(od_cmp_pos[0]), f(od_cmp_k_w1[0]), f(od_cmp_k_w2[0]), f(od_cmp_v_w1[0]), f(od_cmp_v_w2[0])))
    o1 = nsa_gather(rs, _T)
    nc = build_dense(_NT, True, True, "final")
    rs = _run(nc, [dict(x=sh(x2, c), ident=ident, yT=shT(o1, c), w_out=f(od_w_out[0]), nw_mlp=rep(mlp_norm[1]),
                        w_up=f(w_up[1]), w_down=f(w_down[1]), nw_tail=rep(final_norm)) for c in range(NCORES)])
    out = cat(rs, "out")
    return out.reshape(2, _T, D_MODEL).astype(np.float32)
```

```python
import numpy as np
import ml_dtypes
from contextlib import ExitStack
import concourse.bass as bass
import concourse.mybir as mybir
from concourse.bass_utils import run_bass_kernel_spmd

F32 = mybir.dt.float32
BF16 = mybir.dt.bfloat16
AF = mybir.ActivationFunctionType
ALU = mybir.AluOpType
AX = mybir.AxisListType

NCORES = 8
D_MODEL = 1024
D_FF = 4096
EPS = 1e-6
_DBG = {}


class _Ins:
    __slots__ = ("eng", "fn", "deps", "isdma", "need_inc", "tok", "q")

    def __init__(self, eng, fn, isdma):
        self.eng = eng
        self.fn = fn
        self.deps = []
        self.isdma = isdma
        self.need_inc = isdma
        self.tok = None


class Prog:
    ENGS = ("pe", "act", "dve", "pool", "sp")
    NDSEM = 12

    def __init__(self, nc, es):
        self.nc = nc
        self.es = es
        self.streams = {e: [] for e in self.ENGS}
        self.last_write = {}
        self.readers = {}
        self.all_dma = []

    def _add(self, eng, fn, reads, writes, isdma):
        ins = _Ins(eng, fn, isdma)
        excl = [k for k in reads if isinstance(k, tuple) and k[0] in ("bk", "ps")]
        if excl:
            writes = list(writes) + [k for k in excl if k not in writes]
        deps = {}
        for k in reads:
            w = self.last_write.get(k)
            if w is not None:
                deps[id(w)] = (w, "raw")
        for k in writes:
            w = self.last_write.get(k)
            if w is not None and id(w) not in deps:
                deps[id(w)] = (w, "waw")
            for r in self.readers.get(k, ()):
                if id(r) not in deps:
                    deps[id(r)] = (r, "war")
        for d, kind in deps.values():
            if d is ins:
                continue
            if d.eng == eng and not d.isdma and not isdma:
                if eng != "pool" and (kind != "raw" or eng == "pe"):
                    continue
            d.need_inc = True
            ins.deps.append(d)
        for k in reads:
            self.readers.setdefault(k, []).append(ins)
        for k in writes:
            self.last_write[k] = ins
            self.readers[k] = []
        self.streams[eng].append(ins)
        if isdma:
            self.all_dma.append(ins)
        return ins

    def op(self, eng, fn, reads=(), writes=()):
        return self._add(eng, fn, reads, writes, False)

    def dma(self, out, in_, reads=(), writes=(), q="sp"):
        return self._add(q, lambda e: e.dma_start(out=out, in_=in_), reads, writes, True)

    def _selfcheck(self, csem, dsem):
        val = {}
        pos = {e: 0 for e in self.ENGS}
        dl = {e: [i for i in self.streams[e] if i.isdma] for e in self.ENGS}
        total = sum(len(v) for v in self.streams.values())
        done = 0
        while done < total:
            progress = False
            for e in self.ENGS:
                while pos[e] < len(self.streams[e]):
                    ins = self.streams[e][pos[e]]
                    waits = [d.tok for d in ins.deps]
                    if ins.isdma and ins.q >= self.NDSEM:
                        waits.append(dl[e][ins.q - self.NDSEM].tok)
                    if any(w is None for w in waits):
                        raise RuntimeError("dep without token")
                    if all(val.get(id(sm), 0) >= v for sm, v in waits):
                        if ins.need_inc:
                            val[id(ins.tok[0])] = val.get(id(ins.tok[0]), 0) + (16 if ins.isdma else 1)
                            if val[id(ins.tok[0])] != ins.tok[1]:
                                raise RuntimeError("token mismatch %s %s" % (val[id(ins.tok[0])], ins.tok[1]))
                        pos[e] += 1
                        done += 1
                        progress = True
                    else:
                        break
            if not progress:
                raise RuntimeError("deadlock in program: %s" % {e: (pos[e], len(self.streams[e])) for e in self.ENGS})

    def emit(self):
        nc = self.nc
        es = self.es
        csem = {e: es.enter_context(nc.semaphore("cs_" + e)) for e in self.ENGS}
        dsem = {e: [es.enter_context(nc.semaphore("ds_%s%d" % (e, i))) for i in range(self.NDSEM)]
                for e in self.ENGS if any(i.isdma for i in self.streams[e])}
        for e in self.ENGS:
            cnt = 0
            dcnt = 0
            for ins in self.streams[e]:
                if ins.isdma:
                    ins.tok = (dsem[e][dcnt % self.NDSEM], 16 * (dcnt // self.NDSEM + 1))
                    ins.q = dcnt
                    dcnt += 1
                elif ins.need_inc:
                    cnt += 1
                    ins.tok = (csem[e], cnt)
        streams = self.streams
        NDSEM = self.NDSEM
        self._selfcheck(csem, dsem)

        def run(ename, eng):
            seen = {}
            dlist = [i for i in streams[ename] if i.isdma]
            for ins in streams[ename]:
                waits = [d.tok for d in ins.deps]
                if ins.isdma and ins.q >= NDSEM:
                    waits.append(dlist[ins.q - NDSEM].tok)
                for sem, val in waits:
                    if seen.get(id(sem), 0) >= val:
                        continue
                    seen[id(sem)] = val
                    eng.wait_ge(sem, val)
                r = ins.fn(eng)
                if ins.need_inc:
                    r.then_inc(ins.tok[0], 16 if ins.isdma else 1)
            for ins in dlist[-NDSEM:]:
                sem, val = ins.tok
                if seen.get(id(sem), 0) >= val:
                    continue
                seen[id(sem)] = val
                eng.wait_ge(sem, val)

        with nc.Block() as block:
            @block.tensor
            def _(e):
                run("pe", e)

            @block.scalar
            def _(e):
                run("act", e)

            @block.vector
            def _(e):
                run("dve", e)

            @block.gpsimd
            def _(e):
                run("pool", e)

            @block.sync
            def _(e):
                run("sp", e)


def _sb(nc, es, name, shape, dt):
    return es.enter_context(nc.sbuf_tensor(name, list(shape), dt))


def _ps(nc, es, name, shape, dt):
    return es.enter_context(nc.psum_tensor(name, list(shape), dt))


def build_dense(NT, has_mix, has_mlp, tail, n_in_cols=0):
    nc = bass.Bass("TRN2", target_bir_lowering=False)
    GT = 4
    NG = NT // (128 * GT)
    x = nc.dram_tensor("x", [NT, D_MODEL], F32, kind="ExternalInput").ap()
    ident_d = nc.dram_tensor("ident", [128, 128], F32, kind="ExternalInput").ap()
    if has_mix:
        yT = nc.dram_tensor("yT", [D_MODEL, NT], F32, kind="ExternalInput").ap()
        w_out = nc.dram_tensor("w_out", [D_MODEL, D_MODEL], F32, kind="ExternalInput").ap()
    if has_mlp:
        nw_mlp = nc.dram_tensor("nw_mlp", [128, D_MODEL], F32, kind="ExternalInput").ap()
        w_up = nc.dram_tensor("w_up", [D_MODEL, D_FF], F32, kind="ExternalInput").ap()
        w_down = nc.dram_tensor("w_down", [D_FF, D_MODEL], F32, kind="ExternalInput").ap()
    nw_tail = nc.dram_tensor("nw_tail", [128, D_MODEL], F32, kind="ExternalInput").ap()
    if tail == "inproj":
        w_in = nc.dram_tensor("w_in", [D_MODEL, n_in_cols], F32, kind="ExternalInput").ap()
        p_out = nc.dram_tensor("p_out", [NT, n_in_cols], F32, kind="ExternalOutput").ap()
        if has_mix or has_mlp:
            x_out = nc.dram_tensor("x_out", [NT, D_MODEL], F32, kind="ExternalOutput").ap()
    else:
        out = nc.dram_tensor("out", [NT, D_MODEL], F32, kind="ExternalOutput").ap()

    with ExitStack() as es:
        P = Prog(nc, es)
        ident = _sb(nc, es, "ident_sb", [128, 128], F32)
        xg = [_sb(nc, es, "xg%d" % i, [128, GT, D_MODEL], F32) for i in range(2)]
        hn = [_sb(nc, es, "hn%d" % i, [128, D_MODEL], F32) for i in range(2)]
        hT = _sb(nc, es, "hT", [128, 8, 128 * GT], BF16)
        wst = [_sb(nc, es, "wst%d" % i, [128, 4096], F32) for i in range(2)]
        wbf = [_sb(nc, es, "wbf%d" % i, [128, 4096], BF16) for i in range(2)]
        nwt = _sb(nc, es, "nwt", [128, D_MODEL], F32)
        ss = _sb(nc, es, "ss", [128, 8], F32)
        sq_scr = _sb(nc, es, "sq_scr", [128, D_MODEL], F32)
        ot = [_sb(nc, es, "ot%d" % i, [128, 512], F32) for i in range(2)]
        if has_mlp:
            nwm = _sb(nc, es, "nwm", [128, D_MODEL], F32)
            aT = _sb(nc, es, "aT", [128, 32, 128 * GT], BF16)
            rl = [_sb(nc, es, "rl%d" % i, [128, 512], F32) for i in range(2)]
        if has_mix:
            yst = _sb(nc, es, "yst", [128, 8, 128 * GT], F32)
        ps = [_ps(nc, es, "ps%d" % i, [128, 512], F32) for i in range(8)]

        P.dma(ident[:], ident_d, writes=["ident"])
        P.dma(nwt[:], nw_tail, writes=["nwt"])
        if has_mlp:
            P.dma(nwm[:], nw_mlp, writes=["nwm"])

        cnt = {"w": 0, "n": 0, "ps": 0, "ot": 0, "rl": 0}

        def load_w(src_ap, shape3):
            i = cnt["w"] % 2
            cnt["w"] += 1
            a, b = shape3
            stv = wst[i][:, 0:a * b].rearrange("p (a b) -> p a b", a=a)
            bfv = wbf[i][:, 0:a * b].rearrange("p (a b) -> p a b", a=a)
            P.dma(stv, src_ap, writes=[("wst", i)])
            ceng = "dve" if (cnt["w"] % 2 == 0) else "act"
            if ceng == "dve":
                P.op("dve", lambda e: e.tensor_copy(out=wbf[i][:, 0:a * b], in_=wst[i][:, 0:a * b]),
                     reads=[("wst", i)], writes=[("wbf", i)])
            else:
                P.op("act", lambda e: e.copy(out=wbf[i][:, 0:a * b], in_=wst[i][:, 0:a * b]),
                     reads=[("wst", i)], writes=[("wbf", i)])
            return bfv, ("wbf", i)

        def kmajor(w_ap, c0, ncols):
            return w_ap.rearrange("(kc p) c -> p kc c", p=128)[:, :, c0:c0 + ncols]

        def norm_to_hT(xb, gkey, nw_tile, nwkey):
            for t in range(GT):
                j = cnt["n"] % 2
                cnt["n"] += 1
                col = cnt["n"] % 8
                P.op("act", lambda e, t=t, col=col: e.activation(
                    out=sq_scr[:], in_=xg[xb][:, t, :], func=AF.Square, accum_out=ss[:, col:col + 1]),
                    reads=[gkey], writes=["sq_scr", ("ss", col)])
                P.op("act", lambda e, col=col: e.activation(
                    out=ss[:, col:col + 1], in_=ss[:, col:col + 1], func=AF.Sqrt,
                    scale=1.0 / D_MODEL, bias=EPS), reads=[("ss", col)], writes=[("ss", col)])
                P.op("dve", lambda e, col=col: e.reciprocal(out=ss[:, col:col + 1], in_=ss[:, col:col + 1]),
                     reads=[("ss", col)], writes=[("ss", col)])
                P.op("dve", lambda e, t=t, j=j, col=col: e.scalar_tensor_tensor(
                    out=hn[j][:], in0=xg[xb][:, t, :], scalar=ss[:, col:col + 1], in1=nw_tile[:],
                    op0=ALU.mult, op1=ALU.mult), reads=[gkey, ("ss", col), nwkey], writes=[("hn", j)])
                for half in range(2):
                    b = cnt["ps"] % 8
                    cnt["ps"] += 1
                    for q in range(4):
                        kc = half * 4 + q
                        P.op("pe", lambda e, b=b, q=q, kc=kc, j=j: e.transpose(
                            out=ps[b][:, q * 128:(q + 1) * 128], in_=hn[j][:, kc * 128:(kc + 1) * 128],
                            identity=ident[:]), reads=[("hn", j), "ident"], writes=[("ps", b)])
                    P.op("dve", lambda e, b=b, half=half, t=t: e.tensor_copy(
                        out=hT[:, half * 4:half * 4 + 4, t * 128:(t + 1) * 128],
                        in_=ps[b][:].rearrange("p (q c) -> p q c", q=4)),
                        reads=[("ps", b)], writes=["hT"])

        def do_group(g):
            xb = g % 2
            gkey = ("xg", xb)
            tok0 = g * GT * 128
            P.dma(xg[xb][:], x[tok0:tok0 + GT * 128, :].rearrange("(t p) d -> p t d", p=128), writes=[gkey])

            if has_mix:
                P.dma(yst[:], yT.rearrange("(kc p) n -> p kc n", p=128)[:, :, tok0:tok0 + GT * 128],
                      writes=["yst"])
                P.op("pool", lambda e: e.tensor_copy(out=hT[:], in_=yst[:]), reads=["yst"], writes=["hT"])
                for c in range(2):
                    wv, wk = load_w(kmajor(w_out, c * 512, 512), (8, 512))
                    for t in range(GT):
                        b = cnt["ps"] % 8
                        cnt["ps"] += 1
                        for kc in range(8):
                            P.op("pe", lambda e, b=b, kc=kc, t=t, wv=wv: e.matmul(
                                ps[b][:], lhsT=hT[:, kc, t * 128:(t + 1) * 128], rhs=wv[:, kc, :],
                                start=(kc == 0), stop=(kc == 7)), reads=["hT", wk], writes=[("ps", b)])
                        P.op("dve", lambda e, b=b, t=t, c=c: e.tensor_tensor(
                            out=xg[xb][:, t, c * 512:(c + 1) * 512], in0=xg[xb][:, t, c * 512:(c + 1) * 512],
                            in1=ps[b][:], op=ALU.add), reads=[("ps", b), gkey], writes=[gkey])

            if has_mlp:
                norm_to_hT(xb, gkey, nwm, "nwm")
                for uc in range(8):
                    wv, wk = load_w(kmajor(w_up, uc * 512, 512), (8, 512))
                    for fi in range(4):
                        fc = uc * 4 + fi
                        b = cnt["ps"] % 8
                        cnt["ps"] += 1
                        for kc in range(8):
                            P.op("pe", lambda e, b=b, kc=kc, fi=fi, wv=wv: e.matmul(
                                ps[b][:], lhsT=wv[:, kc, fi * 128:(fi + 1) * 128], rhs=hT[:, kc, :],
                                start=(kc == 0), stop=(kc == 7)), reads=["hT", wk], writes=[("ps", b)])
                        r = cnt["rl"] % 2
                        cnt["rl"] += 1
                        P.op("act", lambda e, b=b, r=r: e.activation(out=rl[r][:], in_=ps[b][:], func=AF.Relu),
                             reads=[("ps", b)], writes=[("rl", r)])
                        P.op("dve", lambda e, r=r, fc=fc: e.tensor_tensor(
                            out=aT[:, fc, :], in0=rl[r][:], in1=rl[r][:], op=ALU.mult),
                            reads=[("rl", r)], writes=["aT"])
                for dc in range(8):
                    wv, wk = load_w(w_down.rearrange("(fc p) d -> p fc d", p=128)[:, dc * 4:dc * 4 + 4, :],
                                    (4, 1024))
                    for j in range(4):
                        fc = dc * 4 + j
                        for t in range(GT):
                            for half in range(2):
                                b = t * 2 + half
                                P.op("pe", lambda e, b=b, fc=fc, t=t, j=j, half=half, wv=wv: e.matmul(
                                    ps[b][:], lhsT=aT[:, fc, t * 128:(t + 1) * 128],
                                    rhs=wv[:, j, half * 512:(half + 1) * 512],
                                    start=(fc == 0), stop=(fc == 31)), reads=["aT", wk], writes=[("ps", b)])
                for t in range(GT):
                    for half in range(2):
                        b = t * 2 + half
                        P.op("dve", lambda e, b=b, t=t, half=half: e.tensor_tensor(
                            out=xg[xb][:, t, half * 512:(half + 1) * 512],
                            in0=xg[xb][:, t, half * 512:(half + 1) * 512], in1=ps[b][:], op=ALU.add),
                            reads=[("ps", b), gkey], writes=[gkey])
                cnt["ps"] = 0

            if tail == "inproj":
                if has_mix or has_mlp:
                    P.dma(x_out[tok0:tok0 + GT * 128, :].rearrange("(t p) d -> p t d", p=128), xg[xb][:],
                          reads=[gkey], q="act")
                norm_to_hT(xb, gkey, nwt, "nwt")
                c0 = 0
                while c0 < n_in_cols:
                    ncol = min(512, n_in_cols - c0)
                    wv, wk = load_w(kmajor(w_in, c0, ncol), (8, ncol))
                    for t in range(GT):
                        b = cnt["ps"] % 8
                        cnt["ps"] += 1
                        for kc in range(8):
                            P.op("pe", lambda e, b=b, kc=kc, t=t, wv=wv, ncol=ncol: e.matmul(
                                ps[b][:, 0:ncol], lhsT=hT[:, kc, t * 128:(t + 1) * 128], rhs=wv[:, kc, :],
                                start=(kc == 0), stop=(kc == 7)), reads=["hT", wk], writes=[("ps", b)])
                        o = cnt["ot"] % 2
                        cnt["ot"] += 1
                        P.op("act", lambda e, b=b, o=o, ncol=ncol: e.copy(out=ot[o][:, 0:ncol], in_=ps[b][:, 0:ncol]),
                             reads=[("ps", b)], writes=[("ot", o)])
                        P.dma(p_out[tok0 + t * 128:tok0 + (t + 1) * 128, c0:c0 + ncol], ot[o][:, 0:ncol],
                              reads=[("ot", o)], q="act")
                    c0 += ncol
            else:
                for t in range(GT):
                    col = cnt["n"] % 8
                    cnt["n"] += 1
                    P.op("act", lambda e, t=t, col=col: e.activation(
                        out=sq_scr[:], in_=xg[xb][:, t, :], func=AF.Square, accum_out=ss[:, col:col + 1]),
                        reads=[gkey], writes=["sq_scr", ("ss", col)])
                    P.op("act", lambda e, col=col: e.activation(
                        out=ss[:, col:col + 1], in_=ss[:, col:col + 1], func=AF.Sqrt,
                        scale=1.0 / D_MODEL, bias=EPS), reads=[("ss", col)], writes=[("ss", col)])
                    P.op("dve", lambda e, col=col: e.reciprocal(out=ss[:, col:col + 1], in_=ss[:, col:col + 1]),
                         reads=[("ss", col)], writes=[("ss", col)])
                    j = cnt["n"] % 2
                    P.op("dve", lambda e, t=t, j=j, col=col: e.scalar_tensor_tensor(
                        out=hn[j][:], in0=xg[xb][:, t, :], scalar=ss[:, col:col + 1], in1=nwt[:],
                        op0=ALU.mult, op1=ALU.mult), reads=[gkey, ("ss", col), "nwt"], writes=[("hn", j)])
                    P.dma(out[tok0 + t * 128:tok0 + (t + 1) * 128, :], hn[j][:], reads=[("hn", j)], q="act")
        for g in range(NG):
            do_group(g)
        P.emit()
    return nc


def gdn_consts():
    p = np.arange(128)
    same = (p[:, None] // 64) == (p[None, :] // 64)
    c = np.zeros((8, 128, 128), np.float32)
    c[0] = np.eye(128)
    c[1] = same & (p[:, None] <= p[None, :])
    c[2] = same
    c[3] = (p[:, None] < 64) * np.ones((1, 128))
    c[4] = (p[:, None] >= 64) * np.ones((1, 128))
    c[5] = 0.125 * (same & (p[:, None] <= p[None, :]))
    c[6] = same & (p[None, :] < p[:, None])
    c[7] = 1.0
    return np.ascontiguousarray(c.transpose(1, 0, 2))


def build_gdn(T):
    nc = bass.Bass("TRN2", target_bir_lowering=False)
    NTL = T // 128
    NCH = T // 64
    SEG = min(T, 2048)
    NSEG = T // SEG
    qkvT = nc.dram_tensor("qkvT", [3, 128, T], F32, kind="ExternalInput").ap()
    cw_d = nc.dram_tensor("cw", [128, 12], F32, kind="ExternalInput").ap()
    z_d = nc.dram_tensor("z_l", [64, NCH, 128], F32, kind="ExternalInput").ap()
    ab_d = nc.dram_tensor("ab", [128, 4, NTL], F32, kind="ExternalInput").ap()
    hp_d = nc.dram_tensor("hp", [128, 4], F32, kind="ExternalInput").ap()
    onw_d = nc.dram_tensor("onw", [64, 64], F32, kind="ExternalInput").ap()
    scT = nc.dram_tensor("scT", [3, 128, T], F32, kind="ExternalInput").ap()
    scw_d = nc.dram_tensor("scw", [128, 3], F32, kind="ExternalInput").ap()
    cst_d = nc.dram_tensor("cst", [128, 8, 128], F32, kind="ExternalInput").ap()
    ya_d = nc.dram_tensor("ya_l", [64, NCH, 128], F32, kind="ExternalOutput").ap()
    yb_d = nc.dram_tensor("ybT", [128, T], F32, kind="ExternalOutput").ap()

    with ExitStack() as es:
        P = Prog(nc, es)
        S = lambda name, shape, dt=F32: _sb(nc, es, name, shape, dt)
        cst = S("cst_sb", [128, 8, 128])
        ident, LTm, BO, SEL0, SEL1, MUs, ML, ONES = [cst[:, i, :] for i in range(8)]
        ident_t = S("ident_t", [128, 128])
        ident = ident_t[:]
        cw = S("cw_sb", [128, 12])
        hp = S("hp_sb", [128, 4])
        onw = S("onw_sb", [64, 64])
        scw = S("scw_sb", [128, 3])
        ab = S("ab_sb", [128, 4, NTL])
        fT = [S("fT%d" % s, [128, T]) for s in range(3)]
        raw = S("raw", [128, SEG + 3])
        raw2 = S("raw2", [128, SEG + 3])
        raw3 = S("raw3", [128, SEG + 3])
        acc = S("acc", [128, SEG])
        sqs = S("sqs", [128, SEG])
        rn = [S("rn%d" % i, [128, 512]) for i in range(2)]
        ps = [_ps(nc, es, "ps%d" % i, [128, 512], F32) for i in range(8)]

        P.dma(cst[:], cst_d, writes=["cst"])
        P.dma(ident_t[:], cst_d[:, 0, :], writes=["cst"])
        P.dma(cw[:], cw_d, writes=["cw"])
        P.dma(hp[:], hp_d, writes=["hp"])
        P.dma(onw[:], onw_d, writes=["onw"])
        P.dma(scw[:], scw_d, writes=["scw"])
        P.dma(ab[:], ab_d, writes=["ab"])

        def do_sc(sg):
            t0 = sg * SEG
            lo = 0 if sg > 0 else 2
            for i, (buf, key) in enumerate(((raw, "raw"), (raw2, "raw2"), (raw3, "raw3"))):
                if sg == 0:
                    P.op("pool", lambda e, buf=buf: e.memset(buf[:, 0:2], 0.0), writes=[key])
                P.dma(buf[:, lo:SEG + 2], scT[i][:, t0 - 2 + lo:t0 + SEG], writes=[key])
            P.op("pool", lambda e: e.tensor_tensor(out=raw2[:, 0:SEG + 2], in0=raw2[:, 0:SEG + 2],
                                                    in1=raw3[:, 0:SEG + 2], op=ALU.mult),
                 reads=["raw2", "raw3"], writes=["raw2"])
            P.op("dve", lambda e: e.tensor_scalar(out=acc[:], in0=raw2[:, 0:SEG], scalar1=scw[:, 0:1], scalar2=None,
                                                  op0=ALU.mult), reads=["raw2", "scw"], writes=["acc"])
            for j in (1, 2):
                P.op("dve", lambda e, j=j: e.scalar_tensor_tensor(out=acc[:], in0=raw2[:, j:j + SEG],
                                                                   scalar=scw[:, j:j + 1], in1=acc[:],
                                                                   op0=ALU.mult, op1=ALU.add),
                     reads=["raw2", "scw", "acc"], writes=["acc"])
            P.op("pool", lambda e: e.tensor_tensor(out=acc[:], in0=acc[:], in1=raw[:, 2:SEG + 2], op=ALU.mult),
                 reads=["acc", "raw"], writes=["acc"])
            P.dma(yb_d[:, t0:t0 + SEG], acc[:], reads=["acc"])

        for sg in range(NSEG):
            do_sc(sg)

        def do_p1(sg, s):
            t0 = sg * SEG
            lo = 0 if sg > 0 else 3
            if sg == 0:
                P.op("pool", lambda e: e.memset(raw[:, 0:3], 0.0), writes=["raw"])
            P.dma(raw[:, lo:SEG + 3], qkvT[s][:, t0 - 3 + lo:t0 + SEG], writes=["raw"])
            P.op("dve", lambda e: e.tensor_scalar(out=acc[:], in0=raw[:, 0:SEG], scalar1=cw[:, s * 4:s * 4 + 1],
                                                  scalar2=None, op0=ALU.mult), reads=["raw", "cw"], writes=["acc"])
            for j in (1, 2, 3):
                P.op("dve", lambda e, j=j: e.scalar_tensor_tensor(out=acc[:], in0=raw[:, j:j + SEG],
                                                                   scalar=cw[:, s * 4 + j:s * 4 + j + 1], in1=acc[:],
                                                                   op0=ALU.mult, op1=ALU.add),
                     reads=["raw", "cw", "acc"], writes=["acc"])
            dst = fT[s][:, t0:t0 + SEG]
            fkey = ("fT", s)
            P.op("act", lambda e: e.activation(out=dst, in_=acc[:], func=AF.Silu), reads=["acc"], writes=[fkey])
            if s < 2:
                P.op("act", lambda e: e.activation(out=sqs[:], in_=dst, func=AF.Square), reads=[fkey], writes=["sqs"])
                for blk in range(SEG // 512):
                    r = blk % 2
                    sl = slice(blk * 512, (blk + 1) * 512)
                    P.op("pe", lambda e, sl=sl: e.matmul(ps[7][:], lhsT=BO, rhs=sqs[:, sl], start=True, stop=True),
                         reads=["sqs", "cst"], writes=[("ps", 7)])
                    P.op("act", lambda e, r=r: e.activation(out=rn[r][:], in_=ps[7][:], func=AF.Sqrt, bias=EPS),
                         reads=[("ps", 7)], writes=[("rn", r)])
                    P.op("dve", lambda e, r=r: e.reciprocal(out=rn[r][:], in_=rn[r][:]),
                         reads=[("rn", r)], writes=[("rn", r)])
                    P.op("dve", lambda e, r=r, sl=sl: e.tensor_tensor(
                        out=fT[s][:, t0 + sl.start:t0 + sl.stop], in0=fT[s][:, t0 + sl.start:t0 + sl.stop],
                        in1=rn[r][:], op=ALU.mult), reads=[("rn", r), fkey], writes=[fkey])

        for sg in range(NSEG):
            for s in range(3):
                if _DBG.get("stop", 9) >= 2:
                    do_p1(sg, s)

        sc = {}
        for nm in ("g", "beta", "gc", "ngc", "egc", "bg", "dk", "egl0", "egl1"):
            sc[nm] = [S("sc_%s%d" % (nm, h), [128, NTL]) for h in range(2)]
        nea = S("nea", [128, 2])
        P.op("act", lambda e: e.activation(out=nea[:], in_=hp[:, 0:2], func=AF.Exp), reads=["hp"], writes=["nea"])
        P.op("dve", lambda e: e.tensor_scalar(out=nea[:], in0=nea[:], scalar1=-1.0, scalar2=None, op0=ALU.mult),
             reads=["nea"], writes=["nea"])

        def do_scal(h):
            k = lambda nm: ("sc", nm, h)
            g, beta = sc["g"][h], sc["beta"][h]
            P.op("act", lambda e: e.activation(out=g[:], in_=ab[:, h, :], func=AF.Exp, bias=hp[:, 2 + h:3 + h]),
                 reads=["ab", "hp"], writes=[k("g")])
            P.op("act", lambda e: e.activation(out=g[:], in_=g[:], func=AF.Ln, bias=1.0),
                 reads=[k("g")], writes=[k("g")])
            P.op("dve", lambda e: e.tensor_scalar(out=g[:], in0=g[:], scalar1=nea[:, h:h + 1], scalar2=None,
                                                  op0=ALU.mult), reads=[k("g"), "nea"], writes=[k("g")])
            P.op("act", lambda e: e.activation(out=beta[:], in_=ab[:, 2 + h, :], func=AF.Sigmoid),
                 reads=["ab"], writes=[k("beta")])
            P.op("pe", lambda e: e.matmul(ps[7][:, 0:NTL], lhsT=LTm, rhs=g[:], start=True, stop=True),
                 reads=[k("g"), "cst"], writes=[("ps", 7)])
            P.op("dve", lambda e: e.tensor_copy(out=sc["gc"][h][:], in_=ps[7][:, 0:NTL]),
                 reads=[("ps", 7)], writes=[k("gc")])
            P.op("dve", lambda e: e.tensor_scalar(out=sc["ngc"][h][:], in0=ps[7][:, 0:NTL], scalar1=-1.0, scalar2=None,
                                                  op0=ALU.mult), reads=[("ps", 7)], writes=[k("ngc")])
            P.op("act", lambda e: e.activation(out=sc["egc"][h][:], in_=ps[7][:, 0:NTL], func=AF.Exp),
                 reads=[("ps", 7)], writes=[k("egc")])
            P.op("dve", lambda e: e.tensor_tensor(out=sc["bg"][h][:], in0=sc["egc"][h][:], in1=beta[:], op=ALU.mult),
                 reads=[k("egc"), k("beta")], writes=[k("bg")])
            P.op("pe", lambda e: e.matmul(ps[7][:, 0:NTL], lhsT=BO, rhs=g[:], start=True, stop=True),
                 reads=[k("g"), "cst"], writes=[("ps", 7)])
            P.op("dve", lambda e: e.tensor_tensor(out=sc["dk"][h][:], in0=ps[7][:, 0:NTL], in1=sc["gc"][h][:],
                                                  op=ALU.subtract), reads=[("ps", 7), k("gc")], writes=[k("dk")])
            P.op("act", lambda e: e.activation(out=sc["dk"][h][:], in_=sc["dk"][h][:], func=AF.Exp),
                 reads=[k("dk")], writes=[k("dk")])
            for c, SEL in ((0, SEL0), (1, SEL1)):
                P.op("pe", lambda e, SEL=SEL: e.matmul(ps[7][:, 0:NTL], lhsT=SEL, rhs=g[:], start=True, stop=True),
                     reads=[k("g"), "cst"], writes=[("ps", 7)])
                P.op("act", lambda e, c=c: e.activation(out=sc["egl%d" % c][h][:], in_=ps[7][:, 0:NTL], func=AF.Exp),
                     reads=[("ps", 7)], writes=[k("egl%d" % c)])

        for h in range(2):
            if _DBG.get("stop", 9) >= 3:
                do_scal(h)

        NQ = 24
        qcnt = [0]

        def nq():
            i = qcnt[0] % NQ
            qcnt[0] += 1
            bk, qt = i % 6, i // 6
            return ps[bk][:, qt * 128:(qt + 1) * 128], ("bk", bk)

        NB = 2
        tmp = {}

        def T_(nm, shape, n=NB * 2):
            tmp[nm] = [S("t_%s%d" % (nm, i), shape) for i in range(n)]

        T_("ktok", [128, 128], NB)
        T_("vtok", [128, 128], NB)
        T_("qtok", [128, 128], NB)
        for nm in ("dg", "dsym", "du", "dl", "attnT", "M", "MT", "RT", "Pa", "PaT", "Pb", "PbT"):
            T_(nm, [128, 128])
        T_("vb", [128, 64]); T_("kbg", [128, 64]); T_("kd0", [128, 64]); T_("kd1", [128, 64])
        T_("u", [128, 64]); T_("wT", [64, 128]); T_("qgT", [64, 128]); T_("qg", [128, 64])
        T_("tabs", [128, 128])
        vnew = [S("vnew%d" % h, [128, 64]) for h in range(2)]
        Sst = [S("Sst%d" % h, [64, 64]) for h in range(2)]
        for h in range(2):
            P.op("pool", lambda e, h=h: e.memset(vnew[h][:], 0.0), writes=[("vnew", h)])
            P.op("pool", lambda e, h=h: e.memset(Sst[h][:], 0.0), writes=[("S", h)])
        osb = [S("osb%d" % i, [64, 4, 64]) for i in range(2)]
        osq = [S("osq%d" % i, [64, 4, 64]) for i in range(2)]
        oss = [S("oss%d" % i, [64, 4]) for i in range(2)]
        zt = [S("zt%d" % i, [64, 2, 128]) for i in range(2)]
        yt = [S("yt%d" % i, [64, 2, 128]) for i in range(2)]

        stt = {}

        def front(tt):
            tb = tt % NB
            c0 = tt * 128
            toks = {}
            for s, nm in ((0, "qtok"), (1, "ktok"), (2, "vtok")):
                pq, kq = nq()
                P.op("pe", lambda e, pq=pq, s=s: e.matmul(pq, lhsT=fT[s][:, c0:c0 + 128], rhs=ident, start=True, stop=True),
                     reads=[("fT", s), "cst"], writes=[kq])
                dstt = tmp[nm][tb]
                if _DBG.get('nocopy'):
                    continue
                P.op("dve", lambda e, pq=pq, dstt=dstt: e.tensor_copy(out=dstt[:], in_=pq), reads=[kq], writes=[(nm, tb)])
                toks[nm] = dstt
            P.dma(zt[tb][:], z_d[:, 2 * tt:2 * tt + 2, :], writes=[("zt", tb)])
            P.op("act", lambda e: e.activation(out=zt[tb][:], in_=zt[tb][:], func=AF.Silu),
                 reads=[("zt", tb)], writes=[("zt", tb)])

            st = {}
            stt[tt] = st

            def prep_head(h):
                ix = tb * 2 + h
                hb = 64 * h
                kk = lambda nm, ix=ix: (nm, ix)
                g = lambda nm, ix=ix: tmp[nm][ix]
                col = lambda nm, h=h: sc[nm][h][:, tt:tt + 1]
                kTh = fT[1][hb:hb + 64, c0:c0 + 128]
                qTh = fT[0][hb:hb + 64, c0:c0 + 128]
                P.op("dve", lambda e, g=g, col=col: e.tensor_scalar(out=g("dg")[:], in0=ident, scalar1=col("gc"),
                                                                     scalar2=None, op0=ALU.mult),
                     reads=["cst", ("sc", "gc", h)], writes=[kk("dg")])
                pR, kR = nq()
                P.op("pe", lambda e, pR=pR, g=g: e.matmul(pR, lhsT=ONES, rhs=g("dg")[:], start=True, stop=True),
                     reads=[kk("dg"), "cst"], writes=[kR])
                P.op("act", lambda e, pR=pR, g=g, col=col: e.activation(
                    out=g("tabs")[:], in_=pR, func=AF.Abs, bias=col("ngc")),
                    reads=[kR, ("sc", "ngc", h)], writes=[kk("tabs")])
                P.op("act", lambda e, g=g: e.activation(out=g("dsym")[:], in_=g("tabs")[:], func=AF.Exp, scale=-1.0),
                     reads=[kk("tabs")], writes=[kk("dsym")])
                P.op("pool", lambda e, g=g: e.tensor_tensor(out=g("du")[:], in0=g("dsym")[:], in1=MUs, op=ALU.mult),
                     reads=[kk("dsym"), "cst"], writes=[kk("du")])
                P.op("pool", lambda e, g=g: e.tensor_tensor(out=g("dl")[:], in0=g("dsym")[:], in1=ML, op=ALU.mult),
                     reads=[kk("dsym"), "cst"], writes=[kk("dl")])
                pKK, kKK = nq()
                P.op("pe", lambda e, pKK=pKK, kTh=kTh: e.matmul(pKK, lhsT=kTh, rhs=kTh, start=True, stop=True),
                     reads=[("fT", 1)], writes=[kKK])
                pQK, kQK = nq()
                P.op("pe", lambda e, pQK=pQK, kTh=kTh, qTh=qTh: e.matmul(pQK, lhsT=kTh, rhs=qTh, start=True, stop=True),
                     reads=[("fT", 1), ("fT", 0)], writes=[kQK])
                P.op("dve", lambda e, pQK=pQK, g=g: e.tensor_tensor(out=g("attnT")[:], in0=pQK, in1=g("du")[:],
                                                                     op=ALU.mult),
                     reads=[kQK, kk("du")], writes=[kk("attnT")])
                P.op("dve", lambda e, pKK=pKK, g=g, col=col: e.scalar_tensor_tensor(
                    out=g("M")[:], in0=pKK, scalar=col("beta"), in1=g("dl")[:], op0=ALU.mult, op1=ALU.mult),
                    reads=[kKK, kk("dl"), ("sc", "beta", h)], writes=[kk("M")])
                ktok, vtok, qtok = toks["ktok"], toks["vtok"], toks["qtok"]
                P.op("pool", lambda e, g=g, col=col, vtok=vtok: e.tensor_scalar(
                    out=g("vb")[:], in0=vtok[:, hb:hb + 64], scalar1=col("beta"), scalar2=None, op0=ALU.mult),
                    reads=[("vtok", tb), ("sc", "beta", h)], writes=[kk("vb")])
                P.op("pool", lambda e, g=g, col=col, ktok=ktok: e.tensor_scalar(
                    out=g("kbg")[:], in0=ktok[:, hb:hb + 64], scalar1=col("bg"), scalar2=None, op0=ALU.mult),
                    reads=[("ktok", tb), ("sc", "bg", h)], writes=[kk("kbg")])
                P.op("pool", lambda e, g=g, col=col, ktok=ktok: e.tensor_scalar(
                    out=g("kd0")[:], in0=ktok[:, hb:hb + 64], scalar1=col("dk"), scalar2=SEL0[:, 0:1],
                    op0=ALU.mult, op1=ALU.mult), reads=[("ktok", tb), ("sc", "dk", h), "cst"], writes=[kk("kd0")])
                P.op("pool", lambda e, g=g, col=col, ktok=ktok: e.tensor_scalar(
                    out=g("kd1")[:], in0=ktok[:, hb:hb + 64], scalar1=col("dk"), scalar2=SEL1[:, 0:1],
                    op0=ALU.mult, op1=ALU.mult), reads=[("ktok", tb), ("sc", "dk", h), "cst"], writes=[kk("kd1")])
                P.op("dve", lambda e, g=g, col=col, qtok=qtok: e.tensor_scalar(
                    out=g("qg")[:], in0=qtok[:, hb:hb + 64], scalar1=col("egc"), scalar2=0.125,
                    op0=ALU.mult, op1=ALU.mult), reads=[("qtok", tb), ("sc", "egc", h)], writes=[kk("qg")])
                pqg, kqg = nq()
                P.op("pe", lambda e, pqg=pqg, g=g: e.matmul(pqg[0:64, :], lhsT=g("qg")[:], rhs=ident, start=True, stop=True),
                     reads=[kk("qg"), "cst"], writes=[kqg])
                P.op("act", lambda e, pqg=pqg, g=g: e.copy(out=g("qgT")[:], in_=pqg[0:64, :]),
                     reads=[kqg], writes=[kk("qgT")])
                pMT, kMT = nq()
                P.op("pe", lambda e, pMT=pMT, g=g: e.matmul(pMT, lhsT=g("M")[:], rhs=ident, start=True, stop=True),
                     reads=[kk("M"), "cst"], writes=[kMT])
                P.op("act", lambda e, pMT=pMT, g=g: e.copy(out=g("MT")[:], in_=pMT), reads=[kMT], writes=[kk("MT")])
                P.op("dve", lambda e, pMT=pMT, g=g: e.tensor_tensor(out=g("RT")[:], in0=ident, in1=pMT, op=ALU.subtract),
                     reads=[kMT, "cst"], writes=[kk("RT")])
                st[h] = dict(ix=ix, P=("M", "MT"))

            for h in range(2):
                prep_head(h)

            yield
            for lvl in range(5):
                if lvl > 0:
                    yield
                last = lvl == 4
                nxt = ("Pa", "PaT") if lvl % 2 == 0 else ("Pb", "PbT")
                pend = {}
                for h in range(2):
                    ix = st[h]["ix"]
                    Pn, PTn = st[h]["P"]
                    Pk, PTk = tmp[Pn][ix], tmp[PTn][ix]
                    p1, k1 = nq()
                    P.op("pe", lambda e, p1=p1, Pk=Pk, PTk=PTk: e.matmul(p1, lhsT=PTk[:], rhs=Pk[:], start=True, stop=True),
                         reads=[(Pn, ix), (PTn, ix)], writes=[k1])
                    p2 = k2 = None
                    if not last:
                        p2, k2 = nq()
                        P.op("pe", lambda e, p2=p2, Pk=Pk, PTk=PTk: e.matmul(p2, lhsT=Pk[:], rhs=PTk[:], start=True, stop=True),
                             reads=[(Pn, ix), (PTn, ix)], writes=[k2])
                    pend[h] = (p1, k1, p2, k2)
                for h in range(2):
                    ix = st[h]["ix"]
                    p1, k1, p2, k2 = pend[h]
                    Pnew, PTnew = tmp[nxt[0]][ix], tmp[nxt[1]][ix]
                    P.op("act", lambda e, p1=p1, Pnew=Pnew: e.copy(out=Pnew[:], in_=p1), reads=[k1], writes=[(nxt[0], ix)])
                    if not last:
                        P.op("dve", lambda e, p2=p2, PTnew=PTnew: e.tensor_copy(out=PTnew[:], in_=p2),
                             reads=[k2], writes=[(nxt[1], ix)])
                for h in range(2):
                    ix = st[h]["ix"]
                    Pnew = tmp[nxt[0]][ix]
                    RT = tmp["RT"][ix]
                    p3, k3 = nq()
                    P.op("pe", lambda e, p3=p3, Pnew=Pnew, RT=RT: e.matmul(p3, lhsT=Pnew[:], rhs=RT[:], start=True, stop=True),
                         reads=[(nxt[0], ix), ("RT", ix)], writes=[k3])
                    P.op("dve", lambda e, p3=p3, RT=RT: e.tensor_tensor(out=RT[:], in0=RT[:], in1=p3, op=ALU.add),
                         reads=[k3, ("RT", ix)], writes=[("RT", ix)])
                    st[h]["P"] = nxt

            yield
            for h in range(2):
                ix = st[h]["ix"]
                RT = tmp["RT"][ix]
                pu, ku = nq()
                P.op("pe", lambda e, pu=pu, RT=RT, ix=ix: e.matmul(pu[:, 0:64], lhsT=RT[:], rhs=tmp["vb"][ix][:],
                                                                   start=True, stop=True),
                     reads=[("RT", ix), ("vb", ix)], writes=[ku])
                P.op("act", lambda e, pu=pu, ix=ix: e.copy(out=tmp["u"][ix][:], in_=pu[:, 0:64]),
                     reads=[ku], writes=[("u", ix)])
                pw, kw = nq()
                P.op("pe", lambda e, pw=pw, RT=RT, ix=ix: e.matmul(pw[0:64, :], lhsT=tmp["kbg"][ix][:], rhs=RT[:],
                                                                   start=True, stop=True),
                     reads=[("RT", ix), ("kbg", ix)], writes=[kw])
                P.op("dve", lambda e, pw=pw, ix=ix: e.tensor_copy(out=tmp["wT"][ix][:], in_=pw[0:64, :]),
                     reads=[kw], writes=[("wT", ix)])

            yield

        def back(tt):
            tb = tt % NB
            st = stt.pop(tt)
            ob = tt % 2
            for c in range(2):
                yield
                rs = slice(c * 64, (c + 1) * 64)
                for h in range(2):
                    ix = st[h]["ix"]
                    pws, kws = nq()
                    P.op("pe", lambda e, pws=pws, ix=ix, h=h: e.matmul(pws[:, 0:64], lhsT=tmp["wT"][ix][:], rhs=Sst[h][:],
                                                                       start=True, stop=True),
                         reads=[("wT", ix), ("S", h)], writes=[kws])
                    P.op("dve", lambda e, pws=pws, ix=ix, h=h, rs=rs: e.tensor_tensor(
                        out=vnew[h][rs, :], in0=tmp["u"][ix][rs, :], in1=pws[rs, 0:64], op=ALU.subtract),
                        reads=[kws, ("u", ix)], writes=[("vnew", h)])
                    po = ps[6 + h][0:64, c * 64:(c + 1) * 64]
                    ko = ("bk", 6 + h)
                    P.op("pe", lambda e, po=po, ix=ix, h=h, rs=rs: e.matmul(po, lhsT=tmp["qgT"][ix][:, rs], rhs=Sst[h][:],
                                                                            start=True, stop=False),
                         reads=[("qgT", ix), ("S", h)], writes=[ko])
                    P.op("pe", lambda e, po=po, ix=ix, h=h, rs=rs: e.matmul(po, lhsT=tmp["attnT"][ix][:, rs], rhs=vnew[h][:],
                                                                            start=False, stop=True),
                         reads=[("attnT", ix), ("vnew", h)], writes=[ko])
                    pS, kS = nq()
                    kdn = "kd%d" % c
                    P.op("pe", lambda e, pS=pS, ix=ix, h=h, kdn=kdn: e.matmul(pS[0:64, 0:64], lhsT=tmp[kdn][ix][:], rhs=vnew[h][:],
                                                                              start=True, stop=True),
                         reads=[(kdn, ix), ("vnew", h)], writes=[kS])
                    P.op("dve", lambda e, pS=pS, h=h, c=c: e.scalar_tensor_tensor(
                        out=Sst[h][:], in0=Sst[h][:], scalar=sc["egl%d" % c][h][0:64, tt:tt + 1], in1=pS[0:64, 0:64],
                        op0=ALU.mult, op1=ALU.add), reads=[kS, ("S", h), ("sc", "egl%d" % c, h)], writes=[("S", h)])
                    P.op("act", lambda e, po=po, c=c, h=h: e.copy(out=osb[ob][:, c * 2 + h, :], in_=po),
                         reads=[ko], writes=[("osb", ob)])
            yield
            P.op("pool", lambda e: e.tensor_tensor(out=osq[ob][:], in0=osb[ob][:], in1=osb[ob][:], op=ALU.mult),
                 reads=[("osb", ob)], writes=[("osq", ob)])
            P.op("dve", lambda e: e.tensor_reduce(out=oss[ob][:], in_=osq[ob][:], axis=AX.X, op=ALU.add),
                 reads=[("osq", ob)], writes=[("oss", ob)])
            P.op("act", lambda e: e.activation(out=oss[ob][:], in_=oss[ob][:], func=AF.Sqrt, scale=1.0 / 64, bias=EPS),
                 reads=[("oss", ob)], writes=[("oss", ob)])
            P.op("dve", lambda e: e.reciprocal(out=oss[ob][:], in_=oss[ob][:]), reads=[("oss", ob)], writes=[("oss", ob)])
            for c in range(2):
                for h in range(2):
                    P.op("dve", lambda e, c=c, h=h: e.scalar_tensor_tensor(
                        out=yt[ob][:, c, h * 64:(h + 1) * 64], in0=osb[ob][:, c * 2 + h, :],
                        scalar=oss[ob][:, c * 2 + h:c * 2 + h + 1], in1=onw[:], op0=ALU.mult, op1=ALU.mult),
                        reads=[("osb", ob), ("oss", ob), "onw"], writes=[("yt", ob)])
            P.op("pool", lambda e: e.tensor_tensor(out=yt[ob][:], in0=yt[ob][:], in1=zt[tb][:], op=ALU.mult),
                 reads=[("yt", ob), ("zt", tb)], writes=[("yt", ob)])
            P.dma(ya_d[:, 2 * tt:2 * tt + 2, :], yt[ob][:], reads=[("yt", ob)])

        def drive(gens):
            gens = list(gens)
            while gens:
                for g_ in list(gens):
                    try:
                        next(g_)
                    except StopIteration:
                        gens.remove(g_)

        drive([front(0)])
        for tt in range(NTL):
            drive([back(tt)] + ([front(tt + 1)] if tt + 1 < NTL else []))
        P.emit()
    return nc


def gdn_in_maps(P0, T, qkv_conv, a_log, dt_bias, o_norm, sc_conv):
    P0 = P0.reshape(2, T, -1)
    cst = gdn_consts()
    maps = []
    for c in range(NCORES):
        b, hg = c // 4, c % 4
        o = 128 * hg
        pb = P0[b]
        qkvT = np.stack([pb[:, s * 512 + o:s * 512 + o + 128].T for s in range(3)])
        cw = np.concatenate([qkv_conv[:, s * 512 + o:s * 512 + o + 128].T for s in range(3)], axis=1)
        abc = np.concatenate([pb[:, 2048 + 2 * hg:2048 + 2 * hg + 2], pb[:, 2056 + 2 * hg:2056 + 2 * hg + 2]], axis=1)
        ab = abc.reshape(T // 128, 128, 4).transpose(1, 2, 0)
        hp = np.tile(np.concatenate([a_log[2 * hg:2 * hg + 2], dt_bias[2 * hg:2 * hg + 2]])[None], (128, 1))
        z_l = pb[:, 1536 + o:1536 + o + 128].reshape(T // 64, 64, 128).transpose(1, 0, 2)
        scT = np.stack([pb[:, 2064 + s * 512 + o:2064 + s * 512 + o + 128].T for s in range(3)])
        scw = sc_conv[:, o:o + 128].T
        maps.append(dict(qkvT=np.ascontiguousarray(qkvT, np.float32), cw=np.ascontiguousarray(cw, np.float32),
                         z_l=np.ascontiguousarray(z_l, np.float32), ab=np.ascontiguousarray(ab, np.float32),
                         hp=np.ascontiguousarray(hp, np.float32), onw=np.ascontiguousarray(np.tile(o_norm[None], (64, 1)), np.float32),
                         scT=np.ascontiguousarray(scT, np.float32), scw=np.ascontiguousarray(scw, np.float32), cst=cst))
    return maps


def gdn_gather(results, T):
    y = np.zeros((2, T, 1024), np.float32)
    for c in range(NCORES):
        b, hg = c // 4, c % 4
        o = 128 * hg
        y[b, :, o:o + 128] = results[c]["ya_l"].transpose(1, 0, 2).reshape(T, 128)
        y[b, :, 512 + o:512 + o + 128] = results[c]["ybT"].T
    return y.reshape(2 * T, 1024)


def nsa_consts(T):
    NTL = T // 128
    p = np.arange(128)
    half = 8
    inv_freq = (500000.0 ** (-(np.arange(0, 16, 2, dtype=np.float32)) / np.float32(16))).astype(np.float32)
    ang = np.arange(T, dtype=np.float32)[:, None] * inv_freq[None, :]
    cos, sin = np.cos(ang).astype(np.float32), np.sin(ang).astype(np.float32)
    C = np.ones((64, T), np.float32)
    C[0:8] = cos.T
    C[8:16] = cos.T
    Sg = np.zeros((16, T), np.float32)
    Sg[0:8] = -sin.T
    Sg[8:16] = sin.T
    C4 = np.ascontiguousarray(np.broadcast_to(C[:, None, :], (64, 4, T)))
    S4 = np.ascontiguousarray(np.broadcast_to(Sg[:, None, :], (16, 4, T)))
    key = np.arange(T)
    Eg = np.zeros((128, T), np.float32)
    Eg[64:128] = (np.arange(64)[:, None] == ((key[None, :] // 64) % 64))
    PM = np.zeros((128, 17, 128), np.float32)
    for r in range(17):
        PM[:, r, :] = (16 * p[:, None] + 31) <= (p[None, :] + 128 * r)
    CM = (p[:, None] <= p[None, :]).astype(np.float32)
    AM = (p[:, None] > p[None, :]).astype(np.float32)
    msk = np.ascontiguousarray(np.concatenate([PM, CM[:, None, :], AM[:, None, :], np.eye(128, dtype=np.float32)[:, None, :]], axis=1))
    c = np.arange(512)
    s = np.arange(128)
    ov = ((16 * c[:, None] < 64 * s[None, :] + 64) & (16 * c[:, None] + 32 > 64 * s[None, :])).astype(np.float32)
    ov[511] = 0.0
    ov = np.ascontiguousarray(ov.reshape(4, 128, 128).transpose(1, 0, 2))
    NQB = T // 128
    bq = np.zeros((NQB, 128, 128), np.float32)
    for qb in range(NQB):
        cur = 2 * qb + (p >= 64)
        js = s[None, :]
        forced = (js == 0) | (js == cur[:, None]) | (js == cur[:, None] - 1)
        bq[qb] = np.where(js > cur[:, None], -100.0, np.where(forced, 100.0, 0.0))
    return dict(C4=C4, S4=S4, Eg=Eg, msk=msk, ov=ov, bq=bq)


def build_nsa(T):
    nc = bass.Bass("TRN2", target_bir_lowering=False)
    NTL = T // 128
    NQB = NTL
    SEG = min(T, 2048)
    NSEG = T // SEG
    D = lambda name, shape: nc.dram_tensor(name, list(shape), F32, kind="ExternalInput").ap()
    qT_d = D("qT", [64, 4, T]); qsw_d = D("qsw", [16, 4, T])
    kT_d = D("kT", [2, 64, T]); ksw_d = D("ksw", [2, 16, T])
    v_d = D("vtok", [2, 128, NTL, 64])
    KP_d = D("KP", [2, 128, T // 2])
    posP_d = D("posP", [128, 16])
    w1_d = D("w1", [2, 128, 16, 256]); w2_d = D("w2", [2, 128, 2, 64])
    gat_d = D("gates", [128, NTL, 12])
    C4_d = D("C4", [64, 4, T]); S4_d = D("S4", [16, 4, T])
    Eg_d = D("Eg", [128, T]); msk_d = D("msk", [128, 20, 128]); ov_d = D("ov", [128, 4, 128])
    bq_d = D("bq", [NQB, 128, 128])
    o_d = nc.dram_tensor("o_out", [T, 256], F32, kind="ExternalOutput").ap()

    with ExitStack() as es:
        P = Prog(nc, es)
        S = lambda name, shape, dt=F32: _sb(nc, es, name, shape, dt)
        msk = S("msk_sb", [128, 20, 128])
        CM, AM, ident = msk[:, 17, :], msk[:, 18, :], msk[:, 19, :]
        KT = [S("KTb%d" % i, [128 if i == 0 else 64, T], BF16) for i in range(2)]
        NGRP = (2 * NTL + 63) // 64
        Vb = [S("Vb%d" % i, [128, NTL, 65], BF16) for i in range(2)]
        kcT = S("kcT", [64, 512], BF16)
        Vc = S("Vc", [128, 4, 193], BF16)
        gs = S("gs", [128, NTL, 12])
        posP = S("posP_sb", [128, 16])
        stg = [S("stg%d" % i, [128, 4096]) for i in range(2)]
        stg16 = [S("stg16_%d" % i, [16, SEG]) for i in range(3)]
        KPp = S("KPp", [128, 16, 512], BF16)
        w1b = S("w1b", [128, 16, 256], BF16)
        w2b = S("w2b", [128, 2, 64], BF16)
        HT = S("HT", [128, 2, 512], BF16)
        ps = [_ps(nc, es, "ps%d" % i, [128, 512], F32) for i in range(8)]
        bk = lambda i: ("bk", i)

        P.dma(msk[:], msk_d, writes=["msk"])
        P.dma(posP[:], posP_d, writes=["posP"])
        for sg in range(NSEG):
            sv = stg[sg % 2][64:128, 0:SEG]
            P.dma(sv, Eg_d[64:128, sg * SEG:(sg + 1) * SEG], writes=[("stg", sg % 2)])
            P.op("pool", lambda e, sv=sv, sg=sg: e.tensor_copy(out=KT[0][64:128, sg * SEG:(sg + 1) * SEG], in_=sv),
                 reads=[("stg", sg % 2)], writes=[("KTe", 0)])
        P.dma(gs[:], gat_d, writes=["gs"])
        P.op("act", lambda e: e.activation(out=gs[:], in_=gs[:], func=AF.Sigmoid), reads=["gs"], writes=["gs"])

        def do_k(i, sg):
            t0 = sg * SEG
            a, b = stg[0][0:64, 0:SEG], stg[1][0:64, 0:SEG]
            P.dma(a, kT_d[i][:, t0:t0 + SEG], writes=[("stg", 0)])
            P.dma(b, C4_d[:, 0, t0:t0 + SEG], writes=[("stg", 1)])
            P.dma(stg16[0][:], ksw_d[i][:, t0:t0 + SEG], writes=[("s16", 0)])
            P.dma(stg16[1][:], S4_d[:, 0, t0:t0 + SEG], writes=[("s16", 1)])
            P.op("dve", lambda e: e.tensor_tensor(out=KT[i][0:64, t0:t0 + SEG], in0=a, in1=b, op=ALU.mult),
                 reads=[("stg", 0), ("stg", 1)], writes=[("KT", i)])
            P.op("pool", lambda e: e.tensor_tensor(out=stg16[0][:], in0=stg16[0][:], in1=stg16[1][:], op=ALU.mult),
                 reads=[("s16", 0), ("s16", 1)], writes=[("s16", 0)])
            P.op("pool", lambda e: e.tensor_tensor(out=stg16[2][:], in0=a[0:16, :], in1=b[0:16, :], op=ALU.mult),
                 reads=[("stg", 0), ("stg", 1)], writes=[("s16", 2)])
            P.op("dve", lambda e: e.tensor_tensor(out=KT[i][0:16, t0:t0 + SEG], in0=stg16[0][:], in1=stg16[2][:], op=ALU.add),
                 reads=[("s16", 0), ("s16", 2), ("KT", i)], writes=[("KT", i)])

        for i in range(2):
            for sg in range(NSEG):
                do_k(i, sg)

        def do_v(i):
            sv = stg[i][:, 0:NTL * 64].rearrange("p (a b) -> p a b", a=NTL)
            P.dma(sv, v_d[i], writes=[("stg", i)])
            P.op("pool", lambda e: e.memset(Vb[i][:, :, 64:65], 1.0), writes=[("Vb", i)])
            P.op("dve", lambda e: e.tensor_copy(out=Vb[i][:, :, 0:64], in_=sv), reads=[("stg", i)], writes=[("Vb", i)])

        for i in range(2):
            do_v(i)

        P.op("pool", lambda e: e.memset(kcT[:], 0.0), writes=["kcT"])
        P.op("pool", lambda e: e.memset(Vc[:], 0.0), writes=["Vc"])

        def do_cmp(i):
            kp = stg[0][:, 0:T // 2]
            P.dma(kp, KP_d[i], writes=[("stg", 0)])
            kpv = kp.rearrange("p (c e) -> p c e", e=8)
            NCB = T // 16 - 1
            for lp in range(16):
                src = kpv[:, 0:NCB, lp] if lp < 8 else kpv[:, 1:NCB + 1, lp - 8]
                eng = "dve" if lp % 2 == 0 else "pool"
                P.op(eng, lambda e, src=src, lp=lp: e.tensor_scalar(out=KPp[:, lp, 0:NCB], in0=src, scalar1=posP[:, lp:lp + 1],
                                                                   scalar2=None, op0=ALU.add),
                     reads=[("stg", 0), "posP"], writes=["KPp"])
            w1v = stg[1][:, 0:4096].rearrange("p (a b) -> p a b", a=16)
            P.dma(w1v, w1_d[i], writes=[("stg", 1)])
            P.op("act", lambda e: e.copy(out=w1b[:], in_=w1v), reads=[("stg", 1)], writes=["w1b"])
            w2v = stg[0][:, 0:128].rearrange("p (a b) -> p a b", a=2)
            P.dma(w2v, w2_d[i], reads=["KPp"], writes=[("stg", 0)])
            P.op("act", lambda e: e.copy(out=w2b[:], in_=w2v), reads=[("stg", 0)], writes=["w2b"])
            for jc in range(2):
                for lp in range(16):
                    P.op("pe", lambda e, jc=jc, lp=lp: e.matmul(ps[jc][:, 0:NCB], lhsT=w1b[:, lp, jc * 128:(jc + 1) * 128],
                                                                rhs=KPp[:, lp, 0:NCB], start=(lp == 0), stop=(lp == 15)),
                         reads=["w1b", "KPp"], writes=[bk(jc)])
                P.op("act", lambda e, jc=jc: e.activation(out=HT[:, jc, 0:NCB], in_=ps[jc][:, 0:NCB], func=AF.Silu),
                     reads=[bk(jc)], writes=["HT"])
            if i == 0:
                for jc in range(2):
                    P.op("pe", lambda e, jc=jc: e.matmul(ps[2][0:64, 0:NCB], lhsT=w2b[:, jc, :], rhs=HT[:, jc, 0:NCB],
                                                         start=(jc == 0), stop=(jc == 1)), reads=["w2b", "HT"], writes=[bk(2)])
                P.op("act", lambda e: e.copy(out=kcT[:, 0:NCB], in_=ps[2][0:64, 0:NCB]), reads=[bk(2)], writes=["kcT"])
            else:
                for ct in range((NCB + 127) // 128):
                    n = min(128, NCB - ct * 128)
                    for jc in range(2):
                        P.op("pe", lambda e, jc=jc, ct=ct, n=n: e.matmul(ps[3][0:n, 0:64], lhsT=HT[:, jc, ct * 128:ct * 128 + n],
                                                                         rhs=w2b[:, jc, :], start=(jc == 0), stop=(jc == 1)),
                             reads=["w2b", "HT"], writes=[bk(3)])
                    P.op("act", lambda e, ct=ct, n=n: e.copy(out=Vc[0:n, ct, 0:64], in_=ps[3][0:n, 0:64]),
                         reads=[bk(3)], writes=["Vc"])
                    P.op("pool", lambda e, ct=ct, n=n: e.memset(Vc[0:n, ct, 64:65], 1.0), reads=["Vc"], writes=["Vc"])

        do_cmp(0)
        do_cmp(1)
        ovs = stg[1][:, 0:512].rearrange("p (a b) -> p a b", a=4)
        P.dma(ovs, ov_d, reads=["w1b"], writes=[("stg", 1)])
        P.op("dve", lambda e: e.tensor_copy(out=Vc[:, :, 65:193], in_=ovs), reads=[("stg", 1), "Vc"], writes=["Vc"])

        qf = [S("qf%d" % i, [64, 4, 128]) for i in range(2)]
        c4 = [S("c4_%d" % i, [64, 4, 128]) for i in range(2)]
        qs = [S("qs%d" % i, [16, 4, 128]) for i in range(2)]
        s4 = [S("s4_%d" % i, [16, 4, 128]) for i in range(2)]
        v16 = [S("v16_%d" % i, [16, 4, 128]) for i in range(2)]
        Qn = [S("Qn%d" % i, [64, 512], BF16) for i in range(2)]
        QA = [[S("QA%d_%d" % (i, g_), [128, 512], BF16) for g_ in range(NGRP)] for i in range(2)]
        Qr = [QA[i][0][0:64, :] for i in range(2)]
        nselsw = S("nselsw", [128, 128])
        oT = [S("oT%d" % i, [65, 512]) for i in range(2)]
        bqs = [S("bqs%d" % i, [128, 128]) for i in range(2)]
        Pc = S("Pc", [128, 4, 512], BF16)
        NPB = 4
        Pb = [S("Pb%d" % i, [128, 512], BF16) for i in range(NPB)]
        ocmp = S("ocmp", [128, 4, 193])
        ow = S("ow", [128, 4, 65])
        osl = S("osl", [128, 4, 65])
        rd = S("rd", [128, 12])
        impm = S("impm", [128, 128])
        imp2 = S("imp2", [128, 128])
        m8 = S("m8", [128, 16])
        nsel = S("nsel", [128, 128])
        osum = [S("osum%d" % i, [128, 4, 64]) for i in range(2)]
        pcnt = [0]
        scnt = [0]

        def prep_qb(qb):
            j = qb % 2
            q0 = qb * 128
            P.dma(qf[j][:], qT_d[:, :, q0:q0 + 128], writes=[("qf", j)])
            P.dma(c4[j][:], C4_d[:, :, q0:q0 + 128], writes=[("c4", j)])
            P.dma(qs[j][:], qsw_d[:, :, q0:q0 + 128], writes=[("qs", j)])
            P.dma(s4[j][:], S4_d[:, :, q0:q0 + 128], writes=[("s4", j)])
            P.dma(bqs[j][:], bq_d[qb], writes=[("bqs", j)])
            fl = lambda t: t[:].rearrange("p a b -> p (a b)")
            P.op("pool", lambda e: e.tensor_copy(out=Qn[j][:], in_=fl(qf[j])), reads=[("qf", j)], writes=[("Qn", j)])
            P.op("pool", lambda e: e.tensor_tensor(out=Qr[j], in0=fl(qf[j]), in1=fl(c4[j]), op=ALU.mult),
                 reads=[("qf", j), ("c4", j)], writes=[("Qr", j)])
            P.op("pool", lambda e: e.tensor_tensor(out=fl(qs[j]), in0=fl(qs[j]), in1=fl(s4[j]), op=ALU.mult),
                 reads=[("qs", j), ("s4", j)], writes=[("qs", j)])
            P.op("pool", lambda e: e.tensor_tensor(out=fl(v16[j]), in0=qf[j][0:16, :, :].rearrange("p a b -> p (a b)"),
                                                    in1=c4[j][0:16, :, :].rearrange("p a b -> p (a b)"), op=ALU.mult),
                 reads=[("qf", j), ("c4", j)], writes=[("v16", j)])
            P.op("pool", lambda e: e.tensor_tensor(out=QA[j][0][0:16, :], in0=fl(qs[j]), in1=fl(v16[j]), op=ALU.add),
                 reads=[("qs", j), ("v16", j), ("Qr", j)], writes=[("Qr", j)])
            ngrp = min(NGRP, (2 * qb + 1) // 64 + 1)
            for g_ in range(1, ngrp):
                P.op("pool", lambda e, g_=g_: e.tensor_copy(out=QA[j][g_][0:64, :], in_=QA[j][0][0:64, :]),
                     reads=[("Qr", j)], writes=[("QAq", j, g_)])

        def do_qb(qb):
            j = qb % 2
            q0 = qb * 128
            ngrp = min(NGRP, (2 * qb + 1) // 64 + 1)
            cts = [ct for ct in range(4) if qb - 16 * ct >= 0 and ct * 128 < T // 16 - 1]
            for ct in cts:
                r = qb - 16 * ct
                sb_ = (0, 1, 6, 7)[scnt[0] % 4]
                scnt[0] += 1
                P.op("pe", lambda e, ct=ct, sb_=sb_: e.matmul(ps[sb_][:], lhsT=kcT[:, ct * 128:(ct + 1) * 128], rhs=Qn[j][:],
                                                              start=True, stop=True), reads=["kcT", ("Qn", j)], writes=[bk(sb_)])
                P.op("act", lambda e, ct=ct, sb_=sb_: e.activation(out=Pc[:, ct, :], in_=ps[sb_][:], func=AF.Exp, scale=0.125),
                     reads=[bk(sb_)], writes=[("Pc", ct)])
                if r <= 16:
                    for hh in range(4):
                        P.op("dve", lambda e, ct=ct, hh=hh, r=r: e.tensor_tensor(
                            out=Pc[:, ct, hh * 128:(hh + 1) * 128], in0=Pc[:, ct, hh * 128:(hh + 1) * 128],
                            in1=msk[:, r, :], op=ALU.mult), reads=[("Pc", ct), "msk"], writes=[("Pc", ct)])
            for hh in range(4):
                bank = 2 + hh // 2
                off = (hh % 2) * 193
                for n_, ct in enumerate(cts):
                    P.op("pe", lambda e, hh=hh, ct=ct, bank=bank, off=off, n_=n_: e.matmul(
                        ps[bank][:, off:off + 193], lhsT=Pc[:, ct, hh * 128:(hh + 1) * 128], rhs=Vc[:, ct, :],
                        start=(n_ == 0), stop=(n_ == len(cts) - 1)), reads=[("Pc", ct), "Vc"], writes=[bk(bank)])
            for half in range(2):
                P.op("act", lambda e, half=half: e.copy(
                    out=ocmp[:, 2 * half:2 * half + 2, :].rearrange("p a b -> p (a b)"), in_=ps[2 + half][:, 0:386]),
                    reads=[bk(2 + half)], writes=["ocmp"])
            P.op("dve", lambda e: e.tensor_scalar(out=rd[:, 0:4], in0=ocmp[:, :, 64], scalar1=1e-30, scalar2=None, op0=ALU.add),
                 reads=["ocmp"], writes=["rd"])
            P.op("dve", lambda e: e.reciprocal(out=rd[:, 0:4], in_=rd[:, 0:4]), reads=["rd"], writes=["rd"])
            P.op("dve", lambda e: e.scalar_tensor_tensor(out=impm[:], in0=ocmp[:, 0, 65:193], scalar=rd[:, 0:1], in1=bqs[j][:],
                                                         op0=ALU.mult, op1=ALU.add), reads=["ocmp", "rd", ("bqs", j)], writes=["impm"])
            for hh in range(1, 4):
                P.op("dve", lambda e, hh=hh: e.scalar_tensor_tensor(out=impm[:], in0=ocmp[:, hh, 65:193], scalar=rd[:, hh:hh + 1],
                                                                    in1=impm[:], op0=ALU.mult, op1=ALU.add),
                     reads=["ocmp", "rd", "impm"], writes=["impm"])
            P.op("dve", lambda e: e.max(out=m8[:, 0:8], in_=impm[:]), reads=["impm"], writes=["m8"])
            P.op("dve", lambda e: e.match_replace(out=imp2[:], in_to_replace=m8[:, 0:8], in_values=impm[:], imm_value=-1e9),
                 reads=["impm", "m8"], writes=["imp2"])
            P.op("dve", lambda e: e.max(out=m8[:, 8:16], in_=imp2[:]), reads=["imp2"], writes=["m8"])
            P.op("dve", lambda e: e.tensor_scalar(out=nsel[:], in0=impm[:], scalar1=m8[:, 15:16], scalar2=1.0,
                                                  op0=ALU.is_ge, op1=ALU.subtract), reads=["impm", "m8"], writes=["nsel"])
            SB = (0, 1, 6, 7)

            def attn(kts, i, with_sel, odst, okey):
                obank = 5 if with_sel else 4
                ot_ = oT[1 if with_sel else 0]
                otk = ("oT", 1 if with_sel else 0)

                def score(kt):
                    sb_ = SB[scnt[0] % 4]
                    scnt[0] += 1
                    if with_sel:
                        g_ = kt // 32
                        P.op("pe", lambda e, kt=kt, sb_=sb_, g_=g_: e.matmul(ps[sb_][:], lhsT=KT[0][:, kt * 128:(kt + 1) * 128],
                                                                            rhs=QA[j][g_][:], start=True, stop=True),
                             reads=[("KT", 0), ("KTe", 0), ("Qr", j), ("QAq", j, g_), ("QAs", j, g_)], writes=[bk(sb_)])
                    else:
                        P.op("pe", lambda e, kt=kt, sb_=sb_: e.matmul(ps[sb_][:], lhsT=KT[1][:, kt * 128:(kt + 1) * 128], rhs=Qr[j],
                                                                      start=True, stop=True),
                             reads=[("KT", 1), ("Qr", j)], writes=[bk(sb_)])
                    pb = pcnt[0] % NPB
                    pcnt[0] += 1
                    P.op("act", lambda e, sb_=sb_, pb=pb: e.activation(out=Pb[pb][:], in_=ps[sb_][:], func=AF.Exp, scale=0.125),
                         reads=[bk(sb_)], writes=[("Pb", pb)])
                    mk = None
                    if kt == qb:
                        mk = CM
                    elif (not with_sel) and kt == qb - 4:
                        mk = AM
                    if mk is not None:
                        for hh in range(4):
                            P.op("dve", lambda e, hh=hh, pb=pb, mk=mk: e.tensor_tensor(
                                out=Pb[pb][:, hh * 128:(hh + 1) * 128], in0=Pb[pb][:, hh * 128:(hh + 1) * 128], in1=mk, op=ALU.mult),
                                reads=[("Pb", pb), "msk"], writes=[("Pb", pb)])
                    return pb

                def pv(n_, kt, pb):
                    P.op("pe", lambda e, pb=pb, kt=kt, n_=n_: e.matmul(
                        ps[obank][0:65, :], lhsT=Vb[i][:, kt, :], rhs=Pb[pb][:],
                        start=(n_ == 0), stop=(n_ == len(kts) - 1)), reads=[("Pb", pb), ("Vb", i)], writes=[bk(obank)])

                DEPTH = 2
                pbs = {}
                for n_ in range(min(DEPTH, len(kts))):
                    pbs[n_] = score(kts[n_])
                for n_, kt in enumerate(kts):
                    if n_ + DEPTH < len(kts):
                        pbs[n_ + DEPTH] = score(kts[n_ + DEPTH])
                    pv(n_, kt, pbs.pop(n_))
                P.op("act", lambda e: e.copy(out=ot_[:], in_=ps[obank][0:65, :]), reads=[bk(obank)], writes=[otk])
                for hh in range(4):
                    P.op("pe", lambda e, hh=hh: e.transpose(out=ps[obank][:, hh * 65:(hh + 1) * 65], in_=ot_[:, hh * 128:(hh + 1) * 128],
                                                            identity=ident[0:65, 0:65]), reads=[otk, "msk"], writes=[bk(obank)])
                P.op("act", lambda e: e.copy(out=odst[:].rearrange("p a b -> p (a b)"), in_=ps[obank][:, 0:260]),
                     reads=[bk(obank)], writes=[okey])

            attn(list(range(max(0, qb - 4), qb + 1)), 1, False, ow, "ow")
            for g_ in range(ngrp):
                if g_ == 0:
                    P.op("pool", lambda e: e.tensor_copy(out=nselsw[:, 64:128], in_=nsel[:, 0:64]), reads=["nsel"], writes=["nselsw"])
                    src, skey = nselsw, "nselsw"
                else:
                    src, skey = nsel, "nsel"
                P.op("pe", lambda e, src=src: e.transpose(out=ps[4][:, 0:128], in_=src[:], identity=ident),
                     reads=[skey, "msk"], writes=[bk(4)])
                for hh in range(4):
                    P.op("act", lambda e, hh=hh, g_=g_: e.activation(out=QA[j][g_][64:128, hh * 128:(hh + 1) * 128],
                                                                    in_=ps[4][64:128, 0:128], func=AF.Copy, scale=30000.0),
                         reads=[bk(4)], writes=[("QAs", j, g_)])

            attn(list(range(0, qb + 1)), 0, True, osl, "osl")

            ob = osum[j]
            for x, (src, key) in enumerate(((ocmp, "ocmp"), (osl, "osl"), (ow, "ow"))):
                if x > 0:
                    P.op("dve", lambda e, src=src, x=x: e.reciprocal(out=rd[:, 4 * x:4 * x + 4], in_=src[:, :, 64]),
                         reads=[key], writes=["rd"])
                P.op("dve", lambda e, x=x: e.tensor_tensor(
                    out=rd[:, 4 * x:4 * x + 4], in0=rd[:, 4 * x:4 * x + 4],
                    in1=gs[:, qb, :].rearrange("p (h x) -> p h x", x=3)[:, :, x], op=ALU.mult), reads=["rd", "gs"], writes=["rd"])
                for hh in range(4):
                    if x == 0:
                        P.op("dve", lambda e, hh=hh, src=src, x=x: e.tensor_scalar(
                            out=ob[:, hh, :], in0=src[:, hh, 0:64], scalar1=rd[:, 4 * x + hh:4 * x + hh + 1], scalar2=None,
                            op0=ALU.mult), reads=[key, "rd"], writes=[("osum", j)])
                    else:
                        P.op("dve", lambda e, hh=hh, src=src, x=x: e.scalar_tensor_tensor(
                            out=ob[:, hh, :], in0=src[:, hh, 0:64], scalar=rd[:, 4 * x + hh:4 * x + hh + 1], in1=ob[:, hh, :],
                            op0=ALU.mult, op1=ALU.add), reads=[key, "rd", ("osum", j)], writes=[("osum", j)])
            P.dma(o_d[q0:q0 + 128, :], ob[:].rearrange("p a b -> p (a b)"), reads=[("osum", j)])

        prep_qb(0)
        for qb in range(NQB):
            if qb + 1 < NQB:
                prep_qb(qb + 1)
            do_qb(qb)
        P.emit()
    return nc


def nsa_in_maps(P1, T, cmp_pos, k_w1, k_w2, v_w1, v_w2):
    P1 = P1.reshape(2, T, -1)
    cst = nsa_consts(T)
    NTL = T // 128
    maps = []
    posP = np.zeros((128, 16), np.float32)
    for lp in range(16):
        posP[0:64, lp] = cmp_pos[2 * lp]
        posP[64:128, lp] = cmp_pos[2 * lp + 1]
    w1 = np.stack([w.reshape(16, 128, 256).transpose(1, 0, 2) for w in (k_w1, v_w1)])
    w2 = np.stack([w.reshape(2, 128, 64).transpose(1, 0, 2) for w in (k_w2, v_w2)])
    swap = np.concatenate([np.arange(8, 16), np.arange(0, 8)])
    for c in range(NCORES):
        b, g = c // 4, c % 4
        pb = P1[b]
        q = pb[:, 256 * g:256 * g + 256].reshape(T, 4, 64)
        qT = q.transpose(2, 1, 0)
        col = lambda i: pb[:, 1024 + 256 * i + 64 * g:1024 + 256 * i + 64 * g + 64]
        k_c, v_c, k_s, v_s, k_w, v_w = [col(i) for i in range(6)]
        kT = np.stack([k_s.T, k_w.T])
        vt = np.stack([v.reshape(NTL, 128, 64).transpose(1, 0, 2) for v in (v_s, v_w)])
        KP = np.stack([np.concatenate([a[0::2].T, a[1::2].T], axis=0) for a in (k_c, v_c)])
        gates = pb[:, 2560 + 12 * g:2560 + 12 * g + 12].reshape(NTL, 128, 12).transpose(1, 0, 2)
        f = lambda a: np.ascontiguousarray(a, np.float32)
        maps.append(dict(qT=f(qT), qsw=f(qT[swap]), kT=f(kT), ksw=f(kT[:, swap]), vtok=f(vt), KP=f(KP), posP=posP,
                         w1=f(w1), w2=f(w2), gates=f(gates), C4=cst["C4"], S4=cst["S4"], Eg=cst["Eg"], msk=cst["msk"],
                         ov=cst["ov"], bq=cst["bq"]))
    return maps


def nsa_gather(results, T):
    o = np.zeros((2, T, 1024), np.float32)
    for c in range(NCORES):
        b, g = c // 4, c % 4
        o[b, :, 256 * g:256 * g + 256] = results[c]["o_out"]
    return o.reshape(2 * T, 1024)


_T = 8192
_NT = 2048


def _run(nc, maps):
    res = run_bass_kernel_spmd(nc, maps, core_ids=list(range(NCORES)))
    return res.results


def kernel(x, mix_norm, mlp_norm, w_up, w_down, final_norm,
           ev_w_in, ev_qkv_conv, ev_a_log, ev_dt_bias, ev_o_norm, ev_sc_conv, ev_w_out,
           od_w_in, od_cmp_pos, od_cmp_k_w1, od_cmp_k_w2, od_cmp_v_w1, od_cmp_v_w2, od_w_out):
    f = lambda a: np.ascontiguousarray(np.asarray(a), np.float32)
    x = f(x).reshape(2 * _T, D_MODEL)
    ident = np.eye(128, dtype=np.float32)
    rep = lambda v: np.ascontiguousarray(np.tile(f(v)[None, :], (128, 1)))
    sh = lambda a, c: np.ascontiguousarray(a[c * _NT:(c + 1) * _NT])
    shT = lambda a, c: np.ascontiguousarray(a[c * _NT:(c + 1) * _NT].T)
    cat = lambda rs, k: np.concatenate([r[k] for r in rs], axis=0)

    nc = build_dense(_NT, False, False, "inproj", 3600)
    rs = _run(nc, [dict(x=sh(x, c), ident=ident, nw_tail=rep(mix_norm[0]), w_in=f(ev_w_in[0])) for c in range(NCORES)])
    P0 = cat(rs, "p_out")
    nc = build_gdn(_T)
    rs = _run(nc, gdn_in_maps(P0, _T, f(ev_qkv_conv[0]), f(ev_a_log[0]), f(ev_dt_bias[0]), f(ev_o_norm[0]), f(ev_sc_conv[0])))
    y0 = gdn_gather(rs, _T)
    nc = build_dense(_NT, True, True, "inproj", 2608)
    rs = _run(nc, [dict(x=sh(x, c), ident=ident, yT=shT(y0, c), w_out=f(ev_w_out[0]), nw_mlp=rep(mlp_norm[0]),
                        w_up=f(w_up[0]), w_down=f(w_down[0]), nw_tail=rep(mix_norm[1]), w_in=f(od_w_in[0]))
                   for c in range(NCORES)])
    x2 = cat(rs, "x_out")
    P1 = cat(rs, "p_out")
    nc = build_nsa(_T)
    rs = _run(nc, nsa_in_maps(P1, _T, f(od_cmp_pos[0]), f(od_cmp_k_w1[0]), f(od_cmp_k_w2[0]), f(od_cmp_v_w1[0]), f(od_cmp_v_w2[0])))
    o1 = nsa_gather(rs, _T)
    nc = build_dense(_NT, True, True, "final")
    rs = _run(nc, [dict(x=sh(x2, c), ident=ident, yT=shT(o1, c), w_out=f(od_w_out[0]), nw_mlp=rep(mlp_norm[1]),
                        w_up=f(w_up[1]), w_down=f(w_down[1]), nw_tail=rep(final_norm)) for c in range(NCORES)])
    out = cat(rs, "out")
    return out.reshape(2, _T, D_MODEL).astype(np.float32)
```
